# Optimizing a Trainium2 kernel written in Bass

```python
import numpy as np
import jax, jax.numpy as jnp
from jax import lax

D_MODEL = 1024
BATCH = 8
SEQ = 2048
DEPTH = 4

CHUNK = 64
Q_BLOCK = 128
EPS = 1e-6
NEG_INF = -1e30

A_HEADS = 8
A_HEAD_DIM = 64
B_HEADS = 8
B_NOPE = 64
B_ROPE = 32
B_V = 64
B_Q_LORA = 384
B_KV_LORA = 256
ROPE_THETA = 10000.0
C_HEADS = 8
C_HEAD_DIM = 64
C_LEFT_CHUNKS = 8
C_BAND = (C_LEFT_CHUNKS + 1) * CHUNK
REL_CLIP = 256
N_BRANCH = 3
BRANCH_WIDTH = 512
D_FF = 4 * D_MODEL

A_W = A_HEADS * A_HEAD_DIM
C_W = C_HEADS * C_HEAD_DIM
B_QUP = B_HEADS * (B_NOPE + B_ROPE)
B_KVUP = B_HEADS * (B_NOPE + B_V)
IN_SIZES = (A_W, A_W, A_W, A_HEADS, B_Q_LORA, B_KV_LORA, B_ROPE, C_W, C_W, C_W, N_BRANCH * D_MODEL)
IN_WIDTH = sum(IN_SIZES)

kernel_name = "hybrid_fox_mla_chunkrel_gated_encoder"


def rms_norm(x, g):
    xf = x.astype(jnp.float32)
    y = xf * lax.rsqrt(jnp.mean(xf * xf, axis=-1, keepdims=True) + EPS)
    return (y * g.astype(jnp.float32)).astype(x.dtype)


def rope(x, pos):
    half = B_ROPE // 2
    inv = ROPE_THETA ** (-jnp.arange(half, dtype=jnp.float32) / half)
    ang = pos.astype(jnp.float32)[:, None] * inv[None, :]
    cos = jnp.cos(ang)[:, None, :]
    sin = jnp.sin(ang)[:, None, :]
    xf = x.astype(jnp.float32)
    x1, x2 = xf[..., :half], xf[..., half:]
    return jnp.concatenate([x1 * cos - x2 * sin, x2 * cos + x1 * sin], axis=-1).astype(x.dtype)


def prefix_block_attention(q, k, v, frame_causal, cum_logf=None):
    S = q.shape[1]
    scale = q.shape[-1] ** -0.5
    outs = []
    for i in range(S // Q_BLOCK):
        q0, q1 = i * Q_BLOCK, (i + 1) * Q_BLOCK
        s = jnp.einsum('bqhd,bkhd->bhqk', q[:, q0:q1], k[:, :q1],
                       preferred_element_type=jnp.float32) * scale
        qpos = jnp.arange(q0, q1)[:, None]
        kpos = jnp.arange(q1)[None, :]
        if frame_causal:
            mask = kpos <= qpos
        else:
            mask = (kpos // CHUNK) <= (qpos // CHUNK)
        if cum_logf is not None:
            s = s + (cum_logf[:, :, q0:q1, None] - cum_logf[:, :, None, :q1])
        p = jax.nn.softmax(jnp.where(mask, s, NEG_INF), axis=-1)
        outs.append(jnp.einsum('bhqk,bkhd->bqhd', p.astype(v.dtype), v[:, :q1]))
    return jnp.concatenate(outs, axis=1)


def chunk_band_attention(q, k, v, rel_bias):
    B, S, H, D = q.shape
    NC = S // CHUNK
    qc = q.reshape(B, NC, CHUNK, H, D)

    def band(t):
        tc = t.reshape(B, NC, CHUNK, H, D)
        tp = jnp.pad(tc, ((0, 0), (C_LEFT_CHUNKS, 0), (0, 0), (0, 0), (0, 0)))
        return jnp.concatenate([tp[:, j:j + NC] for j in range(C_LEFT_CHUNKS + 1)], axis=2)

    kb, vb = band(k), band(v)
    s = jnp.einsum('bnqhd,bnkhd->bhnqk', qc, kb, preferred_element_type=jnp.float32) * (D ** -0.5)
    qpos = jnp.arange(CHUNK)[:, None]
    kpos = jnp.arange(C_BAND)[None, :] - C_LEFT_CHUNKS * CHUNK
    rel = jnp.clip(qpos - kpos, -REL_CLIP, REL_CLIP) + REL_CLIP
    bias = rel_bias.astype(jnp.float32)[:, rel]
    valid = (jnp.arange(NC)[:, None] - C_LEFT_CHUNKS + jnp.arange(C_BAND)[None, :] // CHUNK) >= 0
    s = jnp.where(valid[None, None, :, None, :], s + bias[None, :, None], NEG_INF)
    p = jax.nn.softmax(s, axis=-1)
    o = jnp.einsum('bhnqk,bnkhd->bnqhd', p.astype(v.dtype), vb)
    return o.reshape(B, S, H, D)


def setup_inputs(seed: int = 0) -> dict:
    key = jax.random.key(seed)
    ks = jax.random.split(key, 20)
    f32 = jnp.float32

    def nrm(k, shape, fan_in, gain=1.0):
        return jax.random.normal(k, shape, f32) * (gain * fan_in ** -0.5)

    def gain(k, shape):
        return 1.0 + 0.02 * jax.random.normal(k, shape, f32)

    res = (2 * DEPTH) ** -0.5
    return {
        "x": jax.random.normal(ks[0], (BATCH, SEQ, D_MODEL), f32),
        "norm_mix": gain(ks[1], (DEPTH, D_MODEL)),
        "w_in": nrm(ks[2], (DEPTH, D_MODEL, IN_WIDTH), D_MODEL),
        "b_forget": jax.random.uniform(ks[3], (DEPTH, A_HEADS), f32, 1.0, 4.0),
        "b_gate": 0.1 * jax.random.normal(ks[4], (DEPTH, N_BRANCH * D_MODEL), f32),
        "qk_norm_a": gain(ks[5], (DEPTH, 2, A_HEAD_DIM)),
        "mla_q_norm": gain(ks[6], (DEPTH, B_Q_LORA)),
        "mla_kv_norm": gain(ks[7], (DEPTH, B_KV_LORA)),
        "w_q_up": nrm(ks[8], (DEPTH, B_Q_LORA, B_QUP), B_Q_LORA),
        "w_kv_up": nrm(ks[9], (DEPTH, B_KV_LORA, B_KVUP), B_KV_LORA),
        "qk_norm_b_nope": gain(ks[10], (DEPTH, 2, B_NOPE)),
        "qk_norm_b_rope": gain(ks[11], (DEPTH, 2, B_ROPE)),
        "qk_norm_c": gain(ks[12], (DEPTH, 2, C_HEAD_DIM)),
        "rel_bias": 0.1 * jax.random.normal(ks[13], (DEPTH, C_HEADS, 2 * REL_CLIP + 1), f32),
        "w_branch": nrm(ks[14], (DEPTH, N_BRANCH, BRANCH_WIDTH, D_MODEL), BRANCH_WIDTH),
        "w_out": nrm(ks[15], (DEPTH, D_MODEL, D_MODEL), D_MODEL, res),
        "norm_ffn": gain(ks[16], (DEPTH, D_MODEL)),
        "w_ff1": nrm(ks[17], (DEPTH, D_MODEL, D_FF), D_MODEL),
        "w_ff2": nrm(ks[18], (DEPTH, D_FF, D_MODEL), D_FF, res),
    }


def reference(x, norm_mix, w_in, b_forget, b_gate, qk_norm_a, mla_q_norm, mla_kv_norm,
              w_q_up, w_kv_up, qk_norm_b_nope, qk_norm_b_rope, qk_norm_c, rel_bias,
              w_branch, w_out, norm_ffn, w_ff1, w_ff2):
    B, S, _ = x.shape
    pos = jnp.arange(S)
    split_at = tuple(int(i) for i in np.cumsum(IN_SIZES)[:-1])
    for l in range(DEPTH):
        h = rms_norm(x, norm_mix[l])
        u = h @ w_in[l]
        (qa, ka, va, fa, qd, kvd, kr, qc, kc, vc, gl) = jnp.split(u, split_at, axis=-1)

        qa = rms_norm(qa.reshape(B, S, A_HEADS, A_HEAD_DIM), qk_norm_a[l, 0])
        ka = rms_norm(ka.reshape(B, S, A_HEADS, A_HEAD_DIM), qk_norm_a[l, 1])
        va = va.reshape(B, S, A_HEADS, A_HEAD_DIM)
        log_f = jax.nn.log_sigmoid(fa.astype(jnp.float32) + b_forget[l].astype(jnp.float32))
        cum_logf = jnp.cumsum(log_f, axis=1).transpose(0, 2, 1)
        ya = prefix_block_attention(qa, ka, va, True, cum_logf).reshape(B, S, BRANCH_WIDTH)

        qb = (rms_norm(qd, mla_q_norm[l]) @ w_q_up[l]).reshape(B, S, B_HEADS, B_NOPE + B_ROPE)
        kvb = (rms_norm(kvd, mla_kv_norm[l]) @ w_kv_up[l]).reshape(B, S, B_HEADS, B_NOPE + B_V)
        q_nope = rms_norm(qb[..., :B_NOPE], qk_norm_b_nope[l, 0])
        q_rope = rope(rms_norm(qb[..., B_NOPE:], qk_norm_b_rope[l, 0]), pos)
        k_nope = rms_norm(kvb[..., :B_NOPE], qk_norm_b_nope[l, 1])
        vb = kvb[..., B_NOPE:]
        k_rope = rope(rms_norm(kr.reshape(B, S, 1, B_ROPE), qk_norm_b_rope[l, 1]), pos)
        k_rope = jnp.broadcast_to(k_rope, (B, S, B_HEADS, B_ROPE))
        q_mla = jnp.concatenate([q_nope, q_rope], axis=-1)
        k_mla = jnp.concatenate([k_nope, k_rope], axis=-1)
        yb = prefix_block_attention(q_mla, k_mla, vb, False).reshape(B, S, BRANCH_WIDTH)

        qc = rms_norm(qc.reshape(B, S, C_HEADS, C_HEAD_DIM), qk_norm_c[l, 0])
        kc = rms_norm(kc.reshape(B, S, C_HEADS, C_HEAD_DIM), qk_norm_c[l, 1])
        vc = vc.reshape(B, S, C_HEADS, C_HEAD_DIM)
        yc = chunk_band_attention(qc, kc, vc, rel_bias[l]).reshape(B, S, BRANCH_WIDTH)

        y = jnp.stack([ya, yb, yc], axis=2)
        proj = jnp.einsum('bsnc,ncd->bsnd', y, w_branch[l])
        gates = jax.nn.sigmoid(gl + b_gate[l]).reshape(B, S, N_BRANCH, D_MODEL)
        merged = jnp.sum(gates * proj, axis=2)
        x = x + merged @ w_out[l]

        h = rms_norm(x, norm_ffn[l])
        x = x + jnp.square(jax.nn.relu(h @ w_ff1[l])) @ w_ff2[l]
    return x
```

```python
import numpy as np
from contextlib import ExitStack
import concourse.bass as bass
import concourse.mybir as mybir
from concourse.bass_utils import run_bass_kernel_spmd

F32 = mybir.dt.float32
BF16 = mybir.dt.bfloat16
AF = mybir.ActivationFunctionType
ALU = mybir.AluOpType
AX = mybir.AxisListType

S = 2048
D = 1024
NT = 16
DEPTH = 4
EPS = 1e-6
INW = 6824
C_QA, C_KA, C_VA, C_FA, C_QD, C_KVD, C_KR, C_QC, C_KC, C_VC, C_G = 0, 512, 1024, 1536, 1544, 1928, 2184, 2216, 2728, 3240, 3752


class Prog:
    ENG = ("pe", "act", "dve", "pool", "sp")

    def __init__(self):
        self.ops = {e: [] for e in self.ENG}
        self.state = {}
        self.known = {e: {} for e in self.ENG}
        self.flag = {e: set() for e in self.ENG}
        self.semcnt = {}

    def _add(self, eng, fn, R, W, dma_sem):
        need = {}
        for k in R:
            st = self.state.get(k)
            if st and st[0] is not None:
                t = st[0]
                need[t[0]] = max(need.get(t[0], -1), t[1])
        for k in W:
            st = self.state.get(k)
            if st:
                if st[0] is not None:
                    t = st[0]
                    need[t[0]] = max(need.get(t[0], -1), t[1])
                for tk, tv in st[1].items():
                    need[tk] = max(need.get(tk, -1), tv)
        kn = self.known[eng]
        waits = []
        for tk, tv in need.items():
            if dma_sem is None and eng == "pe" and tk == ("E", "pe"):
                continue
            if kn.get(tk, -1) >= tv:
                continue
            kn[tk] = tv
            waits.append((tk, tv))
            if tk[0] == "E":
                self.flag[tk[1]].add(tv)
        idx = len(self.ops[eng])
        if dma_sem is None:
            tok = (("E", eng), idx)
        else:
            c = self.semcnt.get(dma_sem, 0) + 1
            self.semcnt[dma_sem] = c
            tok = (("S", dma_sem), c)
        self.ops[eng].append((fn, waits, dma_sem))
        for k in R:
            if k in W:
                continue
            st = self.state.setdefault(k, [None, {}])
            st[1][tok[0]] = max(st[1].get(tok[0], -1), tok[1])
        for k in W:
            self.state[k] = [tok, {}]
        return tok

    def op(self, eng, fn, R=(), W=()):
        return self._add(eng, fn, list(R), list(W), None)

    def dma(self, q, fn, R=(), W=(), sem=None):
        return self._add(q, fn, list(R), list(W), sem)

    def barrier(self):
        last = {}
        for e in self.ENG:
            n = len(self.ops[e])
            for i in range(n - 1, -1, -1):
                if self.ops[e][i][2] is None and self.ops[e][i][0] is not None:
                    last[("E", e)] = i
                    break
        for s, c in self.semcnt.items():
            last[("S", s)] = c
        for e in self.ENG:
            kn = self.known[e]
            waits = []
            for tk, tv in last.items():
                if kn.get(tk, -1) >= tv:
                    continue
                kn[tk] = tv
                waits.append((tk, tv))
                if tk[0] == "E":
                    self.flag[tk[1]].add(tv)
            if waits:
                self.ops[e].append((None, waits, None))

    def emit(self, nc, stack):
        rank = {}
        for e in self.ENG:
            rank[e] = {idx: r + 1 for r, idx in enumerate(sorted(self.flag[e]))}
        BS = 2000
        esem = {e: [stack.enter_context(nc.semaphore("es_%s_%d" % (e, b))) for b in range(len(rank[e]) // BS + 1)] for e in self.ENG}
        dsem = {s: stack.enter_context(nc.semaphore("ds_" + str(i))) for i, s in enumerate(self.semcnt)}
        block = stack.enter_context(nc.Block())

        def run(e, h):
            for idx, (fn, waits, ds) in enumerate(self.ops[e]):
                for tk, tv in waits:
                    if tk[0] == "E":
                        r_ = rank[tk[1]][tv] - 1
                        h.wait_ge(esem[tk[1]][r_ // BS], r_ % BS + 1)
                    else:
                        h.wait_ge(dsem[tk[1]], tv * 16)
                if fn is None:
                    continue
                ins = fn(h)
                if ds is not None:
                    ins.then_inc(dsem[ds], 16)
                elif idx in rank[e]:
                    ins.then_inc(esem[e][(rank[e][idx] - 1) // BS], 1)

        block.tensor(lambda h: run("pe", h))
        block.scalar(lambda h: run("act", h))
        block.vector(lambda h: run("dve", h))
        block.gpsimd(lambda h: run("pool", h))
        block.sync(lambda h: run("sp", h))


def build(nlayers=DEPTH):
    nc = bass.Bass("TRN2", target_bir_lowering=False)
    P = Prog()
    st = ExitStack()

    def din(name, shape):
        return nc.dram_tensor(name, list(shape), F32, kind="ExternalInput")

    x_d = din("x", [S, D]).ap()
    norm_mix = din("norm_mix", [DEPTH, D]).ap()
    w_in = din("w_in", [DEPTH, D, INW]).ap()
    b_forget = din("b_forget", [DEPTH, 8]).ap()
    b_gate = din("b_gate", [DEPTH, 3072]).ap()
    qk_norm_a = din("qk_norm_a", [DEPTH, 2, 64]).ap()
    mla_q_norm = din("mla_q_norm", [DEPTH, 384]).ap()
    mla_kv_norm = din("mla_kv_norm", [DEPTH, 256]).ap()
    w_q_up = din("w_q_up", [DEPTH, 384, 768]).ap()
    w_kv_up = din("w_kv_up", [DEPTH, 256, 1024]).ap()
    qk_norm_b_nope = din("qk_norm_b_nope", [DEPTH, 2, 64]).ap()
    qk_norm_b_rope = din("qk_norm_b_rope", [DEPTH, 2, 32]).ap()
    qk_norm_c = din("qk_norm_c", [DEPTH, 2, 64]).ap()
    relx = din("rel_ext", [DEPTH * 8, 128, 640]).ap()
    w_branch = din("w_branch", [DEPTH, 3, 512, D]).ap()
    w_out = din("w_out", [DEPTH, D, D]).ap()
    norm_ffn = din("norm_ffn", [DEPTH, D]).ap()
    w_ff1 = din("w_ff1", [DEPTH, D, 4096]).ap()
    w_ff2 = din("w_ff2", [DEPTH, 4096, D]).ap()
    cs_d = din("cs_tab", [S, 32]).ap()
    y_d = nc.dram_tensor("y", [S, D], F32, kind="ExternalOutput").ap()
    dbg_d = nc.dram_tensor("dbg", [128, 24576], BF16, kind="ExternalOutput").ap() if DBG else None

    def sb(name, shape, dt):
        return st.enter_context(nc.sbuf_tensor(name, list(shape), dt))

    xs = sb("xs", [128, NT, D], F32)
    hT = sb("hT", [128, 8, S], BF16)
    ident_bf = sb("ident_bf", [128, 128], BF16)
    ident_f = sb("ident_f", [128, 128], F32)
    blockones = sb("blockones", [128, 128], BF16)
    ones_bf = sb("ones_bf", [128, 128], BF16)
    maskA = sb("maskA", [128, 128], BF16)
    maskB = sb("maskB", [128, 128], BF16)
    U_f = sb("U_f", [128, 128], F32)
    ones_f = sb("ones_f", [128, 128], F32)
    validC = sb("validC", [128, 640], BF16)
    vecT = sb("vecT", [128, 132], F32)
    bf_rep = sb("bf_rep", [128, 32], F32)
    gbn = sb("gbn", [128, 512], F32)
    gbr = sb("gbr", [128, 256], F32)
    cs = sb("cs", [128, NT, 32], F32)
    ssq = sb("ssq", [128, NT], F32)
    rstd = sb("rstd", [128, NT], F32)
    small = sb("small", [128, 64], F32)
    cst = sb("cst", [128, 4], F32)
    AR_N = 52000
    arena = sb("arena", [128, AR_N], BF16)
    psb = [st.enter_context(nc.psum_tensor("ps%d" % b, [128, 512], F32)) for b in range(8)]

    def bv(off, n):
        return arena[:, off:off + n]

    def fv(off, n):
        return arena[:, off:off + 2 * n].bitcast(F32)

    def psbf(b):
        return psb[b][:].bitcast(BF16)

    Z0 = 24576
    yT = bv(0, 24576).rearrange("p (n c t) -> p n c t", n=3, c=4)

    def mm(out, lhsT, rhs, start, stop):
        return lambda e: e.matmul(out, lhsT, rhs, start=start, stop=stop)

    def tr(out, in_, ident):
        return lambda e: e.transpose(out, in_, ident)

    def act(out, in_, func, bias=None, scale=None, accum_out=None):
        kw = {}
        if bias is not None:
            kw["bias"] = bias
        if scale is not None:
            kw["scale"] = scale
        if accum_out is not None:
            kw["accum_out"] = accum_out
        return lambda e: e.activation(out=out, in_=in_, func=func, **kw)

    def tt(out, in0, in1, op):
        return lambda e: e.tensor_tensor(out=out, in0=in0, in1=in1, op=op)

    def ts(out, in0, s1, s2, op0, op1=None):
        if op1 is None:
            return lambda e: e.tensor_single_scalar(out=out, in_=in0, scalar=s1, op=op0)
        return lambda e: e.tensor_scalar(out=out, in0=in0, scalar1=s1, scalar2=s2, op0=op0, op1=op1)

    def stt(out, in0, scalar, in1, op0, op1):
        return lambda e: e.scalar_tensor_tensor(out=out, in0=in0, scalar=scalar, in1=in1, op0=op0, op1=op1)

    def cp(out, in_):
        return lambda e: e.tensor_copy(out=out, in_=in_)

    def rsqrt_small(dst, src, mul, eps, Rk, Wk):
        P.op("dve", ts(dst, src, mul, eps, ALU.mult, ALU.add), R=Rk, W=Wk)
        P.op("act", act(dst, dst, AF.Ln), R=Wk, W=Wk)
        P.op("act", act(dst, dst, AF.Exp, scale=-0.5), R=Wk, W=Wk)

    def red(out, in_):
        return lambda e: e.tensor_reduce(out=out, in_=in_, axis=AX.X, op=ALU.add)

    def ms(ap, v):
        return lambda e: e.memset(ap, v)

    def dmaf(out, in_):
        return lambda e: e.dma_start(out=out, in_=in_)

    def wload(dst, src2d, key, R=(), q="pool"):
        P.dma(q, dmaf(dst, src2d.rearrange("(k p) c -> p k c", p=128)), R=R, W=[key], sem=str(key))

    P.op("pool", ms(ones_f[:], 1.0), W=["ones_f"])
    P.op("pool", lambda e: e.affine_select(out=ident_f[:], in_=ones_f[:], pattern=[[-1, 128]], compare_op=ALU.is_equal,
                                           fill=0.0, base=0, channel_multiplier=1), R=["ones_f"], W=["ident_f"])
    P.op("pool", lambda e: e.affine_select(out=U_f[:], in_=ones_f[:], pattern=[[1, 128]], compare_op=ALU.is_ge,
                                           fill=0.0, base=0, channel_multiplier=-1), R=["ones_f"], W=["U_f"])
    P.op("pool", cp(ident_bf[:], ident_f[:]), R=["ident_f"], W=["ident_bf"])
    P.op("pool", ts(maskA[:], U_f[:], 60000.0, -60000.0, ALU.mult, ALU.add), R=["U_f"], W=["maskA"])
    P.op("pool", ms(ones_bf[:], 1.0), W=["ones_bf"])
    P.op("pool", ms(cst[:, 0:1], 1.0), W=["cst"])
    P.op("pool", ms(cst[:, 1:2], 64 * EPS), W=["cst"])
    P.op("pool", ms(cst[:, 2:3], 384 * EPS), W=["cst"])
    P.op("pool", ms(cst[:, 3:4], 256 * EPS), W=["cst"])
    P.op("pool", ms(blockones[:], 0.0), W=["blockones"])
    P.op("pool", ms(blockones[0:64, 0:64], 1.0), W=["blockones"])
    P.op("pool", ms(blockones[64:128, 64:128], 1.0), W=["blockones"])
    P.op("pool", ms(maskB[:], 1.0), W=["maskB"])
    P.op("pool", ms(maskB[64:128, 0:64], 0.0), W=["maskB"])
    P.op("pool", ms(validC[:], 0.0), W=["validC"])
    P.op("pool", ms(validC[0:64, 0:576], 1.0), W=["validC"])
    P.op("pool", ms(validC[64:128, 64:640], 1.0), W=["validC"])

    stage1 = fv(Z0, 128)
    stage2 = fv(Z0 + 256, 128)
    P.dma("sp", dmaf(stage1[0:96, :], b_gate.rearrange("l (c p) -> (l c) p", p=128)), W=["stage1"], sem="stage1")
    qa2 = qk_norm_a.rearrange("l q d -> (l q) d")
    qc2 = qk_norm_c.rearrange("l q d -> (l q) d")
    P.dma("sp", dmaf(stage2[0:8, 0:64], qa2), W=["stage2a"], sem="stage2")
    P.dma("sp", dmaf(stage2[0:8, 64:128], qa2), W=["stage2b"], sem="stage2")
    P.dma("sp", dmaf(stage2[8:16, 0:64], qc2), W=["stage2c"], sem="stage2")
    P.dma("sp", dmaf(stage2[8:16, 64:128], qc2), W=["stage2d"], sem="stage2")
    P.dma("sp", dmaf(stage2[16:28, :], mla_q_norm.rearrange("l (c p) -> (l c) p", p=128)), W=["stage2e"], sem="stage2")
    P.dma("sp", dmaf(stage2[28:36, :], mla_kv_norm.rearrange("l (c p) -> (l c) p", p=128)), W=["stage2f"], sem="stage2")
    P.op("pe", tr(psb[0][:, 0:96], stage1[0:96, :], ident_f[0:96, 0:96]), R=["stage1", "ident_f"], W=[("ps", 0)])
    P.op("pe", tr(psb[1][:, 0:36], stage2[0:36, :], ident_f[0:36, 0:36]),
         R=["stage2a", "stage2b", "stage2c", "stage2d", "stage2e", "stage2f", "ident_f"], W=[("ps", 1)])
    P.op("dve", cp(vecT[:, 0:96], psb[0][:, 0:96]), R=[("ps", 0)], W=["vecT"])
    P.op("dve", ts(vecT[:, 96:112], psb[1][:, 0:16], 8.0, None, ALU.mult), R=[("ps", 1)], W=["vecT"])
    P.op("dve", ts(vecT[:, 112:124], psb[1][:, 16:28], float(np.sqrt(384.0)), None, ALU.mult), R=[("ps", 1)], W=["vecT"])
    P.op("dve", ts(vecT[:, 124:132], psb[1][:, 28:36], 16.0, None, ALU.mult), R=[("ps", 1)], W=["vecT"])
    P.dma("sp", dmaf(bf_rep[:], b_forget.rearrange("l h -> (l h)").partition_broadcast(128)), W=["bf_rep"], sem="c1")
    P.dma("sp", dmaf(gbn[:], qk_norm_b_nope.rearrange("l q d -> (l q d)").partition_broadcast(128)), W=["gbn"], sem="c2")
    P.dma("sp", dmaf(gbr[:], qk_norm_b_rope.rearrange("l q d -> (l q d)").partition_broadcast(128)), W=["gbr"], sem="c3")
    P.dma("sp", dmaf(cs[:], cs_d.rearrange("(t p) c -> p t c", p=128)), W=["cs"], sem="c4")
    for i in range(NT):
        P.dma("sp", dmaf(xs[:, i, :], x_d[i * 128:(i + 1) * 128, :]), W=[("x", i)], sem="x%d" % i)
    P.barrier()

    def norm_phase(gain_row):
        gnorm = fv(Z0, 1024)
        htok = [bv(Z0 + 2048 + k * 1024, 1024) for k in range(2)]
        junk = bv(Z0 + 4096, 1024)
        P.dma("sp", dmaf(gnorm, gain_row.partition_broadcast(128)), W=["gnorm"], sem="gnorm")
        for i in range(NT):
            P.op("act", act(junk, xs[:, i, :], AF.Square, accum_out=ssq[:, i:i + 1]), R=[("x", i)], W=["junk", "ssq"])
        rsqrt_small(rstd[:], ssq[:], 1.0 / D, EPS, ["ssq"], ["rstd"])
        for i in range(NT):
            P.op("dve", stt(htok[i % 2], xs[:, i, :], rstd[:, i:i + 1], gnorm, ALU.mult, ALU.mult),
                 R=[("x", i), "rstd", "gnorm"], W=[("htok", i % 2)])
            bank = 6 + (i % 2)
            ptv = psbf(bank).rearrange("p (a b) -> p a b", a=8)
            for kc in range(8):
                P.op("pe", tr(ptv[:, kc, :], htok[i % 2][:, kc * 128:(kc + 1) * 128], ident_bf[:]),
                     R=[("htok", i % 2), "ident_bf"], W=[("ps", bank)])
            P.op("act", cp_act(hT[:, :, i * 128:(i + 1) * 128], ptv), R=[("ps", bank)], W=[("hT", i // 4)])
        P.barrier()

    def cp_act(out, in_):
        return lambda e: e.activation(out=out, in_=in_, func=AF.Copy)

    unit_ctr = [0]

    def proj_norm(l, col0, nchunk, onesmat, nfeat, gcol0, outs, okey, wslots, sqb, rsb, gstep=0):
        for c in range(nchunk):
            wload(wslots[c][0], w_in[l, :, col0 + c * 128: col0 + (c + 1) * 128], wslots[c][1])
        for tc in range(4):
            u = unit_ctr[0] % 2
            unit_ctr[0] += 1
            b0 = 4 * u
            tsl = slice(tc * 512, (tc + 1) * 512)
            for c in range(nchunk):
                for kc in range(8):
                    P.op("pe", mm(psb[b0 + c][:], wslots[c][0][:, kc, :], hT[:, kc, tsl], kc == 0, kc == 7),
                         R=[wslots[c][1], ("hT", tc)], W=[("ps", b0 + c)])
                P.op("act", act(sqb[u][c], psb[b0 + c][:], AF.Square), R=[("ps", b0 + c)], W=[("sq", u, c)])
            for c in range(nchunk):
                P.op("pe", mm(psb[b0 + 3][:], onesmat[:], sqb[u][c], c == 0, c == nchunk - 1),
                     R=[("sq", u, c), "ones_bf", "blockones"], W=[("ps", b0 + 3)])
            ecol = {64: 1, 384: 2, 256: 3}[nfeat]
            P.op("act", act(rsb[u], psb[b0 + 3][:], AF.Ln, bias=cst[:, ecol:ecol + 1]), R=[("ps", b0 + 3), "cst"], W=[("rs", u)])
            P.op("act", act(rsb[u], rsb[u], AF.Exp, scale=-0.5), R=[("rs", u)], W=[("rs", u)])
            for c in range(nchunk):
                P.op("dve", stt(outs[c][:, tsl], psb[b0 + c][:], vecT[:, gcol0 + gstep * c:gcol0 + gstep * c + 1], rsb[u], ALU.mult, ALU.mult),
                     R=[("ps", b0 + c), ("rs", u), "vecT"], W=[okey])

    attn_ctr = [0]

    def attention(kind, hf, nbr, kslice, qslice, qkey, vaug, Pt, ytile, scale, biasA=None, bias_prep=None, EBrev=None, qprep=None, qterm=None):
        for i in range(NT):
            if qprep is not None:
                qprep(i)
            if bias_prep is not None:
                bias_prep(i)
            js = list(range(max(0, i - 4), i + 1)) if kind == "C" else list(range(0, i + 1))
            groups = [js[a:a + 4] for a in range(0, len(js), 4)]
            ob = 3 + (i % 2)
            pov = psb[ob][:, 0:260].rearrange("p (h c) -> p h c", h=4)
            items = [(hh, grp) for hh in range(4) for grp in groups]
            pend = []

            def second(hh, grp, sbk, i=i, js=js, ob=ob, pov=pov):
                n = len(grp)
                if kind == "A":
                    h = 4 * hf + hh
                    for jj, j in enumerate(grp):
                        P.op("act", act(Pt[sbk][:, jj * 128:(jj + 1) * 128], psb[sbk][:, jj * 128:(jj + 1) * 128], AF.Exp,
                                        bias=biasA[i % 2][:, j, h:h + 1], scale=scale),
                             R=[("ps", sbk), ("biasA", i % 2)], W=[("Pt", sbk)])
                else:
                    P.op("act", act(Pt[sbk][:, 0:n * 128], psb[sbk][:, 0:n * 128], AF.Exp, scale=scale),
                         R=[("ps", sbk)], W=[("Pt", sbk)])
                if kind == "C":
                    d0 = 4 - (i - grp[0])
                    P.op(MASK_ENG, tt(Pt[sbk][:, 0:n * 128], Pt[sbk][:, 0:n * 128], EBrev[:, hh, d0 * 128:(d0 + n) * 128], ALU.mult),
                         R=["EBrev"], W=[("Pt", sbk)])
                elif i in grp and kind == "B":
                    jj = grp.index(i)
                    mk = maskB
                    P.op(MASK_ENG, tt(Pt[sbk][:, jj * 128:(jj + 1) * 128], Pt[sbk][:, jj * 128:(jj + 1) * 128], mk[:], ALU.mult),
                         R=["maskA", "maskB"], W=[("Pt", sbk)])
                for jj, j in enumerate(grp):
                    P.op("pe", mm(pov[:, hh, :], Pt[sbk][:, jj * 128:(jj + 1) * 128], vaug[:, j, hh, :], j == js[0], j == js[-1]),
                         R=[("Pt", sbk), "v"], W=[("ps", ob)])

            for (hh, grp) in items:
                sbk = attn_ctr[0] % 3
                attn_ctr[0] += 1
                for jj, j in enumerate(grp):
                    osl = psb[sbk][:, jj * 128:(jj + 1) * 128]
                    P.op("pe", mm(osl, kslice(hh, j), qslice(hh, i), True, qterm is None),
                         R=["kT", qkey(i)], W=[("ps", sbk)])
                    if qterm is not None:
                        rq_ap, rq_key = qterm(i, hh)
                        P.op("pe", mm(osl, ones_bf[:], rq_ap, False, j != i), R=[rq_key, "ones_bf"], W=[("ps", sbk)])
                        if j == i:
                            P.op("pe", mm(osl, ident_bf[:], maskA[:], False, True), R=["maskA", "ident_bf"], W=[("ps", sbk)])
                pend.append((hh, grp, sbk))
                if len(pend) > 2:
                    second(*pend.pop(0))
            while pend:
                second(*pend.pop(0))
            rec = small[:, (i % 2) * 4:(i % 2) * 4 + 4]
            P.op("dve", (lambda rec, pov: lambda e: e.reciprocal(out=rec.unsqueeze(2), in_=pov[:, :, 64:65]))(rec, pov),
                 R=[("ps", ob)], W=[("rec", i % 2)])
            yt = ytile[i % 2]
            P.op("dve", tt(yt.rearrange("p (h c) -> p h c", h=4), pov[:, :, 0:64], rec.unsqueeze(2).broadcast_to([128, 4, 64]), ALU.mult),
                 R=[("ps", ob), ("rec", i % 2)], W=[("ytile", i % 2)])
            ptv = psbf(5).rearrange("p (a b) -> p a b", a=8)
            for c in range(2):
                P.op("pe", tr(ptv[:, c, :], yt[:, c * 128:(c + 1) * 128], ident_bf[:]), R=[("ytile", i % 2), "ident_bf"], W=[("ps", 5)])
            P.op("act", cp_act(yT[:, nbr, 2 * hf:2 * hf + 2, i * 128:(i + 1) * 128], ptv[:, 0:2, :]), R=[("ps", 5)], W=[("yT", nbr)])

    def v_proj(l, col0, hf, wv_unused, vaug):
        for sh in range(2):
            wsl, wkey = wqk_ref[0][sh]
            wload(wsl, w_in[l, :, col0 + hf * 256 + sh * 128: col0 + hf * 256 + (sh + 1) * 128], wkey)
        for i in range(NT):
            bank = 6 + (i % 2)
            for sh in range(2):
                wsl, wkey = wqk_ref[0][sh]
                for kc in range(8):
                    P.op("pe", mm(psb[bank][:, sh * 128:(sh + 1) * 128], hT[:, kc, i * 128:(i + 1) * 128], wsl[:, kc, :], kc == 0, kc == 7),
                         R=[wkey, ("hT", i // 4)], W=[("ps", bank)])
            P.op("act", cp_act(vaug[:, i, :, 0:64], psb[bank][:, 0:256].rearrange("p (h c) -> p h c", h=4)), R=[("ps", bank)], W=["v"])

    wqk_ref = [None]

    z = Z0
    ZK = z
    kT_ac = bv(z, 4096).rearrange("p (m t) -> p m t", m=2)
    kT_b = bv(z, 8192).rearrange("p (m t) -> p m t", m=4)
    z += 8192
    ZQ = z
    qT_ac = bv(z, 4096).rearrange("p (m t) -> p m t", m=2)
    qTt = [bv(z + k * 512, 512).rearrange("p (h t) -> p h t", h=4) for k in range(2)]
    z += 4096
    vaug = bv(z, 4160).rearrange("p (i h c) -> p i h c", i=NT, h=4)
    z += 4160
    sqb = [[bv(z + (u * 3 + c) * 512, 512) for c in range(3)] for u in range(2)]
    Pt = [bv(z + k * 512, 512) for k in range(3)]
    ytile = [bv(z + 1536 + k * 256, 256) for k in range(2)]
    z += 3072
    rsb = [fv(z + u * 1024, 512) for u in range(2)]
    z += 2048
    wqk = [(bv(z + k * 1024, 1024).rearrange("p (k c) -> p k c", k=8), ("wqk", k)) for k in range(4)]
    z += 4096
    wv = None
    wqk_ref[0] = wqk
    assert z <= AR_N, z

    def set_vones():
        P.op("pool", ms(vaug[:, :, :, 64:65], 1.0), W=["v"])

    def phase_A(l):
        z = ZK + 4096
        wf = bv(z, 64).rearrange("p (k c) -> p k c", k=8); z += 64
        zt = fv(z, 128); z += 256
        lp = fv(z, 128); z += 256
        Lp = fv(z, 128).rearrange("p (i h) -> p i h", i=NT); z += 256
        PTt = fv(z, 128).rearrange("p (i h) -> p i h", i=NT); z += 256
        biasA = [fv(z + k * 256, 128).rearrange("p (i h) -> p i h", i=NT) for k in range(2)]; z += 512
        rq_bf = bv(z, 128); z += 128
        rqm = [bv(z + k * 512, 512).rearrange("p (h t) -> p h t", h=4) for k in range(2)]; z += 1024
        assert z <= ZK + 8192
        loc = zt
        set_vones()
        for k_ in range(2):
            P.op("pool", ms(rqm[k_], 0.0), W=[("rqm", k_)])
        wload(wf, w_in[l, :, C_FA:C_FA + 8], "wf")
        for i in range(NT):
            for kc in range(8):
                P.op("pe", mm(psb[5][:, i * 8:(i + 1) * 8], hT[:, kc, i * 128:(i + 1) * 128], wf[:, kc, :], kc == 0, kc == 7),
                     R=["wf", ("hT", i // 4)], W=[("ps", 5)])
        ztv = zt.rearrange("p (i h) -> p i h", i=NT)
        P.op("dve", tt(ztv, psb[5][:, 0:128].rearrange("p (i h) -> p i h", i=NT),
                       bf_rep[:, l * 8:(l + 1) * 8].unsqueeze(1).broadcast_to([128, NT, 8]), ALU.add), R=[("ps", 5), "bf_rep"], W=["zt"])
        P.op("act", act(zt, zt, AF.Exp, scale=-1.0), R=["zt"], W=["zt"])
        P.op("act", act(lp, zt, AF.Ln, bias=cst[:, 0:1]), R=["zt", "cst"], W=["lp"])
        lpv = lp.rearrange("p (i h) -> p i h", i=NT)
        for i in range(NT):
            for ip in range(i):
                P.op("pe", mm(psb[6][:, i * 8:(i + 1) * 8], ones_f[:], lpv[:, ip, :], ip == 0, ip == i - 1), R=["lp", "ones_f"], W=[("ps", 6)])
            for ip in range(i):
                P.op("pe", mm(psb[7][:, i * 8:(i + 1) * 8], ones_f[:], lpv[:, ip, :], ip == 0, False), R=["lp", "ones_f"], W=[("ps", 7)])
            P.op("pe", mm(psb[7][:, i * 8:(i + 1) * 8], U_f[:], lpv[:, i, :], i == 0, True), R=["lp", "U_f"], W=[("ps", 7)])
        P.op("dve", ms(PTt[:, 0, :], 0.0), W=["PT"])
        P.op("dve", cp(PTt[:, 1:NT, :], psb[6][:, 8:128].rearrange("p (i h) -> p i h", h=8)), R=[("ps", 6)], W=["PT"])
        P.op("dve", cp(Lp, psb[7][:, 0:128].rearrange("p (i h) -> p i h", h=8)), R=[("ps", 7)], W=["Lp"])
        P.op("dve", tt(loc.rearrange("p (i h) -> p i h", i=NT), Lp, PTt, ALU.subtract), R=["Lp", "PT", "zt"], W=["loc"])
        P.op("dve", ts(rq_bf, loc, -8.0, None, ALU.mult), R=["loc"], W=["rq_bf"])

        def bias_prep(i):
            P.op("dve", tt(biasA[i % 2][:, 0:i + 1, :], Lp[:, 0:i + 1, :], PTt[:, i:i + 1, :].broadcast_to([128, i + 1, 8]), ALU.subtract),
                 R=["Lp", "PT"], W=[("biasA", i % 2)])

        for hf in range(2):
            for c_ in range(2):
                proj_norm(l, C_KA + hf * 256 + c_ * 128, 1, blockones, 64, 96 + 2 * l + 1, [kT_ac[:, c_, :]], "kT", [wqk[c_]], sqb, rsb)
            for c_ in range(2):
                proj_norm(l, C_QA + hf * 256 + c_ * 128, 1, blockones, 64, 96 + 2 * l, [qT_ac[:, c_, :]], "qT", [wqk[2 + c_]], sqb, rsb)
            v_proj(l, C_VA, hf, wv, vaug)
            P.barrier()
            for i in range(NT):
                bank = 6 + i // 8
                ptv = psbf(bank).rearrange("p (a b) -> p a b", a=8)
                P.op("pe", tr(ptv[0:8, i % 8, :], rq_bf[:, i * 8:(i + 1) * 8], ident_bf[:]), R=["rq_bf", "ident_bf"], W=[("ps", bank)])

            def bias_prep2(i, hf=hf):
                bias_prep(i)
                bank = 6 + i // 8
                ptv = psbf(bank).rearrange("p (a b) -> p a b", a=8)
                P.op("dve", tt(rqm[i % 2][0:8, :, :], ptv[0:8, i % 8:i % 8 + 1, :].broadcast_to([8, 4, 128]),
                               ident_f[0:8, 4 * hf:4 * hf + 4].unsqueeze(2).broadcast_to([8, 4, 128]), ALU.mult),
                     R=[("ps", bank), "ident_f"], W=[("rqm", i % 2)])

            attention("A", hf, 0,
                      lambda hh, j: kT_ac[(hh % 2) * 64:(hh % 2) * 64 + 64, hh // 2, j * 128:(j + 1) * 128],
                      lambda hh, i: qT_ac[(hh % 2) * 64:(hh % 2) * 64 + 64, hh // 2, i * 128:(i + 1) * 128],
                      lambda i: "qT", vaug, Pt, ytile, 0.125, biasA=biasA, bias_prep=bias_prep2,
                      qterm=lambda i, hh: (rqm[i % 2][:, hh, :], ("rqm", i % 2)))
            P.barrier()
            if DBG == 2 and l == 0 and hf == 1:
                P.dma("sp", dmaf(dbg_d, arena[:, Z0:Z0 + 24576]), sem="dbg")
                P.barrier()

    def phase_C(l):
        z = ZK + 4096
        EBrev = bv(z, 2560).rearrange("p (h c) -> p h c", h=4); z += 2560
        Tst = [fv(z, 640), fv(z, 640)]; z += 1280
        assert z <= ZK + 8192
        set_vones()
        for hf in range(2):
            for hh in range(4):
                h = 4 * hf + hh
                src = relx[l * 8 + h, :, :]
                P.dma("sp", dmaf(Tst[0], src), W=[("Tst", 0)], sem="Tst0")
                P.op("act", act(Tst[0], Tst[0], AF.Exp), R=[("Tst", 0)], W=[("Tst", 0)])
                for d in range(5):
                    P.op("dve", tt(EBrev[:, hh, (4 - d) * 128:(5 - d) * 128], Tst[0][:, d * 128:(d + 1) * 128], validC[:, d * 128:(d + 1) * 128], ALU.mult),
                         R=[("Tst", 0), "validC"], W=["EBrev"])
            for c_ in range(2):
                proj_norm(l, C_KC + hf * 256 + c_ * 128, 1, blockones, 64, 104 + 2 * l + 1, [kT_ac[:, c_, :]], "kT", [wqk[c_]], sqb, rsb)
            for c_ in range(2):
                proj_norm(l, C_QC + hf * 256 + c_ * 128, 1, blockones, 64, 104 + 2 * l, [qT_ac[:, c_, :]], "qT", [wqk[2 + c_]], sqb, rsb)
            v_proj(l, C_VC, hf, wv, vaug)
            P.barrier()
            attention("C", hf, 2,
                      lambda hh, j: kT_ac[(hh % 2) * 64:(hh % 2) * 64 + 64, hh // 2, j * 128:(j + 1) * 128],
                      lambda hh, i: qT_ac[(hh % 2) * 64:(hh % 2) * 64 + 64, hh // 2, i * 128:(i + 1) * 128],
                      lambda i: "qT", vaug, Pt, ytile, 0.125, EBrev=EBrev)
            P.barrier()

    def rope_ops(dst1, dst2, x1, x2, cosv, sinv, t1, t2, Rk, Wk):
        P.op("dve", tt(t1, x1, cosv, ALU.mult), R=Rk, W=["rt1"])
        P.op("dve", tt(t2, x2, sinv, ALU.mult), R=Rk, W=["rt2"])
        P.op("dve", tt(dst1, t1, t2, ALU.subtract), R=["rt1", "rt2"], W=Wk)
        P.op("dve", tt(t1, x2, cosv, ALU.mult), R=Rk, W=["rt1"])
        P.op("dve", tt(t2, x1, sinv, ALU.mult), R=Rk, W=["rt2"])
        P.op("dve", tt(dst2, t1, t2, ALU.add), R=["rt1", "rt2"], W=Wk)

    def phase_B(l):
        z = ZQ + 1024
        wkr = bv(z, 256).rearrange("p (k c) -> p k c", k=8); z += 256
        wqup = bv(z, 1152).rearrange("p (k c) -> p k c", k=3); z += 1152
        wkvup = bv(z, 1024).rearrange("p (k c) -> p k c", k=2); z += 1024
        krope = bv(z, 512).rearrange("p (i c) -> p i c", i=NT); z += 512
        assert z <= ZQ + 4096
        z = 6144
        tmpf = fv(z, 512); z += 1024
        tmpq = fv(z, 384).rearrange("p (h c) -> p h c", h=4); z += 768
        assert z <= 8192
        z = 16384 + 4096
        ktok = [bv(z + k * 512, 512).rearrange("p (h c) -> p h c", h=4) for k in range(2)]; z += 1024
        qtok = [bv(z + k * 512, 512).rearrange("p (h c) -> p h c", h=4) for k in range(2)]; z += 1024
        for k_ in range(2):
            P.op("pool", ms(ktok[k_], 0.0), W=[("ktok", k_)])
            P.op("pool", ms(qtok[k_], 0.0), W=[("qtok", k_)])
        rt1 = fv(z, 256); z += 512
        rt2 = fv(z, 256); z += 512
        kvsb = fv(z, 512); z += 1024
        assert z <= 24576, z
        qdnT = bv(0, 6144).rearrange("p (c t) -> p c t", c=3)
        kvdnT = bv(16384, 4096).rearrange("p (c t) -> p c t", c=2)
        set_vones()
        proj_norm(l, C_QD, 3, ones_bf, 384, 112 + 3 * l, [qdnT[:, c, :] for c in range(3)], "qdnT", wqk[0:3], sqb, rsb, gstep=1)
        proj_norm(l, C_KVD, 2, ones_bf, 256, 124 + 2 * l, [kvdnT[:, c, :] for c in range(2)], "kvdnT", [wqk[3], wqk[0]], sqb, rsb, gstep=1)
        if BSTOP == 1:
            P.op("pool", ms(yT[:, 1, :, :], 0.0), W=[("yT", 1)]); P.barrier(); return
        wload(wkr, w_in[l, :, C_KR:C_KR + 32], "wkr")
        for i in range(NT):
            for kc in range(8):
                P.op("pe", mm(psb[5][:, i * 32:(i + 1) * 32], hT[:, kc, i * 128:(i + 1) * 128], wkr[:, kc, :], kc == 0, kc == 7),
                     R=["wkr", ("hT", i // 4)], W=[("ps", 5)])
        pk = psb[5][:].rearrange("p (i c) -> p i c", i=NT)
        tfv = tmpf.rearrange("p (i c) -> p i c", i=NT)
        sm = small[:, 16:32]
        P.op("act", act(tmpf, psb[5][:], AF.Square), R=[("ps", 5)], W=["tmpf"])
        P.op("dve", red(sm, tfv), R=["tmpf"], W=["sm"])
        rsqrt_small(sm, sm, 1.0 / 32, EPS, ["sm"], ["sm"])
        P.op("dve", tt(tfv, pk, sm.unsqueeze(2).broadcast_to([128, NT, 32]), ALU.mult), R=[("ps", 5), "sm"], W=["tmpf"])
        gk = gbr[:, l * 64 + 32:l * 64 + 64]
        P.op("dve", tt(tfv, tfv, gk.unsqueeze(1).broadcast_to([128, NT, 32]), ALU.mult), R=["tmpf", "gbr"], W=["tmpf"])
        r1 = rt1.rearrange("p (i c) -> p i c", i=NT)
        r2 = rt2.rearrange("p (i c) -> p i c", i=NT)
        rope_ops(krope[:, :, 0:16], krope[:, :, 16:32], tfv[:, :, 0:16], tfv[:, :, 16:32], cs[:, :, 0:16], cs[:, :, 16:32],
                 r1, r2, ["tmpf", "cs"], ["krope"])
        P.barrier()
        if BSTOP == 2:
            P.op("pool", ms(yT[:, 1, :, :], 0.0), W=[("yT", 1)]); P.barrier(); return
        for hf in range(2):
            wload(wkvup, w_kv_up[l, :, hf * 512:(hf + 1) * 512], "wkvup")
            wload(wqup, w_q_up[l, :, hf * 384:(hf + 1) * 384], "wqup")
            for i in range(NT):
                bank = 6
                pkv = psb[bank][:].rearrange("p (h c) -> p h c", h=4)
                for kc in range(2):
                    P.op("pe", mm(psb[bank][:], kvdnT[:, kc, i * 128:(i + 1) * 128], wkvup[:, kc, :], kc == 0, kc == 1),
                         R=["kvdnT", "wkvup"], W=[("ps", bank)])
                t4 = tmpf[:, 0:256].rearrange("p (h c) -> p h c", h=4)
                s4 = small[:, 32:36]
                P.op("act", cp_act(kvsb, psb[bank][:]), R=[("ps", bank)], W=["kvsb"])
                pkv = kvsb.rearrange("p (h c) -> p h c", h=4)
                P.op("act", act(t4, pkv[:, :, 0:64], AF.Square), R=["kvsb"], W=["tmpf"])
                P.op("dve", red(s4, t4), R=["tmpf"], W=["s4"])
                rsqrt_small(s4, s4, 1.0 / 64, EPS, ["s4"], ["s4"])
                P.op("dve", tt(t4, pkv[:, :, 0:64], s4.unsqueeze(2).broadcast_to([128, 4, 64]), ALU.mult), R=["kvsb", "s4"], W=["tmpf"])
                kt = ktok[i % 2]
                gn = gbn[:, l * 128 + 64:l * 128 + 128]
                P.op("dve", tt(kt[:, :, 0:64], t4, gn.unsqueeze(1).broadcast_to([128, 4, 64]), ALU.mult), R=["tmpf", "gbn"], W=[("ktok", i % 2)])
                for hh in range(4):
                    P.op("dve", cp(kt[:, hh, 64:96], krope[:, i, :]), R=["krope"], W=[("ktok", i % 2)])
                P.op("act", cp_act(vaug[:, i, :, 0:64], pkv[:, :, 64:128]), R=["kvsb"], W=["v"])
                ptv = psbf(7).rearrange("p (a b) -> p a b", a=8)
                for hh in range(4):
                    P.op("pe", tr(ptv[:, hh, :], kt[:, hh, :], ident_bf[:]), R=[("ktok", i % 2), "ident_bf"], W=[("ps", 7)])
                P.op("act", cp_act(kT_b[0:96, :, i * 128:(i + 1) * 128], ptv[0:96, 0:4, :]), R=[("ps", 7)], W=["kT"])

            if BSTOP == 3:
                P.barrier(); P.op("pool", ms(yT[:, 1, :, :], 0.0), W=[("yT", 1)]); P.barrier(); return
            def qprep(i):
                bank = 6
                pq = psb[bank][:, 0:384].rearrange("p (h c) -> p h c", h=4)
                for kc in range(3):
                    P.op("pe", mm(psb[bank][:, 0:384], qdnT[:, kc, i * 128:(i + 1) * 128], wqup[:, kc, :], kc == 0, kc == 2),
                         R=["qdnT", "wqup"], W=[("ps", bank)])
                t4 = tmpf[:, 0:384].rearrange("p (h c) -> p h c", h=4)
                sn = small[:, 36:40]
                sr = small[:, 40:44]
                P.op("act", cp_act(kvsb[:, 0:384], psb[bank][:, 0:384]), R=[("ps", bank)], W=["kvsb"])
                pq = kvsb[:, 0:384].rearrange("p (h c) -> p h c", h=4)
                P.op("act", act(t4, pq, AF.Square), R=["kvsb"], W=["tmpf"])
                P.op("dve", red(sn, t4[:, :, 0:64]), R=["tmpf"], W=["sn"])
                P.op("dve", red(sr, t4[:, :, 64:96]), R=["tmpf"], W=["sr"])
                rsqrt_small(sn, sn, 1.0 / 64, EPS, ["sn"], ["sn"])
                rsqrt_small(sr, sr, 1.0 / 32, EPS, ["sr"], ["sr"])
                P.op("dve", tt(tmpq[:, :, 0:64], pq[:, :, 0:64], sn.unsqueeze(2).broadcast_to([128, 4, 64]), ALU.mult), R=["kvsb", "sn"], W=["tmpq"])
                P.op("dve", tt(tmpq[:, :, 64:96], pq[:, :, 64:96], sr.unsqueeze(2).broadcast_to([128, 4, 32]), ALU.mult), R=["kvsb", "sr"], W=["tmpq"])
                qt = qtok[i % 2]
                gqn = gbn[:, l * 128:l * 128 + 64]
                gqr = gbr[:, l * 64:l * 64 + 32]
                P.op("dve", tt(qt[:, :, 0:64], tmpq[:, :, 0:64], gqn.unsqueeze(1).broadcast_to([128, 4, 64]), ALU.mult), R=["tmpq", "gbn"], W=[("qtok", i % 2)])
                P.op("dve", tt(tmpq[:, :, 64:96], tmpq[:, :, 64:96], gqr.unsqueeze(1).broadcast_to([128, 4, 32]), ALU.mult), R=["tmpq", "gbr"], W=["tmpq"])
                c4 = cs[:, i, 0:16].unsqueeze(1).broadcast_to([128, 4, 16])
                s4b = cs[:, i, 16:32].unsqueeze(1).broadcast_to([128, 4, 16])
                q1 = rt1[:, 0:64].rearrange("p (h c) -> p h c", h=4)
                q2 = rt2[:, 0:64].rearrange("p (h c) -> p h c", h=4)
                rope_ops(qt[:, :, 64:80], qt[:, :, 80:96], tmpq[:, :, 64:80], tmpq[:, :, 80:96], c4, s4b, q1, q2, ["tmpq", "cs"], [("qtok", i % 2)])
                ptv = psbf(7).rearrange("p (a b) -> p a b", a=8)
                for hh in range(4):
                    P.op("pe", tr(ptv[:, hh, :], qt[:, hh, :], ident_bf[:]), R=[("qtok", i % 2), "ident_bf"], W=[("ps", 7)])
                P.op("act", cp_act(qTt[i % 2][0:96, :, :], ptv[0:96, 0:4, :]), R=[("ps", 7)], W=[("qTt", i % 2)])

            attention("B", hf, 1,
                      lambda hh, j: kT_b[0:96, hh, j * 128:(j + 1) * 128],
                      lambda hh, i: qTt[i % 2][0:96, hh, :],
                      lambda i: ("qTt", i % 2), vaug, Pt, ytile, float(96.0 ** -0.5), qprep=qprep)
            P.barrier()

    def merge_phase(l):
        z = Z0
        mT = bv(z, 8192).rearrange("p (c t) -> p c t", c=8); z += 8192
        wg = [bv(z + k * 3072, 3072).rearrange("p (k n c) -> p k n c", k=8, n=3) for k in range(2)]; z += 6144
        wb = [bv(z + k * 1536, 1536).rearrange("p (k n c) -> p k n c", k=4, n=3) for k in range(2)]; z += 3072
        wo = [bv(z + k * 2048, 2048).rearrange("p (k c) -> p k c", k=8) for k in range(2)]; z += 4096
        gate = [bv(z + k * 512, 512) for k in range(3)]; z += 1536
        acc = fv(z, 512); z += 1024
        tmp = fv(z, 512); z += 1024
        assert z <= AR_N, z
        cnt = 0
        for th in range(2):
            for m in range(8):
                sl = cnt % 2
                cnt += 1
                for n in range(3):
                    c0 = C_G + n * 1024 + m * 128
                    P.dma("pool", dmaf(wg[sl][:, :, n, :], w_in[l, :, c0:c0 + 128].rearrange("(k p) c -> p k c", p=128)), W=[("wg", sl, n)], sem="wg%d_%d" % (sl, n))
                    P.dma("pool", dmaf(wb[sl][:, :, n, :], w_branch[l, n, :, m * 128:(m + 1) * 128].rearrange("(k p) c -> p k c", p=128)), W=[("wb", sl, n)], sem="wb%d_%d" % (sl, n))
                for tq in range(2):
                    tc = th * 2 + tq
                    tsl = slice(tc * 512, (tc + 1) * 512)
                    u = (m * 2 + tq) % 2
                    for n in range(3):
                        gb = u * 4 + n if n < 2 else u * 4 + 2
                        gb = u * 4 + n
                        for kc in range(8):
                            P.op("pe", mm(psb[gb][:], wg[sl][:, kc, n, :], hT[:, kc, tsl], kc == 0, kc == 7), R=[("wg", sl, n), ("hT", tc)], W=[("ps", gb)])
                        P.op("act", act(gate[n], psb[gb][:], AF.Sigmoid, bias=vecT[:, l * 24 + n * 8 + m:l * 24 + n * 8 + m + 1]),
                             R=[("ps", gb), "vecT"], W=[("gate", n)])
                        pb = u * 4 + 3
                        for kc in range(4):
                            P.op("pe", mm(psb[pb][:], wb[sl][:, kc, n, :], yT[:, n, kc, tsl], kc == 0, kc == 3), R=[("wb", sl, n), ("yT", n)], W=[("ps", pb)])
                        if n == 0:
                            P.op("dve", tt(acc, psb[pb][:], gate[n], ALU.mult), R=[("ps", pb), ("gate", n)], W=["acc"])
                        else:
                            P.op("dve", tt(tmp, psb[pb][:], gate[n], ALU.mult), R=[("ps", pb), ("gate", n)], W=["tmp"])
                            if n == 1:
                                P.op("dve", tt(acc, acc, tmp, ALU.add), R=["tmp"], W=["acc"])
                            else:
                                P.op("dve", tt(mT[:, m, tq * 512:(tq + 1) * 512], acc, tmp, ALU.add), R=["tmp", "acc"], W=["mT"])
            for cq in range(4):
                sl = cq % 2
                wload(wo[sl], w_out[l, :, cq * 256:(cq + 1) * 256], ("wo", sl))
                for ii in range(8):
                    i = th * 8 + ii
                    bank = ii % 2 + 6 if False else (ii % 4)
                    for kc in range(8):
                        P.op("pe", mm(psb[bank][:, 0:256], mT[:, kc, ii * 128:(ii + 1) * 128], wo[sl][:, kc, :], kc == 0, kc == 7),
                             R=["mT", ("wo", sl)], W=[("ps", bank)])
                    P.op("dve", tt(xs[:, i, cq * 256:(cq + 1) * 256], xs[:, i, cq * 256:(cq + 1) * 256], psb[bank][:, 0:256], ALU.add),
                         R=[("ps", bank)], W=[("x", i)])
        P.barrier()

    def ffn_phase(l):
        aT = [bv(0, 16384).rearrange("p (c t) -> p c t", c=8), bv(Z0, 16384).rearrange("p (c t) -> p c t", c=8)]
        w2 = [bv(16384 + k * 4096, 4096).rearrange("p (k c) -> p k c", k=8) for k in range(2)]
        z = Z0 + 16384
        w1 = [bv(z + k * 2048, 2048).rearrange("p (k c) -> p k c", k=8) for k in range(2)]; z += 4096
        rt = [fv(z + k * 1024, 512) for k in range(2)]; z += 2048
        assert z <= AR_N, z
        c1 = 0
        c2 = 0
        bk = 0
        for g in range(4):
            a = aT[g % 2]
            for fp in range(4):
                sl = c1 % 2
                c1 += 1
                wload(w1[sl], w_ff1[l, :, g * 1024 + fp * 256: g * 1024 + (fp + 1) * 256], ("w1", sl))
                for f2 in range(2):
                    f = fp * 2 + f2
                    for tc in range(4):
                        bank = bk % 4
                        bk += 1
                        for kc in range(8):
                            P.op("pe", mm(psb[bank][:], w1[sl][:, kc, f2 * 128:(f2 + 1) * 128], hT[:, kc, tc * 512:(tc + 1) * 512], kc == 0, kc == 7),
                                 R=[("w1", sl), ("hT", tc)], W=[("ps", bank)])
                        r = rt[bank % 2]
                        P.op("act", act(r, psb[bank][:], AF.Relu), R=[("ps", bank)], W=[("rt", bank % 2)])
                        P.op(SQ_ENG, tt(a[:, f, tc * 512:(tc + 1) * 512], r, r, ALU.mult), R=[("rt", bank % 2)], W=[("aT", g % 2)])
            for ch in range(2):
                sl = c2 % 2
                c2 += 1
                wload(w2[sl], w_ff2[l, g * 1024:(g + 1) * 1024, ch * 512:(ch + 1) * 512], ("w2", sl))
                for i in range(NT):
                    bank = 4 + (i % 4)
                    for f in range(8):
                        P.op("pe", mm(psb[bank][:], a[:, f, i * 128:(i + 1) * 128], w2[sl][:, f, :], f == 0, f == 7),
                             R=[("aT", g % 2), ("w2", sl)], W=[("ps", bank)])
                    P.op("dve", tt(xs[:, i, ch * 512:(ch + 1) * 512], xs[:, i, ch * 512:(ch + 1) * 512], psb[bank][:], ALU.add),
                         R=[("ps", bank)], W=[("x", i)])
        P.barrier()

    MASK_ENG = "pool"
    SQ_ENG = "pool"
    for l in range(nlayers):
        if "N" in PH:
            norm_phase(norm_mix[l, :])
        for nb_, ch_ in enumerate("ABC"):
            if ch_ not in PH:
                P.op("pool", ms(yT[:, nb_, :, :], 0.0), W=[("yT", nb_)])
        if "B" in PH:
            phase_B(l)
        if "A" in PH:
            phase_A(l)
        if "C" in PH:
            phase_C(l)
        if DBG == 1 and l == 0:
            P.barrier()
            P.dma("sp", dmaf(dbg_d, arena[:, 0:24576]), R=[("yT", 0), ("yT", 1), ("yT", 2)], sem="dbg")
            P.barrier()
        if "M" in PH:
            merge_phase(l)
        if "F" in PH:
            norm_phase(norm_ffn[l, :])
            ffn_phase(l)
    for i in range(NT):
        P.dma("sp", dmaf(y_d[i * 128:(i + 1) * 128, :], xs[:, i, :]), R=[("x", i)], sem="y%d" % i)
    P.barrier()
    P.emit(nc, st)
    st.close()
    return nc


def _host_consts(rel_bias):
    half = 16
    inv = (10000.0 ** (-np.arange(half, dtype=np.float32) / half)).astype(np.float32)
    ang = np.arange(S, dtype=np.float32)[:, None] * inv[None, :]
    cs_tab = np.concatenate([np.cos(ang), np.sin(ang)], axis=1).astype(np.float32)
    kl = np.arange(128)[:, None]
    c = np.arange(640)[None, :]
    idx = np.clip(c - kl, -256, 256) + 256
    rel_ext = np.ascontiguousarray(rel_bias[:, :, idx]).reshape(DEPTH * 8, 128, 640).astype(np.float32)
    return cs_tab, rel_ext


PH = "NBACMF"
DBG = False
BSTOP = 0
_NC_CACHE = {}


def kernel(**inputs):
    inp = {k: np.ascontiguousarray(np.asarray(v, dtype=np.float32)) for k, v in inputs.items()}
    x = inp.pop("x")
    rel_bias = inp.pop("rel_bias")
    cs_tab, rel_ext = _host_consts(rel_bias)
    key = (DEPTH, PH)
    if key not in _NC_CACHE:
        _NC_CACHE[key] = build(DEPTH)
    nc = _NC_CACHE[key]
    B = x.shape[0]
    in_maps = []
    for b in range(B):
        m = dict(inp)
        m["x"] = np.ascontiguousarray(x[b])
        m["rel_ext"] = rel_ext
        m["cs_tab"] = cs_tab
        in_maps.append(m)
    res = run_bass_kernel_spmd(nc, in_maps, core_ids=list(range(B)))
    return np.stack([np.asarray(r["y"], dtype=np.float32) for r in res.results], axis=0)
```

```python
import numpy as np
from contextlib import ExitStack
import concourse.bass as bass
import concourse.mybir as mybir
from concourse.bass_utils import run_bass_kernel_spmd

F32 = mybir.dt.float32
BF16 = mybir.dt.bfloat16
AF = mybir.ActivationFunctionType
ALU = mybir.AluOpType
AX = mybir.AxisListType

S = 2048
D = 1024
NT = 16
DEPTH = 4
EPS = 1e-6
INW = 6824
C_QA, C_KA, C_VA, C_FA, C_QD, C_KVD, C_KR, C_QC, C_KC, C_VC, C_G = 0, 512, 1024, 1536, 1544, 1928, 2184, 2216, 2728, 3240, 3752


class Prog:
    ENG = ("pe", "act", "dve", "pool", "sp")

    def __init__(self):
        self.ops = {e: [] for e in self.ENG}
        self.state = {}
        self.known = {e: {} for e in self.ENG}
        self.flag = {e: set() for e in self.ENG}
        self.semcnt = {}

    def _add(self, eng, fn, R, W, dma_sem):
        need = {}
        for k in R:
            st = self.state.get(k)
            if st and st[0] is not None:
                t = st[0]
                need[t[0]] = max(need.get(t[0], -1), t[1])
        for k in W:
            st = self.state.get(k)
            if st:
                if st[0] is not None:
                    t = st[0]
                    need[t[0]] = max(need.get(t[0], -1), t[1])
                for tk, tv in st[1].items():
                    need[tk] = max(need.get(tk, -1), tv)
        kn = self.known[eng]
        waits = []
        for tk, tv in need.items():
            if dma_sem is None and eng == "pe" and tk == ("E", "pe"):
                continue
            if kn.get(tk, -1) >= tv:
                continue
            kn[tk] = tv
            waits.append((tk, tv))
            if tk[0] == "E":
                self.flag[tk[1]].add(tv)
        idx = len(self.ops[eng])
        if dma_sem is None:
            tok = (("E", eng), idx)
        else:
            c = self.semcnt.get(dma_sem, 0) + 1
            self.semcnt[dma_sem] = c
            tok = (("S", dma_sem), c)
        self.ops[eng].append((fn, waits, dma_sem))
        for k in R:
            if k in W:
                continue
            st = self.state.setdefault(k, [None, {}])
            st[1][tok[0]] = max(st[1].get(tok[0], -1), tok[1])
        for k in W:
            self.state[k] = [tok, {}]
        return tok

    def op(self, eng, fn, R=(), W=()):
        return self._add(eng, fn, list(R), list(W), None)

    def dma(self, q, fn, R=(), W=(), sem=None):
        return self._add(q, fn, list(R), list(W), sem)

    def barrier(self):
        last = {}
        for e in self.ENG:
            n = len(self.ops[e])
            for i in range(n - 1, -1, -1):
                if self.ops[e][i][2] is None and self.ops[e][i][0] is not None:
                    last[("E", e)] = i
                    break
        for s, c in self.semcnt.items():
            last[("S", s)] = c
        for e in self.ENG:
            kn = self.known[e]
            waits = []
            for tk, tv in last.items():
                if kn.get(tk, -1) >= tv:
                    continue
                kn[tk] = tv
                waits.append((tk, tv))
                if tk[0] == "E":
                    self.flag[tk[1]].add(tv)
            if waits:
                self.ops[e].append((None, waits, None))

    def emit(self, nc, stack):
        rank = {}
        for e in self.ENG:
            rank[e] = {idx: r + 1 for r, idx in enumerate(sorted(self.flag[e]))}
        BS = 2000
        esem = {e: [stack.enter_context(nc.semaphore("es_%s_%d" % (e, b))) for b in range(len(rank[e]) // BS + 1)] for e in self.ENG}
        dsem = {s: stack.enter_context(nc.semaphore("ds_" + str(i))) for i, s in enumerate(self.semcnt)}
        block = stack.enter_context(nc.Block())

        def run(e, h):
            for idx, (fn, waits, ds) in enumerate(self.ops[e]):
                for tk, tv in waits:
                    if tk[0] == "E":
                        r_ = rank[tk[1]][tv] - 1
                        h.wait_ge(esem[tk[1]][r_ // BS], r_ % BS + 1)
                    else:
                        h.wait_ge(dsem[tk[1]], tv * 16)
                if fn is None:
                    continue
                ins = fn(h)
                if ds is not None:
                    ins.then_inc(dsem[ds], 16)
                elif idx in rank[e]:
                    ins.then_inc(esem[e][(rank[e][idx] - 1) // BS], 1)

        block.tensor(lambda h: run("pe", h))
        block.scalar(lambda h: run("act", h))
        block.vector(lambda h: run("dve", h))
        block.gpsimd(lambda h: run("pool", h))
        block.sync(lambda h: run("sp", h))


def build(nlayers=DEPTH):
    nc = bass.Bass("TRN2", target_bir_lowering=False)
    P = Prog()
    st = ExitStack()

    def din(name, shape):
        return nc.dram_tensor(name, list(shape), F32, kind="ExternalInput")

    x_d = din("x", [S, D]).ap()
    norm_mix = din("norm_mix", [DEPTH, D]).ap()
    w_in = din("w_in", [DEPTH, D, INW]).ap()
    b_forget = din("b_forget", [DEPTH, 8]).ap()
    b_gate = din("b_gate", [DEPTH, 3072]).ap()
    qk_norm_a = din("qk_norm_a", [DEPTH, 2, 64]).ap()
    mla_q_norm = din("mla_q_norm", [DEPTH, 384]).ap()
    mla_kv_norm = din("mla_kv_norm", [DEPTH, 256]).ap()
    w_q_up = din("w_q_up", [DEPTH, 384, 768]).ap()
    w_kv_up = din("w_kv_up", [DEPTH, 256, 1024]).ap()
    qk_norm_b_nope = din("qk_norm_b_nope", [DEPTH, 2, 64]).ap()
    qk_norm_b_rope = din("qk_norm_b_rope", [DEPTH, 2, 32]).ap()
    qk_norm_c = din("qk_norm_c", [DEPTH, 2, 64]).ap()
    relx = din("rel_ext", [DEPTH * 8, 128, 640]).ap()
    w_branch = din("w_branch", [DEPTH, 3, 512, D]).ap()
    w_out = din("w_out", [DEPTH, D, D]).ap()
    norm_ffn = din("norm_ffn", [DEPTH, D]).ap()
    w_ff1 = din("w_ff1", [DEPTH, D, 4096]).ap()
    w_ff2 = din("w_ff2", [DEPTH, 4096, D]).ap()
    cs_d = din("cs_tab", [S, 32]).ap()
    y_d = nc.dram_tensor("y", [S, D], F32, kind="ExternalOutput").ap()
    dbg_d = nc.dram_tensor("dbg", [128, 24576], BF16, kind="ExternalOutput").ap() if DBG else None

    def sb(name, shape, dt):
        return st.enter_context(nc.sbuf_tensor(name, list(shape), dt))

    xs = sb("xs", [128, NT, D], F32)
    hT = sb("hT", [128, 8, S], BF16)
    ident_bf = sb("ident_bf", [128, 128], BF16)
    ident_f = sb("ident_f", [128, 128], F32)
    blockones = sb("blockones", [128, 128], BF16)
    ones_bf = sb("ones_bf", [128, 128], BF16)
    maskA = sb("maskA", [128, 128], BF16)
    maskB = sb("maskB", [128, 128], BF16)
    U_f = sb("U_f", [128, 128], F32)
    ones_f = sb("ones_f", [128, 128], F32)
    validC = sb("validC", [128, 640], BF16)
    vecT = sb("vecT", [128, 132], F32)
    bf_rep = sb("bf_rep", [128, 32], F32)
    gbn = sb("gbn", [128, 512], F32)
    gbr = sb("gbr", [128, 256], F32)
    cs = sb("cs", [128, NT, 32], F32)
    ssq = sb("ssq", [128, NT], F32)
    rstd = sb("rstd", [128, NT], F32)
    small = sb("small", [128, 64], F32)
    cst = sb("cst", [128, 4], F32)
    AR_N = 52000
    arena = sb("arena", [128, AR_N], BF16)
    psb = [st.enter_context(nc.psum_tensor("ps%d" % b, [128, 512], F32)) for b in range(8)]

    def bv(off, n):
        return arena[:, off:off + n]

    def fv(off, n):
        return arena[:, off:off + 2 * n].bitcast(F32)

    def psbf(b):
        return psb[b][:].bitcast(BF16)

    Z0 = 24576
    yT = bv(0, 24576).rearrange("p (n c t) -> p n c t", n=3, c=4)

    def mm(out, lhsT, rhs, start, stop):
        return lambda e: e.matmul(out, lhsT, rhs, start=start, stop=stop)

    def tr(out, in_, ident):
        return lambda e: e.transpose(out, in_, ident)

    def act(out, in_, func, bias=None, scale=None, accum_out=None):
        kw = {}
        if bias is not None:
            kw["bias"] = bias
        if scale is not None:
            kw["scale"] = scale
        if accum_out is not None:
            kw["accum_out"] = accum_out
        return lambda e: e.activation(out=out, in_=in_, func=func, **kw)

    def tt(out, in0, in1, op):
        return lambda e: e.tensor_tensor(out=out, in0=in0, in1=in1, op=op)

    def ts(out, in0, s1, s2, op0, op1=None):
        if op1 is None:
            return lambda e: e.tensor_single_scalar(out=out, in_=in0, scalar=s1, op=op0)
        return lambda e: e.tensor_scalar(out=out, in0=in0, scalar1=s1, scalar2=s2, op0=op0, op1=op1)

    def stt(out, in0, scalar, in1, op0, op1):
        return lambda e: e.scalar_tensor_tensor(out=out, in0=in0, scalar=scalar, in1=in1, op0=op0, op1=op1)

    def cp(out, in_):
        return lambda e: e.tensor_copy(out=out, in_=in_)

    def rsqrt_small(dst, src, mul, eps, Rk, Wk):
        P.op("dve", ts(dst, src, mul, eps, ALU.mult, ALU.add), R=Rk, W=Wk)
        P.op("act", act(dst, dst, AF.Ln), R=Wk, W=Wk)
        P.op("act", act(dst, dst, AF.Exp, scale=-0.5), R=Wk, W=Wk)

    def red(out, in_):
        return lambda e: e.tensor_reduce(out=out, in_=in_, axis=AX.X, op=ALU.add)

    def ms(ap, v):
        return lambda e: e.memset(ap, v)

    def dmaf(out, in_):
        return lambda e: e.dma_start(out=out, in_=in_)

    def wload(dst, src2d, key, R=(), q="pool"):
        P.dma(q, dmaf(dst, src2d.rearrange("(k p) c -> p k c", p=128)), R=R, W=[key], sem=str(key))

    P.op("pool", ms(ones_f[:], 1.0), W=["ones_f"])
    P.op("pool", lambda e: e.affine_select(out=ident_f[:], in_=ones_f[:], pattern=[[-1, 128]], compare_op=ALU.is_equal,
                                           fill=0.0, base=0, channel_multiplier=1), R=["ones_f"], W=["ident_f"])
    P.op("pool", lambda e: e.affine_select(out=U_f[:], in_=ones_f[:], pattern=[[1, 128]], compare_op=ALU.is_ge,
                                           fill=0.0, base=0, channel_multiplier=-1), R=["ones_f"], W=["U_f"])
    P.op("pool", cp(ident_bf[:], ident_f[:]), R=["ident_f"], W=["ident_bf"])
    P.op("pool", ts(maskA[:], U_f[:], 60000.0, -60000.0, ALU.mult, ALU.add), R=["U_f"], W=["maskA"])
    P.op("pool", ms(ones_bf[:], 1.0), W=["ones_bf"])
    P.op("pool", ms(cst[:, 0:1], 1.0), W=["cst"])
    P.op("pool", ms(cst[:, 1:2], 64 * EPS), W=["cst"])
    P.op("pool", ms(cst[:, 2:3], 384 * EPS), W=["cst"])
    P.op("pool", ms(cst[:, 3:4], 256 * EPS), W=["cst"])
    P.op("pool", ms(blockones[:], 0.0), W=["blockones"])
    P.op("pool", ms(blockones[0:64, 0:64], 1.0), W=["blockones"])
    P.op("pool", ms(blockones[64:128, 64:128], 1.0), W=["blockones"])
    P.op("pool", ms(maskB[:], 1.0), W=["maskB"])
    P.op("pool", ms(maskB[64:128, 0:64], 0.0), W=["maskB"])
    P.op("pool", ms(validC[:], 0.0), W=["validC"])
    P.op("pool", ms(validC[0:64, 0:576], 1.0), W=["validC"])
    P.op("pool", ms(validC[64:128, 64:640], 1.0), W=["validC"])

    stage1 = fv(Z0, 128)
    stage2 = fv(Z0 + 256, 128)
    P.dma("sp", dmaf(stage1[0:96, :], b_gate.rearrange("l (c p) -> (l c) p", p=128)), W=["stage1"], sem="stage1")
    qa2 = qk_norm_a.rearrange("l q d -> (l q) d")
    qc2 = qk_norm_c.rearrange("l q d -> (l q) d")
    P.dma("sp", dmaf(stage2[0:8, 0:64], qa2), W=["stage2a"], sem="stage2")
    P.dma("sp", dmaf(stage2[0:8, 64:128], qa2), W=["stage2b"], sem="stage2")
    P.dma("sp", dmaf(stage2[8:16, 0:64], qc2), W=["stage2c"], sem="stage2")
    P.dma("sp", dmaf(stage2[8:16, 64:128], qc2), W=["stage2d"], sem="stage2")
    P.dma("sp", dmaf(stage2[16:28, :], mla_q_norm.rearrange("l (c p) -> (l c) p", p=128)), W=["stage2e"], sem="stage2")
    P.dma("sp", dmaf(stage2[28:36, :], mla_kv_norm.rearrange("l (c p) -> (l c) p", p=128)), W=["stage2f"], sem="stage2")
    P.op("pe", tr(psb[0][:, 0:96], stage1[0:96, :], ident_f[0:96, 0:96]), R=["stage1", "ident_f"], W=[("ps", 0)])
    P.op("pe", tr(psb[1][:, 0:36], stage2[0:36, :], ident_f[0:36, 0:36]),
         R=["stage2a", "stage2b", "stage2c", "stage2d", "stage2e", "stage2f", "ident_f"], W=[("ps", 1)])
    P.op("dve", cp(vecT[:, 0:96], psb[0][:, 0:96]), R=[("ps", 0)], W=["vecT"])
    P.op("dve", ts(vecT[:, 96:112], psb[1][:, 0:16], 8.0, None, ALU.mult), R=[("ps", 1)], W=["vecT"])
    P.op("dve", ts(vecT[:, 112:124], psb[1][:, 16:28], float(np.sqrt(384.0)), None, ALU.mult), R=[("ps", 1)], W=["vecT"])
    P.op("dve", ts(vecT[:, 124:132], psb[1][:, 28:36], 16.0, None, ALU.mult), R=[("ps", 1)], W=["vecT"])
    P.dma("sp", dmaf(bf_rep[:], b_forget.rearrange("l h -> (l h)").partition_broadcast(128)), W=["bf_rep"], sem="c1")
    P.dma("sp", dmaf(gbn[:], qk_norm_b_nope.rearrange("l q d -> (l q d)").partition_broadcast(128)), W=["gbn"], sem="c2")
    P.dma("sp", dmaf(gbr[:], qk_norm_b_rope.rearrange("l q d -> (l q d)").partition_broadcast(128)), W=["gbr"], sem="c3")
    P.dma("sp", dmaf(cs[:], cs_d.rearrange("(t p) c -> p t c", p=128)), W=["cs"], sem="c4")
    for i in range(NT):
        P.dma("sp", dmaf(xs[:, i, :], x_d[i * 128:(i + 1) * 128, :]), W=[("x", i)], sem="x%d" % i)
    P.barrier()

    def norm_phase(gain_row):
        gnorm = fv(Z0, 1024)
        htok = [bv(Z0 + 2048 + k * 1024, 1024) for k in range(2)]
        junk = bv(Z0 + 4096, 1024)
        P.dma("sp", dmaf(gnorm, gain_row.partition_broadcast(128)), W=["gnorm"], sem="gnorm")
        for i in range(NT):
            P.op("act", act(junk, xs[:, i, :], AF.Square, accum_out=ssq[:, i:i + 1]), R=[("x", i)], W=["junk", "ssq"])
        rsqrt_small(rstd[:], ssq[:], 1.0 / D, EPS, ["ssq"], ["rstd"])
        for i in range(NT):
            P.op("dve", stt(htok[i % 2], xs[:, i, :], rstd[:, i:i + 1], gnorm, ALU.mult, ALU.mult),
                 R=[("x", i), "rstd", "gnorm"], W=[("htok", i % 2)])
            bank = 6 + (i % 2)
            ptv = psbf(bank).rearrange("p (a b) -> p a b", a=8)
            for kc in range(8):
                P.op("pe", tr(ptv[:, kc, :], htok[i % 2][:, kc * 128:(kc + 1) * 128], ident_bf[:]),
                     R=[("htok", i % 2), "ident_bf"], W=[("ps", bank)])
            P.op("act", cp_act(hT[:, :, i * 128:(i + 1) * 128], ptv), R=[("ps", bank)], W=[("hT", i // 4)])
        P.barrier()

    def cp_act(out, in_):
        return lambda e: e.activation(out=out, in_=in_, func=AF.Copy)

    unit_ctr = [0]

    def proj_norm(l, col0, nchunk, onesmat, nfeat, gcol0, outs, okey, wslots, sqb, rsb, gstep=0):
        for c in range(nchunk):
            wload(wslots[c][0], w_in[l, :, col0 + c * 128: col0 + (c + 1) * 128], wslots[c][1])
        for tc in range(4):
            u = unit_ctr[0] % 2
            unit_ctr[0] += 1
            b0 = 4 * u
            tsl = slice(tc * 512, (tc + 1) * 512)
            for c in range(nchunk):
                for kc in range(8):
                    P.op("pe", mm(psb[b0 + c][:], wslots[c][0][:, kc, :], hT[:, kc, tsl], kc == 0, kc == 7),
                         R=[wslots[c][1], ("hT", tc)], W=[("ps", b0 + c)])
                P.op("act", act(sqb[u][c], psb[b0 + c][:], AF.Square), R=[("ps", b0 + c)], W=[("sq", u, c)])
            for c in range(nchunk):
                P.op("pe", mm(psb[b0 + 3][:], onesmat[:], sqb[u][c], c == 0, c == nchunk - 1),
                     R=[("sq", u, c), "ones_bf", "blockones"], W=[("ps", b0 + 3)])
            ecol = {64: 1, 384: 2, 256: 3}[nfeat]
            P.op("act", act(rsb[u], psb[b0 + 3][:], AF.Ln, bias=cst[:, ecol:ecol + 1]), R=[("ps", b0 + 3), "cst"], W=[("rs", u)])
            P.op("act", act(rsb[u], rsb[u], AF.Exp, scale=-0.5), R=[("rs", u)], W=[("rs", u)])
            for c in range(nchunk):
                P.op("dve", stt(outs[c][:, tsl], psb[b0 + c][:], vecT[:, gcol0 + gstep * c:gcol0 + gstep * c + 1], rsb[u], ALU.mult, ALU.mult),
                     R=[("ps", b0 + c), ("rs", u), "vecT"], W=[okey])

    attn_ctr = [0]

    def attention(kind, hf, nbr, kslice, qslice, qkey, vaug, Pt, ytile, scale, biasA=None, bias_prep=None, EBrev=None, qprep=None, qterm=None):
        def finish(i):
            ob = 3 + (i % 2)
            pov = psb[ob][:, 0:260].rearrange("p (h c) -> p h c", h=4)
            rec = small[:, (i % 2) * 4:(i % 2) * 4 + 4]
            P.op("dve", (lambda rec, pov: lambda e: e.reciprocal(out=rec.unsqueeze(2), in_=pov[:, :, 64:65]))(rec, pov),
                 R=[("ps", ob)], W=[("rec", i % 2)])
            yt = ytile[i % 2]
            P.op("dve", tt(yt.rearrange("p (h c) -> p h c", h=4), pov[:, :, 0:64], rec.unsqueeze(2).broadcast_to([128, 4, 64]), ALU.mult),
                 R=[("ps", ob), ("rec", i % 2)], W=[("ytile", i % 2)])
            ptv = psbf(5).rearrange("p (a b) -> p a b", a=8)
            for c in range(2):
                P.op("pe", tr(ptv[:, c, :], yt[:, c * 128:(c + 1) * 128], ident_bf[:]), R=[("ytile", i % 2), "ident_bf"], W=[("ps", 5)])
            P.op("act", cp_act(yT[:, nbr, 2 * hf:2 * hf + 2, i * 128:(i + 1) * 128], ptv[:, 0:2, :]), R=[("ps", 5)], W=[("yT", nbr)])

        if qprep is not None:
            qprep[0](0)
            qprep[1](0)
        if bias_prep is not None:
            bias_prep(0)
        fin_pending = None
        for i in range(NT):
            js = list(range(max(0, i - 4), i + 1)) if kind == "C" else list(range(0, i + 1))
            groups = [js[a:a + 4] for a in range(0, len(js), 4)]
            ob = 3 + (i % 2)
            pov = psb[ob][:, 0:260].rearrange("p (h c) -> p h c", h=4)
            items = [(hh, grp) for hh in range(4) for grp in groups]
            pend = []

            def second(hh, grp, sbk, i=i, js=js, ob=ob, pov=pov):
                n = len(grp)
                if kind == "A":
                    h = 4 * hf + hh
                    for jj, j in enumerate(grp):
                        P.op("act", act(Pt[sbk][:, jj * 128:(jj + 1) * 128], psb[sbk][:, jj * 128:(jj + 1) * 128], AF.Exp,
                                        bias=biasA[i % 2][:, j, h:h + 1], scale=scale),
                             R=[("ps", sbk), ("biasA", i % 2)], W=[("Pt", sbk)])
                else:
                    P.op("act", act(Pt[sbk][:, 0:n * 128], psb[sbk][:, 0:n * 128], AF.Exp, scale=scale),
                         R=[("ps", sbk)], W=[("Pt", sbk)])
                if kind == "C":
                    d0 = 4 - (i - grp[0])
                    P.op(MASK_ENG, tt(Pt[sbk][:, 0:n * 128], Pt[sbk][:, 0:n * 128], EBrev[:, hh, d0 * 128:(d0 + n) * 128], ALU.mult),
                         R=["EBrev"], W=[("Pt", sbk)])
                elif i in grp and kind == "B":
                    jj = grp.index(i)
                    P.op(MASK_ENG, tt(Pt[sbk][:, jj * 128:(jj + 1) * 128], Pt[sbk][:, jj * 128:(jj + 1) * 128], maskB[:], ALU.mult),
                         R=["maskB"], W=[("Pt", sbk)])
                for jj, j in enumerate(grp):
                    P.op("pe", mm(pov[:, hh, :], Pt[sbk][:, jj * 128:(jj + 1) * 128], vaug[:, j, hh, :], j == js[0], j == js[-1]),
                         R=[("Pt", sbk), "v"], W=[("ps", ob)])

            cnt = 0
            for (hh, grp) in items:
                sbk = attn_ctr[0] % 3
                attn_ctr[0] += 1
                for jj, j in enumerate(grp):
                    osl = psb[sbk][:, jj * 128:(jj + 1) * 128]
                    P.op("pe", mm(osl, kslice(hh, j), qslice(hh, i), True, qterm is None),
                         R=["kT", qkey(i)], W=[("ps", sbk)])
                    if qterm is not None:
                        rq_ap, rq_key = qterm(i, hh)
                        P.op("pe", mm(osl, ones_bf[:], rq_ap, False, j != i), R=[rq_key, "ones_bf"], W=[("ps", sbk)])
                        if j == i:
                            P.op("pe", mm(osl, ident_bf[:], maskA[:], False, True), R=["maskA", "ident_bf"], W=[("ps", sbk)])
                pend.append((hh, grp, sbk))
                cnt += 1
                if cnt == 2:
                    if fin_pending is not None:
                        finish(fin_pending)
                        fin_pending = None
                    if i + 1 < NT:
                        if qprep is not None:
                            qprep[0](i + 1)
                        if bias_prep is not None:
                            bias_prep(i + 1)
                if len(pend) > 2:
                    second(*pend.pop(0))
            while pend:
                second(*pend.pop(0))
            if qprep is not None and i + 1 < NT:
                qprep[1](i + 1)
            fin_pending = i
        finish(fin_pending)

    def attention2(kind, hf, nbr, kslice, qchunk, vaug, Pt, ytile4, scale, biasA=None, prep=None, rqm=None, EB=None):
        OB = [3, 4, 6, 7]

        def finish(c):
            ptv = psbf(5).rearrange("p (a b) -> p a b", a=8)
            for t in range(4):
                ob = OB[t]
                pov = psb[ob][:, 0:260].rearrange("p (h c) -> p h c", h=4)
                rec = small[:, t * 4:t * 4 + 4]
                P.op("dve", (lambda rec, pov: lambda e: e.reciprocal(out=rec.unsqueeze(2), in_=pov[:, :, 64:65]))(rec, pov),
                     R=[("ps", ob)], W=[("rec", t)])
                yt = ytile4[t]
                P.op("dve", tt(yt.rearrange("p (h c) -> p h c", h=4), pov[:, :, 0:64], rec.unsqueeze(2).broadcast_to([128, 4, 64]), ALU.mult),
                     R=[("ps", ob), ("rec", t)], W=[("ytile", t)])
                for c2 in range(2):
                    P.op("pe", tr(ptv[:, c2 * 4 + t, :], yt[:, c2 * 128:(c2 + 1) * 128], ident_bf[:]), R=[("ytile", t), "ident_bf"], W=[("ps", 5)])
            P.op("act", cp_act(yT[:, nbr, 2 * hf:2 * hf + 2, c * 512:(c + 1) * 512], psbf(5).rearrange("p (a b) -> p a b", a=2)),
                 R=[("ps", 5)], W=[("yT", nbr)])

        if prep is not None:
            prep(0)
        for c in range(4):
            j_lo = max(0, 4 * c - 4) if kind == "C" else 0
            items = [(hh, j) for hh in range(4) for j in range(j_lo, 4 * c + 4)]
            pend = []

            def second(hh, j, sb, pb, t0, t1, c=c):
                cols = slice(t0 * 128, (t1 + 1) * 128)
                if kind == "A":
                    h = 4 * hf + hh
                    P.op("act", act(Pt[pb][:, cols], psb[sb][:, cols], AF.Exp, bias=biasA[c % 2][:, j, h:h + 1], scale=scale),
                         R=[("ps", sb), ("biasA", c % 2)], W=[("Pt", pb)])
                else:
                    P.op("act", act(Pt[pb][:, cols], psb[sb][:, cols], AF.Exp, scale=scale), R=[("ps", sb)], W=[("Pt", pb)])
                if kind == "C":
                    d0 = 4 * c + t0 - j
                    P.op(MASK_ENG, tt(Pt[pb][:, cols], Pt[pb][:, cols], EB[:, hh, d0 * 128:(d0 + t1 - t0 + 1) * 128], ALU.mult),
                         R=["EBrev"], W=[("Pt", pb)])
                for t in range(t0, t1 + 1):
                    i = 4 * c + t
                    first_j = max(0, i - 4) if kind == "C" else 0
                    pov = psb[OB[t]][:, 0:260].rearrange("p (h c) -> p h c", h=4)
                    P.op("pe", mm(pov[:, hh, :], Pt[pb][:, t * 128:(t + 1) * 128], vaug[:, j, hh, :], j == first_j, j == i),
                         R=[("Pt", pb), "v"], W=[("ps", OB[t])])

            for (hh, j) in items:
                t0 = max(0, j - 4 * c)
                t1 = 3 if kind != "C" else min(3, j + 4 - 4 * c)
                cols = slice(t0 * 128, (t1 + 1) * 128)
                sb = attn_ctr[0] % 2
                pb = attn_ctr[0] % 3
                attn_ctr[0] += 1
                diag = (kind == "A") and j >= 4 * c
                P.op("pe", mm(psb[sb][:, cols], kslice(hh, j), qchunk(c, hh)[:, cols], True, kind != "A"),
                     R=["kT", "qT"], W=[("ps", sb)])
                if kind == "A":
                    P.op("pe", mm(psb[sb][:, cols], ones_bf[:], rqm[:, hh, cols], False, not diag), R=["rqm", "ones_bf"], W=[("ps", sb)])
                    if diag:
                        P.op("pe", lambda e, o=psb[sb][:, t0 * 128:(t0 + 1) * 128]: e.matmul(o, ident_bf[:], maskA[:], start=False, stop=True, skip_group_check=True),
                             R=["maskA", "ident_bf"], W=[("ps", sb)])
                pend.append((hh, j, sb, pb, t0, t1))
                if len(pend) > 1:
                    second(*pend.pop(0))
            while pend:
                second(*pend.pop(0))
            if prep is not None and c + 1 < 4:
                prep(c + 1)
            finish(c)

    def v_proj(l, col0, hf, wv_unused, vaug):
        for sh in range(2):
            wsl, wkey = wqk_ref[0][sh]
            wload(wsl, w_in[l, :, col0 + hf * 256 + sh * 128: col0 + hf * 256 + (sh + 1) * 128], wkey)
        for i in range(NT):
            bank = 6 + (i % 2)
            for sh in range(2):
                wsl, wkey = wqk_ref[0][sh]
                for kc in range(8):
                    P.op("pe", mm(psb[bank][:, sh * 128:(sh + 1) * 128], hT[:, kc, i * 128:(i + 1) * 128], wsl[:, kc, :], kc == 0, kc == 7),
                         R=[wkey, ("hT", i // 4)], W=[("ps", bank)])
            P.op("act", cp_act(vaug[:, i, :, 0:64], psb[bank][:, 0:256].rearrange("p (h c) -> p h c", h=4)), R=[("ps", bank)], W=["v"])

    wqk_ref = [None]

    z = Z0
    ZK = z
    kT_ac = bv(z, 4096).rearrange("p (m t) -> p m t", m=2)
    kT_b = bv(z, 8192).rearrange("p (m t) -> p m t", m=4)
    z += 8192
    ZQ = z
    qT_ac = bv(z, 4096).rearrange("p (m t) -> p m t", m=2)
    qTt = [bv(z + k * 512, 512).rearrange("p (h t) -> p h t", h=4) for k in range(2)]
    z += 4096
    vaug = bv(z, 4160).rearrange("p (i h c) -> p i h c", i=NT, h=4)
    z += 4160
    sqb = [[bv(z + (u * 3 + c) * 512, 512) for c in range(3)] for u in range(2)]
    Pt = [bv(z + k * 512, 512) for k in range(3)]
    ytile = [bv(z + 1536 + k * 256, 256) for k in range(2)]
    ytile4 = [bv(z + 1536 + k * 256, 256) for k in range(4)]
    z += 3072
    rsb = [fv(z + u * 1024, 512) for u in range(2)]
    z += 2048
    wqk = [(bv(z + k * 1024, 1024).rearrange("p (k c) -> p k c", k=8), ("wqk", k)) for k in range(4)]
    z += 4096
    wv = None
    wqk_ref[0] = wqk
    assert z <= AR_N, z

    def set_vones():
        P.op("pool", ms(vaug[:, :, :, 64:65], 1.0), W=["v"])

    def phase_A(l):
        z = ZK + 4096
        wf = bv(z, 64).rearrange("p (k c) -> p k c", k=8); z += 64
        zt = fv(z, 128); z += 256
        lp = fv(z, 128); z += 256
        Lp = fv(z, 128).rearrange("p (i h) -> p i h", i=NT); z += 256
        PTt = fv(z, 128).rearrange("p (i h) -> p i h", i=NT); z += 256
        biasA = [fv(z + k * 256, 128).rearrange("p (i h) -> p i h", i=NT) for k in range(2)]; z += 512
        rq_bf = bv(z, 128); z += 128
        rqm = bv(z, 2048).rearrange("p (h t) -> p h t", h=4); z += 2048
        assert z <= ZK + 8192, z
        loc = zt
        set_vones()
        P.op("pool", ms(rqm, 0.0), W=["rqm"])
        wload(wf, w_in[l, :, C_FA:C_FA + 8], "wf")
        for i in range(NT):
            for kc in range(8):
                P.op("pe", mm(psb[5][:, i * 8:(i + 1) * 8], hT[:, kc, i * 128:(i + 1) * 128], wf[:, kc, :], kc == 0, kc == 7),
                     R=["wf", ("hT", i // 4)], W=[("ps", 5)])
        ztv = zt.rearrange("p (i h) -> p i h", i=NT)
        P.op("dve", tt(ztv, psb[5][:, 0:128].rearrange("p (i h) -> p i h", i=NT),
                       bf_rep[:, l * 8:(l + 1) * 8].unsqueeze(1).broadcast_to([128, NT, 8]), ALU.add), R=[("ps", 5), "bf_rep"], W=["zt"])
        P.op("act", act(zt, zt, AF.Exp, scale=-1.0), R=["zt"], W=["zt"])
        P.op("act", act(lp, zt, AF.Ln, bias=cst[:, 0:1]), R=["zt", "cst"], W=["lp"])
        lpv = lp.rearrange("p (i h) -> p i h", i=NT)
        for i in range(NT):
            for ip in range(i):
                P.op("pe", mm(psb[6][:, i * 8:(i + 1) * 8], ones_f[:], lpv[:, ip, :], ip == 0, ip == i - 1), R=["lp", "ones_f"], W=[("ps", 6)])
            for ip in range(i):
                P.op("pe", mm(psb[7][:, i * 8:(i + 1) * 8], ones_f[:], lpv[:, ip, :], ip == 0, False), R=["lp", "ones_f"], W=[("ps", 7)])
            P.op("pe", mm(psb[7][:, i * 8:(i + 1) * 8], U_f[:], lpv[:, i, :], i == 0, True), R=["lp", "U_f"], W=[("ps", 7)])
        P.op("dve", ms(PTt[:, 0, :], 0.0), W=["PT"])
        P.op("dve", cp(PTt[:, 1:NT, :], psb[6][:, 8:128].rearrange("p (i h) -> p i h", h=8)), R=[("ps", 6)], W=["PT"])
        P.op("dve", cp(Lp, psb[7][:, 0:128].rearrange("p (i h) -> p i h", h=8)), R=[("ps", 7)], W=["Lp"])
        locv = loc.rearrange("p (i h) -> p i h", i=NT)

        for hf in range(2):
            for c_ in range(2):
                proj_norm(l, C_KA + hf * 256 + c_ * 128, 1, blockones, 64, 96 + 2 * l + 1, [kT_ac[:, c_, :]], "kT", [wqk[c_]], sqb, rsb)
            for c_ in range(2):
                proj_norm(l, C_QA + hf * 256 + c_ * 128, 1, blockones, 64, 96 + 2 * l, [qT_ac[:, c_, :]], "qT", [wqk[2 + c_]], sqb, rsb)
            v_proj(l, C_VA, hf, wv, vaug)
            P.barrier()
            def prepA(c, hf=hf):
                P.op("dve", tt(biasA[c % 2][:, 0:4 * c + 4, :], Lp[:, 0:4 * c + 4, :], PTt[:, 4 * c:4 * c + 1, :].broadcast_to([128, 4 * c + 4, 8]), ALU.subtract),
                     R=["Lp", "PT"], W=[("biasA", c % 2)])
                P.op("dve", tt(locv[:, 0:4, :], Lp[:, 4 * c:4 * c + 4, :], PTt[:, 4 * c:4 * c + 1, :].broadcast_to([128, 4, 8]), ALU.subtract),
                     R=["Lp", "PT", "zt"], W=["loc"])
                P.op("dve", ts(rq_bf[:, 0:32], loc[:, 0:32], -8.0, None, ALU.mult), R=["loc"], W=["rq_bf"])
                ptv = psbf(2).rearrange("p (a b) -> p a b", a=8)
                for t in range(4):
                    P.op("pe", tr(ptv[0:8, t, :], rq_bf[:, t * 8:(t + 1) * 8], ident_bf[:]), R=["rq_bf", "ident_bf"], W=[("ps", 2)])
                P.op("dve", tt(rqm[0:8, :, :], psbf(2)[0:8, 0:512].unsqueeze(1).broadcast_to([8, 4, 512]),
                               ident_f[0:8, 4 * hf:4 * hf + 4].unsqueeze(2).broadcast_to([8, 4, 512]), ALU.mult),
                     R=[("ps", 2), "ident_f"], W=["rqm"])

            attention2("A", hf, 0,
                       lambda hh, j: kT_ac[(hh % 2) * 64:(hh % 2) * 64 + 64, hh // 2, j * 128:(j + 1) * 128],
                       lambda c, hh: qT_ac[(hh % 2) * 64:(hh % 2) * 64 + 64, hh // 2, c * 512:(c + 1) * 512],
                       vaug, Pt, ytile4, 0.125, biasA=biasA, prep=prepA, rqm=rqm)
            P.barrier()
            if DBG == 2 and l == 0 and hf == 1:
                P.dma("sp", dmaf(dbg_d, arena[:, Z0:Z0 + 24576]), sem="dbg")
                P.barrier()

    def phase_C(l):
        z = ZK + 4096
        EBrev = bv(z, 2560).rearrange("p (h c) -> p h c", h=4); z += 2560
        Tst = [fv(z, 640), fv(z, 640)]; z += 1280
        assert z <= ZK + 8192
        set_vones()
        for hf in range(2):
            for hh in range(4):
                h = 4 * hf + hh
                src = relx[l * 8 + h, :, :]
                P.dma("sp", dmaf(Tst[0], src), W=[("Tst", 0)], sem="Tst0")
                P.op("act", act(Tst[0], Tst[0], AF.Exp), R=[("Tst", 0)], W=[("Tst", 0)])
                for d in range(5):
                    P.op("dve", tt(EBrev[:, hh, d * 128:(d + 1) * 128], Tst[0][:, d * 128:(d + 1) * 128], validC[:, d * 128:(d + 1) * 128], ALU.mult),
                         R=[("Tst", 0), "validC"], W=["EBrev"])
            for c_ in range(2):
                proj_norm(l, C_KC + hf * 256 + c_ * 128, 1, blockones, 64, 104 + 2 * l + 1, [kT_ac[:, c_, :]], "kT", [wqk[c_]], sqb, rsb)
            for c_ in range(2):
                proj_norm(l, C_QC + hf * 256 + c_ * 128, 1, blockones, 64, 104 + 2 * l, [qT_ac[:, c_, :]], "qT", [wqk[2 + c_]], sqb, rsb)
            v_proj(l, C_VC, hf, wv, vaug)
            P.barrier()
            attention2("C", hf, 2,
                       lambda hh, j: kT_ac[(hh % 2) * 64:(hh % 2) * 64 + 64, hh // 2, j * 128:(j + 1) * 128],
                       lambda c, hh: qT_ac[(hh % 2) * 64:(hh % 2) * 64 + 64, hh // 2, c * 512:(c + 1) * 512],
                       vaug, Pt, ytile4, 0.125, EB=EBrev)
            P.barrier()

    def rope_ops(dst1, dst2, x1, x2, cosv, sinv, t1, t2, Rk, Wk):
        P.op("dve", tt(t1, x1, cosv, ALU.mult), R=Rk, W=["rt1"])
        P.op("dve", tt(t2, x2, sinv, ALU.mult), R=Rk, W=["rt2"])
        P.op("dve", tt(dst1, t1, t2, ALU.subtract), R=["rt1", "rt2"], W=Wk)
        P.op("dve", tt(t1, x2, cosv, ALU.mult), R=Rk, W=["rt1"])
        P.op("dve", tt(t2, x1, sinv, ALU.mult), R=Rk, W=["rt2"])
        P.op("dve", tt(dst2, t1, t2, ALU.add), R=["rt1", "rt2"], W=Wk)

    def phase_B(l):
        z = ZQ + 1024
        wkr = bv(z, 256).rearrange("p (k c) -> p k c", k=8); z += 256
        wqup = bv(z, 1152).rearrange("p (k c) -> p k c", k=3); z += 1152
        wkvup = bv(z, 1024).rearrange("p (k c) -> p k c", k=2); z += 1024
        krope = bv(z, 512).rearrange("p (i c) -> p i c", i=NT); z += 512
        assert z <= ZQ + 4096
        z = 6144
        tmpf = fv(z, 512); z += 1024
        tmpq = fv(z, 384).rearrange("p (h c) -> p h c", h=4); z += 768
        assert z <= 8192
        z = 16384 + 4096
        ktok = [bv(z + k * 512, 512).rearrange("p (h c) -> p h c", h=4) for k in range(2)]; z += 1024
        qtok = [bv(z + k * 512, 512).rearrange("p (h c) -> p h c", h=4) for k in range(2)]; z += 1024
        for k_ in range(2):
            P.op("pool", ms(ktok[k_], 0.0), W=[("ktok", k_)])
            P.op("pool", ms(qtok[k_], 0.0), W=[("qtok", k_)])
        rt1 = fv(z, 256); z += 512
        rt2 = fv(z, 256); z += 512
        kvsb = fv(z, 512); z += 1024
        assert z <= 24576, z
        qdnT = bv(0, 6144).rearrange("p (c t) -> p c t", c=3)
        kvdnT = bv(16384, 4096).rearrange("p (c t) -> p c t", c=2)
        set_vones()
        proj_norm(l, C_QD, 3, ones_bf, 384, 112 + 3 * l, [qdnT[:, c, :] for c in range(3)], "qdnT", wqk[0:3], sqb, rsb, gstep=1)
        proj_norm(l, C_KVD, 2, ones_bf, 256, 124 + 2 * l, [kvdnT[:, c, :] for c in range(2)], "kvdnT", [wqk[3], wqk[0]], sqb, rsb, gstep=1)
        if BSTOP == 1:
            P.op("pool", ms(yT[:, 1, :, :], 0.0), W=[("yT", 1)]); P.barrier(); return
        wload(wkr, w_in[l, :, C_KR:C_KR + 32], "wkr")
        for i in range(NT):
            for kc in range(8):
                P.op("pe", mm(psb[5][:, i * 32:(i + 1) * 32], hT[:, kc, i * 128:(i + 1) * 128], wkr[:, kc, :], kc == 0, kc == 7),
                     R=["wkr", ("hT", i // 4)], W=[("ps", 5)])
        pk = psb[5][:].rearrange("p (i c) -> p i c", i=NT)
        tfv = tmpf.rearrange("p (i c) -> p i c", i=NT)
        sm = small[:, 16:32]
        P.op("act", act(tmpf, psb[5][:], AF.Square), R=[("ps", 5)], W=["tmpf"])
        P.op("dve", red(sm, tfv), R=["tmpf"], W=["sm"])
        rsqrt_small(sm, sm, 1.0 / 32, EPS, ["sm"], ["sm"])
        P.op("dve", tt(tfv, pk, sm.unsqueeze(2).broadcast_to([128, NT, 32]), ALU.mult), R=[("ps", 5), "sm"], W=["tmpf"])
        gk = gbr[:, l * 64 + 32:l * 64 + 64]
        P.op("dve", tt(tfv, tfv, gk.unsqueeze(1).broadcast_to([128, NT, 32]), ALU.mult), R=["tmpf", "gbr"], W=["tmpf"])
        r1 = rt1.rearrange("p (i c) -> p i c", i=NT)
        r2 = rt2.rearrange("p (i c) -> p i c", i=NT)
        rope_ops(krope[:, :, 0:16], krope[:, :, 16:32], tfv[:, :, 0:16], tfv[:, :, 16:32], cs[:, :, 0:16], cs[:, :, 16:32],
                 r1, r2, ["tmpf", "cs"], ["krope"])
        P.barrier()
        if BSTOP == 2:
            P.op("pool", ms(yT[:, 1, :, :], 0.0), W=[("yT", 1)]); P.barrier(); return
        for hf in range(2):
            wload(wkvup, w_kv_up[l, :, hf * 512:(hf + 1) * 512], "wkvup")
            wload(wqup, w_q_up[l, :, hf * 384:(hf + 1) * 384], "wqup")
            for i in range(NT):
                bank = 6
                pkv = psb[bank][:].rearrange("p (h c) -> p h c", h=4)
                for kc in range(2):
                    P.op("pe", mm(psb[bank][:], kvdnT[:, kc, i * 128:(i + 1) * 128], wkvup[:, kc, :], kc == 0, kc == 1),
                         R=["kvdnT", "wkvup"], W=[("ps", bank)])
                t4 = tmpf[:, 0:256].rearrange("p (h c) -> p h c", h=4)
                s4 = small[:, 32:36]
                P.op("act", cp_act(kvsb, psb[bank][:]), R=[("ps", bank)], W=["kvsb"])
                pkv = kvsb.rearrange("p (h c) -> p h c", h=4)
                P.op("act", act(t4, pkv[:, :, 0:64], AF.Square), R=["kvsb"], W=["tmpf"])
                P.op("dve", red(s4, t4), R=["tmpf"], W=["s4"])
                rsqrt_small(s4, s4, 1.0 / 64, EPS, ["s4"], ["s4"])
                P.op("dve", tt(t4, pkv[:, :, 0:64], s4.unsqueeze(2).broadcast_to([128, 4, 64]), ALU.mult), R=["kvsb", "s4"], W=["tmpf"])
                kt = ktok[i % 2]
                gn = gbn[:, l * 128 + 64:l * 128 + 128]
                P.op("dve", tt(kt[:, :, 0:64], t4, gn.unsqueeze(1).broadcast_to([128, 4, 64]), ALU.mult), R=["tmpf", "gbn"], W=[("ktok", i % 2)])
                P.op("dve", cp(kt[:, :, 64:96], krope[:, i, :].unsqueeze(1).broadcast_to([128, 4, 32])), R=["krope"], W=[("ktok", i % 2)])
                P.op("act", cp_act(vaug[:, i, :, 0:64], pkv[:, :, 64:128]), R=["kvsb"], W=["v"])
                ptv = psbf(7).rearrange("p (a b) -> p a b", a=8)
                for hh in range(4):
                    P.op("pe", tr(ptv[:, hh, :], kt[:, hh, :], ident_bf[:]), R=[("ktok", i % 2), "ident_bf"], W=[("ps", 7)])
                P.op("act", cp_act(kT_b[0:96, :, i * 128:(i + 1) * 128], ptv[0:96, 0:4, :]), R=[("ps", 7)], W=["kT"])

            if BSTOP == 3:
                P.barrier(); P.op("pool", ms(yT[:, 1, :, :], 0.0), W=[("yT", 1)]); P.barrier(); return
            def qprep(i):
                bank = 6
                pq = psb[bank][:, 0:384].rearrange("p (h c) -> p h c", h=4)
                for kc in range(3):
                    P.op("pe", mm(psb[bank][:, 0:384], qdnT[:, kc, i * 128:(i + 1) * 128], wqup[:, kc, :], kc == 0, kc == 2),
                         R=["qdnT", "wqup"], W=[("ps", bank)])
                t4 = tmpf[:, 0:384].rearrange("p (h c) -> p h c", h=4)
                sn = small[:, 36:40]
                sr = small[:, 40:44]
                P.op("act", cp_act(kvsb[:, 0:384], psb[bank][:, 0:384]), R=[("ps", bank)], W=["kvsb"])
                pq = kvsb[:, 0:384].rearrange("p (h c) -> p h c", h=4)
                P.op("act", act(t4, pq, AF.Square), R=["kvsb"], W=["tmpf"])
                P.op("dve", red(sn, t4[:, :, 0:64]), R=["tmpf"], W=["sn"])
                P.op("dve", red(sr, t4[:, :, 64:96]), R=["tmpf"], W=["sr"])
                rsqrt_small(sn, sn, 1.0 / 64, EPS, ["sn"], ["sn"])
                rsqrt_small(sr, sr, 1.0 / 32, EPS, ["sr"], ["sr"])
                P.op("dve", tt(tmpq[:, :, 0:64], pq[:, :, 0:64], sn.unsqueeze(2).broadcast_to([128, 4, 64]), ALU.mult), R=["kvsb", "sn"], W=["tmpq"])
                P.op("dve", tt(tmpq[:, :, 64:96], pq[:, :, 64:96], sr.unsqueeze(2).broadcast_to([128, 4, 32]), ALU.mult), R=["kvsb", "sr"], W=["tmpq"])
                qt = qtok[i % 2]
                gqn = gbn[:, l * 128:l * 128 + 64]
                gqr = gbr[:, l * 64:l * 64 + 32]
                P.op("dve", tt(qt[:, :, 0:64], tmpq[:, :, 0:64], gqn.unsqueeze(1).broadcast_to([128, 4, 64]), ALU.mult), R=["tmpq", "gbn"], W=[("qtok", i % 2)])
                P.op("dve", tt(tmpq[:, :, 64:96], tmpq[:, :, 64:96], gqr.unsqueeze(1).broadcast_to([128, 4, 32]), ALU.mult), R=["tmpq", "gbr"], W=["tmpq"])
                c4 = cs[:, i, 0:16].unsqueeze(1).broadcast_to([128, 4, 16])
                s4b = cs[:, i, 16:32].unsqueeze(1).broadcast_to([128, 4, 16])
                q1 = rt1[:, 0:64].rearrange("p (h c) -> p h c", h=4)
                q2 = rt2[:, 0:64].rearrange("p (h c) -> p h c", h=4)
                rope_ops(qt[:, :, 64:80], qt[:, :, 80:96], tmpq[:, :, 64:80], tmpq[:, :, 80:96], c4, s4b, q1, q2, ["tmpq", "cs"], [("qtok", i % 2)])

            def qprep_b(i):
                qt = qtok[i % 2]
                ptv = psbf(7).rearrange("p (a b) -> p a b", a=8)
                for hh in range(4):
                    P.op("pe", tr(ptv[:, hh, :], qt[:, hh, :], ident_bf[:]), R=[("qtok", i % 2), "ident_bf"], W=[("ps", 7)])
                P.op("act", cp_act(qTt[i % 2][0:96, :, :], ptv[0:96, 0:4, :]), R=[("ps", 7)], W=[("qTt", i % 2)])

            attention("B", hf, 1,
                      lambda hh, j: kT_b[0:96, hh, j * 128:(j + 1) * 128],
                      lambda hh, i: qTt[i % 2][0:96, hh, :],
                      lambda i: ("qTt", i % 2), vaug, Pt, ytile, float(96.0 ** -0.5), qprep=(qprep, qprep_b))
            P.barrier()

    def merge_phase(l):
        z = Z0
        mT = bv(z, 8192).rearrange("p (c t) -> p c t", c=8); z += 8192
        wg = [bv(z + k * 3072, 3072).rearrange("p (k n c) -> p k n c", k=8, n=3) for k in range(2)]; z += 6144
        wb = [bv(z + k * 1536, 1536).rearrange("p (k n c) -> p k n c", k=4, n=3) for k in range(2)]; z += 3072
        wo = [bv(z + k * 2048, 2048).rearrange("p (k c) -> p k c", k=8) for k in range(2)]; z += 4096
        gate = [bv(z + k * 512, 512) for k in range(3)]; z += 1536
        acc = fv(z, 512); z += 1024
        tmp = fv(z, 512); z += 1024
        assert z <= AR_N, z
        cnt = 0
        for th in range(2):
            for m in range(8):
                sl = cnt % 2
                cnt += 1
                for n in range(3):
                    c0 = C_G + n * 1024 + m * 128
                    P.dma("pool", dmaf(wg[sl][:, :, n, :], w_in[l, :, c0:c0 + 128].rearrange("(k p) c -> p k c", p=128)), W=[("wg", sl, n)], sem="wg%d_%d" % (sl, n))
                    P.dma("pool", dmaf(wb[sl][:, :, n, :], w_branch[l, n, :, m * 128:(m + 1) * 128].rearrange("(k p) c -> p k c", p=128)), W=[("wb", sl, n)], sem="wb%d_%d" % (sl, n))
                for tq in range(2):
                    tc = th * 2 + tq
                    tsl = slice(tc * 512, (tc + 1) * 512)
                    u = (m * 2 + tq) % 2
                    for n in range(3):
                        gb = u * 4 + n if n < 2 else u * 4 + 2
                        gb = u * 4 + n
                        for kc in range(8):
                            P.op("pe", mm(psb[gb][:], wg[sl][:, kc, n, :], hT[:, kc, tsl], kc == 0, kc == 7), R=[("wg", sl, n), ("hT", tc)], W=[("ps", gb)])
                        P.op("act", act(gate[n], psb[gb][:], AF.Sigmoid, bias=vecT[:, l * 24 + n * 8 + m:l * 24 + n * 8 + m + 1]),
                             R=[("ps", gb), "vecT"], W=[("gate", n)])
                        pb = u * 4 + 3
                        for kc in range(4):
                            P.op("pe", mm(psb[pb][:], wb[sl][:, kc, n, :], yT[:, n, kc, tsl], kc == 0, kc == 3), R=[("wb", sl, n), ("yT", n)], W=[("ps", pb)])
                        if n == 0:
                            P.op("dve", tt(acc, psb[pb][:], gate[n], ALU.mult), R=[("ps", pb), ("gate", n)], W=["acc"])
                        else:
                            P.op("dve", tt(tmp, psb[pb][:], gate[n], ALU.mult), R=[("ps", pb), ("gate", n)], W=["tmp"])
                            if n == 1:
                                P.op("dve", tt(acc, acc, tmp, ALU.add), R=["tmp"], W=["acc"])
                            else:
                                P.op("dve", tt(mT[:, m, tq * 512:(tq + 1) * 512], acc, tmp, ALU.add), R=["tmp", "acc"], W=["mT"])
            for cq in range(4):
                sl = cq % 2
                wload(wo[sl], w_out[l, :, cq * 256:(cq + 1) * 256], ("wo", sl))
                for ii in range(8):
                    i = th * 8 + ii
                    bank = ii % 2 + 6 if False else (ii % 4)
                    for kc in range(8):
                        P.op("pe", mm(psb[bank][:, 0:256], mT[:, kc, ii * 128:(ii + 1) * 128], wo[sl][:, kc, :], kc == 0, kc == 7),
                             R=["mT", ("wo", sl)], W=[("ps", bank)])
                    P.op("dve", tt(xs[:, i, cq * 256:(cq + 1) * 256], xs[:, i, cq * 256:(cq + 1) * 256], psb[bank][:, 0:256], ALU.add),
                         R=[("ps", bank)], W=[("x", i)])
        P.barrier()

    def ffn_phase(l):
        aT = [bv(0, 16384).rearrange("p (c t) -> p c t", c=8), bv(Z0, 16384).rearrange("p (c t) -> p c t", c=8)]
        w2 = [bv(16384 + k * 4096, 4096).rearrange("p (k c) -> p k c", k=8) for k in range(2)]
        z = Z0 + 16384
        w1 = [bv(z + k * 2048, 2048).rearrange("p (k c) -> p k c", k=8) for k in range(2)]; z += 4096
        rt = [fv(z + k * 1024, 512) for k in range(2)]; z += 2048
        assert z <= AR_N, z
        c1 = 0
        c2 = 0
        bk = 0
        for g in range(4):
            a = aT[g % 2]
            for fp in range(4):
                sl = c1 % 2
                c1 += 1
                wload(w1[sl], w_ff1[l, :, g * 1024 + fp * 256: g * 1024 + (fp + 1) * 256], ("w1", sl))
                for f2 in range(2):
                    f = fp * 2 + f2
                    for tc in range(4):
                        bank = bk % 4
                        bk += 1
                        for kc in range(8):
                            P.op("pe", mm(psb[bank][:], w1[sl][:, kc, f2 * 128:(f2 + 1) * 128], hT[:, kc, tc * 512:(tc + 1) * 512], kc == 0, kc == 7),
                                 R=[("w1", sl), ("hT", tc)], W=[("ps", bank)])
                        r = rt[bank % 2]
                        P.op("act", act(r, psb[bank][:], AF.Relu), R=[("ps", bank)], W=[("rt", bank % 2)])
                        P.op(SQ_ENG, tt(a[:, f, tc * 512:(tc + 1) * 512], r, r, ALU.mult), R=[("rt", bank % 2)], W=[("aT", g % 2)])
            for ch in range(2):
                sl = c2 % 2
                c2 += 1
                wload(w2[sl], w_ff2[l, g * 1024:(g + 1) * 1024, ch * 512:(ch + 1) * 512], ("w2", sl))
                for i in range(NT):
                    bank = 4 + (i % 4)
                    for f in range(8):
                        P.op("pe", mm(psb[bank][:], a[:, f, i * 128:(i + 1) * 128], w2[sl][:, f, :], f == 0, f == 7),
                             R=[("aT", g % 2), ("w2", sl)], W=[("ps", bank)])
                    P.op("dve", tt(xs[:, i, ch * 512:(ch + 1) * 512], xs[:, i, ch * 512:(ch + 1) * 512], psb[bank][:], ALU.add),
                         R=[("ps", bank)], W=[("x", i)])
        P.barrier()

    MASK_ENG = "pool"
    SQ_ENG = "pool"
    for l in range(nlayers):
        if "N" in PH:
            norm_phase(norm_mix[l, :])
        for nb_, ch_ in enumerate("ABC"):
            if ch_ not in PH:
                P.op("pool", ms(yT[:, nb_, :, :], 0.0), W=[("yT", nb_)])
        if "B" in PH:
            phase_B(l)
        if "A" in PH:
            phase_A(l)
        if "C" in PH:
            phase_C(l)
        if DBG == 1 and l == 0:
            P.barrier()
            P.dma("sp", dmaf(dbg_d, arena[:, 0:24576]), R=[("yT", 0), ("yT", 1), ("yT", 2)], sem="dbg")
            P.barrier()
        if "M" in PH:
            merge_phase(l)
        if "F" in PH:
            norm_phase(norm_ffn[l, :])
            ffn_phase(l)
    for i in range(NT):
        P.dma("sp", dmaf(y_d[i * 128:(i + 1) * 128, :], xs[:, i, :]), R=[("x", i)], sem="y%d" % i)
    P.barrier()
    P.emit(nc, st)
    st.close()
    return nc


def _host_consts(rel_bias):
    half = 16
    inv = (10000.0 ** (-np.arange(half, dtype=np.float32) / half)).astype(np.float32)
    ang = np.arange(S, dtype=np.float32)[:, None] * inv[None, :]
    cs_tab = np.concatenate([np.cos(ang), np.sin(ang)], axis=1).astype(np.float32)
    kl = np.arange(128)[:, None]
    c = np.arange(640)[None, :]
    idx = np.clip(c - kl, -256, 256) + 256
    rel_ext = np.ascontiguousarray(rel_bias[:, :, idx]).reshape(DEPTH * 8, 128, 640).astype(np.float32)
    return cs_tab, rel_ext


PH = "NBACMF"
DBG = False
BSTOP = 0
_NC_CACHE = {}


def kernel(**inputs):
    inp = {k: np.ascontiguousarray(np.asarray(v, dtype=np.float32)) for k, v in inputs.items()}
    x = inp.pop("x")
    rel_bias = inp.pop("rel_bias")
    cs_tab, rel_ext = _host_consts(rel_bias)
    key = (DEPTH, PH)
    if key not in _NC_CACHE:
        _NC_CACHE[key] = build(DEPTH)
    nc = _NC_CACHE[key]
    B = x.shape[0]
    in_maps = []
    for b in range(B):
        m = dict(inp)
        m["x"] = np.ascontiguousarray(x[b])
        m["rel_ext"] = rel_ext
        m["cs_tab"] = cs_tab
        in_maps.append(m)
    res = run_bass_kernel_spmd(nc, in_maps, core_ids=list(range(B)))
    return np.stack([np.asarray(r["y"], dtype=np.float32) for r in res.results], axis=0)
```

```python
import numpy as np
from contextlib import ExitStack
import concourse.bass as bass
import concourse.mybir as mybir
from concourse.bass_utils import run_bass_kernel_spmd

F32 = mybir.dt.float32
BF16 = mybir.dt.bfloat16
AF = mybir.ActivationFunctionType
ALU = mybir.AluOpType
AX = mybir.AxisListType

S = 2048
D = 1024
NT = 16
DEPTH = 4
EPS = 1e-6
INW = 6824
C_QA, C_KA, C_VA, C_FA, C_QD, C_KVD, C_KR, C_QC, C_KC, C_VC, C_G = 0, 512, 1024, 1536, 1544, 1928, 2184, 2216, 2728, 3240, 3752


class Prog:
    ENG = ("pe", "act", "dve", "pool", "sp")

    def __init__(self):
        self.ops = {e: [] for e in self.ENG}
        self.state = {}
        self.known = {e: {} for e in self.ENG}
        self.flag = {e: set() for e in self.ENG}
        self.semcnt = {}
        self.tag = ""
        self.tags = {e: [] for e in self.ENG}

    def _add(self, eng, fn, R, W, dma_sem):
        self.tags[eng].append(self.tag)
        need = {}
        for k in R:
            st = self.state.get(k)
            if st and st[0] is not None:
                t = st[0]
                need[t[0]] = max(need.get(t[0], -1), t[1])
        for k in W:
            st = self.state.get(k)
            if st:
                if st[0] is not None:
                    t = st[0]
                    need[t[0]] = max(need.get(t[0], -1), t[1])
                for tk, tv in st[1].items():
                    need[tk] = max(need.get(tk, -1), tv)
        kn = self.known[eng]
        waits = []
        for tk, tv in need.items():
            if dma_sem is None and eng == "pe" and tk == ("E", "pe"):
                continue
            if kn.get(tk, -1) >= tv:
                continue
            kn[tk] = tv
            waits.append((tk, tv))
            if tk[0] == "E":
                self.flag[tk[1]].add(tv)
        idx = len(self.ops[eng])
        if dma_sem is None:
            tok = (("E", eng), idx)
        else:
            c = self.semcnt.get(dma_sem, 0) + 1
            self.semcnt[dma_sem] = c
            tok = (("S", dma_sem), c)
        self.ops[eng].append((fn, waits, dma_sem))
        for k in R:
            if k in W:
                continue
            st = self.state.setdefault(k, [None, {}])
            st[1][tok[0]] = max(st[1].get(tok[0], -1), tok[1])
        for k in W:
            self.state[k] = [tok, {}]
        return tok

    def op(self, eng, fn, R=(), W=()):
        return self._add(eng, fn, list(R), list(W), None)

    def dma(self, q, fn, R=(), W=(), sem=None):
        return self._add(q, fn, list(R), list(W), sem)

    def barrier(self):
        last = {}
        for e in self.ENG:
            n = len(self.ops[e])
            for i in range(n - 1, -1, -1):
                if self.ops[e][i][2] is None and self.ops[e][i][0] is not None:
                    last[("E", e)] = i
                    break
        for s, c in self.semcnt.items():
            last[("S", s)] = c
        for e in self.ENG:
            kn = self.known[e]
            waits = []
            for tk, tv in last.items():
                if kn.get(tk, -1) >= tv:
                    continue
                kn[tk] = tv
                waits.append((tk, tv))
                if tk[0] == "E":
                    self.flag[tk[1]].add(tv)
            if waits:
                self.ops[e].append((None, waits, None))
                self.tags[e].append(self.tag)

    def emit(self, nc, stack):
        rank = {}
        for e in self.ENG:
            rank[e] = {idx: r + 1 for r, idx in enumerate(sorted(self.flag[e]))}
        BS = 2000
        esem = {e: [stack.enter_context(nc.semaphore("es_%s_%d" % (e, b))) for b in range(len(rank[e]) // BS + 1)] for e in self.ENG}
        dsem = {s: stack.enter_context(nc.semaphore("ds_" + str(i))) for i, s in enumerate(self.semcnt)}
        block = stack.enter_context(nc.Block())

        def run(e, h):
            for idx, (fn, waits, ds) in enumerate(self.ops[e]):
                for tk, tv in waits:
                    if tk[0] == "E":
                        r_ = rank[tk[1]][tv] - 1
                        h.wait_ge(esem[tk[1]][r_ // BS], r_ % BS + 1)
                    else:
                        h.wait_ge(dsem[tk[1]], tv * 16)
                if fn is None:
                    continue
                ins = fn(h)
                if ANNOT:
                    ins.annotate(self.tags[e][idx])
                if ds is not None:
                    ins.then_inc(dsem[ds], 16)
                elif idx in rank[e]:
                    ins.then_inc(esem[e][(rank[e][idx] - 1) // BS], 1)

        block.tensor(lambda h: run("pe", h))
        block.scalar(lambda h: run("act", h))
        block.vector(lambda h: run("dve", h))
        block.gpsimd(lambda h: run("pool", h))
        block.sync(lambda h: run("sp", h))


def build(nlayers=DEPTH):
    nc = bass.Bass("TRN2", target_bir_lowering=False)
    P = Prog()
    st = ExitStack()

    def din(name, shape):
        return nc.dram_tensor(name, list(shape), F32, kind="ExternalInput")

    x_d = din("x", [S, D]).ap()
    norm_mix = din("norm_mix", [DEPTH, D]).ap()
    w_in = din("w_in", [DEPTH, D, INW]).ap()
    b_forget = din("b_forget", [DEPTH, 8]).ap()
    b_gate = din("b_gate", [DEPTH, 3072]).ap()
    qk_norm_a = din("qk_norm_a", [DEPTH, 2, 64]).ap()
    mla_q_norm = din("mla_q_norm", [DEPTH, 384]).ap()
    mla_kv_norm = din("mla_kv_norm", [DEPTH, 256]).ap()
    w_q_up = din("w_q_up", [DEPTH, 384, 768]).ap()
    w_kv_up = din("w_kv_up", [DEPTH, 256, 1024]).ap()
    qk_norm_b_nope = din("qk_norm_b_nope", [DEPTH, 2, 64]).ap()
    qk_norm_b_rope = din("qk_norm_b_rope", [DEPTH, 2, 32]).ap()
    qk_norm_c = din("qk_norm_c", [DEPTH, 2, 64]).ap()
    relx = din("rel_ext", [DEPTH * 8, 128, 640]).ap()
    w_branch = din("w_branch", [DEPTH, 3, 512, D]).ap()
    w_out = din("w_out", [DEPTH, D, D]).ap()
    norm_ffn = din("norm_ffn", [DEPTH, D]).ap()
    w_ff1 = din("w_ff1", [DEPTH, D, 4096]).ap()
    w_ff2 = din("w_ff2", [DEPTH, 4096, D]).ap()
    cs_d = din("cs_tab", [S, 32]).ap()
    y_d = nc.dram_tensor("y", [S, D], F32, kind="ExternalOutput").ap()
    dbg_d = nc.dram_tensor("dbg", [128, 24576], BF16, kind="ExternalOutput").ap() if DBG else None

    def sb(name, shape, dt):
        return st.enter_context(nc.sbuf_tensor(name, list(shape), dt))

    xs = sb("xs", [128, NT, D], F32)
    hT = sb("hT", [128, 8, S], BF16)
    ident_bf = sb("ident_bf", [128, 128], BF16)
    ident_f = sb("ident_f", [128, 128], F32)
    blockones = sb("blockones", [128, 128], BF16)
    ones_bf = sb("ones_bf", [128, 128], BF16)
    maskA = sb("maskA", [128, 128], BF16)
    maskB = sb("maskB", [128, 128], BF16)
    U_f = sb("U_f", [128, 128], F32)
    ones_f = sb("ones_f", [128, 128], F32)
    validC = sb("validC", [128, 640], BF16)
    vecT = sb("vecT", [128, 132], F32)
    bf_rep = sb("bf_rep", [128, 32], F32)
    gbn = sb("gbn", [128, 512], F32)
    gbr = sb("gbr", [128, 256], F32)
    cs = sb("cs", [128, NT, 32], F32)
    ssq = sb("ssq", [128, NT], F32)
    rstd = sb("rstd", [128, NT], F32)
    small = sb("small", [128, 64], F32)
    cst = sb("cst", [128, 4], F32)
    AR_N = 52000
    arena = sb("arena", [128, AR_N], BF16)
    psb = [st.enter_context(nc.psum_tensor("ps%d" % b, [128, 512], F32)) for b in range(8)]

    def bv(off, n):
        return arena[:, off:off + n]

    def fv(off, n):
        return arena[:, off:off + 2 * n].bitcast(F32)

    def psbf(b):
        return psb[b][:].bitcast(BF16)

    Z0 = 24576
    yT = bv(0, 24576).rearrange("p (n c t) -> p n c t", n=3, c=4)

    def mm(out, lhsT, rhs, start, stop):
        return lambda e: e.matmul(out, lhsT, rhs, start=start, stop=stop)

    def tr(out, in_, ident):
        return lambda e: e.transpose(out, in_, ident)

    def act(out, in_, func, bias=None, scale=None, accum_out=None):
        kw = {}
        if bias is not None:
            kw["bias"] = bias
        if scale is not None:
            kw["scale"] = scale
        if accum_out is not None:
            kw["accum_out"] = accum_out
        return lambda e: e.activation(out=out, in_=in_, func=func, **kw)

    def tt(out, in0, in1, op):
        return lambda e: e.tensor_tensor(out=out, in0=in0, in1=in1, op=op)

    def ts(out, in0, s1, s2, op0, op1=None):
        if op1 is None:
            return lambda e: e.tensor_single_scalar(out=out, in_=in0, scalar=s1, op=op0)
        return lambda e: e.tensor_scalar(out=out, in0=in0, scalar1=s1, scalar2=s2, op0=op0, op1=op1)

    def stt(out, in0, scalar, in1, op0, op1):
        return lambda e: e.scalar_tensor_tensor(out=out, in0=in0, scalar=scalar, in1=in1, op0=op0, op1=op1)

    def cp(out, in_):
        return lambda e: e.tensor_copy(out=out, in_=in_)

    def rsqrt_small(dst, src, mul, eps, Rk, Wk):
        P.op("dve", ts(dst, src, mul, eps, ALU.mult, ALU.add), R=Rk, W=Wk)
        P.op("act", act(dst, dst, AF.Ln), R=Wk, W=Wk)
        P.op("act", act(dst, dst, AF.Exp, scale=-0.5), R=Wk, W=Wk)

    def red(out, in_):
        return lambda e: e.tensor_reduce(out=out, in_=in_, axis=AX.X, op=ALU.add)

    def ms(ap, v):
        return lambda e: e.memset(ap, v)

    def dmaf(out, in_):
        return lambda e: e.dma_start(out=out, in_=in_)

    def wload(dst, src2d, key, R=(), q="pool"):
        P.dma(q, dmaf(dst, src2d.rearrange("(k p) c -> p k c", p=128)), R=R, W=[key], sem=str(key))

    P.op("pool", ms(ones_f[:], 1.0), W=["ones_f"])
    P.op("pool", lambda e: e.affine_select(out=ident_f[:], in_=ones_f[:], pattern=[[-1, 128]], compare_op=ALU.is_equal,
                                           fill=0.0, base=0, channel_multiplier=1), R=["ones_f"], W=["ident_f"])
    P.op("pool", lambda e: e.affine_select(out=U_f[:], in_=ones_f[:], pattern=[[1, 128]], compare_op=ALU.is_ge,
                                           fill=0.0, base=0, channel_multiplier=-1), R=["ones_f"], W=["U_f"])
    P.op("pool", cp(ident_bf[:], ident_f[:]), R=["ident_f"], W=["ident_bf"])
    P.op("pool", ts(maskA[:], U_f[:], 60000.0, -60000.0, ALU.mult, ALU.add), R=["U_f"], W=["maskA"])
    P.op("pool", ms(ones_bf[:], 1.0), W=["ones_bf"])
    P.op("pool", ms(cst[:, 0:1], 1.0), W=["cst"])
    P.op("pool", ms(cst[:, 1:2], 64 * EPS), W=["cst"])
    P.op("pool", ms(cst[:, 2:3], 384 * EPS), W=["cst"])
    P.op("pool", ms(cst[:, 3:4], 256 * EPS), W=["cst"])
    P.op("pool", ms(blockones[:], 0.0), W=["blockones"])
    P.op("pool", ms(blockones[0:64, 0:64], 1.0), W=["blockones"])
    P.op("pool", ms(blockones[64:128, 64:128], 1.0), W=["blockones"])
    P.op("pool", ms(maskB[:], 1.0), W=["maskB"])
    P.op("pool", ms(maskB[64:128, 0:64], 0.0), W=["maskB"])
    P.op("pool", ms(validC[:], 0.0), W=["validC"])
    P.op("pool", ms(validC[0:64, 0:576], 1.0), W=["validC"])
    P.op("pool", ms(validC[64:128, 64:640], 1.0), W=["validC"])

    stage1 = fv(Z0, 128)
    stage2 = fv(Z0 + 256, 128)
    P.dma("sp", dmaf(stage1[0:96, :], b_gate.rearrange("l (c p) -> (l c) p", p=128)), W=["stage1"], sem="stage1")
    qa2 = qk_norm_a.rearrange("l q d -> (l q) d")
    qc2 = qk_norm_c.rearrange("l q d -> (l q) d")
    P.dma("sp", dmaf(stage2[0:8, 0:64], qa2), W=["stage2a"], sem="stage2")
    P.dma("sp", dmaf(stage2[0:8, 64:128], qa2), W=["stage2b"], sem="stage2")
    P.dma("sp", dmaf(stage2[8:16, 0:64], qc2), W=["stage2c"], sem="stage2")
    P.dma("sp", dmaf(stage2[8:16, 64:128], qc2), W=["stage2d"], sem="stage2")
    P.dma("sp", dmaf(stage2[16:28, :], mla_q_norm.rearrange("l (c p) -> (l c) p", p=128)), W=["stage2e"], sem="stage2")
    P.dma("sp", dmaf(stage2[28:36, :], mla_kv_norm.rearrange("l (c p) -> (l c) p", p=128)), W=["stage2f"], sem="stage2")
    P.op("pe", tr(psb[0][:, 0:96], stage1[0:96, :], ident_f[0:96, 0:96]), R=["stage1", "ident_f"], W=[("ps", 0)])
    P.op("pe", tr(psb[1][:, 0:36], stage2[0:36, :], ident_f[0:36, 0:36]),
         R=["stage2a", "stage2b", "stage2c", "stage2d", "stage2e", "stage2f", "ident_f"], W=[("ps", 1)])
    P.op("dve", cp(vecT[:, 0:96], psb[0][:, 0:96]), R=[("ps", 0)], W=["vecT"])
    P.op("dve", ts(vecT[:, 96:112], psb[1][:, 0:16], 8.0, None, ALU.mult), R=[("ps", 1)], W=["vecT"])
    P.op("dve", ts(vecT[:, 112:124], psb[1][:, 16:28], float(np.sqrt(384.0)), None, ALU.mult), R=[("ps", 1)], W=["vecT"])
    P.op("dve", ts(vecT[:, 124:132], psb[1][:, 28:36], 16.0, None, ALU.mult), R=[("ps", 1)], W=["vecT"])
    P.dma("sp", dmaf(bf_rep[:], b_forget.rearrange("l h -> (l h)").partition_broadcast(128)), W=["bf_rep"], sem="c1")
    P.dma("sp", dmaf(gbn[:], qk_norm_b_nope.rearrange("l q d -> (l q d)").partition_broadcast(128)), W=["gbn"], sem="c2")
    P.dma("sp", dmaf(gbr[:], qk_norm_b_rope.rearrange("l q d -> (l q d)").partition_broadcast(128)), W=["gbr"], sem="c3")
    P.dma("sp", dmaf(cs[:], cs_d.rearrange("(t p) c -> p t c", p=128)), W=["cs"], sem="c4")
    for i in range(NT):
        P.dma("sp", dmaf(xs[:, i, :], x_d[i * 128:(i + 1) * 128, :]), W=[("x", i)], sem="x%d" % i)
    P.barrier()

    def norm_phase(gain_row):
        gnorm = fv(Z0, 1024)
        htok = [bv(Z0 + 2048 + k * 1024, 1024) for k in range(2)]
        junk = bv(Z0 + 4096, 1024)
        P.dma("sp", dmaf(gnorm, gain_row.partition_broadcast(128)), W=["gnorm"], sem="gnorm")
        for i in range(NT):
            P.op("act", act(junk, xs[:, i, :], AF.Square, accum_out=ssq[:, i:i + 1]), R=[("x", i)], W=["junk", "ssq"])
        rsqrt_small(rstd[:], ssq[:], 1.0 / D, EPS, ["ssq"], ["rstd"])
        for i in range(NT):
            P.op("dve", stt(htok[i % 2], xs[:, i, :], rstd[:, i:i + 1], gnorm, ALU.mult, ALU.mult),
                 R=[("x", i), "rstd", "gnorm"], W=[("htok", i % 2)])
            bank = 6 + (i % 2)
            ptv = psbf(bank).rearrange("p (a b) -> p a b", a=8)
            for kc in range(8):
                P.op("pe", tr(ptv[:, kc, :], htok[i % 2][:, kc * 128:(kc + 1) * 128], ident_bf[:]),
                     R=[("htok", i % 2), "ident_bf"], W=[("ps", bank)])
            P.op("act", cp_act(hT[:, :, i * 128:(i + 1) * 128], ptv), R=[("ps", bank)], W=[("hT", i // 4)])
        P.barrier()

    def cp_act(out, in_):
        return lambda e: e.activation(out=out, in_=in_, func=AF.Copy)

    unit_ctr = [0]

    def proj_norm(l, col0, nchunk, onesmat, nfeat, gcol0, outs, okey, wslots, sqb, rsb, gstep=0, split=None):
        for c in range(nchunk):
            wload(wslots[c][0], w_in[l, :, col0 + c * 128: col0 + (c + 1) * 128], wslots[c][1])
        for tc in range(4):
            u = unit_ctr[0] % 2
            unit_ctr[0] += 1
            b0 = 4 * u
            tsl = slice(tc * 512, (tc + 1) * 512)
            for c in range(nchunk):
                for kc in range(8):
                    P.op("pe", mm(psb[b0 + c][:], wslots[c][0][:, kc, :], hT[:, kc, tsl], kc == 0, kc == 7),
                         R=[wslots[c][1], ("hT", tc)], W=[("ps", b0 + c)])
                P.op("act", act(sqb[u][c], psb[b0 + c][:], AF.Square), R=[("ps", b0 + c)], W=[("sq", u, c)])
            for c in range(nchunk):
                P.op("pe", mm(psb[b0 + 3][:], onesmat[:], sqb[u][c], c == 0, c == nchunk - 1),
                     R=[("sq", u, c), "ones_bf", "blockones"], W=[("ps", b0 + 3)])
            ecol = {64: 1, 384: 2, 256: 3}[nfeat]
            P.op("act", act(rsb[u], psb[b0 + 3][:], AF.Ln, bias=cst[:, ecol:ecol + 1]), R=[("ps", b0 + 3), "cst"], W=[("rs", u)])
            P.op("act", act(rsb[u], rsb[u], AF.Exp, scale=-0.5), R=[("rs", u)], W=[("rs", u)])
            for c in range(nchunk):
                if split is not None:
                    for (p0, oap) in ((0, split[0]), (64, split[1])):
                        P.op("dve", stt(oap[p0:p0 + 64, tsl], psb[b0 + c][p0:p0 + 64, :], vecT[p0:p0 + 64, gcol0:gcol0 + 1], rsb[u][p0:p0 + 64, :], ALU.mult, ALU.mult),
                             R=[("ps", b0 + c), ("rs", u), "vecT"], W=[okey])
                    continue
                P.op("dve", stt(outs[c][:, tsl], psb[b0 + c][:], vecT[:, gcol0 + gstep * c:gcol0 + gstep * c + 1], rsb[u], ALU.mult, ALU.mult),
                     R=[("ps", b0 + c), ("rs", u), "vecT"], W=[okey])

    attn_ctr = [0]

    def attention(kind, hf, nbr, kslice, qslice, qkey, vaug, Pt, ytile, scale, biasA=None, bias_prep=None, EBrev=None, qprep=None, qterm=None):
        def finish(i):
            ob = 3 + (i % 2)
            pov = psb[ob][:, 0:260].rearrange("p (h c) -> p h c", h=4)
            rec = small[:, (i % 2) * 4:(i % 2) * 4 + 4]
            P.op("dve", (lambda rec, pov: lambda e: e.reciprocal(out=rec.unsqueeze(2), in_=pov[:, :, 64:65]))(rec, pov),
                 R=[("ps", ob)], W=[("rec", i % 2)])
            yt = ytile[i % 2]
            P.op("dve", tt(yt.rearrange("p (h c) -> p h c", h=4), pov[:, :, 0:64], rec.unsqueeze(2).broadcast_to([128, 4, 64]), ALU.mult),
                 R=[("ps", ob), ("rec", i % 2)], W=[("ytile", i % 2)])
            ptv = psbf(5).rearrange("p (a b) -> p a b", a=8)
            for c in range(2):
                P.op("pe", tr(ptv[:, c, :], yt[:, c * 128:(c + 1) * 128], ident_bf[:]), R=[("ytile", i % 2), "ident_bf"], W=[("ps", 5)])
            P.op("act", cp_act(yT[:, nbr, 2 * hf:2 * hf + 2, i * 128:(i + 1) * 128], ptv[:, 0:2, :]), R=[("ps", 5)], W=[("yT", nbr)])

        if qprep is not None:
            qprep[0](0)
            qprep[1](0)
        if bias_prep is not None:
            bias_prep(0)
        fin_pending = None
        for i in range(NT):
            js = list(range(max(0, i - 4), i + 1)) if kind == "C" else list(range(0, i + 1))
            groups = [js[a:a + 4] for a in range(0, len(js), 4)]
            ob = 3 + (i % 2)
            pov = psb[ob][:, 0:260].rearrange("p (h c) -> p h c", h=4)
            items = [(hh, grp) for hh in range(4) for grp in groups]
            pend = []

            def second(hh, grp, sbk, i=i, js=js, ob=ob, pov=pov):
                n = len(grp)
                if kind == "A":
                    h = 4 * hf + hh
                    for jj, j in enumerate(grp):
                        P.op("act", act(Pt[sbk][:, jj * 128:(jj + 1) * 128], psb[sbk][:, jj * 128:(jj + 1) * 128], AF.Exp,
                                        bias=biasA[i % 2][:, j, h:h + 1], scale=scale),
                             R=[("ps", sbk), ("biasA", i % 2)], W=[("Pt", sbk)])
                else:
                    P.op("act", act(Pt[sbk][:, 0:n * 128], psb[sbk][:, 0:n * 128], AF.Exp, scale=scale),
                         R=[("ps", sbk)], W=[("Pt", sbk)])
                if kind == "C":
                    d0 = 4 - (i - grp[0])
                    P.op(MASK_ENG, tt(Pt[sbk][:, 0:n * 128], Pt[sbk][:, 0:n * 128], EBrev[:, hh, d0 * 128:(d0 + n) * 128], ALU.mult),
                         R=["EBrev"], W=[("Pt", sbk)])
                elif i in grp and kind == "B":
                    jj = grp.index(i)
                    P.op(MASK_ENG, tt(Pt[sbk][:, jj * 128:(jj + 1) * 128], Pt[sbk][:, jj * 128:(jj + 1) * 128], maskB[:], ALU.mult),
                         R=["maskB"], W=[("Pt", sbk)])
                for jj, j in enumerate(grp):
                    P.op("pe", mm(pov[:, hh, :], Pt[sbk][:, jj * 128:(jj + 1) * 128], vaug[:, j, hh, :], j == js[0], j == js[-1]),
                         R=[("Pt", sbk), "v"], W=[("ps", ob)])

            cnt = 0
            for (hh, grp) in items:
                sbk = attn_ctr[0] % 3
                attn_ctr[0] += 1
                for jj, j in enumerate(grp):
                    osl = psb[sbk][:, jj * 128:(jj + 1) * 128]
                    P.op("pe", mm(osl, kslice(hh, j), qslice(hh, i), True, qterm is None),
                         R=["kT", qkey(i)], W=[("ps", sbk)])
                    if qterm is not None:
                        rq_ap, rq_key = qterm(i, hh)
                        P.op("pe", mm(osl, ones_bf[:], rq_ap, False, j != i), R=[rq_key, "ones_bf"], W=[("ps", sbk)])
                        if j == i:
                            P.op("pe", mm(osl, ident_bf[:], maskA[:], False, True), R=["maskA", "ident_bf"], W=[("ps", sbk)])
                pend.append((hh, grp, sbk))
                cnt += 1
                if cnt == 2:
                    if fin_pending is not None:
                        finish(fin_pending)
                        fin_pending = None
                    if i + 1 < NT:
                        if qprep is not None:
                            qprep[0](i + 1)
                        if bias_prep is not None:
                            bias_prep(i + 1)
                if len(pend) > 2:
                    second(*pend.pop(0))
            while pend:
                second(*pend.pop(0))
            if qprep is not None and i + 1 < NT:
                qprep[1](i + 1)
            fin_pending = i
        finish(fin_pending)

    def attention2(kind, hf, nbr, kslice, qchunk, vaug, Pt, ytile4, scale, biasA=None, prep=None, rqm=None, EB=None):
        OB = [3, 4, 6, 7]

        def finish(c):
            ptv = psbf(5).rearrange("p (a b) -> p a b", a=8)
            for t in range(4):
                ob = OB[t]
                pov = psb[ob][:, 0:260].rearrange("p (h c) -> p h c", h=4)
                rec = small[:, t * 4:t * 4 + 4]
                P.op("dve", (lambda rec, pov: lambda e: e.reciprocal(out=rec.unsqueeze(2), in_=pov[:, :, 64:65]))(rec, pov),
                     R=[("ps", ob)], W=[("rec", t)])
                yt = ytile4[t]
                P.op("dve", tt(yt.rearrange("p (h c) -> p h c", h=4), pov[:, :, 0:64], rec.unsqueeze(2).broadcast_to([128, 4, 64]), ALU.mult),
                     R=[("ps", ob), ("rec", t)], W=[("ytile", t)])
                for c2 in range(2):
                    P.op("pe", tr(ptv[:, c2 * 4 + t, :], yt[:, c2 * 128:(c2 + 1) * 128], ident_bf[:]), R=[("ytile", t), "ident_bf"], W=[("ps", 5)])
            P.op("act", cp_act(yT[:, nbr, 2 * hf:2 * hf + 2, c * 512:(c + 1) * 512], psbf(5).rearrange("p (a b) -> p a b", a=2)),
                 R=[("ps", 5)], W=[("yT", nbr)])

        if prep is not None:
            prep(0)
        for c in range(4):
            j_lo = max(0, 4 * c - 4) if kind == "C" else 0
            items = [(hh, j) for hh in range(4) for j in range(j_lo, 4 * c + 4)]
            pend = []

            def second(hh, j, sb, pb, t0, t1, c=c):
                cols = slice(t0 * 128, (t1 + 1) * 128)
                if kind == "A":
                    h = 4 * hf + hh
                    P.op("act", act(Pt[pb][:, cols], psb[sb][:, cols], AF.Exp, bias=biasA[c % 2][:, j, h:h + 1], scale=scale),
                         R=[("ps", sb), ("biasA", c % 2)], W=[("Pt", pb)])
                else:
                    P.op("act", act(Pt[pb][:, cols], psb[sb][:, cols], AF.Exp, scale=scale), R=[("ps", sb)], W=[("Pt", pb)])
                if kind == "C":
                    d0 = 4 * c + t0 - j
                    P.op(MASK_ENG, tt(Pt[pb][:, cols], Pt[pb][:, cols], EB[:, hh, d0 * 128:(d0 + t1 - t0 + 1) * 128], ALU.mult),
                         R=["EBrev"], W=[("Pt", pb)])
                for t in range(t0, t1 + 1):
                    i = 4 * c + t
                    first_j = max(0, i - 4) if kind == "C" else 0
                    pov = psb[OB[t]][:, 0:260].rearrange("p (h c) -> p h c", h=4)
                    P.op("pe", mm(pov[:, hh, :], Pt[pb][:, t * 128:(t + 1) * 128], vaug[:, j, hh, :], j == first_j, j == i),
                         R=[("Pt", pb), "v"], W=[("ps", OB[t])])

            for (hh, j) in items:
                t0 = max(0, j - 4 * c)
                t1 = 3 if kind != "C" else min(3, j + 4 - 4 * c)
                cols = slice(t0 * 128, (t1 + 1) * 128)
                sb = attn_ctr[0] % 2
                pb = attn_ctr[0] % 3
                attn_ctr[0] += 1
                diag = (kind == "A") and j >= 4 * c
                P.op("pe", mm(psb[sb][:, cols], kslice(hh, j), qchunk(c, hh)[:, cols], True, kind != "A"),
                     R=["kT", "qT"], W=[("ps", sb)])
                if kind == "A":
                    P.op("pe", mm(psb[sb][:, cols], ones_bf[:], rqm[:, hh, cols], False, not diag), R=["rqm", "ones_bf"], W=[("ps", sb)])
                    if diag:
                        P.op("pe", lambda e, o=psb[sb][:, t0 * 128:(t0 + 1) * 128]: e.matmul(o, ident_bf[:], maskA[:], start=False, stop=True, skip_group_check=True),
                             R=["maskA", "ident_bf"], W=[("ps", sb)])
                pend.append((hh, j, sb, pb, t0, t1))
                if len(pend) > 1:
                    second(*pend.pop(0))
            while pend:
                second(*pend.pop(0))
            if prep is not None and c + 1 < 4:
                prep(c + 1)
            finish(c)

    def v_proj(l, col0, hf, wv_unused, vaug):
        for sh in range(2):
            wsl, wkey = wqk_ref[0][sh]
            wload(wsl, w_in[l, :, col0 + hf * 256 + sh * 128: col0 + hf * 256 + (sh + 1) * 128], wkey)
        for i in range(NT):
            bank = 6 + (i % 2)
            for sh in range(2):
                wsl, wkey = wqk_ref[0][sh]
                for kc in range(8):
                    P.op("pe", mm(psb[bank][:, sh * 128:(sh + 1) * 128], hT[:, kc, i * 128:(i + 1) * 128], wsl[:, kc, :], kc == 0, kc == 7),
                         R=[wkey, ("hT", i // 4)], W=[("ps", bank)])
            P.op("act", cp_act(vaug[:, i, :, 0:64], psb[bank][:, 0:256].rearrange("p (h c) -> p h c", h=4)), R=[("ps", bank)], W=["v"])

    wqk_ref = [None]

    z = Z0
    ZK = z
    kT_ac = bv(z, 4096).rearrange("p (m t) -> p m t", m=2)
    kTm = bv(z, 8192).rearrange("p (m t) -> p m t", m=4)
    kT_b = bv(z, 8192).rearrange("p (m t) -> p m t", m=4)
    z += 8192
    ZQ = z
    qT_ac = bv(z, 4096).rearrange("p (m t) -> p m t", m=2)
    qTt = [bv(z + k * 512, 512).rearrange("p (h t) -> p h t", h=4) for k in range(2)]
    z += 4096
    vaug = bv(z, 4160).rearrange("p (i h c) -> p i h c", i=NT, h=4)
    z += 4160
    sqb = [[bv(z + (u * 3 + c) * 512, 512) for c in range(3)] for u in range(2)]
    Pt = [bv(z + k * 512, 512) for k in range(3)]
    ytile = [bv(z + 1536 + k * 256, 256) for k in range(2)]
    ytile4 = [bv(z + 1536 + k * 256, 256) for k in range(4)]
    z += 3072
    rsb = [fv(z + u * 1024, 512) for u in range(2)]
    z += 2048
    ZW = z
    wqk = [(bv(z + k * 1024, 1024).rearrange("p (k c) -> p k c", k=8), ("wqk", k)) for k in range(4)]
    z += 4096
    wv = None
    wqk_ref[0] = wqk
    assert z <= AR_N, z

    def set_vones():
        P.op("pool", ms(vaug[:, :, :, 64:65], 1.0), W=["v"])

    def phase_A(l):
        z = 16384
        wf = bv(z, 64).rearrange("p (k c) -> p k c", k=8); z += 64
        zt = fv(z, 128); z += 256
        lp = fv(z, 128); z += 256
        Lp = fv(z, 128).rearrange("p (i h) -> p i h", i=NT); z += 256
        PTt = fv(z, 128).rearrange("p (i h) -> p i h", i=NT); z += 256
        biasA = [fv(z + k * 256, 128).rearrange("p (i h) -> p i h", i=NT) for k in range(2)]; z += 512
        rq_bf = bv(z, 128); z += 128
        rqm = bv(z, 2048).rearrange("p (h t) -> p h t", h=4); z += 2048
        assert z <= 24576, z
        P.op("pool", ms(bv(ZK, 8192), 0.0), W=["kT"])
        loc = zt
        set_vones()
        P.op("pool", ms(rqm, 0.0), W=["rqm"])
        wload(wf, w_in[l, :, C_FA:C_FA + 8], "wf")
        for i in range(NT):
            for kc in range(8):
                P.op("pe", mm(psb[5][:, i * 8:(i + 1) * 8], hT[:, kc, i * 128:(i + 1) * 128], wf[:, kc, :], kc == 0, kc == 7),
                     R=["wf", ("hT", i // 4)], W=[("ps", 5)])
        ztv = zt.rearrange("p (i h) -> p i h", i=NT)
        P.op("dve", tt(ztv, psb[5][:, 0:128].rearrange("p (i h) -> p i h", i=NT),
                       bf_rep[:, l * 8:(l + 1) * 8].unsqueeze(1).broadcast_to([128, NT, 8]), ALU.add), R=[("ps", 5), "bf_rep"], W=["zt"])
        P.op("act", act(zt, zt, AF.Exp, scale=-1.0), R=["zt"], W=["zt"])
        P.op("act", act(lp, zt, AF.Ln, bias=cst[:, 0:1]), R=["zt", "cst"], W=["lp"])
        lpv = lp.rearrange("p (i h) -> p i h", i=NT)
        for i in range(NT):
            P.op("pe", mm(psb[7][:, i * 8:(i + 1) * 8], U_f[:], lpv[:, i, :], True, True), R=["lp", "U_f"], W=[("ps", 7)])
            P.op("pe", mm(psb[6][:, i * 8:(i + 1) * 8], ones_f[:], lpv[:, i, :], True, True), R=["lp", "ones_f"], W=[("ps", 6)])
        P.op("dve", ms(PTt[:, 0, :], 0.0), W=["PT"])
        for i in range(1, NT):
            P.op("dve", tt(PTt[:, i, :], PTt[:, i - 1, :], psb[6][:, (i - 1) * 8:i * 8], ALU.add), R=[("ps", 6)], W=["PT"])
        P.op("dve", tt(Lp, psb[7][:, 0:128].rearrange("p (i h) -> p i h", h=8), PTt, ALU.add), R=[("ps", 7), "PT"], W=["Lp"])
        locv = loc.rearrange("p (i h) -> p i h", i=NT)

        for hf in range(2):
            for c_ in range(2):
                proj_norm(l, C_KA + hf * 256 + c_ * 128, 1, blockones, 64, 96 + 2 * l + 1, [None], "kT", [wqk[c_]], sqb, rsb,
                          split=(kTm[:, 2 * c_, :], kTm[:, 2 * c_ + 1, :]))
            for c_ in range(2):
                proj_norm(l, C_QA + hf * 256 + c_ * 128, 1, blockones, 64, 96 + 2 * l, [qT_ac[:, c_, :]], "qT", [wqk[2 + c_]], sqb, rsb)
            v_proj(l, C_VA, hf, wv, vaug)
            P.barrier()
            def prepA(c, hf=hf):
                P.op("dve", tt(biasA[c % 2][:, 0:4 * c + 4, :], Lp[:, 0:4 * c + 4, :], PTt[:, 4 * c:4 * c + 1, :].broadcast_to([128, 4 * c + 4, 8]), ALU.subtract),
                     R=["Lp", "PT"], W=[("biasA", c % 2)])
                P.op("dve", tt(locv[:, 0:4, :], Lp[:, 4 * c:4 * c + 4, :], PTt[:, 4 * c:4 * c + 1, :].broadcast_to([128, 4, 8]), ALU.subtract),
                     R=["Lp", "PT", "zt"], W=["loc"])
                P.op("dve", ts(rq_bf[:, 0:32], loc[:, 0:32], -8.0, None, ALU.mult), R=["loc"], W=["rq_bf"])
                ptv = psbf(2).rearrange("p (a b) -> p a b", a=8)
                for t in range(4):
                    P.op("pe", tr(ptv[0:8, t, :], rq_bf[:, t * 8:(t + 1) * 8], ident_bf[:]), R=["rq_bf", "ident_bf"], W=[("ps", 2)])
                P.op("dve", tt(rqm[0:8, :, :], psbf(2)[0:8, 0:512].unsqueeze(1).broadcast_to([8, 4, 512]),
                               ident_f[0:8, 4 * hf:4 * hf + 4].unsqueeze(2).broadcast_to([8, 4, 512]), ALU.mult),
                     R=[("ps", 2), "ident_f"], W=["rqm"])

            attention2("A", hf, 0,
                       lambda hh, j: kTm[:, hh, j * 128:(j + 1) * 128],
                       lambda c, hh: qT_ac[:, hh // 2, c * 512:(c + 1) * 512],
                       vaug, Pt, ytile4, 0.125, biasA=biasA, prep=prepA, rqm=rqm)
            P.barrier()
            if DBG == 2 and l == 0 and hf == 1:
                P.dma("sp", dmaf(dbg_d, arena[:, Z0:Z0 + 24576]), sem="dbg")
                P.barrier()

    def phase_C(l):
        EBrev = bv(ZW, 2560).rearrange("p (h c) -> p h c", h=4)
        Tst = [fv(ZW + 2560, 640)]
        set_vones()
        P.op("pool", ms(bv(ZK, 8192), 0.0), W=["kT"])
        for hf in range(2):
            for c_ in range(2):
                proj_norm(l, C_KC + hf * 256 + c_ * 128, 1, blockones, 64, 104 + 2 * l + 1, [None], "kT", [wqk[c_]], sqb, rsb,
                          split=(kTm[:, 2 * c_, :], kTm[:, 2 * c_ + 1, :]))
            for c_ in range(2):
                proj_norm(l, C_QC + hf * 256 + c_ * 128, 1, blockones, 64, 104 + 2 * l, [qT_ac[:, c_, :]], "qT", [wqk[2 + c_]], sqb, rsb)
            v_proj(l, C_VC, hf, wv, vaug)
            P.barrier()
            for hh in range(4):
                h = 4 * hf + hh
                src = relx[l * 8 + h, :, :]
                P.dma("sp", dmaf(Tst[0], src), W=[("Tst", 0)], sem="Tst0")
                P.op("act", act(Tst[0], Tst[0], AF.Exp), R=[("Tst", 0)], W=[("Tst", 0)])
                for d in range(5):
                    P.op("dve", tt(EBrev[:, hh, d * 128:(d + 1) * 128], Tst[0][:, d * 128:(d + 1) * 128], validC[:, d * 128:(d + 1) * 128], ALU.mult),
                         R=[("Tst", 0), "validC"], W=["EBrev"])
            attention2("C", hf, 2,
                       lambda hh, j: kTm[:, hh, j * 128:(j + 1) * 128],
                       lambda c, hh: qT_ac[:, hh // 2, c * 512:(c + 1) * 512],
                       vaug, Pt, ytile4, 0.125, EB=EBrev)
            P.barrier()

    def rope_ops(dst1, dst2, x1, x2, cosv, sinv, t1, t2, Rk, Wk):
        P.op("dve", tt(t1, x1, cosv, ALU.mult), R=Rk, W=["rt1"])
        P.op("dve", tt(t2, x2, sinv, ALU.mult), R=Rk, W=["rt2"])
        P.op("dve", tt(dst1, t1, t2, ALU.subtract), R=["rt1", "rt2"], W=Wk)
        P.op("dve", tt(t1, x2, cosv, ALU.mult), R=Rk, W=["rt1"])
        P.op("dve", tt(t2, x1, sinv, ALU.mult), R=Rk, W=["rt2"])
        P.op("dve", tt(dst2, t1, t2, ALU.add), R=["rt1", "rt2"], W=Wk)

    def phase_B(l):
        z = ZQ + 1024
        wkr = bv(z, 256).rearrange("p (k c) -> p k c", k=8); z += 256
        wqup = bv(z, 1152).rearrange("p (k c) -> p k c", k=3); z += 1152
        wkvup = bv(z, 1024).rearrange("p (k c) -> p k c", k=2); z += 1024
        krope = bv(z, 512).rearrange("p (i c) -> p i c", i=NT); z += 512
        assert z <= ZQ + 4096
        z = 6144
        tmpf = fv(z, 512); z += 1024
        tmpq = fv(z, 384).rearrange("p (h c) -> p h c", h=4); z += 768
        assert z <= 8192
        z = 16384 + 4096
        ktok = [bv(z + k * 512, 512).rearrange("p (h c) -> p h c", h=4) for k in range(2)]; z += 1024
        qtok = [bv(z + k * 512, 512).rearrange("p (h c) -> p h c", h=4) for k in range(2)]; z += 1024
        for k_ in range(2):
            P.op("pool", ms(ktok[k_], 0.0), W=[("ktok", k_)])
            P.op("pool", ms(qtok[k_], 0.0), W=[("qtok", k_)])
        rt1 = fv(z, 256); z += 512
        rt2 = fv(z, 256); z += 512
        kvsb = fv(z, 512); z += 1024
        assert z <= 24576, z
        qdnT = bv(0, 6144).rearrange("p (c t) -> p c t", c=3)
        kvdnT = bv(16384, 4096).rearrange("p (c t) -> p c t", c=2)
        set_vones()
        proj_norm(l, C_QD, 3, ones_bf, 384, 112 + 3 * l, [qdnT[:, c, :] for c in range(3)], "qdnT", wqk[0:3], sqb, rsb, gstep=1)
        proj_norm(l, C_KVD, 2, ones_bf, 256, 124 + 2 * l, [kvdnT[:, c, :] for c in range(2)], "kvdnT", [wqk[3], wqk[0]], sqb, rsb, gstep=1)
        if BSTOP == 1:
            P.op("pool", ms(yT[:, 1, :, :], 0.0), W=[("yT", 1)]); P.barrier(); return
        wload(wkr, w_in[l, :, C_KR:C_KR + 32], "wkr")
        for i in range(NT):
            for kc in range(8):
                P.op("pe", mm(psb[5][:, i * 32:(i + 1) * 32], hT[:, kc, i * 128:(i + 1) * 128], wkr[:, kc, :], kc == 0, kc == 7),
                     R=["wkr", ("hT", i // 4)], W=[("ps", 5)])
        pk = psb[5][:].rearrange("p (i c) -> p i c", i=NT)
        tfv = tmpf.rearrange("p (i c) -> p i c", i=NT)
        sm = small[:, 16:32]
        P.op("act", act(tmpf, psb[5][:], AF.Square), R=[("ps", 5)], W=["tmpf"])
        P.op("dve", red(sm, tfv), R=["tmpf"], W=["sm"])
        rsqrt_small(sm, sm, 1.0 / 32, EPS, ["sm"], ["sm"])
        P.op("dve", tt(tfv, pk, sm.unsqueeze(2).broadcast_to([128, NT, 32]), ALU.mult), R=[("ps", 5), "sm"], W=["tmpf"])
        gk = gbr[:, l * 64 + 32:l * 64 + 64]
        P.op("dve", tt(tfv, tfv, gk.unsqueeze(1).broadcast_to([128, NT, 32]), ALU.mult), R=["tmpf", "gbr"], W=["tmpf"])
        r1 = rt1.rearrange("p (i c) -> p i c", i=NT)
        r2 = rt2.rearrange("p (i c) -> p i c", i=NT)
        rope_ops(krope[:, :, 0:16], krope[:, :, 16:32], tfv[:, :, 0:16], tfv[:, :, 16:32], cs[:, :, 0:16], cs[:, :, 16:32],
                 r1, r2, ["tmpf", "cs"], ["krope"])
        P.barrier()
        if BSTOP == 2:
            P.op("pool", ms(yT[:, 1, :, :], 0.0), W=[("yT", 1)]); P.barrier(); return
        for hf in range(2):
            wload(wkvup, w_kv_up[l, :, hf * 512:(hf + 1) * 512], "wkvup")
            wload(wqup, w_q_up[l, :, hf * 384:(hf + 1) * 384], "wqup")
            for i in range(NT):
                bank = 6
                pkv = psb[bank][:].rearrange("p (h c) -> p h c", h=4)
                for kc in range(2):
                    P.op("pe", mm(psb[bank][:], kvdnT[:, kc, i * 128:(i + 1) * 128], wkvup[:, kc, :], kc == 0, kc == 1),
                         R=["kvdnT", "wkvup"], W=[("ps", bank)])
                t4 = tmpf[:, 0:256].rearrange("p (h c) -> p h c", h=4)
                s4 = small[:, 32:36]
                P.op("act", cp_act(kvsb, psb[bank][:]), R=[("ps", bank)], W=["kvsb"])
                pkv = kvsb.rearrange("p (h c) -> p h c", h=4)
                P.op("act", act(t4, pkv[:, :, 0:64], AF.Square), R=["kvsb"], W=["tmpf"])
                P.op("dve", red(s4, t4), R=["tmpf"], W=["s4"])
                rsqrt_small(s4, s4, 1.0 / 64, EPS, ["s4"], ["s4"])
                P.op("dve", tt(t4, pkv[:, :, 0:64], s4.unsqueeze(2).broadcast_to([128, 4, 64]), ALU.mult), R=["kvsb", "s4"], W=["tmpf"])
                kt = ktok[i % 2]
                gn = gbn[:, l * 128 + 64:l * 128 + 128]
                P.op("dve", tt(kt[:, :, 0:64], t4, gn.unsqueeze(1).broadcast_to([128, 4, 64]), ALU.mult), R=["tmpf", "gbn"], W=[("ktok", i % 2)])
                P.op("dve", cp(kt[:, :, 64:96], krope[:, i, :].unsqueeze(1).broadcast_to([128, 4, 32])), R=["krope"], W=[("ktok", i % 2)])
                P.op("act", cp_act(vaug[:, i, :, 0:64], pkv[:, :, 64:128]), R=["kvsb"], W=["v"])
                ptv = psbf(7).rearrange("p (a b) -> p a b", a=8)
                for hh in range(4):
                    P.op("pe", tr(ptv[:, hh, :], kt[:, hh, :], ident_bf[:]), R=[("ktok", i % 2), "ident_bf"], W=[("ps", 7)])
                P.op("act", cp_act(kT_b[0:96, :, i * 128:(i + 1) * 128], ptv[0:96, 0:4, :]), R=[("ps", 7)], W=["kT"])

            if BSTOP == 3:
                P.barrier(); P.op("pool", ms(yT[:, 1, :, :], 0.0), W=[("yT", 1)]); P.barrier(); return
            def qprep(i):
                bank = 6
                pq = psb[bank][:, 0:384].rearrange("p (h c) -> p h c", h=4)
                for kc in range(3):
                    P.op("pe", mm(psb[bank][:, 0:384], qdnT[:, kc, i * 128:(i + 1) * 128], wqup[:, kc, :], kc == 0, kc == 2),
                         R=["qdnT", "wqup"], W=[("ps", bank)])
                t4 = tmpf[:, 0:384].rearrange("p (h c) -> p h c", h=4)
                sn = small[:, 36:40]
                sr = small[:, 40:44]
                P.op("act", cp_act(kvsb[:, 0:384], psb[bank][:, 0:384]), R=[("ps", bank)], W=["kvsb"])
                pq = kvsb[:, 0:384].rearrange("p (h c) -> p h c", h=4)
                P.op("act", act(t4, pq, AF.Square), R=["kvsb"], W=["tmpf"])
                P.op("dve", red(sn, t4[:, :, 0:64]), R=["tmpf"], W=["sn"])
                P.op("dve", red(sr, t4[:, :, 64:96]), R=["tmpf"], W=["sr"])
                rsqrt_small(sn, sn, 1.0 / 64, EPS, ["sn"], ["sn"])
                rsqrt_small(sr, sr, 1.0 / 32, EPS, ["sr"], ["sr"])
                P.op("dve", tt(tmpq[:, :, 0:64], pq[:, :, 0:64], sn.unsqueeze(2).broadcast_to([128, 4, 64]), ALU.mult), R=["kvsb", "sn"], W=["tmpq"])
                P.op("dve", tt(tmpq[:, :, 64:96], pq[:, :, 64:96], sr.unsqueeze(2).broadcast_to([128, 4, 32]), ALU.mult), R=["kvsb", "sr"], W=["tmpq"])
                qt = qtok[i % 2]
                gqn = gbn[:, l * 128:l * 128 + 64]
                gqr = gbr[:, l * 64:l * 64 + 32]
                P.op("dve", tt(qt[:, :, 0:64], tmpq[:, :, 0:64], gqn.unsqueeze(1).broadcast_to([128, 4, 64]), ALU.mult), R=["tmpq", "gbn"], W=[("qtok", i % 2)])
                P.op("dve", tt(tmpq[:, :, 64:96], tmpq[:, :, 64:96], gqr.unsqueeze(1).broadcast_to([128, 4, 32]), ALU.mult), R=["tmpq", "gbr"], W=["tmpq"])
                c4 = cs[:, i, 0:16].unsqueeze(1).broadcast_to([128, 4, 16])
                s4b = cs[:, i, 16:32].unsqueeze(1).broadcast_to([128, 4, 16])
                q1 = rt1[:, 0:64].rearrange("p (h c) -> p h c", h=4)
                q2 = rt2[:, 0:64].rearrange("p (h c) -> p h c", h=4)
                rope_ops(qt[:, :, 64:80], qt[:, :, 80:96], tmpq[:, :, 64:80], tmpq[:, :, 80:96], c4, s4b, q1, q2, ["tmpq", "cs"], [("qtok", i % 2)])

            def qprep_b(i):
                qt = qtok[i % 2]
                ptv = psbf(7).rearrange("p (a b) -> p a b", a=8)
                for hh in range(4):
                    P.op("pe", tr(ptv[:, hh, :], qt[:, hh, :], ident_bf[:]), R=[("qtok", i % 2), "ident_bf"], W=[("ps", 7)])
                P.op("act", cp_act(qTt[i % 2][0:96, :, :], ptv[0:96, 0:4, :]), R=[("ps", 7)], W=[("qTt", i % 2)])

            attention("B", hf, 1,
                      lambda hh, j: kT_b[0:96, hh, j * 128:(j + 1) * 128],
                      lambda hh, i: qTt[i % 2][0:96, hh, :],
                      lambda i: ("qTt", i % 2), vaug, Pt, ytile, float(96.0 ** -0.5), qprep=(qprep, qprep_b))
            P.barrier()

    def merge_phase(l):
        z = Z0
        mT = bv(z, 8192).rearrange("p (c t) -> p c t", c=8); z += 8192
        wg = [bv(z + k * 3072, 3072).rearrange("p (k n c) -> p k n c", k=8, n=3) for k in range(2)]; z += 6144
        wb = [bv(z + k * 1536, 1536).rearrange("p (k n c) -> p k n c", k=4, n=3) for k in range(2)]; z += 3072
        wo = [bv(z + k * 2048, 2048).rearrange("p (k c) -> p k c", k=8) for k in range(2)]; z += 4096
        gate = [bv(z + k * 512, 512) for k in range(3)]; z += 1536
        acc = fv(z, 512); z += 1024
        tmp = fv(z, 512); z += 1024
        assert z <= AR_N, z
        cnt = 0
        for th in range(2):
            for m in range(8):
                sl = cnt % 2
                cnt += 1
                for n in range(3):
                    c0 = C_G + n * 1024 + m * 128
                    P.dma("pool", dmaf(wg[sl][:, :, n, :], w_in[l, :, c0:c0 + 128].rearrange("(k p) c -> p k c", p=128)), W=[("wg", sl, n)], sem="wg%d_%d" % (sl, n))
                    P.dma("pool", dmaf(wb[sl][:, :, n, :], w_branch[l, n, :, m * 128:(m + 1) * 128].rearrange("(k p) c -> p k c", p=128)), W=[("wb", sl, n)], sem="wb%d_%d" % (sl, n))
                for tq in range(2):
                    tc = th * 2 + tq
                    tsl = slice(tc * 512, (tc + 1) * 512)
                    u = (m * 2 + tq) % 2
                    for n in range(3):
                        gb = u * 4 + n if n < 2 else u * 4 + 2
                        gb = u * 4 + n
                        for kc in range(8):
                            P.op("pe", mm(psb[gb][:], wg[sl][:, kc, n, :], hT[:, kc, tsl], kc == 0, kc == 7), R=[("wg", sl, n), ("hT", tc)], W=[("ps", gb)])
                        P.op("act", act(gate[n], psb[gb][:], AF.Sigmoid, bias=vecT[:, l * 24 + n * 8 + m:l * 24 + n * 8 + m + 1]),
                             R=[("ps", gb), "vecT"], W=[("gate", n)])
                        pb = u * 4 + 3
                        for kc in range(4):
                            P.op("pe", mm(psb[pb][:], wb[sl][:, kc, n, :], yT[:, n, kc, tsl], kc == 0, kc == 3), R=[("wb", sl, n), ("yT", n)], W=[("ps", pb)])
                        if n == 0:
                            P.op("dve", tt(acc, psb[pb][:], gate[n], ALU.mult), R=[("ps", pb), ("gate", n)], W=["acc"])
                        else:
                            P.op("dve", tt(tmp, psb[pb][:], gate[n], ALU.mult), R=[("ps", pb), ("gate", n)], W=["tmp"])
                            if n == 1:
                                P.op("dve", tt(acc, acc, tmp, ALU.add), R=["tmp"], W=["acc"])
                            else:
                                P.op("dve", tt(mT[:, m, tq * 512:(tq + 1) * 512], acc, tmp, ALU.add), R=["tmp", "acc"], W=["mT"])
            for cq in range(4):
                sl = cq % 2
                wload(wo[sl], w_out[l, :, cq * 256:(cq + 1) * 256], ("wo", sl))
                for ii in range(8):
                    i = th * 8 + ii
                    bank = ii % 2 + 6 if False else (ii % 4)
                    for kc in range(8):
                        P.op("pe", mm(psb[bank][:, 0:256], mT[:, kc, ii * 128:(ii + 1) * 128], wo[sl][:, kc, :], kc == 0, kc == 7),
                             R=["mT", ("wo", sl)], W=[("ps", bank)])
                    P.op("dve", tt(xs[:, i, cq * 256:(cq + 1) * 256], xs[:, i, cq * 256:(cq + 1) * 256], psb[bank][:, 0:256], ALU.add),
                         R=[("ps", bank)], W=[("x", i)])
        P.barrier()

    def ffn_phase(l):
        aT = [bv(0, 16384).rearrange("p (c t) -> p c t", c=8), bv(Z0, 16384).rearrange("p (c t) -> p c t", c=8)]
        w2 = [bv(16384 + k * 4096, 4096).rearrange("p (k c) -> p k c", k=8) for k in range(2)]
        z = Z0 + 16384
        w1 = [bv(z + k * 2048, 2048).rearrange("p (k c) -> p k c", k=8) for k in range(2)]; z += 4096
        rt = [fv(z + k * 1024, 512) for k in range(2)]; z += 2048
        assert z <= AR_N, z
        c1 = 0
        c2 = 0
        bk = 0
        for g in range(4):
            a = aT[g % 2]
            for fp in range(4):
                sl = c1 % 2
                c1 += 1
                wload(w1[sl], w_ff1[l, :, g * 1024 + fp * 256: g * 1024 + (fp + 1) * 256], ("w1", sl))
                for f2 in range(2):
                    f = fp * 2 + f2
                    for tc in range(4):
                        bank = bk % 4
                        bk += 1
                        for kc in range(8):
                            P.op("pe", mm(psb[bank][:], w1[sl][:, kc, f2 * 128:(f2 + 1) * 128], hT[:, kc, tc * 512:(tc + 1) * 512], kc == 0, kc == 7),
                                 R=[("w1", sl), ("hT", tc)], W=[("ps", bank)])
                        r = rt[bank % 2]
                        P.op("act", act(r, psb[bank][:], AF.Relu), R=[("ps", bank)], W=[("rt", bank % 2)])
                        P.op(SQ_ENG, tt(a[:, f, tc * 512:(tc + 1) * 512], r, r, ALU.mult), R=[("rt", bank % 2)], W=[("aT", g % 2)])
            for ch in range(2):
                sl = c2 % 2
                c2 += 1
                wload(w2[sl], w_ff2[l, g * 1024:(g + 1) * 1024, ch * 512:(ch + 1) * 512], ("w2", sl))
                for i in range(NT):
                    bank = 4 + (i % 4)
                    for f in range(8):
                        P.op("pe", mm(psb[bank][:], a[:, f, i * 128:(i + 1) * 128], w2[sl][:, f, :], f == 0, f == 7),
                             R=[("aT", g % 2), ("w2", sl)], W=[("ps", bank)])
                    P.op("dve", tt(xs[:, i, ch * 512:(ch + 1) * 512], xs[:, i, ch * 512:(ch + 1) * 512], psb[bank][:], ALU.add),
                         R=[("ps", bank)], W=[("x", i)])
        P.barrier()

    MASK_ENG = "pool"
    SQ_ENG = "pool"
    for l in range(nlayers):
        P.tag = "norm1"
        if "N" in PH:
            norm_phase(norm_mix[l, :])
        for nb_, ch_ in enumerate("ABC"):
            if ch_ not in PH:
                P.op("pool", ms(yT[:, nb_, :, :], 0.0), W=[("yT", nb_)])
        P.tag = "B"
        if "B" in PH:
            phase_B(l)
        P.tag = "A"
        if "A" in PH:
            phase_A(l)
        P.tag = "C"
        if "C" in PH:
            phase_C(l)
        P.tag = "merge"
        if DBG == 1 and l == 0:
            P.barrier()
            P.dma("sp", dmaf(dbg_d, arena[:, 0:24576]), R=[("yT", 0), ("yT", 1), ("yT", 2)], sem="dbg")
            P.barrier()
        if "M" in PH:
            merge_phase(l)
        if "F" in PH:
            P.tag = "norm2"
            norm_phase(norm_ffn[l, :])
            P.tag = "ffn"
            ffn_phase(l)
    for i in range(NT):
        P.dma("sp", dmaf(y_d[i * 128:(i + 1) * 128, :], xs[:, i, :]), R=[("x", i)], sem="y%d" % i)
    P.barrier()
    P.emit(nc, st)
    st.close()
    return nc


def _host_consts(rel_bias):
    half = 16
    inv = (10000.0 ** (-np.arange(half, dtype=np.float32) / half)).astype(np.float32)
    ang = np.arange(S, dtype=np.float32)[:, None] * inv[None, :]
    cs_tab = np.concatenate([np.cos(ang), np.sin(ang)], axis=1).astype(np.float32)
    kl = np.arange(128)[:, None]
    c = np.arange(640)[None, :]
    idx = np.clip(c - kl, -256, 256) + 256
    rel_ext = np.ascontiguousarray(rel_bias[:, :, idx]).reshape(DEPTH * 8, 128, 640).astype(np.float32)
    return cs_tab, rel_ext


PH = "NBACMF"
ANNOT = False
DBG = False
BSTOP = 0
_NC_CACHE = {}


def kernel(**inputs):
    inp = {k: np.ascontiguousarray(np.asarray(v, dtype=np.float32)) for k, v in inputs.items()}
    x = inp.pop("x")
    rel_bias = inp.pop("rel_bias")
    cs_tab, rel_ext = _host_consts(rel_bias)
    key = (DEPTH, PH)
    if key not in _NC_CACHE:
        _NC_CACHE[key] = build(DEPTH)
    nc = _NC_CACHE[key]
    B = x.shape[0]
    in_maps = []
    for b in range(B):
        m = dict(inp)
        m["x"] = np.ascontiguousarray(x[b])
        m["rel_ext"] = rel_ext
        m["cs_tab"] = cs_tab
        in_maps.append(m)
    res = run_bass_kernel_spmd(nc, in_maps, core_ids=list(range(B)))
    return np.stack([np.asarray(r["y"], dtype=np.float32) for r in res.results], axis=0)
```

```python
import numpy as np
from contextlib import ExitStack
import concourse.bass as bass
import concourse.mybir as mybir
from concourse.bass_utils import run_bass_kernel_spmd

F32 = mybir.dt.float32
BF16 = mybir.dt.bfloat16
AF = mybir.ActivationFunctionType
ALU = mybir.AluOpType
AX = mybir.AxisListType

S = 2048
D = 1024
NT = 16
DEPTH = 4
EPS = 1e-6
INW = 6824
C_QA, C_KA, C_VA, C_FA, C_QD, C_KVD, C_KR, C_QC, C_KC, C_VC, C_G = 0, 512, 1024, 1536, 1544, 1928, 2184, 2216, 2728, 3240, 3752


class Prog:
    ENG = ("pe", "act", "dve", "pool", "sp")

    def __init__(self):
        self.ops = {e: [] for e in self.ENG}
        self.state = {}
        self.known = {e: {} for e in self.ENG}
        self.flag = {e: set() for e in self.ENG}
        self.semcnt = {}
        self.tag = ""
        self.tags = {e: [] for e in self.ENG}

    def _add(self, eng, fn, R, W, dma_sem):
        self.tags[eng].append(self.tag)
        need = {}
        for k in R:
            st = self.state.get(k)
            if st and st[0] is not None:
                t = st[0]
                need[t[0]] = max(need.get(t[0], -1), t[1])
        for k in W:
            st = self.state.get(k)
            if st:
                if st[0] is not None:
                    t = st[0]
                    need[t[0]] = max(need.get(t[0], -1), t[1])
                for tk, tv in st[1].items():
                    need[tk] = max(need.get(tk, -1), tv)
        kn = self.known[eng]
        waits = []
        for tk, tv in need.items():
            if dma_sem is None and eng == "pe" and tk == ("E", "pe"):
                continue
            if kn.get(tk, -1) >= tv:
                continue
            kn[tk] = tv
            waits.append((tk, tv))
            if tk[0] == "E":
                self.flag[tk[1]].add(tv)
        idx = len(self.ops[eng])
        if dma_sem is None:
            tok = (("E", eng), idx)
        else:
            c = self.semcnt.get(dma_sem, 0) + 1
            self.semcnt[dma_sem] = c
            tok = (("S", dma_sem), c)
        self.ops[eng].append((fn, waits, dma_sem))
        for k in R:
            if k in W:
                continue
            st = self.state.setdefault(k, [None, {}])
            st[1][tok[0]] = max(st[1].get(tok[0], -1), tok[1])
        for k in W:
            self.state[k] = [tok, {}]
        return tok

    def op(self, eng, fn, R=(), W=()):
        return self._add(eng, fn, list(R), list(W), None)

    def dma(self, q, fn, R=(), W=(), sem=None):
        return self._add(q, fn, list(R), list(W), sem)

    def barrier(self):
        last = {}
        for e in self.ENG:
            n = len(self.ops[e])
            for i in range(n - 1, -1, -1):
                if self.ops[e][i][2] is None and self.ops[e][i][0] is not None:
                    last[("E", e)] = i
                    break
        for s, c in self.semcnt.items():
            last[("S", s)] = c
        for e in self.ENG:
            kn = self.known[e]
            waits = []
            for tk, tv in last.items():
                if kn.get(tk, -1) >= tv:
                    continue
                kn[tk] = tv
                waits.append((tk, tv))
                if tk[0] == "E":
                    self.flag[tk[1]].add(tv)
            if waits:
                self.ops[e].append((None, waits, None))
                self.tags[e].append(self.tag)

    def emit(self, nc, stack):
        rank = {}
        for e in self.ENG:
            rank[e] = {idx: r + 1 for r, idx in enumerate(sorted(self.flag[e]))}
        BS = 2000
        esem = {e: [stack.enter_context(nc.semaphore("es_%s_%d" % (e, b))) for b in range(len(rank[e]) // BS + 1)] for e in self.ENG}
        dsem = {s: stack.enter_context(nc.semaphore("ds_" + str(i))) for i, s in enumerate(self.semcnt)}
        block = stack.enter_context(nc.Block())

        def run(e, h):
            for idx, (fn, waits, ds) in enumerate(self.ops[e]):
                for tk, tv in waits:
                    if tk[0] == "E":
                        r_ = rank[tk[1]][tv] - 1
                        h.wait_ge(esem[tk[1]][r_ // BS], r_ % BS + 1)
                    else:
                        h.wait_ge(dsem[tk[1]], tv * 16)
                if fn is None:
                    continue
                ins = fn(h)
                if ANNOT:
                    ins.annotate(self.tags[e][idx])
                if ds is not None:
                    ins.then_inc(dsem[ds], 16)
                elif idx in rank[e]:
                    ins.then_inc(esem[e][(rank[e][idx] - 1) // BS], 1)

        block.tensor(lambda h: run("pe", h))
        block.scalar(lambda h: run("act", h))
        block.vector(lambda h: run("dve", h))
        block.gpsimd(lambda h: run("pool", h))
        block.sync(lambda h: run("sp", h))


def build(nlayers=DEPTH):
    nc = bass.Bass("TRN2", target_bir_lowering=False)
    P = Prog()
    st = ExitStack()

    def din(name, shape):
        return nc.dram_tensor(name, list(shape), F32, kind="ExternalInput")

    x_d = din("x", [S, D]).ap()
    norm_mix = din("norm_mix", [DEPTH, D]).ap()
    w_in = din("w_in", [DEPTH, D, INW]).ap()
    b_forget = din("b_forget", [DEPTH, 8]).ap()
    b_gate = din("b_gate", [DEPTH, 3072]).ap()
    qk_norm_a = din("qk_norm_a", [DEPTH, 2, 64]).ap()
    mla_q_norm = din("mla_q_norm", [DEPTH, 384]).ap()
    mla_kv_norm = din("mla_kv_norm", [DEPTH, 256]).ap()
    w_q_up = din("w_q_up", [DEPTH, 384, 768]).ap()
    w_kv_up = din("w_kv_up", [DEPTH, 256, 1024]).ap()
    qk_norm_b_nope = din("qk_norm_b_nope", [DEPTH, 2, 64]).ap()
    qk_norm_b_rope = din("qk_norm_b_rope", [DEPTH, 2, 32]).ap()
    qk_norm_c = din("qk_norm_c", [DEPTH, 2, 64]).ap()
    relx = din("rel_ext", [DEPTH * 8, 128, 640]).ap()
    w_branch = din("w_branch", [DEPTH, 3, 512, D]).ap()
    w_out = din("w_out", [DEPTH, D, D]).ap()
    norm_ffn = din("norm_ffn", [DEPTH, D]).ap()
    w_ff1 = din("w_ff1", [DEPTH, D, 4096]).ap()
    w_ff2 = din("w_ff2", [DEPTH, 4096, D]).ap()
    cs_d = din("cs_tab", [S, 32]).ap()
    y_d = nc.dram_tensor("y", [S, D], F32, kind="ExternalOutput").ap()
    dbg_d = nc.dram_tensor("dbg", [128, 24576], BF16, kind="ExternalOutput").ap() if DBG else None

    def sb(name, shape, dt):
        return st.enter_context(nc.sbuf_tensor(name, list(shape), dt))

    xs = sb("xs", [128, NT, D], F32)
    hT = sb("hT", [128, 8, S], BF16)
    ident_bf = sb("ident_bf", [128, 128], BF16)
    ident_f = sb("ident_f", [128, 128], F32)
    blockones = sb("blockones", [128, 128], BF16)
    ones_bf = sb("ones_bf", [128, 128], BF16)
    maskA = sb("maskA", [128, 128], BF16)
    maskB = sb("maskB", [128, 128], BF16)
    U_f = sb("U_f", [128, 128], F32)
    ones_f = sb("ones_f", [128, 128], F32)
    validC = sb("validC", [128, 640], BF16)
    vecT = sb("vecT", [128, 132], F32)
    bf_rep = sb("bf_rep", [128, 32], F32)
    gbn = sb("gbn", [128, 512], F32)
    gbr = sb("gbr", [128, 256], F32)
    cs = sb("cs", [128, NT, 32], F32)
    ssq = sb("ssq", [128, NT], F32)
    rstd = sb("rstd", [128, NT], F32)
    small = sb("small", [128, 64], F32)
    cst = sb("cst", [128, 4], F32)
    AR_N = 52000
    arena = sb("arena", [128, AR_N], BF16)
    psb = [st.enter_context(nc.psum_tensor("ps%d" % b, [128, 512], F32)) for b in range(8)]

    def bv(off, n):
        return arena[:, off:off + n]

    def fv(off, n):
        return arena[:, off:off + 2 * n].bitcast(F32)

    def psbf(b):
        return psb[b][:].bitcast(BF16)

    Z0 = 24576
    yT = bv(0, 24576).rearrange("p (n c t) -> p n c t", n=3, c=4)

    def mm(out, lhsT, rhs, start, stop):
        return lambda e: e.matmul(out, lhsT, rhs, start=start, stop=stop)

    def tr(out, in_, ident):
        return lambda e: e.transpose(out, in_, ident)

    def act(out, in_, func, bias=None, scale=None, accum_out=None):
        kw = {}
        if bias is not None:
            kw["bias"] = bias
        if scale is not None:
            kw["scale"] = scale
        if accum_out is not None:
            kw["accum_out"] = accum_out
        return lambda e: e.activation(out=out, in_=in_, func=func, **kw)

    def tt(out, in0, in1, op):
        return lambda e: e.tensor_tensor(out=out, in0=in0, in1=in1, op=op)

    def ts(out, in0, s1, s2, op0, op1=None):
        if op1 is None:
            return lambda e: e.tensor_single_scalar(out=out, in_=in0, scalar=s1, op=op0)
        return lambda e: e.tensor_scalar(out=out, in0=in0, scalar1=s1, scalar2=s2, op0=op0, op1=op1)

    def stt(out, in0, scalar, in1, op0, op1):
        return lambda e: e.scalar_tensor_tensor(out=out, in0=in0, scalar=scalar, in1=in1, op0=op0, op1=op1)

    def cp(out, in_):
        return lambda e: e.tensor_copy(out=out, in_=in_)

    def rsqrt_small(dst, src, mul, eps, Rk, Wk):
        P.op("dve", ts(dst, src, mul, eps, ALU.mult, ALU.add), R=Rk, W=Wk)
        P.op("act", act(dst, dst, AF.Ln), R=Wk, W=Wk)
        P.op("act", act(dst, dst, AF.Exp, scale=-0.5), R=Wk, W=Wk)

    def red(out, in_):
        return lambda e: e.tensor_reduce(out=out, in_=in_, axis=AX.X, op=ALU.add)

    def ms(ap, v):
        return lambda e: e.memset(ap, v)

    def dmaf(out, in_):
        return lambda e: e.dma_start(out=out, in_=in_)

    def wload(dst, src2d, key, R=(), q="pool"):
        P.dma(q, dmaf(dst, src2d.rearrange("(k p) c -> p k c", p=128)), R=R, W=[key], sem=str(key))

    P.op("pool", ms(ones_f[:], 1.0), W=["ones_f"])
    P.op("pool", lambda e: e.affine_select(out=ident_f[:], in_=ones_f[:], pattern=[[-1, 128]], compare_op=ALU.is_equal,
                                           fill=0.0, base=0, channel_multiplier=1), R=["ones_f"], W=["ident_f"])
    P.op("pool", lambda e: e.affine_select(out=U_f[:], in_=ones_f[:], pattern=[[1, 128]], compare_op=ALU.is_ge,
                                           fill=0.0, base=0, channel_multiplier=-1), R=["ones_f"], W=["U_f"])
    P.op("pool", cp(ident_bf[:], ident_f[:]), R=["ident_f"], W=["ident_bf"])
    P.op("pool", ts(maskA[:], U_f[:], 60000.0, -60000.0, ALU.mult, ALU.add), R=["U_f"], W=["maskA"])
    P.op("pool", ms(ones_bf[:], 1.0), W=["ones_bf"])
    P.op("pool", ms(cst[:, 0:1], 1.0), W=["cst"])
    P.op("pool", ms(cst[:, 1:2], 64 * EPS), W=["cst"])
    P.op("pool", ms(cst[:, 2:3], 384 * EPS), W=["cst"])
    P.op("pool", ms(cst[:, 3:4], 256 * EPS), W=["cst"])
    P.op("pool", ms(blockones[:], 0.0), W=["blockones"])
    P.op("pool", ms(blockones[0:64, 0:64], 1.0), W=["blockones"])
    P.op("pool", ms(blockones[64:128, 64:128], 1.0), W=["blockones"])
    P.op("pool", ms(maskB[:], 1.0), W=["maskB"])
    P.op("pool", ms(maskB[64:128, 0:64], 0.0), W=["maskB"])
    P.op("pool", ms(validC[:], 0.0), W=["validC"])
    P.op("pool", ms(validC[0:64, 0:576], 1.0), W=["validC"])
    P.op("pool", ms(validC[64:128, 64:640], 1.0), W=["validC"])

    stage1 = fv(Z0, 128)
    stage2 = fv(Z0 + 256, 128)
    P.dma("sp", dmaf(stage1[0:96, :], b_gate.rearrange("l (c p) -> (l c) p", p=128)), W=["stage1"], sem="stage1")
    qa2 = qk_norm_a.rearrange("l q d -> (l q) d")
    qc2 = qk_norm_c.rearrange("l q d -> (l q) d")
    P.dma("sp", dmaf(stage2[0:8, 0:64], qa2), W=["stage2a"], sem="stage2")
    P.dma("sp", dmaf(stage2[0:8, 64:128], qa2), W=["stage2b"], sem="stage2")
    P.dma("sp", dmaf(stage2[8:16, 0:64], qc2), W=["stage2c"], sem="stage2")
    P.dma("sp", dmaf(stage2[8:16, 64:128], qc2), W=["stage2d"], sem="stage2")
    P.dma("sp", dmaf(stage2[16:28, :], mla_q_norm.rearrange("l (c p) -> (l c) p", p=128)), W=["stage2e"], sem="stage2")
    P.dma("sp", dmaf(stage2[28:36, :], mla_kv_norm.rearrange("l (c p) -> (l c) p", p=128)), W=["stage2f"], sem="stage2")
    P.op("pe", tr(psb[0][:, 0:96], stage1[0:96, :], ident_f[0:96, 0:96]), R=["stage1", "ident_f"], W=[("ps", 0)])
    P.op("pe", tr(psb[1][:, 0:36], stage2[0:36, :], ident_f[0:36, 0:36]),
         R=["stage2a", "stage2b", "stage2c", "stage2d", "stage2e", "stage2f", "ident_f"], W=[("ps", 1)])
    P.op("dve", cp(vecT[:, 0:96], psb[0][:, 0:96]), R=[("ps", 0)], W=["vecT"])
    P.op("dve", ts(vecT[:, 96:112], psb[1][:, 0:16], 8.0, None, ALU.mult), R=[("ps", 1)], W=["vecT"])
    P.op("dve", ts(vecT[:, 112:124], psb[1][:, 16:28], float(np.sqrt(384.0)), None, ALU.mult), R=[("ps", 1)], W=["vecT"])
    P.op("dve", ts(vecT[:, 124:132], psb[1][:, 28:36], 16.0, None, ALU.mult), R=[("ps", 1)], W=["vecT"])
    P.dma("sp", dmaf(bf_rep[:], b_forget.rearrange("l h -> (l h)").partition_broadcast(128)), W=["bf_rep"], sem="c1")
    P.dma("sp", dmaf(gbn[:], qk_norm_b_nope.rearrange("l q d -> (l q d)").partition_broadcast(128)), W=["gbn"], sem="c2")
    P.dma("sp", dmaf(gbr[:], qk_norm_b_rope.rearrange("l q d -> (l q d)").partition_broadcast(128)), W=["gbr"], sem="c3")
    P.dma("sp", dmaf(cs[:], cs_d.rearrange("(t p) c -> p t c", p=128)), W=["cs"], sem="c4")
    for i in range(NT):
        P.dma("sp", dmaf(xs[:, i, :], x_d[i * 128:(i + 1) * 128, :]), W=[("x", i)], sem="x%d" % i)
    P.barrier()

    def norm_phase(gain_row):
        gnorm = fv(Z0, 1024)
        htok = [bv(Z0 + 2048 + k * 1024, 1024) for k in range(2)]
        junk = bv(Z0 + 4096, 1024)
        P.dma("sp", dmaf(gnorm, gain_row.partition_broadcast(128)), W=["gnorm"], sem="gnorm")
        for i in range(NT):
            P.op("act", act(junk, xs[:, i, :], AF.Square, accum_out=ssq[:, i:i + 1]), R=[("x", i)], W=["junk", "ssq"])
        rsqrt_small(rstd[:], ssq[:], 1.0 / D, EPS, ["ssq"], ["rstd"])
        for i in range(NT):
            P.op("dve", stt(htok[i % 2], xs[:, i, :], rstd[:, i:i + 1], gnorm, ALU.mult, ALU.mult),
                 R=[("x", i), "rstd", "gnorm"], W=[("htok", i % 2)])
            bank = 6 + (i % 2)
            ptv = psbf(bank).rearrange("p (a b) -> p a b", a=8)
            for kc in range(8):
                P.op("pe", tr(ptv[:, kc, :], htok[i % 2][:, kc * 128:(kc + 1) * 128], ident_bf[:]),
                     R=[("htok", i % 2), "ident_bf"], W=[("ps", bank)])
            P.op("act", cp_act(hT[:, :, i * 128:(i + 1) * 128], ptv), R=[("ps", bank)], W=[("hT", i // 4)])
        P.barrier()

    def cp_act(out, in_):
        return lambda e: e.activation(out=out, in_=in_, func=AF.Copy)

    unit_ctr = [0]

    def proj_norm(l, col0, nchunk, onesmat, nfeat, gcol0, outs, okey, wslots, sqb, rsb, gstep=0, split=None):
        for c in range(nchunk):
            wload(wslots[c][0], w_in[l, :, col0 + c * 128: col0 + (c + 1) * 128], wslots[c][1])
        for tc in range(4):
            u = unit_ctr[0] % 2
            unit_ctr[0] += 1
            b0 = 4 * u
            tsl = slice(tc * 512, (tc + 1) * 512)
            for c in range(nchunk):
                for kc in range(8):
                    P.op("pe", mm(psb[b0 + c][:], wslots[c][0][:, kc, :], hT[:, kc, tsl], kc == 0, kc == 7),
                         R=[wslots[c][1], ("hT", tc)], W=[("ps", b0 + c)])
                P.op("act", act(sqb[u][c], psb[b0 + c][:], AF.Square), R=[("ps", b0 + c)], W=[("sq", u, c)])
            for c in range(nchunk):
                P.op("pe", mm(psb[b0 + 3][:], onesmat[:], sqb[u][c], c == 0, c == nchunk - 1),
                     R=[("sq", u, c), "ones_bf", "blockones"], W=[("ps", b0 + 3)])
            ecol = {64: 1, 384: 2, 256: 3}[nfeat]
            P.op("act", act(rsb[u], psb[b0 + 3][:], AF.Ln, bias=cst[:, ecol:ecol + 1]), R=[("ps", b0 + 3), "cst"], W=[("rs", u)])
            P.op("act", act(rsb[u], rsb[u], AF.Exp, scale=-0.5), R=[("rs", u)], W=[("rs", u)])
            for c in range(nchunk):
                if split is not None:
                    for (p0, oap) in ((0, split[0]), (64, split[1])):
                        P.op("dve", stt(oap[p0:p0 + 64, tsl], psb[b0 + c][p0:p0 + 64, :], vecT[p0:p0 + 64, gcol0:gcol0 + 1], rsb[u][p0:p0 + 64, :], ALU.mult, ALU.mult),
                             R=[("ps", b0 + c), ("rs", u), "vecT"], W=[okey])
                    continue
                P.op("dve", stt(outs[c][:, tsl], psb[b0 + c][:], vecT[:, gcol0 + gstep * c:gcol0 + gstep * c + 1], rsb[u], ALU.mult, ALU.mult),
                     R=[("ps", b0 + c), ("rs", u), "vecT"], W=[okey])

    attn_ctr = [0]

    def attention(kind, hf, nbr, kslice, qslice, qkey, vaug, Pt, ytile, scale, biasA=None, bias_prep=None, EBrev=None, qprep=None, qterm=None):
        def finish(i):
            ob = 3 + (i % 2)
            pov = psb[ob][:, 0:260].rearrange("p (h c) -> p h c", h=4)
            rec = small[:, (i % 2) * 4:(i % 2) * 4 + 4]
            P.op("dve", (lambda rec, pov: lambda e: e.reciprocal(out=rec.unsqueeze(2), in_=pov[:, :, 64:65]))(rec, pov),
                 R=[("ps", ob)], W=[("rec", i % 2)])
            yt = ytile[i % 2]
            P.op("dve", tt(yt.rearrange("p (h c) -> p h c", h=4), pov[:, :, 0:64], rec.unsqueeze(2).broadcast_to([128, 4, 64]), ALU.mult),
                 R=[("ps", ob), ("rec", i % 2)], W=[("ytile", i % 2)])
            ptv = psbf(5).rearrange("p (a b) -> p a b", a=8)
            for c in range(2):
                P.op("pe", tr(ptv[:, c, :], yt[:, c * 128:(c + 1) * 128], ident_bf[:]), R=[("ytile", i % 2), "ident_bf"], W=[("ps", 5)])
            P.op("act", cp_act(yT[:, nbr, 2 * hf:2 * hf + 2, i * 128:(i + 1) * 128], ptv[:, 0:2, :]), R=[("ps", 5)], W=[("yT", nbr)])

        if qprep is not None:
            qprep[0](0)
            qprep[1](0)
        if bias_prep is not None:
            bias_prep(0)
        fin_pending = None
        for i in range(NT):
            js = list(range(max(0, i - 4), i + 1)) if kind == "C" else list(range(0, i + 1))
            groups = [js[a:a + 4] for a in range(0, len(js), 4)]
            ob = 3 + (i % 2)
            pov = psb[ob][:, 0:260].rearrange("p (h c) -> p h c", h=4)
            items = [(hh, grp) for hh in range(4) for grp in groups]
            pend = []

            def second(hh, grp, sbk, i=i, js=js, ob=ob, pov=pov):
                n = len(grp)
                if kind == "A":
                    h = 4 * hf + hh
                    for jj, j in enumerate(grp):
                        P.op("act", act(Pt[sbk][:, jj * 128:(jj + 1) * 128], psb[sbk][:, jj * 128:(jj + 1) * 128], AF.Exp,
                                        bias=biasA[i % 2][:, j, h:h + 1], scale=scale),
                             R=[("ps", sbk), ("biasA", i % 2)], W=[("Pt", sbk)])
                else:
                    P.op("act", act(Pt[sbk][:, 0:n * 128], psb[sbk][:, 0:n * 128], AF.Exp, scale=scale),
                         R=[("ps", sbk)], W=[("Pt", sbk)])
                if kind == "C":
                    d0 = 4 - (i - grp[0])
                    P.op(MASK_ENG, tt(Pt[sbk][:, 0:n * 128], Pt[sbk][:, 0:n * 128], EBrev[:, hh, d0 * 128:(d0 + n) * 128], ALU.mult),
                         R=["EBrev"], W=[("Pt", sbk)])
                elif i in grp and kind == "B":
                    jj = grp.index(i)
                    P.op(MASK_ENG, tt(Pt[sbk][:, jj * 128:(jj + 1) * 128], Pt[sbk][:, jj * 128:(jj + 1) * 128], maskB[:], ALU.mult),
                         R=["maskB"], W=[("Pt", sbk)])
                for jj, j in enumerate(grp):
                    P.op("pe", mm(pov[:, hh, :], Pt[sbk][:, jj * 128:(jj + 1) * 128], vaug[:, j, hh, :], j == js[0], j == js[-1]),
                         R=[("Pt", sbk), "v"], W=[("ps", ob)])

            cnt = 0
            for (hh, grp) in items:
                sbk = attn_ctr[0] % 3
                attn_ctr[0] += 1
                for jj, j in enumerate(grp):
                    osl = psb[sbk][:, jj * 128:(jj + 1) * 128]
                    P.op("pe", mm(osl, kslice(hh, j), qslice(hh, i), True, qterm is None),
                         R=["kT", qkey(i)], W=[("ps", sbk)])
                    if qterm is not None:
                        rq_ap, rq_key = qterm(i, hh)
                        P.op("pe", mm(osl, ones_bf[:], rq_ap, False, j != i), R=[rq_key, "ones_bf"], W=[("ps", sbk)])
                        if j == i:
                            P.op("pe", mm(osl, ident_bf[:], maskA[:], False, True), R=["maskA", "ident_bf"], W=[("ps", sbk)])
                pend.append((hh, grp, sbk))
                cnt += 1
                if cnt == 2:
                    if fin_pending is not None:
                        finish(fin_pending)
                        fin_pending = None
                    if i + 1 < NT:
                        if qprep is not None:
                            qprep[0](i + 1)
                        if bias_prep is not None:
                            bias_prep(i + 1)
                if len(pend) > 2:
                    second(*pend.pop(0))
            while pend:
                second(*pend.pop(0))
            if qprep is not None and i + 1 < NT:
                qprep[1](i + 1)
            fin_pending = i
        finish(fin_pending)

    def attention2(kind, hf, nbr, kslice, qchunk, vaug, Pt, ytile4, scale, biasA=None, prep=None, rqm=None, EB=None):
        OB = [3, 4, 6, 7]

        def finish(c):
            ptv = psbf(5).rearrange("p (a b) -> p a b", a=8)
            for t in range(4):
                ob = OB[t]
                pov = psb[ob][:, 0:260].rearrange("p (h c) -> p h c", h=4)
                rec = small[:, t * 4:t * 4 + 4]
                P.op("dve", (lambda rec, pov: lambda e: e.reciprocal(out=rec.unsqueeze(2), in_=pov[:, :, 64:65]))(rec, pov),
                     R=[("ps", ob)], W=[("rec", t)])
                yt = ytile4[t]
                P.op("dve", tt(yt.rearrange("p (h c) -> p h c", h=4), pov[:, :, 0:64], rec.unsqueeze(2).broadcast_to([128, 4, 64]), ALU.mult),
                     R=[("ps", ob), ("rec", t)], W=[("ytile", t)])
                for c2 in range(2):
                    P.op("pe", tr(ptv[:, c2 * 4 + t, :], yt[:, c2 * 128:(c2 + 1) * 128], ident_bf[:]), R=[("ytile", t), "ident_bf"], W=[("ps", 5)])
            P.op("act", cp_act(yT[:, nbr, 2 * hf:2 * hf + 2, c * 512:(c + 1) * 512], psbf(5).rearrange("p (a b) -> p a b", a=2)),
                 R=[("ps", 5)], W=[("yT", nbr)])

        if prep is not None:
            prep(0)
        for c in range(4):
            j_lo = max(0, 4 * c - 4) if kind == "C" else 0
            items = [(hh, j) for hh in range(4) for j in range(j_lo, 4 * c + 4)]
            pend = []

            def second(hh, j, sb, pb, t0, t1, c=c):
                cols = slice(t0 * 128, (t1 + 1) * 128)
                if kind == "A":
                    h = 4 * hf + hh
                    P.op("act", act(Pt[pb][:, cols], psb[sb][:, cols], AF.Exp, bias=biasA[c % 2][:, j, h:h + 1], scale=scale),
                         R=[("ps", sb), ("biasA", c % 2)], W=[("Pt", pb)])
                else:
                    P.op("act", act(Pt[pb][:, cols], psb[sb][:, cols], AF.Exp, scale=scale), R=[("ps", sb)], W=[("Pt", pb)])
                if kind == "C":
                    d0 = 4 * c + t0 - j
                    P.op(MASK_ENG, tt(Pt[pb][:, cols], Pt[pb][:, cols], EB[:, hh, d0 * 128:(d0 + t1 - t0 + 1) * 128], ALU.mult),
                         R=["EBrev"], W=[("Pt", pb)])
                for t in range(t0, t1 + 1):
                    i = 4 * c + t
                    first_j = max(0, i - 4) if kind == "C" else 0
                    pov = psb[OB[t]][:, 0:260].rearrange("p (h c) -> p h c", h=4)
                    P.op("pe", mm(pov[:, hh, :], Pt[pb][:, t * 128:(t + 1) * 128], vaug[:, j, hh, :], j == first_j, j == i),
                         R=[("Pt", pb), "v"], W=[("ps", OB[t])])

            for (hh, j) in items:
                t0 = max(0, j - 4 * c)
                t1 = 3 if kind != "C" else min(3, j + 4 - 4 * c)
                cols = slice(t0 * 128, (t1 + 1) * 128)
                nsb = 3 if kind == "C" else 2
                sb = attn_ctr[0] % nsb
                pb = attn_ctr[0] % 3
                attn_ctr[0] += 1
                diag = (kind == "A") and j >= 4 * c
                P.op("pe", mm(psb[sb][:, cols], kslice(hh, j), qchunk(c, hh)[:, cols], True, kind != "A"),
                     R=["kT", "qT"], W=[("ps", sb)])
                if kind == "A":
                    P.op("pe", mm(psb[sb][:, cols], ones_bf[:], rqm[:, hh, cols], False, not diag), R=["rqm", "ones_bf"], W=[("ps", sb)])
                    if diag:
                        P.op("pe", lambda e, o=psb[sb][:, t0 * 128:(t0 + 1) * 128]: e.matmul(o, ident_bf[:], maskA[:], start=False, stop=True, skip_group_check=True),
                             R=["maskA", "ident_bf"], W=[("ps", sb)])
                pend.append((hh, j, sb, pb, t0, t1))
                if len(pend) > nsb - 1:
                    second(*pend.pop(0))
            while pend:
                second(*pend.pop(0))
            if prep is not None and c + 1 < 4:
                prep(c + 1)
            finish(c)

    def v_proj(l, col0, hf, wv_unused, vaug):
        for sh in range(2):
            wsl, wkey = wqk_ref[0][sh]
            wload(wsl, w_in[l, :, col0 + hf * 256 + sh * 128: col0 + hf * 256 + (sh + 1) * 128], wkey)
        for i in range(NT):
            bank = 6 + (i % 2)
            for sh in range(2):
                wsl, wkey = wqk_ref[0][sh]
                for kc in range(8):
                    P.op("pe", mm(psb[bank][:, sh * 128:(sh + 1) * 128], hT[:, kc, i * 128:(i + 1) * 128], wsl[:, kc, :], kc == 0, kc == 7),
                         R=[wkey, ("hT", i // 4)], W=[("ps", bank)])
            P.op("act", cp_act(vaug[:, i, :, 0:64], psb[bank][:, 0:256].rearrange("p (h c) -> p h c", h=4)), R=[("ps", bank)], W=["v"])

    wqk_ref = [None]

    z = Z0
    ZK = z
    kT_ac = bv(z, 4096).rearrange("p (m t) -> p m t", m=2)
    kTm = bv(z, 8192).rearrange("p (m t) -> p m t", m=4)
    kT_b = bv(z, 8192).rearrange("p (m t) -> p m t", m=4)
    z += 8192
    ZQ = z
    qT_ac = bv(z, 4096).rearrange("p (m t) -> p m t", m=2)
    qTt = [bv(z + k * 512, 512).rearrange("p (h t) -> p h t", h=4) for k in range(2)]
    z += 4096
    vaug = bv(z, 4160).rearrange("p (i h c) -> p i h c", i=NT, h=4)
    z += 4160
    sqb = [[bv(z + (u * 3 + c) * 512, 512) for c in range(3)] for u in range(2)]
    Pt = [bv(z + k * 512, 512) for k in range(3)]
    ytile = [bv(z + 1536 + k * 256, 256) for k in range(2)]
    ytile4 = [bv(z + 1536 + k * 256, 256) for k in range(4)]
    z += 3072
    rsb = [fv(z + u * 1024, 512) for u in range(2)]
    z += 2048
    ZW = z
    wqk = [(bv(z + k * 1024, 1024).rearrange("p (k c) -> p k c", k=8), ("wqk", k)) for k in range(4)]
    z += 4096
    wv = None
    wqk_ref[0] = wqk
    assert z <= AR_N, z

    def set_vones():
        P.op("pool", ms(vaug[:, :, :, 64:65], 1.0), W=["v"])

    def phase_A(l):
        z = 16384
        wf = bv(z, 64).rearrange("p (k c) -> p k c", k=8); z += 64
        zt = fv(z, 128); z += 256
        lp = fv(z, 128); z += 256
        Lp = fv(z, 128).rearrange("p (i h) -> p i h", i=NT); z += 256
        PTt = fv(z, 128).rearrange("p (i h) -> p i h", i=NT); z += 256
        biasA = [fv(z + k * 256, 128).rearrange("p (i h) -> p i h", i=NT) for k in range(2)]; z += 512
        rq_bf = bv(z, 128); z += 128
        rqm = bv(z, 2048).rearrange("p (h t) -> p h t", h=4); z += 2048
        assert z <= 24576, z
        P.op("pool", ms(bv(ZK, 8192), 0.0), W=["kT"])
        loc = zt
        set_vones()
        P.op("pool", ms(rqm, 0.0), W=["rqm"])
        wload(wf, w_in[l, :, C_FA:C_FA + 8], "wf")
        for i in range(NT):
            for kc in range(8):
                P.op("pe", mm(psb[5][:, i * 8:(i + 1) * 8], hT[:, kc, i * 128:(i + 1) * 128], wf[:, kc, :], kc == 0, kc == 7),
                     R=["wf", ("hT", i // 4)], W=[("ps", 5)])
        ztv = zt.rearrange("p (i h) -> p i h", i=NT)
        P.op("dve", tt(ztv, psb[5][:, 0:128].rearrange("p (i h) -> p i h", i=NT),
                       bf_rep[:, l * 8:(l + 1) * 8].unsqueeze(1).broadcast_to([128, NT, 8]), ALU.add), R=[("ps", 5), "bf_rep"], W=["zt"])
        P.op("act", act(zt, zt, AF.Exp, scale=-1.0), R=["zt"], W=["zt"])
        P.op("act", act(lp, zt, AF.Ln, bias=cst[:, 0:1]), R=["zt", "cst"], W=["lp"])
        lpv = lp.rearrange("p (i h) -> p i h", i=NT)
        for i in range(NT):
            P.op("pe", mm(psb[7][:, i * 8:(i + 1) * 8], U_f[:], lpv[:, i, :], True, True), R=["lp", "U_f"], W=[("ps", 7)])
            P.op("pe", mm(psb[6][:, i * 8:(i + 1) * 8], ones_f[:], lpv[:, i, :], True, True), R=["lp", "ones_f"], W=[("ps", 6)])
        P.op("dve", ms(PTt[:, 0, :], 0.0), W=["PT"])
        for i in range(1, NT):
            P.op("dve", tt(PTt[:, i, :], PTt[:, i - 1, :], psb[6][:, (i - 1) * 8:i * 8], ALU.add), R=[("ps", 6)], W=["PT"])
        P.op("dve", tt(Lp, psb[7][:, 0:128].rearrange("p (i h) -> p i h", h=8), PTt, ALU.add), R=[("ps", 7), "PT"], W=["Lp"])
        locv = loc.rearrange("p (i h) -> p i h", i=NT)

        for hf in range(2):
            for c_ in range(2):
                proj_norm(l, C_KA + hf * 256 + c_ * 128, 1, blockones, 64, 96 + 2 * l + 1, [None], "kT", [wqk[c_]], sqb, rsb,
                          split=(kTm[:, 2 * c_, :], kTm[:, 2 * c_ + 1, :]))
            for c_ in range(2):
                proj_norm(l, C_QA + hf * 256 + c_ * 128, 1, blockones, 64, 96 + 2 * l, [qT_ac[:, c_, :]], "qT", [wqk[2 + c_]], sqb, rsb)
            v_proj(l, C_VA, hf, wv, vaug)
            P.barrier()
            def prepA(c, hf=hf):
                P.op("dve", tt(biasA[c % 2][:, 0:4 * c + 4, :], Lp[:, 0:4 * c + 4, :], PTt[:, 4 * c:4 * c + 1, :].broadcast_to([128, 4 * c + 4, 8]), ALU.subtract),
                     R=["Lp", "PT"], W=[("biasA", c % 2)])
                P.op("dve", tt(locv[:, 0:4, :], Lp[:, 4 * c:4 * c + 4, :], PTt[:, 4 * c:4 * c + 1, :].broadcast_to([128, 4, 8]), ALU.subtract),
                     R=["Lp", "PT", "zt"], W=["loc"])
                P.op("dve", ts(rq_bf[:, 0:32], loc[:, 0:32], -8.0, None, ALU.mult), R=["loc"], W=["rq_bf"])
                ptv = psbf(2).rearrange("p (a b) -> p a b", a=8)
                for t in range(4):
                    P.op("pe", tr(ptv[0:8, t, :], rq_bf[:, t * 8:(t + 1) * 8], ident_bf[:]), R=["rq_bf", "ident_bf"], W=[("ps", 2)])
                P.op("dve", tt(rqm[0:8, :, :], psbf(2)[0:8, 0:512].unsqueeze(1).broadcast_to([8, 4, 512]),
                               ident_f[0:8, 4 * hf:4 * hf + 4].unsqueeze(2).broadcast_to([8, 4, 512]), ALU.mult),
                     R=[("ps", 2), "ident_f"], W=["rqm"])

            attention2("A", hf, 0,
                       lambda hh, j: kTm[:, hh, j * 128:(j + 1) * 128],
                       lambda c, hh: qT_ac[:, hh // 2, c * 512:(c + 1) * 512],
                       vaug, Pt, ytile4, 0.125, biasA=biasA, prep=prepA, rqm=rqm)
            P.barrier()
            if DBG == 2 and l == 0 and hf == 1:
                P.dma("sp", dmaf(dbg_d, arena[:, Z0:Z0 + 24576]), sem="dbg")
                P.barrier()

    def phase_C(l):
        EBrev = bv(ZW, 2560).rearrange("p (h c) -> p h c", h=4)
        Tst = [fv(ZW + 2560, 640)]
        set_vones()
        P.op("pool", ms(bv(ZK, 8192), 0.0), W=["kT"])
        for hf in range(2):
            for c_ in range(2):
                proj_norm(l, C_KC + hf * 256 + c_ * 128, 1, blockones, 64, 104 + 2 * l + 1, [None], "kT", [wqk[c_]], sqb, rsb,
                          split=(kTm[:, 2 * c_, :], kTm[:, 2 * c_ + 1, :]))
            for c_ in range(2):
                proj_norm(l, C_QC + hf * 256 + c_ * 128, 1, blockones, 64, 104 + 2 * l, [qT_ac[:, c_, :]], "qT", [wqk[2 + c_]], sqb, rsb)
            v_proj(l, C_VC, hf, wv, vaug)
            P.barrier()
            for hh in range(4):
                h = 4 * hf + hh
                src = relx[l * 8 + h, :, :]
                P.dma("sp", dmaf(Tst[0], src), W=[("Tst", 0)], sem="Tst0")
                P.op("act", act(Tst[0], Tst[0], AF.Exp), R=[("Tst", 0)], W=[("Tst", 0)])
                for d in range(5):
                    P.op("dve", tt(EBrev[:, hh, d * 128:(d + 1) * 128], Tst[0][:, d * 128:(d + 1) * 128], validC[:, d * 128:(d + 1) * 128], ALU.mult),
                         R=[("Tst", 0), "validC"], W=["EBrev"])
            attention2("C", hf, 2,
                       lambda hh, j: kTm[:, hh, j * 128:(j + 1) * 128],
                       lambda c, hh: qT_ac[:, hh // 2, c * 512:(c + 1) * 512],
                       vaug, Pt, ytile4, 0.125, EB=EBrev)
            P.barrier()

    def rope_ops(dst1, dst2, x1, x2, cosv, sinv, t1, t2, Rk, Wk):
        P.op("dve", tt(t1, x1, cosv, ALU.mult), R=Rk, W=["rt1"])
        P.op("dve", tt(t2, x2, sinv, ALU.mult), R=Rk, W=["rt2"])
        P.op("dve", tt(dst1, t1, t2, ALU.subtract), R=["rt1", "rt2"], W=Wk)
        P.op("dve", tt(t1, x2, cosv, ALU.mult), R=Rk, W=["rt1"])
        P.op("dve", tt(t2, x1, sinv, ALU.mult), R=Rk, W=["rt2"])
        P.op("dve", tt(dst2, t1, t2, ALU.add), R=["rt1", "rt2"], W=Wk)

    def phase_B(l):
        z = ZQ + 1024
        wkr = bv(z, 256).rearrange("p (k c) -> p k c", k=8); z += 256
        wqup = bv(z, 1152).rearrange("p (k c) -> p k c", k=3); z += 1152
        wkvup = bv(z, 1024).rearrange("p (k c) -> p k c", k=2); z += 1024
        krope = bv(z, 512).rearrange("p (i c) -> p i c", i=NT); z += 512
        assert z <= ZQ + 4096
        z = 6144
        tmpf = fv(z, 512); z += 1024
        tmpq = fv(z, 384).rearrange("p (h c) -> p h c", h=4); z += 768
        assert z <= 8192
        z = 16384 + 4096
        ktok = [bv(z + k * 512, 512).rearrange("p (h c) -> p h c", h=4) for k in range(2)]; z += 1024
        qtok = [bv(z + k * 512, 512).rearrange("p (h c) -> p h c", h=4) for k in range(2)]; z += 1024
        for k_ in range(2):
            P.op("pool", ms(ktok[k_], 0.0), W=[("ktok", k_)])
            P.op("pool", ms(qtok[k_], 0.0), W=[("qtok", k_)])
        rt1 = fv(z, 256); z += 512
        rt2 = fv(z, 256); z += 512
        kvsb = fv(z, 512); z += 1024
        assert z <= 24576, z
        qdnT = bv(0, 6144).rearrange("p (c t) -> p c t", c=3)
        kvdnT = bv(16384, 4096).rearrange("p (c t) -> p c t", c=2)
        set_vones()
        proj_norm(l, C_QD, 3, ones_bf, 384, 112 + 3 * l, [qdnT[:, c, :] for c in range(3)], "qdnT", wqk[0:3], sqb, rsb, gstep=1)
        proj_norm(l, C_KVD, 2, ones_bf, 256, 124 + 2 * l, [kvdnT[:, c, :] for c in range(2)], "kvdnT", [wqk[3], wqk[0]], sqb, rsb, gstep=1)
        if BSTOP == 1:
            P.op("pool", ms(yT[:, 1, :, :], 0.0), W=[("yT", 1)]); P.barrier(); return
        wload(wkr, w_in[l, :, C_KR:C_KR + 32], "wkr")
        for i in range(NT):
            for kc in range(8):
                P.op("pe", mm(psb[5][:, i * 32:(i + 1) * 32], hT[:, kc, i * 128:(i + 1) * 128], wkr[:, kc, :], kc == 0, kc == 7),
                     R=["wkr", ("hT", i // 4)], W=[("ps", 5)])
        pk = psb[5][:].rearrange("p (i c) -> p i c", i=NT)
        tfv = tmpf.rearrange("p (i c) -> p i c", i=NT)
        sm = small[:, 16:32]
        P.op("act", act(tmpf, psb[5][:], AF.Square), R=[("ps", 5)], W=["tmpf"])
        P.op("dve", red(sm, tfv), R=["tmpf"], W=["sm"])
        rsqrt_small(sm, sm, 1.0 / 32, EPS, ["sm"], ["sm"])
        P.op("dve", tt(tfv, pk, sm.unsqueeze(2).broadcast_to([128, NT, 32]), ALU.mult), R=[("ps", 5), "sm"], W=["tmpf"])
        gk = gbr[:, l * 64 + 32:l * 64 + 64]
        P.op("dve", tt(tfv, tfv, gk.unsqueeze(1).broadcast_to([128, NT, 32]), ALU.mult), R=["tmpf", "gbr"], W=["tmpf"])
        r1 = rt1.rearrange("p (i c) -> p i c", i=NT)
        r2 = rt2.rearrange("p (i c) -> p i c", i=NT)
        rope_ops(krope[:, :, 0:16], krope[:, :, 16:32], tfv[:, :, 0:16], tfv[:, :, 16:32], cs[:, :, 0:16], cs[:, :, 16:32],
                 r1, r2, ["tmpf", "cs"], ["krope"])
        P.barrier()
        if BSTOP == 2:
            P.op("pool", ms(yT[:, 1, :, :], 0.0), W=[("yT", 1)]); P.barrier(); return
        for hf in range(2):
            wload(wkvup, w_kv_up[l, :, hf * 512:(hf + 1) * 512], "wkvup")
            wload(wqup, w_q_up[l, :, hf * 384:(hf + 1) * 384], "wqup")
            for i in range(NT):
                bank = 6
                pkv = psb[bank][:].rearrange("p (h c) -> p h c", h=4)
                for kc in range(2):
                    P.op("pe", mm(psb[bank][:], kvdnT[:, kc, i * 128:(i + 1) * 128], wkvup[:, kc, :], kc == 0, kc == 1),
                         R=["kvdnT", "wkvup"], W=[("ps", bank)])
                t4 = tmpf[:, 0:256].rearrange("p (h c) -> p h c", h=4)
                s4 = small[:, 32:36]
                P.op("act", cp_act(kvsb, psb[bank][:]), R=[("ps", bank)], W=["kvsb"])
                pkv = kvsb.rearrange("p (h c) -> p h c", h=4)
                P.op("act", act(t4, pkv[:, :, 0:64], AF.Square), R=["kvsb"], W=["tmpf"])
                P.op("dve", red(s4, t4), R=["tmpf"], W=["s4"])
                rsqrt_small(s4, s4, 1.0 / 64, EPS, ["s4"], ["s4"])
                P.op("dve", tt(t4, pkv[:, :, 0:64], s4.unsqueeze(2).broadcast_to([128, 4, 64]), ALU.mult), R=["kvsb", "s4"], W=["tmpf"])
                kt = ktok[i % 2]
                gn = gbn[:, l * 128 + 64:l * 128 + 128]
                P.op("dve", tt(kt[:, :, 0:64], t4, gn.unsqueeze(1).broadcast_to([128, 4, 64]), ALU.mult), R=["tmpf", "gbn"], W=[("ktok", i % 2)])
                P.op("dve", cp(kt[:, :, 64:96], krope[:, i, :].unsqueeze(1).broadcast_to([128, 4, 32])), R=["krope"], W=[("ktok", i % 2)])
                P.op("act", cp_act(vaug[:, i, :, 0:64], pkv[:, :, 64:128]), R=["kvsb"], W=["v"])
                ptv = psbf(7).rearrange("p (a b) -> p a b", a=8)
                for hh in range(4):
                    P.op("pe", tr(ptv[:, hh, :], kt[:, hh, :], ident_bf[:]), R=[("ktok", i % 2), "ident_bf"], W=[("ps", 7)])
                P.op("act", cp_act(kT_b[0:96, :, i * 128:(i + 1) * 128], ptv[0:96, 0:4, :]), R=[("ps", 7)], W=["kT"])

            if BSTOP == 3:
                P.barrier(); P.op("pool", ms(yT[:, 1, :, :], 0.0), W=[("yT", 1)]); P.barrier(); return
            def qprep(i):
                bank = 6
                pq = psb[bank][:, 0:384].rearrange("p (h c) -> p h c", h=4)
                for kc in range(3):
                    P.op("pe", mm(psb[bank][:, 0:384], qdnT[:, kc, i * 128:(i + 1) * 128], wqup[:, kc, :], kc == 0, kc == 2),
                         R=["qdnT", "wqup"], W=[("ps", bank)])
                t4 = tmpf[:, 0:384].rearrange("p (h c) -> p h c", h=4)
                sn = small[:, 36:40]
                sr = small[:, 40:44]
                P.op("act", cp_act(kvsb[:, 0:384], psb[bank][:, 0:384]), R=[("ps", bank)], W=["kvsb"])
                pq = kvsb[:, 0:384].rearrange("p (h c) -> p h c", h=4)
                P.op("act", act(t4, pq, AF.Square), R=["kvsb"], W=["tmpf"])
                P.op("dve", red(sn, t4[:, :, 0:64]), R=["tmpf"], W=["sn"])
                P.op("dve", red(sr, t4[:, :, 64:96]), R=["tmpf"], W=["sr"])
                rsqrt_small(sn, sn, 1.0 / 64, EPS, ["sn"], ["sn"])
                rsqrt_small(sr, sr, 1.0 / 32, EPS, ["sr"], ["sr"])
                P.op("dve", tt(tmpq[:, :, 0:64], pq[:, :, 0:64], sn.unsqueeze(2).broadcast_to([128, 4, 64]), ALU.mult), R=["kvsb", "sn"], W=["tmpq"])
                P.op("dve", tt(tmpq[:, :, 64:96], pq[:, :, 64:96], sr.unsqueeze(2).broadcast_to([128, 4, 32]), ALU.mult), R=["kvsb", "sr"], W=["tmpq"])
                qt = qtok[i % 2]
                gqn = gbn[:, l * 128:l * 128 + 64]
                gqr = gbr[:, l * 64:l * 64 + 32]
                P.op("dve", tt(qt[:, :, 0:64], tmpq[:, :, 0:64], gqn.unsqueeze(1).broadcast_to([128, 4, 64]), ALU.mult), R=["tmpq", "gbn"], W=[("qtok", i % 2)])
                P.op("dve", tt(tmpq[:, :, 64:96], tmpq[:, :, 64:96], gqr.unsqueeze(1).broadcast_to([128, 4, 32]), ALU.mult), R=["tmpq", "gbr"], W=["tmpq"])
                c4 = cs[:, i, 0:16].unsqueeze(1).broadcast_to([128, 4, 16])
                s4b = cs[:, i, 16:32].unsqueeze(1).broadcast_to([128, 4, 16])
                q1 = rt1[:, 0:64].rearrange("p (h c) -> p h c", h=4)
                q2 = rt2[:, 0:64].rearrange("p (h c) -> p h c", h=4)
                rope_ops(qt[:, :, 64:80], qt[:, :, 80:96], tmpq[:, :, 64:80], tmpq[:, :, 80:96], c4, s4b, q1, q2, ["tmpq", "cs"], [("qtok", i % 2)])

            def qprep_b(i):
                qt = qtok[i % 2]
                ptv = psbf(7).rearrange("p (a b) -> p a b", a=8)
                for hh in range(4):
                    P.op("pe", tr(ptv[:, hh, :], qt[:, hh, :], ident_bf[:]), R=[("qtok", i % 2), "ident_bf"], W=[("ps", 7)])
                P.op("act", cp_act(qTt[i % 2][0:96, :, :], ptv[0:96, 0:4, :]), R=[("ps", 7)], W=[("qTt", i % 2)])

            attention("B", hf, 1,
                      lambda hh, j: kT_b[0:96, hh, j * 128:(j + 1) * 128],
                      lambda hh, i: qTt[i % 2][0:96, hh, :],
                      lambda i: ("qTt", i % 2), vaug, Pt, ytile, float(96.0 ** -0.5), qprep=(qprep, qprep_b))
            P.barrier()

    def merge_phase(l):
        z = Z0
        mT = bv(z, 8192).rearrange("p (c t) -> p c t", c=8); z += 8192
        wg = [bv(z + k * 3072, 3072).rearrange("p (k n c) -> p k n c", k=8, n=3) for k in range(2)]; z += 6144
        wb = [bv(z + k * 1536, 1536).rearrange("p (k n c) -> p k n c", k=4, n=3) for k in range(2)]; z += 3072
        wo = [bv(z + k * 2048, 2048).rearrange("p (k c) -> p k c", k=8) for k in range(2)]; z += 4096
        gate = [bv(z + k * 512, 512) for k in range(3)]; z += 1536
        acc = fv(z, 512); z += 1024
        tmp = fv(z, 512); z += 1024
        assert z <= AR_N, z
        cnt = 0
        for th in range(2):
            for m in range(8):
                sl = cnt % 2
                cnt += 1
                for n in range(3):
                    c0 = C_G + n * 1024 + m * 128
                    P.dma("pool", dmaf(wg[sl][:, :, n, :], w_in[l, :, c0:c0 + 128].rearrange("(k p) c -> p k c", p=128)), W=[("wg", sl, n)], sem="wg%d_%d" % (sl, n))
                    P.dma("pool", dmaf(wb[sl][:, :, n, :], w_branch[l, n, :, m * 128:(m + 1) * 128].rearrange("(k p) c -> p k c", p=128)), W=[("wb", sl, n)], sem="wb%d_%d" % (sl, n))
                for tq in range(2):
                    tc = th * 2 + tq
                    tsl = slice(tc * 512, (tc + 1) * 512)
                    u = (m * 2 + tq) % 2
                    for n in range(3):
                        gb = u * 4 + n if n < 2 else u * 4 + 2
                        gb = u * 4 + n
                        for kc in range(8):
                            P.op("pe", mm(psb[gb][:], wg[sl][:, kc, n, :], hT[:, kc, tsl], kc == 0, kc == 7), R=[("wg", sl, n), ("hT", tc)], W=[("ps", gb)])
                        P.op("act", act(gate[n], psb[gb][:], AF.Sigmoid, bias=vecT[:, l * 24 + n * 8 + m:l * 24 + n * 8 + m + 1]),
                             R=[("ps", gb), "vecT"], W=[("gate", n)])
                        pb = u * 4 + 3
                        for kc in range(4):
                            P.op("pe", mm(psb[pb][:], wb[sl][:, kc, n, :], yT[:, n, kc, tsl], kc == 0, kc == 3), R=[("wb", sl, n), ("yT", n)], W=[("ps", pb)])
                        if n == 0:
                            P.op("dve", tt(acc, psb[pb][:], gate[n], ALU.mult), R=[("ps", pb), ("gate", n)], W=["acc"])
                        else:
                            P.op("dve", tt(tmp, psb[pb][:], gate[n], ALU.mult), R=[("ps", pb), ("gate", n)], W=["tmp"])
                            if n == 1:
                                P.op("dve", tt(acc, acc, tmp, ALU.add), R=["tmp"], W=["acc"])
                            else:
                                P.op("dve", tt(mT[:, m, tq * 512:(tq + 1) * 512], acc, tmp, ALU.add), R=["tmp", "acc"], W=["mT"])
            for cq in range(4):
                sl = cq % 2
                wload(wo[sl], w_out[l, :, cq * 256:(cq + 1) * 256], ("wo", sl))
                for ii in range(8):
                    i = th * 8 + ii
                    bank = ii % 2 + 6 if False else (ii % 4)
                    for kc in range(8):
                        P.op("pe", mm(psb[bank][:, 0:256], mT[:, kc, ii * 128:(ii + 1) * 128], wo[sl][:, kc, :], kc == 0, kc == 7),
                             R=["mT", ("wo", sl)], W=[("ps", bank)])
                    P.op("dve", tt(xs[:, i, cq * 256:(cq + 1) * 256], xs[:, i, cq * 256:(cq + 1) * 256], psb[bank][:, 0:256], ALU.add),
                         R=[("ps", bank)], W=[("x", i)])
        P.barrier()

    def ffn_phase(l):
        aT = [bv(0, 16384).rearrange("p (c t) -> p c t", c=8), bv(Z0, 16384).rearrange("p (c t) -> p c t", c=8)]
        w2 = [bv(16384 + k * 4096, 4096).rearrange("p (k c) -> p k c", k=8) for k in range(2)]
        z = Z0 + 16384
        w1 = [bv(z + k * 2048, 2048).rearrange("p (k c) -> p k c", k=8) for k in range(2)]; z += 4096
        rt = [fv(z + k * 1024, 512) for k in range(2)]; z += 2048
        assert z <= AR_N, z
        c1 = 0
        c2 = 0
        bk = 0
        for g in range(4):
            a = aT[g % 2]
            for fp in range(4):
                sl = c1 % 2
                c1 += 1
                wload(w1[sl], w_ff1[l, :, g * 1024 + fp * 256: g * 1024 + (fp + 1) * 256], ("w1", sl))
                for f2 in range(2):
                    f = fp * 2 + f2
                    for tc in range(4):
                        bank = bk % 4
                        bk += 1
                        for kc in range(8):
                            P.op("pe", mm(psb[bank][:], w1[sl][:, kc, f2 * 128:(f2 + 1) * 128], hT[:, kc, tc * 512:(tc + 1) * 512], kc == 0, kc == 7),
                                 R=[("w1", sl), ("hT", tc)], W=[("ps", bank)])
                        r = rt[bank % 2]
                        P.op("act", act(r, psb[bank][:], AF.Relu), R=[("ps", bank)], W=[("rt", bank % 2)])
                        P.op(SQ_ENG, tt(a[:, f, tc * 512:(tc + 1) * 512], r, r, ALU.mult), R=[("rt", bank % 2)], W=[("aT", g % 2)])
            for ch in range(2):
                sl = c2 % 2
                c2 += 1
                wload(w2[sl], w_ff2[l, g * 1024:(g + 1) * 1024, ch * 512:(ch + 1) * 512], ("w2", sl))
                for i in range(NT):
                    bank = 4 + (i % 4)
                    for f in range(8):
                        P.op("pe", mm(psb[bank][:], a[:, f, i * 128:(i + 1) * 128], w2[sl][:, f, :], f == 0, f == 7),
                             R=[("aT", g % 2), ("w2", sl)], W=[("ps", bank)])
                    P.op("dve", tt(xs[:, i, ch * 512:(ch + 1) * 512], xs[:, i, ch * 512:(ch + 1) * 512], psb[bank][:], ALU.add),
                         R=[("ps", bank)], W=[("x", i)])
        P.barrier()

    MASK_ENG = "dve"
    SQ_ENG = "pool"
    for l in range(nlayers):
        P.tag = "norm1"
        if "N" in PH:
            norm_phase(norm_mix[l, :])
        for nb_, ch_ in enumerate("ABC"):
            if ch_ not in PH:
                P.op("pool", ms(yT[:, nb_, :, :], 0.0), W=[("yT", nb_)])
        P.tag = "B"
        if "B" in PH:
            phase_B(l)
        P.tag = "A"
        if "A" in PH:
            phase_A(l)
        P.tag = "C"
        if "C" in PH:
            phase_C(l)
        P.tag = "merge"
        if DBG == 1 and l == 0:
            P.barrier()
            P.dma("sp", dmaf(dbg_d, arena[:, 0:24576]), R=[("yT", 0), ("yT", 1), ("yT", 2)], sem="dbg")
            P.barrier()
        if "M" in PH:
            merge_phase(l)
        if "F" in PH:
            P.tag = "norm2"
            norm_phase(norm_ffn[l, :])
            P.tag = "ffn"
            ffn_phase(l)
    for i in range(NT):
        P.dma("sp", dmaf(y_d[i * 128:(i + 1) * 128, :], xs[:, i, :]), R=[("x", i)], sem="y%d" % i)
    P.barrier()
    P.emit(nc, st)
    st.close()
    return nc


def _host_consts(rel_bias):
    half = 16
    inv = (10000.0 ** (-np.arange(half, dtype=np.float32) / half)).astype(np.float32)
    ang = np.arange(S, dtype=np.float32)[:, None] * inv[None, :]
    cs_tab = np.concatenate([np.cos(ang), np.sin(ang)], axis=1).astype(np.float32)
    kl = np.arange(128)[:, None]
    c = np.arange(640)[None, :]
    idx = np.clip(c - kl, -256, 256) + 256
    rel_ext = np.ascontiguousarray(rel_bias[:, :, idx]).reshape(DEPTH * 8, 128, 640).astype(np.float32)
    return cs_tab, rel_ext


PH = "NBACMF"
ANNOT = False
DBG = False
BSTOP = 0
_NC_CACHE = {}


def kernel(**inputs):
    inp = {k: np.ascontiguousarray(np.asarray(v, dtype=np.float32)) for k, v in inputs.items()}
    x = inp.pop("x")
    rel_bias = inp.pop("rel_bias")
    cs_tab, rel_ext = _host_consts(rel_bias)
    key = (DEPTH, PH)
    if key not in _NC_CACHE:
        _NC_CACHE[key] = build(DEPTH)
    nc = _NC_CACHE[key]
    B = x.shape[0]
    in_maps = []
    for b in range(B):
        m = dict(inp)
        m["x"] = np.ascontiguousarray(x[b])
        m["rel_ext"] = rel_ext
        m["cs_tab"] = cs_tab
        in_maps.append(m)
    res = run_bass_kernel_spmd(nc, in_maps, core_ids=list(range(B)))
    return np.stack([np.asarray(r["y"], dtype=np.float32) for r in res.results], axis=0)
```

```python
import numpy as np
from contextlib import ExitStack
import concourse.bass as bass
import concourse.mybir as mybir
from concourse.bass_utils import run_bass_kernel_spmd

F32 = mybir.dt.float32
BF16 = mybir.dt.bfloat16
AF = mybir.ActivationFunctionType
ALU = mybir.AluOpType
AX = mybir.AxisListType

S = 2048
D = 1024
NT = 16
DEPTH = 4
EPS = 1e-6
INW = 6824
C_QA, C_KA, C_VA, C_FA, C_QD, C_KVD, C_KR, C_QC, C_KC, C_VC, C_G = 0, 512, 1024, 1536, 1544, 1928, 2184, 2216, 2728, 3240, 3752


class Prog:
    ENG = ("pe", "act", "dve", "pool", "sp")

    def __init__(self):
        self.ops = {e: [] for e in self.ENG}
        self.state = {}
        self.known = {e: {} for e in self.ENG}
        self.flag = {e: set() for e in self.ENG}
        self.semcnt = {}
        self.tag = ""
        self.tags = {e: [] for e in self.ENG}

    def _add(self, eng, fn, R, W, dma_sem):
        self.tags[eng].append(self.tag)
        need = {}
        for k in R:
            st = self.state.get(k)
            if st and st[0] is not None:
                t = st[0]
                need[t[0]] = max(need.get(t[0], -1), t[1])
        for k in W:
            st = self.state.get(k)
            if st:
                if st[0] is not None:
                    t = st[0]
                    need[t[0]] = max(need.get(t[0], -1), t[1])
                for tk, tv in st[1].items():
                    need[tk] = max(need.get(tk, -1), tv)
        kn = self.known[eng]
        waits = []
        for tk, tv in need.items():
            if dma_sem is None and eng == "pe" and tk == ("E", "pe"):
                continue
            if kn.get(tk, -1) >= tv:
                continue
            kn[tk] = tv
            waits.append((tk, tv))
            if tk[0] == "E":
                self.flag[tk[1]].add(tv)
        idx = len(self.ops[eng])
        if dma_sem is None:
            tok = (("E", eng), idx)
        else:
            c = self.semcnt.get(dma_sem, 0) + 1
            self.semcnt[dma_sem] = c
            tok = (("S", dma_sem), c)
        self.ops[eng].append((fn, waits, dma_sem))
        for k in R:
            if k in W:
                continue
            st = self.state.setdefault(k, [None, {}])
            st[1][tok[0]] = max(st[1].get(tok[0], -1), tok[1])
        for k in W:
            self.state[k] = [tok, {}]
        return tok

    def op(self, eng, fn, R=(), W=()):
        return self._add(eng, fn, list(R), list(W), None)

    def dma(self, q, fn, R=(), W=(), sem=None):
        return self._add(q, fn, list(R), list(W), sem)

    def barrier(self):
        last = {}
        for e in self.ENG:
            n = len(self.ops[e])
            for i in range(n - 1, -1, -1):
                if self.ops[e][i][2] is None and self.ops[e][i][0] is not None:
                    last[("E", e)] = i
                    break
        for s, c in self.semcnt.items():
            last[("S", s)] = c
        for e in self.ENG:
            kn = self.known[e]
            waits = []
            for tk, tv in last.items():
                if kn.get(tk, -1) >= tv:
                    continue
                kn[tk] = tv
                waits.append((tk, tv))
                if tk[0] == "E":
                    self.flag[tk[1]].add(tv)
            if waits:
                self.ops[e].append((None, waits, None))
                self.tags[e].append(self.tag)

    def emit(self, nc, stack):
        rank = {}
        for e in self.ENG:
            rank[e] = {idx: r + 1 for r, idx in enumerate(sorted(self.flag[e]))}
        BS = 2000
        esem = {e: [stack.enter_context(nc.semaphore("es_%s_%d" % (e, b))) for b in range(len(rank[e]) // BS + 1)] for e in self.ENG}
        dsem = {s: stack.enter_context(nc.semaphore("ds_" + str(i))) for i, s in enumerate(self.semcnt)}
        block = stack.enter_context(nc.Block())

        def run(e, h):
            for idx, (fn, waits, ds) in enumerate(self.ops[e]):
                for tk, tv in waits:
                    if tk[0] == "E":
                        r_ = rank[tk[1]][tv] - 1
                        h.wait_ge(esem[tk[1]][r_ // BS], r_ % BS + 1)
                    else:
                        h.wait_ge(dsem[tk[1]], tv * 16)
                if fn is None:
                    continue
                ins = fn(h)
                if ANNOT:
                    ins.annotate(self.tags[e][idx])
                if ds is not None:
                    ins.then_inc(dsem[ds], 16)
                elif idx in rank[e]:
                    ins.then_inc(esem[e][(rank[e][idx] - 1) // BS], 1)

        block.tensor(lambda h: run("pe", h))
        block.scalar(lambda h: run("act", h))
        block.vector(lambda h: run("dve", h))
        block.gpsimd(lambda h: run("pool", h))
        block.sync(lambda h: run("sp", h))


def build(nlayers=DEPTH):
    nc = bass.Bass("TRN2", target_bir_lowering=False)
    P = Prog()
    st = ExitStack()

    def din(name, shape):
        return nc.dram_tensor(name, list(shape), F32, kind="ExternalInput")

    x_d = din("x", [S, D]).ap()
    norm_mix = din("norm_mix", [DEPTH, D]).ap()
    w_in = din("w_in", [DEPTH, D, INW]).ap()
    b_forget = din("b_forget", [DEPTH, 8]).ap()
    b_gate = din("b_gate", [DEPTH, 3072]).ap()
    qk_norm_a = din("qk_norm_a", [DEPTH, 2, 64]).ap()
    mla_q_norm = din("mla_q_norm", [DEPTH, 384]).ap()
    mla_kv_norm = din("mla_kv_norm", [DEPTH, 256]).ap()
    w_q_up = din("w_q_up", [DEPTH, 384, 768]).ap()
    w_kv_up = din("w_kv_up", [DEPTH, 256, 1024]).ap()
    qk_norm_b_nope = din("qk_norm_b_nope", [DEPTH, 2, 64]).ap()
    qk_norm_b_rope = din("qk_norm_b_rope", [DEPTH, 2, 32]).ap()
    qk_norm_c = din("qk_norm_c", [DEPTH, 2, 64]).ap()
    relx = din("rel_ext", [DEPTH * 8, 128, 640]).ap()
    w_branch = din("w_branch", [DEPTH, 3, 512, D]).ap()
    w_out = din("w_out", [DEPTH, D, D]).ap()
    norm_ffn = din("norm_ffn", [DEPTH, D]).ap()
    w_ff1 = din("w_ff1", [DEPTH, D, 4096]).ap()
    w_ff2 = din("w_ff2", [DEPTH, 4096, D]).ap()
    cs_d = din("cs_tab", [S, 32]).ap()
    y_d = nc.dram_tensor("y", [S, D], F32, kind="ExternalOutput").ap()
    dbg_d = nc.dram_tensor("dbg", [128, 24576], BF16, kind="ExternalOutput").ap() if DBG else None

    def sb(name, shape, dt):
        return st.enter_context(nc.sbuf_tensor(name, list(shape), dt))

    xs = sb("xs", [128, NT, D], F32)
    hT = sb("hT", [128, 8, S], BF16)
    ident_bf = sb("ident_bf", [128, 128], BF16)
    ident_f = sb("ident_f", [128, 128], F32)
    blockones = sb("blockones", [128, 128], BF16)
    ones_bf = sb("ones_bf", [128, 128], BF16)
    maskA = sb("maskA", [128, 128], BF16)
    maskB = sb("maskB", [128, 128], BF16)
    U_f = sb("U_f", [128, 128], F32)
    ones_f = sb("ones_f", [128, 128], F32)
    validC = sb("validC", [128, 640], BF16)
    vecT = sb("vecT", [128, 132], F32)
    bf_rep = sb("bf_rep", [128, 32], F32)
    gbn = sb("gbn", [128, 512], F32)
    gbr = sb("gbr", [128, 256], F32)
    cs = sb("cs", [128, NT, 32], F32)
    ssq = sb("ssq", [128, NT], F32)
    rstd = sb("rstd", [128, NT], F32)
    small = sb("small", [128, 64], F32)
    cst = sb("cst", [128, 4], F32)
    AR_N = 52000
    arena = sb("arena", [128, AR_N], BF16)
    psb = [st.enter_context(nc.psum_tensor("ps%d" % b, [128, 512], F32)) for b in range(8)]

    def bv(off, n):
        return arena[:, off:off + n]

    def fv(off, n):
        return arena[:, off:off + 2 * n].bitcast(F32)

    def psbf(b):
        return psb[b][:].bitcast(BF16)

    Z0 = 24576
    yT = bv(0, 24576).rearrange("p (n c t) -> p n c t", n=3, c=4)

    def mm(out, lhsT, rhs, start, stop):
        return lambda e: e.matmul(out, lhsT, rhs, start=start, stop=stop)

    def tr(out, in_, ident):
        return lambda e: e.transpose(out, in_, ident)

    def act(out, in_, func, bias=None, scale=None, accum_out=None):
        kw = {}
        if bias is not None:
            kw["bias"] = bias
        if scale is not None:
            kw["scale"] = scale
        if accum_out is not None:
            kw["accum_out"] = accum_out
        return lambda e: e.activation(out=out, in_=in_, func=func, **kw)

    def tt(out, in0, in1, op):
        return lambda e: e.tensor_tensor(out=out, in0=in0, in1=in1, op=op)

    def ts(out, in0, s1, s2, op0, op1=None):
        if op1 is None:
            return lambda e: e.tensor_single_scalar(out=out, in_=in0, scalar=s1, op=op0)
        return lambda e: e.tensor_scalar(out=out, in0=in0, scalar1=s1, scalar2=s2, op0=op0, op1=op1)

    def stt(out, in0, scalar, in1, op0, op1):
        return lambda e: e.scalar_tensor_tensor(out=out, in0=in0, scalar=scalar, in1=in1, op0=op0, op1=op1)

    def cp(out, in_):
        return lambda e: e.tensor_copy(out=out, in_=in_)

    def rsqrt_small(dst, src, mul, eps, Rk, Wk):
        P.op("dve", ts(dst, src, mul, eps, ALU.mult, ALU.add), R=Rk, W=Wk)
        P.op("act", act(dst, dst, AF.Ln), R=Wk, W=Wk)
        P.op("act", act(dst, dst, AF.Exp, scale=-0.5), R=Wk, W=Wk)

    def red(out, in_):
        return lambda e: e.tensor_reduce(out=out, in_=in_, axis=AX.X, op=ALU.add)

    def ms(ap, v):
        return lambda e: e.memset(ap, v)

    def dmaf(out, in_):
        return lambda e: e.dma_start(out=out, in_=in_)

    def wload(dst, src2d, key, R=(), q="pool"):
        P.dma(q, dmaf(dst, src2d.rearrange("(k p) c -> p k c", p=128)), R=R, W=[key], sem=str(key))

    P.op("pool", ms(ones_f[:], 1.0), W=["ones_f"])
    P.op("pool", lambda e: e.affine_select(out=ident_f[:], in_=ones_f[:], pattern=[[-1, 128]], compare_op=ALU.is_equal,
                                           fill=0.0, base=0, channel_multiplier=1), R=["ones_f"], W=["ident_f"])
    P.op("pool", lambda e: e.affine_select(out=U_f[:], in_=ones_f[:], pattern=[[1, 128]], compare_op=ALU.is_ge,
                                           fill=0.0, base=0, channel_multiplier=-1), R=["ones_f"], W=["U_f"])
    P.op("pool", cp(ident_bf[:], ident_f[:]), R=["ident_f"], W=["ident_bf"])
    P.op("pool", ts(maskA[:], U_f[:], 60000.0, -60000.0, ALU.mult, ALU.add), R=["U_f"], W=["maskA"])
    P.op("pool", ms(ones_bf[:], 1.0), W=["ones_bf"])
    P.op("pool", ms(cst[:, 0:1], 1.0), W=["cst"])
    P.op("pool", ms(small[:, 48:49], EPS), W=["epscol"])
    P.op("pool", ms(cst[:, 1:2], 64 * EPS), W=["cst"])
    P.op("pool", ms(cst[:, 2:3], 384 * EPS), W=["cst"])
    P.op("pool", ms(cst[:, 3:4], 256 * EPS), W=["cst"])
    P.op("pool", ms(blockones[:], 0.0), W=["blockones"])
    P.op("pool", ms(blockones[0:64, 0:64], 1.0), W=["blockones"])
    P.op("pool", ms(blockones[64:128, 64:128], 1.0), W=["blockones"])
    P.op("pool", ms(maskB[:], 1.0), W=["maskB"])
    P.op("pool", ms(maskB[64:128, 0:64], 0.0), W=["maskB"])
    P.op("pool", ms(validC[:], 0.0), W=["validC"])
    P.op("pool", ms(validC[0:64, 0:576], 1.0), W=["validC"])
    P.op("pool", ms(validC[64:128, 64:640], 1.0), W=["validC"])

    stage1 = fv(Z0, 128)
    stage2 = fv(Z0 + 256, 128)
    P.dma("sp", dmaf(stage1[0:96, :], b_gate.rearrange("l (c p) -> (l c) p", p=128)), W=["stage1"], sem="stage1")
    qa2 = qk_norm_a.rearrange("l q d -> (l q) d")
    qc2 = qk_norm_c.rearrange("l q d -> (l q) d")
    P.dma("sp", dmaf(stage2[0:8, 0:64], qa2), W=["stage2a"], sem="stage2")
    P.dma("sp", dmaf(stage2[0:8, 64:128], qa2), W=["stage2b"], sem="stage2")
    P.dma("sp", dmaf(stage2[8:16, 0:64], qc2), W=["stage2c"], sem="stage2")
    P.dma("sp", dmaf(stage2[8:16, 64:128], qc2), W=["stage2d"], sem="stage2")
    P.dma("sp", dmaf(stage2[16:28, :], mla_q_norm.rearrange("l (c p) -> (l c) p", p=128)), W=["stage2e"], sem="stage2")
    P.dma("sp", dmaf(stage2[28:36, :], mla_kv_norm.rearrange("l (c p) -> (l c) p", p=128)), W=["stage2f"], sem="stage2")
    P.op("pe", tr(psb[0][:, 0:96], stage1[0:96, :], ident_f[0:96, 0:96]), R=["stage1", "ident_f"], W=[("ps", 0)])
    P.op("pe", tr(psb[1][:, 0:36], stage2[0:36, :], ident_f[0:36, 0:36]),
         R=["stage2a", "stage2b", "stage2c", "stage2d", "stage2e", "stage2f", "ident_f"], W=[("ps", 1)])
    P.op("dve", cp(vecT[:, 0:96], psb[0][:, 0:96]), R=[("ps", 0)], W=["vecT"])
    P.op("dve", ts(vecT[:, 96:112], psb[1][:, 0:16], 8.0, None, ALU.mult), R=[("ps", 1)], W=["vecT"])
    P.op("dve", ts(vecT[:, 112:124], psb[1][:, 16:28], float(np.sqrt(384.0)), None, ALU.mult), R=[("ps", 1)], W=["vecT"])
    P.op("dve", ts(vecT[:, 124:132], psb[1][:, 28:36], 16.0, None, ALU.mult), R=[("ps", 1)], W=["vecT"])
    P.dma("sp", dmaf(bf_rep[:], b_forget.rearrange("l h -> (l h)").partition_broadcast(128)), W=["bf_rep"], sem="c1")
    P.dma("sp", dmaf(gbn[:], qk_norm_b_nope.rearrange("l q d -> (l q d)").partition_broadcast(128)), W=["gbn"], sem="c2")
    P.dma("sp", dmaf(gbr[:], qk_norm_b_rope.rearrange("l q d -> (l q d)").partition_broadcast(128)), W=["gbr"], sem="c3")
    P.dma("sp", dmaf(cs[:], cs_d.rearrange("(t p) c -> p t c", p=128)), W=["cs"], sem="c4")
    for i in range(NT):
        P.dma("sp", dmaf(xs[:, i, :], x_d[i * 128:(i + 1) * 128, :]), W=[("x", i)], sem="x%d" % i)
    P.barrier()

    def norm_phase(gain_row):
        gnorm = fv(Z0, 1024)
        htok = [bv(Z0 + 2048 + k * 1024, 1024) for k in range(2)]
        junk = bv(Z0 + 4096, 1024)
        P.dma("sp", dmaf(gnorm, gain_row.partition_broadcast(128)), W=["gnorm"], sem="gnorm")
        for i in range(NT):
            P.op("act", act(junk, xs[:, i, :], AF.Square, accum_out=ssq[:, i:i + 1]), R=[("x", i)], W=["junk", "ssq"])
        rsqrt_small(rstd[:], ssq[:], 1.0 / D, EPS, ["ssq"], ["rstd"])
        for i in range(NT):
            P.op("dve", stt(htok[i % 2], xs[:, i, :], rstd[:, i:i + 1], gnorm, ALU.mult, ALU.mult),
                 R=[("x", i), "rstd", "gnorm"], W=[("htok", i % 2)])
            bank = 6 + (i % 2)
            ptv = psbf(bank).rearrange("p (a b) -> p a b", a=8)
            for kc in range(8):
                P.op("pe", tr(ptv[:, kc, :], htok[i % 2][:, kc * 128:(kc + 1) * 128], ident_bf[:]),
                     R=[("htok", i % 2), "ident_bf"], W=[("ps", bank)])
            P.op("act", cp_act(hT[:, :, i * 128:(i + 1) * 128], ptv), R=[("ps", bank)], W=[("hT", i // 4)])
        P.barrier()

    def cp_act(out, in_):
        return lambda e: e.activation(out=out, in_=in_, func=AF.Copy)

    unit_ctr = [0]

    def proj_norm(l, col0, nchunk, onesmat, nfeat, gcol0, outs, okey, wslots, sqb, rsb, gstep=0, split=None):
        for c in range(nchunk):
            wload(wslots[c][0], w_in[l, :, col0 + c * 128: col0 + (c + 1) * 128], wslots[c][1])
        for tc in range(4):
            u = unit_ctr[0] % 2
            unit_ctr[0] += 1
            b0 = 4 * u
            tsl = slice(tc * 512, (tc + 1) * 512)
            for c in range(nchunk):
                for kc in range(8):
                    P.op("pe", mm(psb[b0 + c][:], wslots[c][0][:, kc, :], hT[:, kc, tsl], kc == 0, kc == 7),
                         R=[wslots[c][1], ("hT", tc)], W=[("ps", b0 + c)])
                P.op("act", act(sqb[u][c], psb[b0 + c][:], AF.Square), R=[("ps", b0 + c)], W=[("sq", u, c)])
            for c in range(nchunk):
                P.op("pe", mm(psb[b0 + 3][:], onesmat[:], sqb[u][c], c == 0, c == nchunk - 1),
                     R=[("sq", u, c), "ones_bf", "blockones"], W=[("ps", b0 + 3)])
            ecol = {64: 1, 384: 2, 256: 3}[nfeat]
            P.op("act", act(rsb[u], psb[b0 + 3][:], AF.Ln, bias=cst[:, ecol:ecol + 1]), R=[("ps", b0 + 3), "cst"], W=[("rs", u)])
            P.op("act", act(rsb[u], rsb[u], AF.Exp, scale=-0.5), R=[("rs", u)], W=[("rs", u)])
            for c in range(nchunk):
                if split is not None:
                    for (p0, oap) in ((0, split[0]), (64, split[1])):
                        P.op("dve", stt(oap[p0:p0 + 64, tsl], psb[b0 + c][p0:p0 + 64, :], vecT[p0:p0 + 64, gcol0:gcol0 + 1], rsb[u][p0:p0 + 64, :], ALU.mult, ALU.mult),
                             R=[("ps", b0 + c), ("rs", u), "vecT"], W=[okey])
                    continue
                P.op("dve", stt(outs[c][:, tsl], psb[b0 + c][:], vecT[:, gcol0 + gstep * c:gcol0 + gstep * c + 1], rsb[u], ALU.mult, ALU.mult),
                     R=[("ps", b0 + c), ("rs", u), "vecT"], W=[okey])

    attn_ctr = [0]

    def attention(kind, hf, nbr, kslice, qslice, qkey, vaug, Pt, ytile, scale, biasA=None, bias_prep=None, EBrev=None, qprep=None, qterm=None, kkey=None, vkey=None):
        def finish(i):
            ob = 3 + (i % 2)
            pov = psb[ob][:, 0:260].rearrange("p (h c) -> p h c", h=4)
            rec = small[:, (i % 2) * 4:(i % 2) * 4 + 4]
            P.op("dve", (lambda rec, pov: lambda e: e.reciprocal(out=rec.unsqueeze(2), in_=pov[:, :, 64:65]))(rec, pov),
                 R=[("ps", ob)], W=[("rec", i % 2)])
            yt = ytile[i % 2]
            P.op("dve", tt(yt.rearrange("p (h c) -> p h c", h=4), pov[:, :, 0:64], rec.unsqueeze(2).broadcast_to([128, 4, 64]), ALU.mult),
                 R=[("ps", ob), ("rec", i % 2)], W=[("ytile", i % 2)])
            ptv = psbf(5).rearrange("p (a b) -> p a b", a=8)
            for c in range(2):
                P.op("pe", tr(ptv[:, c, :], yt[:, c * 128:(c + 1) * 128], ident_bf[:]), R=[("ytile", i % 2), "ident_bf"], W=[("ps", 5)])
            P.op("act", cp_act(yT[:, nbr, 2 * hf:2 * hf + 2, i * 128:(i + 1) * 128], ptv[:, 0:2, :]), R=[("ps", 5)], W=[("yT", nbr)])

        if qprep is not None:
            qprep[0](0)
            qprep[1](0)
        if bias_prep is not None:
            bias_prep(0)
        fin_pending = None
        for i in range(NT):
            js = list(range(max(0, i - 4), i + 1)) if kind == "C" else list(range(0, i + 1))
            groups = [js[a:a + 4] for a in range(0, len(js), 4)]
            ob = 3 + (i % 2)
            pov = psb[ob][:, 0:260].rearrange("p (h c) -> p h c", h=4)
            items = [(hh, grp) for hh in range(4) for grp in groups]
            pend = []

            def second(hh, grp, sbk, i=i, js=js, ob=ob, pov=pov):
                n = len(grp)
                if kind == "A":
                    h = 4 * hf + hh
                    for jj, j in enumerate(grp):
                        P.op("act", act(Pt[sbk][:, jj * 128:(jj + 1) * 128], psb[sbk][:, jj * 128:(jj + 1) * 128], AF.Exp,
                                        bias=biasA[i % 2][:, j, h:h + 1], scale=scale),
                             R=[("ps", sbk), ("biasA", i % 2)], W=[("Pt", sbk)])
                else:
                    P.op("act", act(Pt[sbk][:, 0:n * 128], psb[sbk][:, 0:n * 128], AF.Exp, scale=scale),
                         R=[("ps", sbk)], W=[("Pt", sbk)])
                if kind == "C":
                    d0 = 4 - (i - grp[0])
                    P.op(MASK_ENG, tt(Pt[sbk][:, 0:n * 128], Pt[sbk][:, 0:n * 128], EBrev[:, hh, d0 * 128:(d0 + n) * 128], ALU.mult),
                         R=["EBrev"], W=[("Pt", sbk)])
                elif i in grp and kind == "B":
                    jj = grp.index(i)
                    P.op(MASK_ENG, tt(Pt[sbk][:, jj * 128:(jj + 1) * 128], Pt[sbk][:, jj * 128:(jj + 1) * 128], maskB[:], ALU.mult),
                         R=["maskB"], W=[("Pt", sbk)])
                for jj, j in enumerate(grp):
                    P.op("pe", mm(pov[:, hh, :], Pt[sbk][:, jj * 128:(jj + 1) * 128], vaug[:, j, hh, :], j == js[0], j == js[-1]),
                         R=[("Pt", sbk), "v"] + ([vkey(j)] if vkey else []), W=[("ps", ob)])

            cnt = 0
            for (hh, grp) in items:
                sbk = attn_ctr[0] % 3
                attn_ctr[0] += 1
                for jj, j in enumerate(grp):
                    osl = psb[sbk][:, jj * 128:(jj + 1) * 128]
                    P.op("pe", mm(osl, kslice(hh, j), qslice(hh, i), True, qterm is None),
                         R=[(kkey(j) if kkey else "kT"), qkey(i)], W=[("ps", sbk)])
                    if qterm is not None:
                        rq_ap, rq_key = qterm(i, hh)
                        P.op("pe", mm(osl, ones_bf[:], rq_ap, False, j != i), R=[rq_key, "ones_bf"], W=[("ps", sbk)])
                        if j == i:
                            P.op("pe", mm(osl, ident_bf[:], maskA[:], False, True), R=["maskA", "ident_bf"], W=[("ps", sbk)])
                pend.append((hh, grp, sbk))
                cnt += 1
                if cnt == 2:
                    if fin_pending is not None:
                        finish(fin_pending)
                        fin_pending = None
                    if i + 1 < NT:
                        if qprep is not None:
                            qprep[0](i + 1)
                        if bias_prep is not None:
                            bias_prep(i + 1)
                if len(pend) > 2:
                    second(*pend.pop(0))
            while pend:
                second(*pend.pop(0))
            if qprep is not None and i + 1 < NT:
                qprep[1](i + 1)
            fin_pending = i
        finish(fin_pending)

    def attention2(kind, hf, nbr, kslice, qchunk, vaug, Pt, ytile4, scale, biasA=None, prep=None, rqm=None, EB=None):
        OB = [3, 4, 6, 7]

        def finish(c):
            ptv = psbf(5).rearrange("p (a b) -> p a b", a=8)
            for t in range(4):
                ob = OB[t]
                pov = psb[ob][:, 0:260].rearrange("p (h c) -> p h c", h=4)
                rec = small[:, t * 4:t * 4 + 4]
                P.op("dve", (lambda rec, pov: lambda e: e.reciprocal(out=rec.unsqueeze(2), in_=pov[:, :, 64:65]))(rec, pov),
                     R=[("ps", ob)], W=[("rec", t)])
                yt = ytile4[t]
                P.op("dve", tt(yt.rearrange("p (h c) -> p h c", h=4), pov[:, :, 0:64], rec.unsqueeze(2).broadcast_to([128, 4, 64]), ALU.mult),
                     R=[("ps", ob), ("rec", t)], W=[("ytile", t)])
                for c2 in range(2):
                    P.op("pe", tr(ptv[:, c2 * 4 + t, :], yt[:, c2 * 128:(c2 + 1) * 128], ident_bf[:]), R=[("ytile", t), "ident_bf"], W=[("ps", 5)])
            P.op("act", cp_act(yT[:, nbr, 2 * hf:2 * hf + 2, c * 512:(c + 1) * 512], psbf(5).rearrange("p (a b) -> p a b", a=2)),
                 R=[("ps", 5)], W=[("yT", nbr)])

        if prep is not None:
            prep(0)
        for c in range(4):
            j_lo = max(0, 4 * c - 4) if kind == "C" else 0
            items = [(hh, j) for hh in range(4) for j in range(j_lo, 4 * c + 4)]
            pend = []

            def second(hh, j, sb, pb, t0, t1, c=c):
                cols = slice(t0 * 128, (t1 + 1) * 128)
                if kind == "A":
                    h = 4 * hf + hh
                    P.op("act", act(Pt[pb][:, cols], psb[sb][:, cols], AF.Exp, bias=biasA[c % 2][:, j, h:h + 1], scale=scale),
                         R=[("ps", sb), ("biasA", c % 2)], W=[("Pt", pb)])
                else:
                    P.op("act", act(Pt[pb][:, cols], psb[sb][:, cols], AF.Exp, scale=scale), R=[("ps", sb)], W=[("Pt", pb)])
                if kind == "C":
                    d0 = 4 * c + t0 - j
                    P.op(MASK_ENG, tt(Pt[pb][:, cols], Pt[pb][:, cols], EB[:, hh, d0 * 128:(d0 + t1 - t0 + 1) * 128], ALU.mult),
                         R=["EBrev"], W=[("Pt", pb)])
                for t in range(t0, t1 + 1):
                    i = 4 * c + t
                    first_j = max(0, i - 4) if kind == "C" else 0
                    pov = psb[OB[t]][:, 0:260].rearrange("p (h c) -> p h c", h=4)
                    P.op("pe", mm(pov[:, hh, :], Pt[pb][:, t * 128:(t + 1) * 128], vaug[:, j, hh, :], j == first_j, j == i),
                         R=[("Pt", pb), "v"], W=[("ps", OB[t])])

            for (hh, j) in items:
                t0 = max(0, j - 4 * c)
                t1 = 3 if kind != "C" else min(3, j + 4 - 4 * c)
                cols = slice(t0 * 128, (t1 + 1) * 128)
                nsb = 3 if kind == "C" else 2
                sb = attn_ctr[0] % nsb
                pb = attn_ctr[0] % 3
                attn_ctr[0] += 1
                diag = (kind == "A") and j >= 4 * c
                P.op("pe", mm(psb[sb][:, cols], kslice(hh, j), qchunk(c, hh)[:, cols], True, kind != "A"),
                     R=["kT", "qT"], W=[("ps", sb)])
                if kind == "A":
                    P.op("pe", mm(psb[sb][:, cols], ones_bf[:], rqm[:, hh, cols], False, not diag), R=["rqm", "ones_bf"], W=[("ps", sb)])
                    if diag:
                        P.op("pe", lambda e, o=psb[sb][:, t0 * 128:(t0 + 1) * 128]: e.matmul(o, ident_bf[:], maskA[:], start=False, stop=True, skip_group_check=True),
                             R=["maskA", "ident_bf"], W=[("ps", sb)])
                pend.append((hh, j, sb, pb, t0, t1))
                if len(pend) > nsb - 1:
                    second(*pend.pop(0))
            while pend:
                second(*pend.pop(0))
            if prep is not None and c + 1 < 4:
                prep(c + 1)
            finish(c)

    def v_proj(l, col0, hf, wv_unused, vaug):
        wv2 = bv(ZW_ref[0], 2048).rearrange("p (k c) -> p k c", k=8)
        keys = [("wqk", 0), ("wqk", 1)]
        P.dma("pool", dmaf(wv2, w_in[l, :, col0 + hf * 256: col0 + (hf + 1) * 256].rearrange("(k p) c -> p k c", p=128)), W=keys, sem="wv2")
        for i in range(NT):
            bank = 6 + (i % 2)
            for kc in range(8):
                P.op("pe", mm(psb[bank][:, 0:256], hT[:, kc, i * 128:(i + 1) * 128], wv2[:, kc, :], kc == 0, kc == 7),
                     R=keys + [("hT", i // 4)], W=[("ps", bank)])
            P.op("act", cp_act(vaug[:, i, :, 0:64], psb[bank][:, 0:256].rearrange("p (h c) -> p h c", h=4)), R=[("ps", bank)], W=["v"])

    ZW_ref = [None]
    wqk_ref = [None]

    z = Z0
    ZK = z
    kT_ac = bv(z, 4096).rearrange("p (m t) -> p m t", m=2)
    kTm = bv(z, 8192).rearrange("p (m t) -> p m t", m=4)
    kT_b = bv(z, 8192).rearrange("p (m t) -> p m t", m=4)
    z += 8192
    ZQ = z
    qT_ac = bv(z, 4096).rearrange("p (m t) -> p m t", m=2)
    qTt = [bv(z + k * 512, 512).rearrange("p (h t) -> p h t", h=4) for k in range(2)]
    z += 4096
    vaug = bv(z, 4160).rearrange("p (i h c) -> p i h c", i=NT, h=4)
    z += 4160
    sqb = [[bv(z + (u * 3 + c) * 512, 512) for c in range(3)] for u in range(2)]
    Pt = [bv(z + k * 512, 512) for k in range(3)]
    ytile = [bv(z + 1536 + k * 256, 256) for k in range(2)]
    ytile4 = [bv(z + 1536 + k * 256, 256) for k in range(4)]
    z += 3072
    rsb = [fv(z + u * 1024, 512) for u in range(2)]
    z += 2048
    ZW = z
    ZW_ref[0] = z
    wqk = [(bv(z + k * 1024, 1024).rearrange("p (k c) -> p k c", k=8), ("wqk", k)) for k in range(4)]
    z += 4096
    wv = None
    wqk_ref[0] = wqk
    assert z <= AR_N, z

    def set_vones():
        P.op("pool", ms(vaug[:, :, :, 64:65], 1.0), W=["v"])

    def phase_A(l):
        z = 16384
        wf = bv(z, 64).rearrange("p (k c) -> p k c", k=8); z += 64
        zt = fv(z, 128); z += 256
        lp = fv(z, 128); z += 256
        Lp = fv(z, 128).rearrange("p (i h) -> p i h", i=NT); z += 256
        PTt = fv(z, 128).rearrange("p (i h) -> p i h", i=NT); z += 256
        biasA = [fv(z + k * 256, 128).rearrange("p (i h) -> p i h", i=NT) for k in range(2)]; z += 512
        rq_bf = bv(z, 128); z += 128
        rqm = bv(z, 2048).rearrange("p (h t) -> p h t", h=4); z += 2048
        assert z <= 24576, z
        P.op("pool", ms(bv(ZK, 8192), 0.0), W=["kT"])
        loc = zt
        set_vones()
        P.op("pool", ms(rqm, 0.0), W=["rqm"])
        wload(wf, w_in[l, :, C_FA:C_FA + 8], "wf")
        for i in range(NT):
            for kc in range(8):
                P.op("pe", mm(psb[5][:, i * 8:(i + 1) * 8], hT[:, kc, i * 128:(i + 1) * 128], wf[:, kc, :], kc == 0, kc == 7),
                     R=["wf", ("hT", i // 4)], W=[("ps", 5)])
        ztv = zt.rearrange("p (i h) -> p i h", i=NT)
        P.op("dve", tt(ztv, psb[5][:, 0:128].rearrange("p (i h) -> p i h", i=NT),
                       bf_rep[:, l * 8:(l + 1) * 8].unsqueeze(1).broadcast_to([128, NT, 8]), ALU.add), R=[("ps", 5), "bf_rep"], W=["zt"])
        P.op("act", act(zt, zt, AF.Exp, scale=-1.0), R=["zt"], W=["zt"])
        P.op("act", act(lp, zt, AF.Ln, bias=cst[:, 0:1]), R=["zt", "cst"], W=["lp"])
        lpv = lp.rearrange("p (i h) -> p i h", i=NT)
        for i in range(NT):
            P.op("pe", mm(psb[7][:, i * 8:(i + 1) * 8], U_f[:], lpv[:, i, :], True, True), R=["lp", "U_f"], W=[("ps", 7)])
            P.op("pe", mm(psb[6][:, i * 8:(i + 1) * 8], ones_f[:], lpv[:, i, :], True, True), R=["lp", "ones_f"], W=[("ps", 6)])
        P.op("dve", ms(PTt[:, 0, :], 0.0), W=["PT"])
        for i in range(1, NT):
            P.op("dve", tt(PTt[:, i, :], PTt[:, i - 1, :], psb[6][:, (i - 1) * 8:i * 8], ALU.add), R=[("ps", 6)], W=["PT"])
        P.op("dve", tt(Lp, psb[7][:, 0:128].rearrange("p (i h) -> p i h", h=8), PTt, ALU.add), R=[("ps", 7), "PT"], W=["Lp"])
        locv = loc.rearrange("p (i h) -> p i h", i=NT)

        for hf in range(2):
            for c_ in range(2):
                proj_norm(l, C_KA + hf * 256 + c_ * 128, 1, blockones, 64, 96 + 2 * l + 1, [None], "kT", [wqk[c_]], sqb, rsb,
                          split=(kTm[:, 2 * c_, :], kTm[:, 2 * c_ + 1, :]))
            for c_ in range(2):
                proj_norm(l, C_QA + hf * 256 + c_ * 128, 1, blockones, 64, 96 + 2 * l, [qT_ac[:, c_, :]], "qT", [wqk[2 + c_]], sqb, rsb)
            v_proj(l, C_VA, hf, wv, vaug)
            P.barrier()
            def prepA(c, hf=hf):
                P.op("dve", tt(biasA[c % 2][:, 0:4 * c + 4, :], Lp[:, 0:4 * c + 4, :], PTt[:, 4 * c:4 * c + 1, :].broadcast_to([128, 4 * c + 4, 8]), ALU.subtract),
                     R=["Lp", "PT"], W=[("biasA", c % 2)])
                P.op("dve", tt(locv[:, 0:4, :], Lp[:, 4 * c:4 * c + 4, :], PTt[:, 4 * c:4 * c + 1, :].broadcast_to([128, 4, 8]), ALU.subtract),
                     R=["Lp", "PT", "zt"], W=["loc"])
                P.op("dve", ts(rq_bf[:, 0:32], loc[:, 0:32], -8.0, None, ALU.mult), R=["loc"], W=["rq_bf"])
                ptv = psbf(2).rearrange("p (a b) -> p a b", a=8)
                for t in range(4):
                    P.op("pe", tr(ptv[0:8, t, :], rq_bf[:, t * 8:(t + 1) * 8], ident_bf[:]), R=["rq_bf", "ident_bf"], W=[("ps", 2)])
                P.op("dve", tt(rqm[0:8, :, :], psbf(2)[0:8, 0:512].unsqueeze(1).broadcast_to([8, 4, 512]),
                               ident_f[0:8, 4 * hf:4 * hf + 4].unsqueeze(2).broadcast_to([8, 4, 512]), ALU.mult),
                     R=[("ps", 2), "ident_f"], W=["rqm"])

            attention2("A", hf, 0,
                       lambda hh, j: kTm[:, hh, j * 128:(j + 1) * 128],
                       lambda c, hh: qT_ac[:, hh // 2, c * 512:(c + 1) * 512],
                       vaug, Pt, ytile4, 0.125, biasA=biasA, prep=prepA, rqm=rqm)
            P.barrier()
            if DBG == 2 and l == 0 and hf == 1:
                P.dma("sp", dmaf(dbg_d, arena[:, Z0:Z0 + 24576]), sem="dbg")
                P.barrier()

    def phase_C(l):
        EBrev = bv(ZW, 2560).rearrange("p (h c) -> p h c", h=4)
        Tst = [fv(ZW + 2560, 640)]
        set_vones()
        P.op("pool", ms(bv(ZK, 8192), 0.0), W=["kT"])
        for hf in range(2):
            for c_ in range(2):
                proj_norm(l, C_KC + hf * 256 + c_ * 128, 1, blockones, 64, 104 + 2 * l + 1, [None], "kT", [wqk[c_]], sqb, rsb,
                          split=(kTm[:, 2 * c_, :], kTm[:, 2 * c_ + 1, :]))
            for c_ in range(2):
                proj_norm(l, C_QC + hf * 256 + c_ * 128, 1, blockones, 64, 104 + 2 * l, [qT_ac[:, c_, :]], "qT", [wqk[2 + c_]], sqb, rsb)
            v_proj(l, C_VC, hf, wv, vaug)
            P.barrier()
            for hh in range(4):
                h = 4 * hf + hh
                src = relx[l * 8 + h, :, :]
                P.dma("sp", dmaf(Tst[0], src), W=[("Tst", 0)], sem="Tst0")
                P.op("act", act(Tst[0], Tst[0], AF.Exp), R=[("Tst", 0)], W=[("Tst", 0)])
                for d in range(5):
                    P.op("dve", tt(EBrev[:, hh, d * 128:(d + 1) * 128], Tst[0][:, d * 128:(d + 1) * 128], validC[:, d * 128:(d + 1) * 128], ALU.mult),
                         R=[("Tst", 0), "validC"], W=["EBrev"])
            attention2("C", hf, 2,
                       lambda hh, j: kTm[:, hh, j * 128:(j + 1) * 128],
                       lambda c, hh: qT_ac[:, hh // 2, c * 512:(c + 1) * 512],
                       vaug, Pt, ytile4, 0.125, EB=EBrev)
            P.barrier()

    def rope_ops(dst1, dst2, x1, x2, cosv, sinv, t1, t2, Rk, Wk):
        P.op("dve", tt(t1, x1, cosv, ALU.mult), R=Rk, W=["rt1"])
        P.op("dve", tt(t2, x2, sinv, ALU.mult), R=Rk, W=["rt2"])
        P.op("dve", tt(dst1, t1, t2, ALU.subtract), R=["rt1", "rt2"], W=Wk)
        P.op("dve", tt(t1, x2, cosv, ALU.mult), R=Rk, W=["rt1"])
        P.op("dve", tt(t2, x1, sinv, ALU.mult), R=Rk, W=["rt2"])
        P.op("dve", tt(dst2, t1, t2, ALU.add), R=["rt1", "rt2"], W=Wk)

    def phase_B(l):
        z = ZQ + 1024
        wkr = bv(z, 256).rearrange("p (k c) -> p k c", k=8); z += 256
        wqup = bv(z, 1152).rearrange("p (k c) -> p k c", k=3); z += 1152
        wkvup = bv(z, 1024).rearrange("p (k c) -> p k c", k=2); z += 1024
        krope = bv(z, 512).rearrange("p (i c) -> p i c", i=NT); z += 512
        assert z <= ZQ + 4096
        z = 6144
        tmpf = fv(z, 512); z += 1024
        tmpq = fv(z, 384).rearrange("p (h c) -> p h c", h=4); z += 768
        assert z <= 8192
        z = 16384 + 4096
        ktok = [bv(z + k * 512, 512).rearrange("p (h c) -> p h c", h=4) for k in range(2)]; z += 1024
        qtok = [bv(z + k * 512, 512).rearrange("p (h c) -> p h c", h=4) for k in range(2)]; z += 1024
        for k_ in range(2):
            P.op("pool", ms(ktok[k_], 0.0), W=[("ktok", k_)])
            P.op("pool", ms(qtok[k_], 0.0), W=[("qtok", k_)])
        rt1 = fv(z, 256); z += 512
        rt2 = fv(z, 256); z += 512
        kvsb = fv(z, 512); z += 1024
        assert z <= 24576, z
        qdnT = bv(0, 6144).rearrange("p (c t) -> p c t", c=3)
        kvdnT = bv(16384, 4096).rearrange("p (c t) -> p c t", c=2)
        set_vones()
        proj_norm(l, C_QD, 3, ones_bf, 384, 112 + 3 * l, [qdnT[:, c, :] for c in range(3)], "qdnT", wqk[0:3], sqb, rsb, gstep=1)
        proj_norm(l, C_KVD, 2, ones_bf, 256, 124 + 2 * l, [kvdnT[:, c, :] for c in range(2)], "kvdnT", [wqk[3], wqk[0]], sqb, rsb, gstep=1)
        if BSTOP == 1:
            P.op("pool", ms(yT[:, 1, :, :], 0.0), W=[("yT", 1)]); P.barrier(); return
        wload(wkr, w_in[l, :, C_KR:C_KR + 32], "wkr")
        for i in range(NT):
            for kc in range(8):
                P.op("pe", mm(psb[5][:, i * 32:(i + 1) * 32], hT[:, kc, i * 128:(i + 1) * 128], wkr[:, kc, :], kc == 0, kc == 7),
                     R=["wkr", ("hT", i // 4)], W=[("ps", 5)])
        pk = psb[5][:].rearrange("p (i c) -> p i c", i=NT)
        tfv = tmpf.rearrange("p (i c) -> p i c", i=NT)
        sm = small[:, 16:32]
        P.op("act", act(tmpf, psb[5][:], AF.Square), R=[("ps", 5)], W=["tmpf"])
        P.op("dve", red(sm, tfv), R=["tmpf"], W=["sm"])
        rsqrt_small(sm, sm, 1.0 / 32, EPS, ["sm"], ["sm"])
        P.op("dve", tt(tfv, pk, sm.unsqueeze(2).broadcast_to([128, NT, 32]), ALU.mult), R=[("ps", 5), "sm"], W=["tmpf"])
        gk = gbr[:, l * 64 + 32:l * 64 + 64]
        P.op("dve", tt(tfv, tfv, gk.unsqueeze(1).broadcast_to([128, NT, 32]), ALU.mult), R=["tmpf", "gbr"], W=["tmpf"])
        r1 = rt1.rearrange("p (i c) -> p i c", i=NT)
        r2 = rt2.rearrange("p (i c) -> p i c", i=NT)
        rope_ops(krope[:, :, 0:16], krope[:, :, 16:32], tfv[:, :, 0:16], tfv[:, :, 16:32], cs[:, :, 0:16], cs[:, :, 16:32],
                 r1, r2, ["tmpf", "cs"], ["krope"])
        P.barrier()
        if BSTOP == 2:
            P.op("pool", ms(yT[:, 1, :, :], 0.0), W=[("yT", 1)]); P.barrier(); return
        kvK = fv(ZW, 512)
        tK = fv(ZW + 1024, 256).rearrange("p (h c) -> p h c", h=4)
        kvQ = fv(ZW + 1536, 384)
        tQ = fv(ZW + 2304, 384).rearrange("p (h c) -> p h c", h=4)
        gg = fv(ZW + 3072, 64)
        epsc = small[:, 48:49]
        P.op("dve", tt(gg, gbn[:, l * 128:l * 128 + 64], gbn[:, l * 128 + 64:l * 128 + 128], ALU.mult), R=["gbn"], W=["gg"])
        for hf in range(2):
            wload(wkvup, w_kv_up[l, :, hf * 512:(hf + 1) * 512], "wkvup")
            wload(wqup, w_q_up[l, :, hf * 384:(hf + 1) * 384], "wqup")

            def kchain(i):
                a_, b_ = [], []
                for kc in range(2):
                    a_.append(("pe", mm(psb[6][:], kvdnT[:, kc, i * 128:(i + 1) * 128], wkvup[:, kc, :], kc == 0, kc == 1), ["kvdnT", "wkvup"], [("ps", 6)]))
                pkv = kvK.rearrange("p (h c) -> p h c", h=4)
                s4 = small[:, 32:36]
                kt = ktok[i % 2]
                a_.append(("act", cp_act(kvK, psb[6][:]), [("ps", 6)], ["kvK"]))
                a_.append(("act", act(tK, pkv[:, :, 0:64], AF.Square), ["kvK"], ["tK"]))
                a_.append(("dve", red(s4, tK), ["tK"], ["s4"]))
                a_.append(("act", act(s4, s4, AF.Ln, bias=epsc, scale=1.0 / 64), ["s4", "epscol"], ["s4"]))
                a_.append(("act", act(s4, s4, AF.Exp, scale=-0.5), ["s4"], ["s4"]))
                a_.append(("dve", tt(tK, pkv[:, :, 0:64], s4.unsqueeze(2).broadcast_to([128, 4, 64]), ALU.mult), ["kvK", "s4"], ["tK"]))
                a_.append(("dve", tt(kt[:, :, 0:64], tK, gg.unsqueeze(1).broadcast_to([128, 4, 64]), ALU.mult), ["tK", "gg"], [("ktok", i % 2)]))
                a_.append(("dve", cp(kt[:, :, 64:96], krope[:, i, :].unsqueeze(1).broadcast_to([128, 4, 32])), ["krope"], [("ktok", i % 2)]))
                a_.append(("act", cp_act(vaug[:, i, :, 0:64], pkv[:, :, 64:128]), ["kvK"], [("v", i)]))
                ptv = psbf(6).rearrange("p (a b) -> p a b", a=8)
                for hh in range(4):
                    b_.append(("pe", tr(ptv[:, hh, :], kt[:, hh, :], ident_bf[:]), [("ktok", i % 2), "ident_bf"], [("ps", 6)]))
                b_.append(("act", cp_act(kT_b[0:96, :, i * 128:(i + 1) * 128], ptv[0:96, 0:4, :]), [("ps", 6)], [("kT", i)]))
                return a_, b_

            def qchain(i):
                a_, b_ = [], []
                for kc in range(3):
                    a_.append(("pe", mm(psb[7][:, 0:384], qdnT[:, kc, i * 128:(i + 1) * 128], wqup[:, kc, :], kc == 0, kc == 2), ["qdnT", "wqup"], [("ps", 7)]))
                pq = kvQ.rearrange("p (h c) -> p h c", h=4)
                sn = small[:, 36:40]
                sr = small[:, 40:44]
                s8 = small[:, 36:44]
                qt = qtok[i % 2]
                a_.append(("act", cp_act(kvQ, psb[7][:, 0:384]), [("ps", 7)], ["kvQ"]))
                a_.append(("act", act(tQ, pq, AF.Square), ["kvQ"], ["tQ"]))
                a_.append(("dve", red(sn, tQ[:, :, 0:64]), ["tQ"], ["s8"]))
                a_.append(("dve", red(sr, tQ[:, :, 64:96]), ["tQ"], ["s8"]))
                a_.append(("act", act(sn, sn, AF.Ln, bias=epsc, scale=1.0 / 64), ["s8", "epscol"], ["s8"]))
                a_.append(("act", act(sr, sr, AF.Ln, bias=epsc, scale=1.0 / 32), ["s8", "epscol"], ["s8"]))
                a_.append(("act", act(s8, s8, AF.Exp, scale=-0.5), ["s8"], ["s8"]))
                a_.append(("dve", tt(qt[:, :, 0:64], pq[:, :, 0:64], sn.unsqueeze(2).broadcast_to([128, 4, 64]), ALU.mult), ["kvQ", "s8"], [("qtok", i % 2)]))
                a_.append(("dve", tt(tmpq[:, :, 64:96], pq[:, :, 64:96], sr.unsqueeze(2).broadcast_to([128, 4, 32]), ALU.mult), ["kvQ", "s8"], ["tmpq"]))
                gqr = gbr[:, l * 64:l * 64 + 32]
                a_.append(("dve", tt(tmpq[:, :, 64:96], tmpq[:, :, 64:96], gqr.unsqueeze(1).broadcast_to([128, 4, 32]), ALU.mult), ["tmpq", "gbr"], ["tmpq"]))
                c4 = cs[:, i, 0:16].unsqueeze(1).broadcast_to([128, 4, 16])
                s4b = cs[:, i, 16:32].unsqueeze(1).broadcast_to([128, 4, 16])
                q1 = rt1[:, 0:64].rearrange("p (h c) -> p h c", h=4)
                q2 = rt2[:, 0:64].rearrange("p (h c) -> p h c", h=4)
                x1, x2 = tmpq[:, :, 64:80], tmpq[:, :, 80:96]
                Rk, Wk = ["tmpq", "cs"], [("qtok", i % 2)]
                a_.append(("dve", tt(q1, x1, c4, ALU.mult), Rk, ["rt1"]))
                a_.append(("dve", tt(q2, x2, s4b, ALU.mult), Rk, ["rt2"]))
                a_.append(("dve", tt(qt[:, :, 64:80], q1, q2, ALU.subtract), ["rt1", "rt2"], Wk))
                a_.append(("dve", tt(q1, x2, c4, ALU.mult), Rk, ["rt1"]))
                a_.append(("dve", tt(q2, x1, s4b, ALU.mult), Rk, ["rt2"]))
                a_.append(("dve", tt(qt[:, :, 80:96], q1, q2, ALU.add), ["rt1", "rt2"], Wk))
                ptv = psbf(7).rearrange("p (a b) -> p a b", a=8)
                for hh in range(4):
                    b_.append(("pe", tr(ptv[:, hh, :], qt[:, hh, :], ident_bf[:]), [("qtok", i % 2), "ident_bf"], [("ps", 7)]))
                b_.append(("act", cp_act(qTt[i % 2][0:96, :, :], ptv[0:96, 0:4, :]), [("ps", 7)], [("qTt", i % 2)]))
                return a_, b_

            chains = {}

            def emit_zip(la, lb):
                n = max(len(la), len(lb))
                for k_ in range(n):
                    for lst in (la, lb):
                        if k_ < len(lst):
                            e_, f_, r_, w_ = lst[k_]
                            P.op(e_, f_, R=r_, W=w_)

            def prep_a(i):
                ka, kb = kchain(i)
                qa, qb = qchain(i)
                chains[i] = (kb, qb)
                emit_zip(ka, qa)

            def prep_b(i):
                kb, qb = chains.pop(i)
                emit_zip(kb, qb)

            attention("B", hf, 1,
                      lambda hh, j: kT_b[0:96, hh, j * 128:(j + 1) * 128],
                      lambda hh, i: qTt[i % 2][0:96, hh, :],
                      lambda i: ("qTt", i % 2), vaug, Pt, ytile, float(96.0 ** -0.5), qprep=(prep_a, prep_b),
                      kkey=lambda j: ("kT", j), vkey=lambda j: ("v", j))
            P.barrier()

    def merge_phase(l):
        z = Z0
        mT = bv(z, 8192).rearrange("p (c t) -> p c t", c=8); z += 8192
        wg = [bv(z + k * 3072, 3072).rearrange("p (k n c) -> p k n c", k=8, n=3) for k in range(2)]; z += 6144
        wb = [bv(z + k * 1536, 1536).rearrange("p (k n c) -> p k n c", k=4, n=3) for k in range(2)]; z += 3072
        wo = [bv(z + k * 2048, 2048).rearrange("p (k c) -> p k c", k=8) for k in range(2)]; z += 4096
        gate = [bv(z + k * 512, 512) for k in range(3)]; z += 1536
        acc = fv(z, 512); z += 1024
        tmp = fv(z, 512); z += 1024
        assert z <= AR_N, z
        cnt = 0
        for th in range(2):
            for m in range(8):
                sl = cnt % 2
                cnt += 1
                for n in range(3):
                    c0 = C_G + n * 1024 + m * 128
                    P.dma("pool", dmaf(wg[sl][:, :, n, :], w_in[l, :, c0:c0 + 128].rearrange("(k p) c -> p k c", p=128)), W=[("wg", sl, n)], sem="wg%d_%d" % (sl, n))
                    P.dma("pool", dmaf(wb[sl][:, :, n, :], w_branch[l, n, :, m * 128:(m + 1) * 128].rearrange("(k p) c -> p k c", p=128)), W=[("wb", sl, n)], sem="wb%d_%d" % (sl, n))
                for tq in range(2):
                    tc = th * 2 + tq
                    tsl = slice(tc * 512, (tc + 1) * 512)
                    u = (m * 2 + tq) % 2
                    for n in range(3):
                        gb = u * 4 + n if n < 2 else u * 4 + 2
                        gb = u * 4 + n
                        for kc in range(8):
                            P.op("pe", mm(psb[gb][:], wg[sl][:, kc, n, :], hT[:, kc, tsl], kc == 0, kc == 7), R=[("wg", sl, n), ("hT", tc)], W=[("ps", gb)])
                        P.op("act", act(gate[n], psb[gb][:], AF.Sigmoid, bias=vecT[:, l * 24 + n * 8 + m:l * 24 + n * 8 + m + 1]),
                             R=[("ps", gb), "vecT"], W=[("gate", n)])
                        pb = u * 4 + 3
                        for kc in range(4):
                            P.op("pe", mm(psb[pb][:], wb[sl][:, kc, n, :], yT[:, n, kc, tsl], kc == 0, kc == 3), R=[("wb", sl, n), ("yT", n)], W=[("ps", pb)])
                        if n == 0:
                            P.op("dve", tt(acc, psb[pb][:], gate[n], ALU.mult), R=[("ps", pb), ("gate", n)], W=["acc"])
                        else:
                            P.op("dve", tt(tmp, psb[pb][:], gate[n], ALU.mult), R=[("ps", pb), ("gate", n)], W=["tmp"])
                            if n == 1:
                                P.op("dve", tt(acc, acc, tmp, ALU.add), R=["tmp"], W=["acc"])
                            else:
                                P.op("dve", tt(mT[:, m, tq * 512:(tq + 1) * 512], acc, tmp, ALU.add), R=["tmp", "acc"], W=["mT"])
            for cq in range(4):
                sl = cq % 2
                wload(wo[sl], w_out[l, :, cq * 256:(cq + 1) * 256], ("wo", sl))
                for ii in range(8):
                    i = th * 8 + ii
                    bank = ii % 2 + 6 if False else (ii % 4)
                    for kc in range(8):
                        P.op("pe", mm(psb[bank][:, 0:256], mT[:, kc, ii * 128:(ii + 1) * 128], wo[sl][:, kc, :], kc == 0, kc == 7),
                             R=["mT", ("wo", sl)], W=[("ps", bank)])
                    P.op("dve", tt(xs[:, i, cq * 256:(cq + 1) * 256], xs[:, i, cq * 256:(cq + 1) * 256], psb[bank][:, 0:256], ALU.add),
                         R=[("ps", bank)], W=[("x", i)])
        P.barrier()

    def ffn_phase(l):
        aT = [bv(0, 16384).rearrange("p (c t) -> p c t", c=8), bv(Z0, 16384).rearrange("p (c t) -> p c t", c=8)]
        w2 = [bv(16384 + k * 4096, 4096).rearrange("p (k c) -> p k c", k=8) for k in range(2)]
        z = Z0 + 16384
        w1 = [bv(z + k * 2048, 2048).rearrange("p (k c) -> p k c", k=8) for k in range(2)]; z += 4096
        rt = [fv(z + k * 1024, 512) for k in range(2)]; z += 2048
        assert z <= AR_N, z
        c1 = 0
        c2 = 0
        bk = 0
        for g in range(4):
            a = aT[g % 2]
            for fp in range(4):
                sl = c1 % 2
                c1 += 1
                wload(w1[sl], w_ff1[l, :, g * 1024 + fp * 256: g * 1024 + (fp + 1) * 256], ("w1", sl))
                for f2 in range(2):
                    f = fp * 2 + f2
                    for tc in range(4):
                        bank = bk % 4
                        bk += 1
                        for kc in range(8):
                            P.op("pe", mm(psb[bank][:], w1[sl][:, kc, f2 * 128:(f2 + 1) * 128], hT[:, kc, tc * 512:(tc + 1) * 512], kc == 0, kc == 7),
                                 R=[("w1", sl), ("hT", tc)], W=[("ps", bank)])
                        r = rt[bank % 2]
                        P.op("act", act(r, psb[bank][:], AF.Relu), R=[("ps", bank)], W=[("rt", bank % 2)])
                        P.op(SQ_ENG, tt(a[:, f, tc * 512:(tc + 1) * 512], r, r, ALU.mult), R=[("rt", bank % 2)], W=[("aT", g % 2)])
            for ch in range(2):
                sl = c2 % 2
                c2 += 1
                wload(w2[sl], w_ff2[l, g * 1024:(g + 1) * 1024, ch * 512:(ch + 1) * 512], ("w2", sl))
                for i in range(NT):
                    bank = 4 + (i % 4)
                    for f in range(8):
                        P.op("pe", mm(psb[bank][:], a[:, f, i * 128:(i + 1) * 128], w2[sl][:, f, :], f == 0, f == 7),
                             R=[("aT", g % 2), ("w2", sl)], W=[("ps", bank)])
                    P.op("dve", tt(xs[:, i, ch * 512:(ch + 1) * 512], xs[:, i, ch * 512:(ch + 1) * 512], psb[bank][:], ALU.add),
                         R=[("ps", bank)], W=[("x", i)])
        P.barrier()

    MASK_ENG = "dve"
    SQ_ENG = "pool"
    for l in range(nlayers):
        P.tag = "norm1"
        if "N" in PH:
            norm_phase(norm_mix[l, :])
        for nb_, ch_ in enumerate("ABC"):
            if ch_ not in PH:
                P.op("pool", ms(yT[:, nb_, :, :], 0.0), W=[("yT", nb_)])
        P.tag = "B"
        if "B" in PH:
            phase_B(l)
        P.tag = "A"
        if "A" in PH:
            phase_A(l)
        P.tag = "C"
        if "C" in PH:
            phase_C(l)
        P.tag = "merge"
        if DBG == 1 and l == 0:
            P.barrier()
            P.dma("sp", dmaf(dbg_d, arena[:, 0:24576]), R=[("yT", 0), ("yT", 1), ("yT", 2)], sem="dbg")
            P.barrier()
        if "M" in PH:
            merge_phase(l)
        if "F" in PH:
            P.tag = "norm2"
            norm_phase(norm_ffn[l, :])
            P.tag = "ffn"
            ffn_phase(l)
    for i in range(NT):
        P.dma("sp", dmaf(y_d[i * 128:(i + 1) * 128, :], xs[:, i, :]), R=[("x", i)], sem="y%d" % i)
    P.barrier()
    P.emit(nc, st)
    st.close()
    return nc


def _host_consts(rel_bias):
    half = 16
    inv = (10000.0 ** (-np.arange(half, dtype=np.float32) / half)).astype(np.float32)
    ang = np.arange(S, dtype=np.float32)[:, None] * inv[None, :]
    cs_tab = np.concatenate([np.cos(ang), np.sin(ang)], axis=1).astype(np.float32)
    kl = np.arange(128)[:, None]
    c = np.arange(640)[None, :]
    idx = np.clip(c - kl, -256, 256) + 256
    rel_ext = np.ascontiguousarray(rel_bias[:, :, idx]).reshape(DEPTH * 8, 128, 640).astype(np.float32)
    return cs_tab, rel_ext


PH = "NBACMF"
ANNOT = False
DBG = False
BSTOP = 0
_NC_CACHE = {}


def kernel(**inputs):
    inp = {k: np.ascontiguousarray(np.asarray(v, dtype=np.float32)) for k, v in inputs.items()}
    x = inp.pop("x")
    rel_bias = inp.pop("rel_bias")
    cs_tab, rel_ext = _host_consts(rel_bias)
    key = (DEPTH, PH)
    if key not in _NC_CACHE:
        _NC_CACHE[key] = build(DEPTH)
    nc = _NC_CACHE[key]
    B = x.shape[0]
    in_maps = []
    for b in range(B):
        m = dict(inp)
        m["x"] = np.ascontiguousarray(x[b])
        m["rel_ext"] = rel_ext
        m["cs_tab"] = cs_tab
        in_maps.append(m)
    res = run_bass_kernel_spmd(nc, in_maps, core_ids=list(range(B)))
    return np.stack([np.asarray(r["y"], dtype=np.float32) for r in res.results], axis=0)
```

```python
import numpy as np
from contextlib import ExitStack
import concourse.bass as bass
import concourse.mybir as mybir
from concourse.bass_utils import run_bass_kernel_spmd

F32 = mybir.dt.float32
BF16 = mybir.dt.bfloat16
AF = mybir.ActivationFunctionType
ALU = mybir.AluOpType
AX = mybir.AxisListType

S = 2048
D = 1024
NT = 16
DEPTH = 4
EPS = 1e-6
INW = 6824
C_QA, C_KA, C_VA, C_FA, C_QD, C_KVD, C_KR, C_QC, C_KC, C_VC, C_G = 0, 512, 1024, 1536, 1544, 1928, 2184, 2216, 2728, 3240, 3752


class Prog:
    ENG = ("pe", "act", "dve", "pool", "sp")

    def __init__(self):
        self.ops = {e: [] for e in self.ENG}
        self.state = {}
        self.known = {e: {} for e in self.ENG}
        self.flag = {e: set() for e in self.ENG}
        self.semcnt = {}
        self.tag = ""
        self.tags = {e: [] for e in self.ENG}

    def _add(self, eng, fn, R, W, dma_sem):
        self.tags[eng].append(self.tag)
        need = {}
        for k in R:
            st = self.state.get(k)
            if st and st[0] is not None:
                t = st[0]
                need[t[0]] = max(need.get(t[0], -1), t[1])
        for k in W:
            st = self.state.get(k)
            if st:
                if st[0] is not None:
                    t = st[0]
                    need[t[0]] = max(need.get(t[0], -1), t[1])
                for tk, tv in st[1].items():
                    need[tk] = max(need.get(tk, -1), tv)
        kn = self.known[eng]
        waits = []
        for tk, tv in need.items():
            if dma_sem is None and eng == "pe" and tk == ("E", "pe"):
                continue
            if kn.get(tk, -1) >= tv:
                continue
            kn[tk] = tv
            waits.append((tk, tv))
            if tk[0] == "E":
                self.flag[tk[1]].add(tv)
        idx = len(self.ops[eng])
        if dma_sem is None:
            tok = (("E", eng), idx)
        else:
            c = self.semcnt.get(dma_sem, 0) + 1
            self.semcnt[dma_sem] = c
            tok = (("S", dma_sem), c)
        self.ops[eng].append((fn, waits, dma_sem))
        for k in R:
            if k in W:
                continue
            st = self.state.setdefault(k, [None, {}])
            st[1][tok[0]] = max(st[1].get(tok[0], -1), tok[1])
        for k in W:
            self.state[k] = [tok, {}]
        return tok

    def op(self, eng, fn, R=(), W=()):
        return self._add(eng, fn, list(R), list(W), None)

    def dma(self, q, fn, R=(), W=(), sem=None):
        return self._add(q, fn, list(R), list(W), sem)

    def barrier(self):
        last = {}
        for e in self.ENG:
            n = len(self.ops[e])
            for i in range(n - 1, -1, -1):
                if self.ops[e][i][2] is None and self.ops[e][i][0] is not None:
                    last[("E", e)] = i
                    break
        for s, c in self.semcnt.items():
            last[("S", s)] = c
        for e in self.ENG:
            kn = self.known[e]
            waits = []
            for tk, tv in last.items():
                if kn.get(tk, -1) >= tv:
                    continue
                kn[tk] = tv
                waits.append((tk, tv))
                if tk[0] == "E":
                    self.flag[tk[1]].add(tv)
            if waits:
                self.ops[e].append((None, waits, None))
                self.tags[e].append(self.tag)

    def emit(self, nc, stack):
        rank = {}
        for e in self.ENG:
            rank[e] = {idx: r + 1 for r, idx in enumerate(sorted(self.flag[e]))}
        BS = 2000
        esem = {e: [stack.enter_context(nc.semaphore("es_%s_%d" % (e, b))) for b in range(len(rank[e]) // BS + 1)] for e in self.ENG}
        dsem = {s: stack.enter_context(nc.semaphore("ds_" + str(i))) for i, s in enumerate(self.semcnt)}
        block = stack.enter_context(nc.Block())

        def run(e, h):
            for idx, (fn, waits, ds) in enumerate(self.ops[e]):
                for tk, tv in waits:
                    if tk[0] == "E":
                        r_ = rank[tk[1]][tv] - 1
                        h.wait_ge(esem[tk[1]][r_ // BS], r_ % BS + 1)
                    else:
                        h.wait_ge(dsem[tk[1]], tv * 16)
                if fn is None:
                    continue
                ins = fn(h)
                if ANNOT:
                    ins.annotate(self.tags[e][idx])
                if ds is not None:
                    ins.then_inc(dsem[ds], 16)
                elif idx in rank[e]:
                    ins.then_inc(esem[e][(rank[e][idx] - 1) // BS], 1)

        block.tensor(lambda h: run("pe", h))
        block.scalar(lambda h: run("act", h))
        block.vector(lambda h: run("dve", h))
        block.gpsimd(lambda h: run("pool", h))
        block.sync(lambda h: run("sp", h))


def build(nlayers=DEPTH):
    nc = bass.Bass("TRN2", target_bir_lowering=False)
    P = Prog()
    st = ExitStack()

    def din(name, shape):
        return nc.dram_tensor(name, list(shape), F32, kind="ExternalInput")

    x_d = din("x", [S, D]).ap()
    norm_mix = din("norm_mix", [DEPTH, D]).ap()
    w_in = din("w_in", [DEPTH, D, INW]).ap()
    b_forget = din("b_forget", [DEPTH, 8]).ap()
    b_gate = din("b_gate", [DEPTH, 3072]).ap()
    qk_norm_a = din("qk_norm_a", [DEPTH, 2, 64]).ap()
    mla_q_norm = din("mla_q_norm", [DEPTH, 384]).ap()
    mla_kv_norm = din("mla_kv_norm", [DEPTH, 256]).ap()
    w_q_up = din("w_q_up", [DEPTH, 384, 768]).ap()
    w_kv_up = din("w_kv_up", [DEPTH, 256, 1024]).ap()
    qk_norm_b_nope = din("qk_norm_b_nope", [DEPTH, 2, 64]).ap()
    qk_norm_b_rope = din("qk_norm_b_rope", [DEPTH, 2, 32]).ap()
    qk_norm_c = din("qk_norm_c", [DEPTH, 2, 64]).ap()
    relx = din("rel_ext", [DEPTH * 8, 128, 640]).ap()
    w_branch = din("w_branch", [DEPTH, 3, 512, D]).ap()
    w_out = din("w_out", [DEPTH, D, D]).ap()
    norm_ffn = din("norm_ffn", [DEPTH, D]).ap()
    w_ff1 = din("w_ff1", [DEPTH, D, 4096]).ap()
    w_ff2 = din("w_ff2", [DEPTH, 4096, D]).ap()
    cs_d = din("cs_tab", [S, 32]).ap()
    y_d = nc.dram_tensor("y", [S, D], F32, kind="ExternalOutput").ap()
    dbg_d = nc.dram_tensor("dbg", [128, 24576], BF16, kind="ExternalOutput").ap() if DBG else None

    def sb(name, shape, dt):
        return st.enter_context(nc.sbuf_tensor(name, list(shape), dt))

    xs = sb("xs", [128, NT, D], F32)
    hT = sb("hT", [128, 8, S], BF16)
    ident_bf = sb("ident_bf", [128, 128], BF16)
    ident_f = sb("ident_f", [128, 128], F32)
    blockones = sb("blockones", [128, 128], BF16)
    ones_bf = sb("ones_bf", [128, 128], BF16)
    maskA = sb("maskA", [128, 128], BF16)
    maskB = sb("maskB", [128, 128], BF16)
    U_f = sb("U_f", [128, 128], F32)
    ones_f = sb("ones_f", [128, 128], F32)
    validC = sb("validC", [128, 640], BF16)
    vecT = sb("vecT", [128, 132], F32)
    bf_rep = sb("bf_rep", [128, 32], F32)
    gbn = sb("gbn", [128, 512], F32)
    gbr = sb("gbr", [128, 256], F32)
    cs = sb("cs", [128, NT, 32], F32)
    ssq = sb("ssq", [128, NT], F32)
    rstd = sb("rstd", [128, NT], F32)
    small = sb("small", [128, 64], F32)
    cst = sb("cst", [128, 4], F32)
    AR_N = 52000
    arena = sb("arena", [128, AR_N], BF16)
    psb = [st.enter_context(nc.psum_tensor("ps%d" % b, [128, 512], F32)) for b in range(8)]

    def bv(off, n):
        return arena[:, off:off + n]

    def fv(off, n):
        return arena[:, off:off + 2 * n].bitcast(F32)

    def psbf(b):
        return psb[b][:].bitcast(BF16)

    Z0 = 24576
    yT = bv(0, 24576).rearrange("p (n c t) -> p n c t", n=3, c=4)

    def mm(out, lhsT, rhs, start, stop):
        return lambda e: e.matmul(out, lhsT, rhs, start=start, stop=stop)

    def tr(out, in_, ident):
        return lambda e: e.transpose(out, in_, ident)

    def act(out, in_, func, bias=None, scale=None, accum_out=None):
        kw = {}
        if bias is not None:
            kw["bias"] = bias
        if scale is not None:
            kw["scale"] = scale
        if accum_out is not None:
            kw["accum_out"] = accum_out
        return lambda e: e.activation(out=out, in_=in_, func=func, **kw)

    def tt(out, in0, in1, op):
        return lambda e: e.tensor_tensor(out=out, in0=in0, in1=in1, op=op)

    def ts(out, in0, s1, s2, op0, op1=None):
        if op1 is None:
            return lambda e: e.tensor_single_scalar(out=out, in_=in0, scalar=s1, op=op0)
        return lambda e: e.tensor_scalar(out=out, in0=in0, scalar1=s1, scalar2=s2, op0=op0, op1=op1)

    def stt(out, in0, scalar, in1, op0, op1):
        return lambda e: e.scalar_tensor_tensor(out=out, in0=in0, scalar=scalar, in1=in1, op0=op0, op1=op1)

    def cp(out, in_):
        return lambda e: e.tensor_copy(out=out, in_=in_)

    def rsqrt_small(dst, src, mul, eps, Rk, Wk):
        P.op("dve", ts(dst, src, mul, eps, ALU.mult, ALU.add), R=Rk, W=Wk)
        P.op("act", act(dst, dst, AF.Ln), R=Wk, W=Wk)
        P.op("act", act(dst, dst, AF.Exp, scale=-0.5), R=Wk, W=Wk)

    def red(out, in_):
        return lambda e: e.tensor_reduce(out=out, in_=in_, axis=AX.X, op=ALU.add)

    def ms(ap, v):
        return lambda e: e.memset(ap, v)

    def dmaf(out, in_):
        return lambda e: e.dma_start(out=out, in_=in_)

    def wload(dst, src2d, key, R=(), q="pool"):
        P.dma(q, dmaf(dst, src2d.rearrange("(k p) c -> p k c", p=128)), R=R, W=[key], sem=str(key))

    P.op("pool", ms(ones_f[:], 1.0), W=["ones_f"])
    P.op("pool", lambda e: e.affine_select(out=ident_f[:], in_=ones_f[:], pattern=[[-1, 128]], compare_op=ALU.is_equal,
                                           fill=0.0, base=0, channel_multiplier=1), R=["ones_f"], W=["ident_f"])
    P.op("pool", lambda e: e.affine_select(out=U_f[:], in_=ones_f[:], pattern=[[1, 128]], compare_op=ALU.is_ge,
                                           fill=0.0, base=0, channel_multiplier=-1), R=["ones_f"], W=["U_f"])
    P.op("pool", cp(ident_bf[:], ident_f[:]), R=["ident_f"], W=["ident_bf"])
    P.op("pool", ts(maskA[:], U_f[:], 60000.0, -60000.0, ALU.mult, ALU.add), R=["U_f"], W=["maskA"])
    P.op("pool", ms(ones_bf[:], 1.0), W=["ones_bf"])
    P.op("pool", ms(cst[:, 0:1], 1.0), W=["cst"])
    P.op("pool", ms(small[:, 48:49], EPS), W=["epscol"])
    P.op("pool", ms(cst[:, 1:2], 64 * EPS), W=["cst"])
    P.op("pool", ms(cst[:, 2:3], 384 * EPS), W=["cst"])
    P.op("pool", ms(cst[:, 3:4], 256 * EPS), W=["cst"])
    P.op("pool", ms(blockones[:], 0.0), W=["blockones"])
    P.op("pool", ms(blockones[0:64, 0:64], 1.0), W=["blockones"])
    P.op("pool", ms(blockones[64:128, 64:128], 1.0), W=["blockones"])
    P.op("pool", ms(maskB[:], 1.0), W=["maskB"])
    P.op("pool", ms(maskB[64:128, 0:64], 0.0), W=["maskB"])
    P.op("pool", ms(validC[:], 0.0), W=["validC"])
    P.op("pool", ms(validC[0:64, 0:576], 1.0), W=["validC"])
    P.op("pool", ms(validC[64:128, 64:640], 1.0), W=["validC"])

    stage1 = fv(Z0, 128)
    stage2 = fv(Z0 + 256, 128)
    P.dma("sp", dmaf(stage1[0:96, :], b_gate.rearrange("l (c p) -> (l c) p", p=128)), W=["stage1"], sem="stage1")
    qa2 = qk_norm_a.rearrange("l q d -> (l q) d")
    qc2 = qk_norm_c.rearrange("l q d -> (l q) d")
    P.dma("sp", dmaf(stage2[0:8, 0:64], qa2), W=["stage2a"], sem="stage2")
    P.dma("sp", dmaf(stage2[0:8, 64:128], qa2), W=["stage2b"], sem="stage2")
    P.dma("sp", dmaf(stage2[8:16, 0:64], qc2), W=["stage2c"], sem="stage2")
    P.dma("sp", dmaf(stage2[8:16, 64:128], qc2), W=["stage2d"], sem="stage2")
    P.dma("sp", dmaf(stage2[16:28, :], mla_q_norm.rearrange("l (c p) -> (l c) p", p=128)), W=["stage2e"], sem="stage2")
    P.dma("sp", dmaf(stage2[28:36, :], mla_kv_norm.rearrange("l (c p) -> (l c) p", p=128)), W=["stage2f"], sem="stage2")
    P.op("pe", tr(psb[0][:, 0:96], stage1[0:96, :], ident_f[0:96, 0:96]), R=["stage1", "ident_f"], W=[("ps", 0)])
    P.op("pe", tr(psb[1][:, 0:36], stage2[0:36, :], ident_f[0:36, 0:36]),
         R=["stage2a", "stage2b", "stage2c", "stage2d", "stage2e", "stage2f", "ident_f"], W=[("ps", 1)])
    P.op("dve", cp(vecT[:, 0:96], psb[0][:, 0:96]), R=[("ps", 0)], W=["vecT"])
    P.op("dve", ts(vecT[:, 96:112], psb[1][:, 0:16], 8.0, None, ALU.mult), R=[("ps", 1)], W=["vecT"])
    P.op("dve", ts(vecT[:, 112:124], psb[1][:, 16:28], float(np.sqrt(384.0)), None, ALU.mult), R=[("ps", 1)], W=["vecT"])
    P.op("dve", ts(vecT[:, 124:132], psb[1][:, 28:36], 16.0, None, ALU.mult), R=[("ps", 1)], W=["vecT"])
    P.dma("sp", dmaf(bf_rep[:], b_forget.rearrange("l h -> (l h)").partition_broadcast(128)), W=["bf_rep"], sem="c1")
    P.dma("sp", dmaf(gbn[:], qk_norm_b_nope.rearrange("l q d -> (l q d)").partition_broadcast(128)), W=["gbn"], sem="c2")
    P.dma("sp", dmaf(gbr[:], qk_norm_b_rope.rearrange("l q d -> (l q d)").partition_broadcast(128)), W=["gbr"], sem="c3")
    P.dma("sp", dmaf(cs[:], cs_d.rearrange("(t p) c -> p t c", p=128)), W=["cs"], sem="c4")
    for i in range(NT):
        P.dma("sp", dmaf(xs[:, i, :], x_d[i * 128:(i + 1) * 128, :]), W=[("x", i)], sem="x%d" % i)
    P.barrier()

    def norm_phase(gain_row):
        gnorm = fv(Z0, 1024)
        htok = [bv(Z0 + 2048 + k * 1024, 1024) for k in range(2)]
        junk = bv(Z0 + 4096, 1024)
        P.dma("sp", dmaf(gnorm, gain_row.partition_broadcast(128)), W=["gnorm"], sem="gnorm")
        for i in range(NT):
            P.op("act", act(junk, xs[:, i, :], AF.Square, accum_out=ssq[:, i:i + 1]), R=[("x", i)], W=["junk", "ssq"])
        rsqrt_small(rstd[:], ssq[:], 1.0 / D, EPS, ["ssq"], ["rstd"])
        for i in range(NT):
            P.op("dve", stt(htok[i % 2], xs[:, i, :], rstd[:, i:i + 1], gnorm, ALU.mult, ALU.mult),
                 R=[("x", i), "rstd", "gnorm"], W=[("htok", i % 2)])
            bank = 6 + (i % 2)
            ptv = psbf(bank).rearrange("p (a b) -> p a b", a=8)
            for kc in range(8):
                P.op("pe", tr(ptv[:, kc, :], htok[i % 2][:, kc * 128:(kc + 1) * 128], ident_bf[:]),
                     R=[("htok", i % 2), "ident_bf"], W=[("ps", bank)])
            P.op("act", cp_act(hT[:, :, i * 128:(i + 1) * 128], ptv), R=[("ps", bank)], W=[("hT", i // 4)])
        P.barrier()

    def cp_act(out, in_):
        return lambda e: e.activation(out=out, in_=in_, func=AF.Copy)

    unit_ctr = [0]

    def proj_norm(l, col0, nchunk, onesmat, nfeat, gcol0, outs, okey, wslots, sqb, rsb, gstep=0, split=None):
        for c in range(nchunk):
            wload(wslots[c][0], w_in[l, :, col0 + c * 128: col0 + (c + 1) * 128], wslots[c][1])
        for tc in range(4):
            if nchunk == 1:
                u = unit3_ctr[0] % 3
                unit3_ctr[0] += 1
                braw = [2 * u]
                bssq = 2 * u + 1
                sqs = [sqb[0][u]]
                rs = [rsb[0], rsb[1], rs_extra[0]][u]
                sqk = [("sq3", u)]
                rsk = ("rs3", u)
            else:
                u = unit_ctr[0] % 2
                unit_ctr[0] += 1
                b0 = 4 * u
                braw = [b0 + c for c in range(nchunk)]
                bssq = b0 + 3
                sqs = [sqb[u][c] for c in range(nchunk)]
                rs = rsb[u]
                sqk = [("sq", u, c) for c in range(nchunk)]
                rsk = ("rs", u)
            tsl = slice(tc * 512, (tc + 1) * 512)
            for c in range(nchunk):
                for kc in range(8):
                    P.op("pe", mm(psb[braw[c]][:], wslots[c][0][:, kc, :], hT[:, kc, tsl], kc == 0, kc == 7),
                         R=[wslots[c][1], ("hT", tc)], W=[("ps", braw[c])])
                P.op("act", act(sqs[c], psb[braw[c]][:], AF.Square), R=[("ps", braw[c])], W=[sqk[c]])
            for c in range(nchunk):
                P.op("pe", mm(psb[bssq][:], onesmat[:], sqs[c], c == 0, c == nchunk - 1),
                     R=[sqk[c], "ones_bf", "blockones"], W=[("ps", bssq)])
            ecol = {64: 1, 384: 2, 256: 3}[nfeat]
            P.op("act", act(rs, psb[bssq][:], AF.Ln, bias=cst[:, ecol:ecol + 1]), R=[("ps", bssq), "cst"], W=[rsk])
            P.op("act", act(rs, rs, AF.Exp, scale=-0.5), R=[rsk], W=[rsk])
            for c in range(nchunk):
                if split is not None:
                    for (p0, oap) in ((0, split[0]), (64, split[1])):
                        P.op("dve", stt(oap[p0:p0 + 64, tsl], psb[braw[c]][p0:p0 + 64, :], vecT[p0:p0 + 64, gcol0:gcol0 + 1], rs[p0:p0 + 64, :], ALU.mult, ALU.mult),
                             R=[("ps", braw[c]), rsk, "vecT"], W=[okey])
                    continue
                P.op("dve", stt(outs[c][:, tsl], psb[braw[c]][:], vecT[:, gcol0 + gstep * c:gcol0 + gstep * c + 1], rs, ALU.mult, ALU.mult),
                     R=[("ps", braw[c]), rsk, "vecT"], W=[okey])

    unit3_ctr = [0]
    rs_extra = [None]

    attn_ctr = [0]

    def attention(kind, hf, nbr, kslice, qslice, qkey, vaug, Pt, ytile, scale, biasA=None, bias_prep=None, EBrev=None, qprep=None, qterm=None, kkey=None, vkey=None):
        def finish(i):
            ob = 3 + (i % 2)
            pov = psb[ob][:, 0:260].rearrange("p (h c) -> p h c", h=4)
            rec = small[:, (i % 2) * 4:(i % 2) * 4 + 4]
            P.op("dve", (lambda rec, pov: lambda e: e.reciprocal(out=rec.unsqueeze(2), in_=pov[:, :, 64:65]))(rec, pov),
                 R=[("ps", ob)], W=[("rec", i % 2)])
            yt = ytile[i % 2]
            P.op("dve", tt(yt.rearrange("p (h c) -> p h c", h=4), pov[:, :, 0:64], rec.unsqueeze(2).broadcast_to([128, 4, 64]), ALU.mult),
                 R=[("ps", ob), ("rec", i % 2)], W=[("ytile", i % 2)])
            ptv = psbf(5).rearrange("p (a b) -> p a b", a=8)
            for c in range(2):
                P.op("pe", tr(ptv[:, c, :], yt[:, c * 128:(c + 1) * 128], ident_bf[:]), R=[("ytile", i % 2), "ident_bf"], W=[("ps", 5)])
            P.op("act", cp_act(yT[:, nbr, 2 * hf:2 * hf + 2, i * 128:(i + 1) * 128], ptv[:, 0:2, :]), R=[("ps", 5)], W=[("yT", nbr)])

        if qprep is not None:
            qprep[0](0)
            qprep[1](0)
        if bias_prep is not None:
            bias_prep(0)
        fin_pending = None
        for i in range(NT):
            js = list(range(max(0, i - 4), i + 1)) if kind == "C" else list(range(0, i + 1))
            groups = [js[a:a + 4] for a in range(0, len(js), 4)]
            ob = 3 + (i % 2)
            pov = psb[ob][:, 0:260].rearrange("p (h c) -> p h c", h=4)
            items = [(hh, grp) for hh in range(4) for grp in groups]
            pend = []

            def second(hh, grp, sbk, i=i, js=js, ob=ob, pov=pov):
                n = len(grp)
                if kind == "A":
                    h = 4 * hf + hh
                    for jj, j in enumerate(grp):
                        P.op("act", act(Pt[sbk][:, jj * 128:(jj + 1) * 128], psb[sbk][:, jj * 128:(jj + 1) * 128], AF.Exp,
                                        bias=biasA[i % 2][:, j, h:h + 1], scale=scale),
                             R=[("ps", sbk), ("biasA", i % 2)], W=[("Pt", sbk)])
                else:
                    P.op("act", act(Pt[sbk][:, 0:n * 128], psb[sbk][:, 0:n * 128], AF.Exp, scale=scale),
                         R=[("ps", sbk)], W=[("Pt", sbk)])
                if kind == "C":
                    d0 = 4 - (i - grp[0])
                    P.op(MASK_ENG, tt(Pt[sbk][:, 0:n * 128], Pt[sbk][:, 0:n * 128], EBrev[:, hh, d0 * 128:(d0 + n) * 128], ALU.mult),
                         R=["EBrev"], W=[("Pt", sbk)])
                elif i in grp and kind == "B":
                    jj = grp.index(i)
                    P.op(MASK_ENG, tt(Pt[sbk][:, jj * 128:(jj + 1) * 128], Pt[sbk][:, jj * 128:(jj + 1) * 128], maskB[:], ALU.mult),
                         R=["maskB"], W=[("Pt", sbk)])
                for jj, j in enumerate(grp):
                    P.op("pe", mm(pov[:, hh, :], Pt[sbk][:, jj * 128:(jj + 1) * 128], vaug[:, j, hh, :], j == js[0], j == js[-1]),
                         R=[("Pt", sbk), "v"] + ([vkey(j)] if vkey else []), W=[("ps", ob)])

            cnt = 0
            for (hh, grp) in items:
                sbk = attn_ctr[0] % 3
                attn_ctr[0] += 1
                for jj, j in enumerate(grp):
                    osl = psb[sbk][:, jj * 128:(jj + 1) * 128]
                    P.op("pe", mm(osl, kslice(hh, j), qslice(hh, i), True, qterm is None),
                         R=[(kkey(j) if kkey else "kT"), qkey(i)], W=[("ps", sbk)])
                    if qterm is not None:
                        rq_ap, rq_key = qterm(i, hh)
                        P.op("pe", mm(osl, ones_bf[:], rq_ap, False, j != i), R=[rq_key, "ones_bf"], W=[("ps", sbk)])
                        if j == i:
                            P.op("pe", mm(osl, ident_bf[:], maskA[:], False, True), R=["maskA", "ident_bf"], W=[("ps", sbk)])
                pend.append((hh, grp, sbk))
                cnt += 1
                if cnt == 2:
                    if fin_pending is not None:
                        finish(fin_pending)
                        fin_pending = None
                    if i + 1 < NT:
                        if qprep is not None:
                            qprep[0](i + 1)
                        if bias_prep is not None:
                            bias_prep(i + 1)
                if len(pend) > 2:
                    second(*pend.pop(0))
            while pend:
                second(*pend.pop(0))
            if qprep is not None and i + 1 < NT:
                qprep[1](i + 1)
            fin_pending = i
        finish(fin_pending)

    def attention2(kind, hf, nbr, kslice, qchunk, vaug, Pt, ytile4, scale, biasA=None, prep=None, rqm=None, EB=None):
        OB = [3, 4, 6, 7]

        def finish(c):
            ptv = psbf(5).rearrange("p (a b) -> p a b", a=8)
            for t in range(4):
                ob = OB[t]
                pov = psb[ob][:, 0:260].rearrange("p (h c) -> p h c", h=4)
                rec = small[:, t * 4:t * 4 + 4]
                P.op("dve", (lambda rec, pov: lambda e: e.reciprocal(out=rec.unsqueeze(2), in_=pov[:, :, 64:65]))(rec, pov),
                     R=[("ps", ob)], W=[("rec", t)])
                yt = ytile4[t]
                P.op("dve", tt(yt.rearrange("p (h c) -> p h c", h=4), pov[:, :, 0:64], rec.unsqueeze(2).broadcast_to([128, 4, 64]), ALU.mult),
                     R=[("ps", ob), ("rec", t)], W=[("ytile", t)])
                for c2 in range(2):
                    P.op("pe", tr(ptv[:, c2 * 4 + t, :], yt[:, c2 * 128:(c2 + 1) * 128], ident_bf[:]), R=[("ytile", t), "ident_bf"], W=[("ps", 5)])
            P.op("act", cp_act(yT[:, nbr, 2 * hf:2 * hf + 2, c * 512:(c + 1) * 512], psbf(5).rearrange("p (a b) -> p a b", a=2)),
                 R=[("ps", 5)], W=[("yT", nbr)])

        if prep is not None:
            prep(0)
        for c in range(4):
            j_lo = max(0, 4 * c - 4) if kind == "C" else 0
            items = [(hh, j) for hh in range(4) for j in range(j_lo, 4 * c + 4)]
            pend = []

            def second(hh, j, sb, pb, t0, t1, c=c):
                cols = slice(t0 * 128, (t1 + 1) * 128)
                if kind == "A":
                    h = 4 * hf + hh
                    P.op("act", act(Pt[pb][:, cols], psb[sb][:, cols], AF.Exp, bias=biasA[c % 2][:, j, h:h + 1], scale=scale),
                         R=[("ps", sb), ("biasA", c % 2)], W=[("Pt", pb)])
                else:
                    P.op("act", act(Pt[pb][:, cols], psb[sb][:, cols], AF.Exp, scale=scale), R=[("ps", sb)], W=[("Pt", pb)])
                if kind == "C":
                    d0 = 4 * c + t0 - j
                    P.op(MASK_ENG, tt(Pt[pb][:, cols], Pt[pb][:, cols], EB[:, hh, d0 * 128:(d0 + t1 - t0 + 1) * 128], ALU.mult),
                         R=["EBrev"], W=[("Pt", pb)])
                for t in range(t0, t1 + 1):
                    i = 4 * c + t
                    first_j = max(0, i - 4) if kind == "C" else 0
                    pov = psb[OB[t]][:, 0:260].rearrange("p (h c) -> p h c", h=4)
                    P.op("pe", mm(pov[:, hh, :], Pt[pb][:, t * 128:(t + 1) * 128], vaug[:, j, hh, :], j == first_j, j == i),
                         R=[("Pt", pb), "v"], W=[("ps", OB[t])])

            for (hh, j) in items:
                t0 = max(0, j - 4 * c)
                t1 = 3 if kind != "C" else min(3, j + 4 - 4 * c)
                cols = slice(t0 * 128, (t1 + 1) * 128)
                nsb = 3 if kind == "C" else 2
                sb = attn_ctr[0] % nsb
                pb = attn_ctr[0] % 3
                attn_ctr[0] += 1
                diag = (kind == "A") and j >= 4 * c
                P.op("pe", mm(psb[sb][:, cols], kslice(hh, j), qchunk(c, hh)[:, cols], True, kind != "A"),
                     R=["kT", "qT"], W=[("ps", sb)])
                if kind == "A":
                    P.op("pe", mm(psb[sb][:, cols], ones_bf[:], rqm[:, hh, cols], False, not diag), R=["rqm", "ones_bf"], W=[("ps", sb)])
                    if diag:
                        P.op("pe", lambda e, o=psb[sb][:, t0 * 128:(t0 + 1) * 128]: e.matmul(o, ident_bf[:], maskA[:], start=False, stop=True, skip_group_check=True),
                             R=["maskA", "ident_bf"], W=[("ps", sb)])
                pend.append((hh, j, sb, pb, t0, t1))
                if len(pend) > nsb - 1:
                    second(*pend.pop(0))
            while pend:
                second(*pend.pop(0))
            if prep is not None and c + 1 < 4:
                prep(c + 1)
            finish(c)

    def v_proj(l, col0, hf, wv_unused, vaug):
        wv2 = bv(ZW_ref[0], 2048).rearrange("p (k c) -> p k c", k=8)
        keys = [("wqk", 0), ("wqk", 1)]
        P.dma("pool", dmaf(wv2, w_in[l, :, col0 + hf * 256: col0 + (hf + 1) * 256].rearrange("(k p) c -> p k c", p=128)), W=keys, sem="wv2")
        for i in range(NT):
            bank = 6 + (i % 2)
            for kc in range(8):
                P.op("pe", mm(psb[bank][:, 0:256], hT[:, kc, i * 128:(i + 1) * 128], wv2[:, kc, :], kc == 0, kc == 7),
                     R=keys + [("hT", i // 4)], W=[("ps", bank)])
            P.op("act", cp_act(vaug[:, i, :, 0:64], psb[bank][:, 0:256].rearrange("p (h c) -> p h c", h=4)), R=[("ps", bank)], W=["v"])

    ZW_ref = [None]
    wqk_ref = [None]

    z = Z0
    ZK = z
    kT_ac = bv(z, 4096).rearrange("p (m t) -> p m t", m=2)
    kTm = bv(z, 8192).rearrange("p (m t) -> p m t", m=4)
    kT_b = bv(z, 8192).rearrange("p (m t) -> p m t", m=4)
    z += 8192
    ZQ = z
    qT_ac = bv(z, 4096).rearrange("p (m t) -> p m t", m=2)
    qTt = [bv(z + k * 512, 512).rearrange("p (h t) -> p h t", h=4) for k in range(2)]
    z += 4096
    vaug = bv(z, 4160).rearrange("p (i h c) -> p i h c", i=NT, h=4)
    z += 4160
    rs_extra[0] = fv(z + 2048, 512)
    sqb = [[bv(z + (u * 3 + c) * 512, 512) for c in range(3)] for u in range(2)]
    Pt = [bv(z + k * 512, 512) for k in range(3)]
    ytile = [bv(z + 1536 + k * 256, 256) for k in range(2)]
    ytile4 = [bv(z + 1536 + k * 256, 256) for k in range(4)]
    z += 3072
    rsb = [fv(z + u * 1024, 512) for u in range(2)]
    z += 2048
    ZW = z
    ZW_ref[0] = z
    wqk = [(bv(z + k * 1024, 1024).rearrange("p (k c) -> p k c", k=8), ("wqk", k)) for k in range(4)]
    z += 4096
    wv = None
    wqk_ref[0] = wqk
    assert z <= AR_N, z

    def set_vones():
        P.op("pool", ms(vaug[:, :, :, 64:65], 1.0), W=["v"])

    def phase_A(l):
        z = 16384
        wf = bv(z, 64).rearrange("p (k c) -> p k c", k=8); z += 64
        zt = fv(z, 128); z += 256
        lp = fv(z, 128); z += 256
        Lp = fv(z, 128).rearrange("p (i h) -> p i h", i=NT); z += 256
        PTt = fv(z, 128).rearrange("p (i h) -> p i h", i=NT); z += 256
        biasA = [fv(z + k * 256, 128).rearrange("p (i h) -> p i h", i=NT) for k in range(2)]; z += 512
        rq_bf = bv(z, 128); z += 128
        rqm = bv(z, 2048).rearrange("p (h t) -> p h t", h=4); z += 2048
        assert z <= 24576, z
        P.op("pool", ms(bv(ZK, 8192), 0.0), W=["kT"])
        loc = zt
        set_vones()
        P.op("pool", ms(rqm, 0.0), W=["rqm"])
        wload(wf, w_in[l, :, C_FA:C_FA + 8], "wf")
        for i in range(NT):
            for kc in range(8):
                P.op("pe", mm(psb[5][:, i * 8:(i + 1) * 8], hT[:, kc, i * 128:(i + 1) * 128], wf[:, kc, :], kc == 0, kc == 7),
                     R=["wf", ("hT", i // 4)], W=[("ps", 5)])
        ztv = zt.rearrange("p (i h) -> p i h", i=NT)
        P.op("dve", tt(ztv, psb[5][:, 0:128].rearrange("p (i h) -> p i h", i=NT),
                       bf_rep[:, l * 8:(l + 1) * 8].unsqueeze(1).broadcast_to([128, NT, 8]), ALU.add), R=[("ps", 5), "bf_rep"], W=["zt"])
        P.op("act", act(zt, zt, AF.Exp, scale=-1.0), R=["zt"], W=["zt"])
        P.op("act", act(lp, zt, AF.Ln, bias=cst[:, 0:1]), R=["zt", "cst"], W=["lp"])
        lpv = lp.rearrange("p (i h) -> p i h", i=NT)
        for i in range(NT):
            P.op("pe", mm(psb[7][:, i * 8:(i + 1) * 8], U_f[:], lpv[:, i, :], True, True), R=["lp", "U_f"], W=[("ps", 7)])
            P.op("pe", mm(psb[6][:, i * 8:(i + 1) * 8], ones_f[:], lpv[:, i, :], True, True), R=["lp", "ones_f"], W=[("ps", 6)])
        P.op("dve", ms(PTt[:, 0, :], 0.0), W=["PT"])
        for i in range(1, NT):
            P.op("dve", tt(PTt[:, i, :], PTt[:, i - 1, :], psb[6][:, (i - 1) * 8:i * 8], ALU.add), R=[("ps", 6)], W=["PT"])
        P.op("dve", tt(Lp, psb[7][:, 0:128].rearrange("p (i h) -> p i h", h=8), PTt, ALU.add), R=[("ps", 7), "PT"], W=["Lp"])
        locv = loc.rearrange("p (i h) -> p i h", i=NT)

        for hf in range(2):
            for c_ in range(2):
                proj_norm(l, C_KA + hf * 256 + c_ * 128, 1, blockones, 64, 96 + 2 * l + 1, [None], "kT", [wqk[c_]], sqb, rsb,
                          split=(kTm[:, 2 * c_, :], kTm[:, 2 * c_ + 1, :]))
            for c_ in range(2):
                proj_norm(l, C_QA + hf * 256 + c_ * 128, 1, blockones, 64, 96 + 2 * l, [qT_ac[:, c_, :]], "qT", [wqk[2 + c_]], sqb, rsb)
            v_proj(l, C_VA, hf, wv, vaug)
            P.barrier()
            def prepA(c, hf=hf):
                P.op("dve", tt(biasA[c % 2][:, 0:4 * c + 4, :], Lp[:, 0:4 * c + 4, :], PTt[:, 4 * c:4 * c + 1, :].broadcast_to([128, 4 * c + 4, 8]), ALU.subtract),
                     R=["Lp", "PT"], W=[("biasA", c % 2)])
                P.op("dve", tt(locv[:, 0:4, :], Lp[:, 4 * c:4 * c + 4, :], PTt[:, 4 * c:4 * c + 1, :].broadcast_to([128, 4, 8]), ALU.subtract),
                     R=["Lp", "PT", "zt"], W=["loc"])
                P.op("dve", ts(rq_bf[:, 0:32], loc[:, 0:32], -8.0, None, ALU.mult), R=["loc"], W=["rq_bf"])
                ptv = psbf(2).rearrange("p (a b) -> p a b", a=8)
                for t in range(4):
                    P.op("pe", tr(ptv[0:8, t, :], rq_bf[:, t * 8:(t + 1) * 8], ident_bf[:]), R=["rq_bf", "ident_bf"], W=[("ps", 2)])
                P.op("dve", tt(rqm[0:8, :, :], psbf(2)[0:8, 0:512].unsqueeze(1).broadcast_to([8, 4, 512]),
                               ident_f[0:8, 4 * hf:4 * hf + 4].unsqueeze(2).broadcast_to([8, 4, 512]), ALU.mult),
                     R=[("ps", 2), "ident_f"], W=["rqm"])

            attention2("A", hf, 0,
                       lambda hh, j: kTm[:, hh, j * 128:(j + 1) * 128],
                       lambda c, hh: qT_ac[:, hh // 2, c * 512:(c + 1) * 512],
                       vaug, Pt, ytile4, 0.125, biasA=biasA, prep=prepA, rqm=rqm)
            P.barrier()
            if DBG == 2 and l == 0 and hf == 1:
                P.dma("sp", dmaf(dbg_d, arena[:, Z0:Z0 + 24576]), sem="dbg")
                P.barrier()

    def phase_C(l):
        EBrev = bv(ZW, 2560).rearrange("p (h c) -> p h c", h=4)
        Tst = [fv(ZW + 2560, 640)]
        set_vones()
        P.op("pool", ms(bv(ZK, 8192), 0.0), W=["kT"])
        for hf in range(2):
            for c_ in range(2):
                proj_norm(l, C_KC + hf * 256 + c_ * 128, 1, blockones, 64, 104 + 2 * l + 1, [None], "kT", [wqk[c_]], sqb, rsb,
                          split=(kTm[:, 2 * c_, :], kTm[:, 2 * c_ + 1, :]))
            for c_ in range(2):
                proj_norm(l, C_QC + hf * 256 + c_ * 128, 1, blockones, 64, 104 + 2 * l, [qT_ac[:, c_, :]], "qT", [wqk[2 + c_]], sqb, rsb)
            v_proj(l, C_VC, hf, wv, vaug)
            P.barrier()
            for hh in range(4):
                h = 4 * hf + hh
                src = relx[l * 8 + h, :, :]
                P.dma("sp", dmaf(Tst[0], src), W=[("Tst", 0)], sem="Tst0")
                P.op("act", act(Tst[0], Tst[0], AF.Exp), R=[("Tst", 0)], W=[("Tst", 0)])
                for d in range(5):
                    P.op("dve", tt(EBrev[:, hh, d * 128:(d + 1) * 128], Tst[0][:, d * 128:(d + 1) * 128], validC[:, d * 128:(d + 1) * 128], ALU.mult),
                         R=[("Tst", 0), "validC"], W=["EBrev"])
            attention2("C", hf, 2,
                       lambda hh, j: kTm[:, hh, j * 128:(j + 1) * 128],
                       lambda c, hh: qT_ac[:, hh // 2, c * 512:(c + 1) * 512],
                       vaug, Pt, ytile4, 0.125, EB=EBrev)
            P.barrier()

    def rope_ops(dst1, dst2, x1, x2, cosv, sinv, t1, t2, Rk, Wk):
        P.op("dve", tt(t1, x1, cosv, ALU.mult), R=Rk, W=["rt1"])
        P.op("dve", tt(t2, x2, sinv, ALU.mult), R=Rk, W=["rt2"])
        P.op("dve", tt(dst1, t1, t2, ALU.subtract), R=["rt1", "rt2"], W=Wk)
        P.op("dve", tt(t1, x2, cosv, ALU.mult), R=Rk, W=["rt1"])
        P.op("dve", tt(t2, x1, sinv, ALU.mult), R=Rk, W=["rt2"])
        P.op("dve", tt(dst2, t1, t2, ALU.add), R=["rt1", "rt2"], W=Wk)

    def phase_B(l):
        z = ZQ + 1024
        wkr = bv(z, 256).rearrange("p (k c) -> p k c", k=8); z += 256
        wqup = bv(z, 1152).rearrange("p (k c) -> p k c", k=3); z += 1152
        wkvup = bv(z, 1024).rearrange("p (k c) -> p k c", k=2); z += 1024
        krope = bv(z, 512).rearrange("p (i c) -> p i c", i=NT); z += 512
        assert z <= ZQ + 4096
        z = 6144
        tmpf = fv(z, 512); z += 1024
        tmpq = fv(z, 384).rearrange("p (h c) -> p h c", h=4); z += 768
        assert z <= 8192
        z = 16384 + 4096
        ktok = [bv(z + k * 512, 512).rearrange("p (h c) -> p h c", h=4) for k in range(2)]; z += 1024
        qtok = [bv(z + k * 512, 512).rearrange("p (h c) -> p h c", h=4) for k in range(2)]; z += 1024
        for k_ in range(2):
            P.op("pool", ms(ktok[k_], 0.0), W=[("ktok", k_)])
            P.op("pool", ms(qtok[k_], 0.0), W=[("qtok", k_)])
        rt1 = fv(z, 256); z += 512
        rt2 = fv(z, 256); z += 512
        kvsb = fv(z, 512); z += 1024
        assert z <= 24576, z
        qdnT = bv(0, 6144).rearrange("p (c t) -> p c t", c=3)
        kvdnT = bv(16384, 4096).rearrange("p (c t) -> p c t", c=2)
        set_vones()
        proj_norm(l, C_QD, 3, ones_bf, 384, 112 + 3 * l, [qdnT[:, c, :] for c in range(3)], "qdnT", wqk[0:3], sqb, rsb, gstep=1)
        proj_norm(l, C_KVD, 2, ones_bf, 256, 124 + 2 * l, [kvdnT[:, c, :] for c in range(2)], "kvdnT", [wqk[3], wqk[0]], sqb, rsb, gstep=1)
        if BSTOP == 1:
            P.op("pool", ms(yT[:, 1, :, :], 0.0), W=[("yT", 1)]); P.barrier(); return
        wload(wkr, w_in[l, :, C_KR:C_KR + 32], "wkr")
        for i in range(NT):
            for kc in range(8):
                P.op("pe", mm(psb[5][:, i * 32:(i + 1) * 32], hT[:, kc, i * 128:(i + 1) * 128], wkr[:, kc, :], kc == 0, kc == 7),
                     R=["wkr", ("hT", i // 4)], W=[("ps", 5)])
        pk = psb[5][:].rearrange("p (i c) -> p i c", i=NT)
        tfv = tmpf.rearrange("p (i c) -> p i c", i=NT)
        sm = small[:, 16:32]
        P.op("act", act(tmpf, psb[5][:], AF.Square), R=[("ps", 5)], W=["tmpf"])
        P.op("dve", red(sm, tfv), R=["tmpf"], W=["sm"])
        rsqrt_small(sm, sm, 1.0 / 32, EPS, ["sm"], ["sm"])
        P.op("dve", tt(tfv, pk, sm.unsqueeze(2).broadcast_to([128, NT, 32]), ALU.mult), R=[("ps", 5), "sm"], W=["tmpf"])
        gk = gbr[:, l * 64 + 32:l * 64 + 64]
        P.op("dve", tt(tfv, tfv, gk.unsqueeze(1).broadcast_to([128, NT, 32]), ALU.mult), R=["tmpf", "gbr"], W=["tmpf"])
        r1 = rt1.rearrange("p (i c) -> p i c", i=NT)
        r2 = rt2.rearrange("p (i c) -> p i c", i=NT)
        rope_ops(krope[:, :, 0:16], krope[:, :, 16:32], tfv[:, :, 0:16], tfv[:, :, 16:32], cs[:, :, 0:16], cs[:, :, 16:32],
                 r1, r2, ["tmpf", "cs"], ["krope"])
        P.barrier()
        if BSTOP == 2:
            P.op("pool", ms(yT[:, 1, :, :], 0.0), W=[("yT", 1)]); P.barrier(); return
        kvK = fv(ZW, 512)
        tK = fv(ZW + 1024, 256).rearrange("p (h c) -> p h c", h=4)
        kvQ = fv(ZW + 1536, 384)
        tQ = fv(ZW + 2304, 384).rearrange("p (h c) -> p h c", h=4)
        gg = fv(ZW + 3072, 64)
        epsc = small[:, 48:49]
        P.op("dve", tt(gg, gbn[:, l * 128:l * 128 + 64], gbn[:, l * 128 + 64:l * 128 + 128], ALU.mult), R=["gbn"], W=["gg"])
        for hf in range(2):
            wload(wkvup, w_kv_up[l, :, hf * 512:(hf + 1) * 512], "wkvup")
            wload(wqup, w_q_up[l, :, hf * 384:(hf + 1) * 384], "wqup")

            def kchain(i):
                a_, b_ = [], []
                for kc in range(2):
                    a_.append(("pe", mm(psb[6][:], kvdnT[:, kc, i * 128:(i + 1) * 128], wkvup[:, kc, :], kc == 0, kc == 1), ["kvdnT", "wkvup"], [("ps", 6)]))
                pkv = kvK.rearrange("p (h c) -> p h c", h=4)
                s4 = small[:, 32:36]
                kt = ktok[i % 2]
                a_.append(("act", cp_act(kvK, psb[6][:]), [("ps", 6)], ["kvK"]))
                a_.append(("act", act(tK, pkv[:, :, 0:64], AF.Square), ["kvK"], ["tK"]))
                a_.append(("dve", red(s4, tK), ["tK"], ["s4"]))
                a_.append(("act", act(s4, s4, AF.Ln, bias=epsc, scale=1.0 / 64), ["s4", "epscol"], ["s4"]))
                a_.append(("act", act(s4, s4, AF.Exp, scale=-0.5), ["s4"], ["s4"]))
                a_.append(("dve", tt(tK, pkv[:, :, 0:64], s4.unsqueeze(2).broadcast_to([128, 4, 64]), ALU.mult), ["kvK", "s4"], ["tK"]))
                a_.append(("dve", tt(kt[:, :, 0:64], tK, gg.unsqueeze(1).broadcast_to([128, 4, 64]), ALU.mult), ["tK", "gg"], [("ktok", i % 2)]))
                a_.append(("dve", cp(kt[:, :, 64:96], krope[:, i, :].unsqueeze(1).broadcast_to([128, 4, 32])), ["krope"], [("ktok", i % 2)]))
                a_.append(("act", cp_act(vaug[:, i, :, 0:64], pkv[:, :, 64:128]), ["kvK"], [("v", i)]))
                ptv = psbf(6).rearrange("p (a b) -> p a b", a=8)
                for hh in range(4):
                    b_.append(("pe", tr(ptv[:, hh, :], kt[:, hh, :], ident_bf[:]), [("ktok", i % 2), "ident_bf"], [("ps", 6)]))
                b_.append(("act", cp_act(kT_b[0:96, :, i * 128:(i + 1) * 128], ptv[0:96, 0:4, :]), [("ps", 6)], [("kT", i)]))
                return a_, b_

            def qchain(i):
                a_, b_ = [], []
                for kc in range(3):
                    a_.append(("pe", mm(psb[7][:, 0:384], qdnT[:, kc, i * 128:(i + 1) * 128], wqup[:, kc, :], kc == 0, kc == 2), ["qdnT", "wqup"], [("ps", 7)]))
                pq = kvQ.rearrange("p (h c) -> p h c", h=4)
                sn = small[:, 36:40]
                sr = small[:, 40:44]
                s8 = small[:, 36:44]
                qt = qtok[i % 2]
                a_.append(("act", cp_act(kvQ, psb[7][:, 0:384]), [("ps", 7)], ["kvQ"]))
                a_.append(("act", act(tQ, pq, AF.Square), ["kvQ"], ["tQ"]))
                a_.append(("dve", red(sn, tQ[:, :, 0:64]), ["tQ"], ["s8"]))
                a_.append(("dve", red(sr, tQ[:, :, 64:96]), ["tQ"], ["s8"]))
                a_.append(("act", act(sn, sn, AF.Ln, bias=epsc, scale=1.0 / 64), ["s8", "epscol"], ["s8"]))
                a_.append(("act", act(sr, sr, AF.Ln, bias=epsc, scale=1.0 / 32), ["s8", "epscol"], ["s8"]))
                a_.append(("act", act(s8, s8, AF.Exp, scale=-0.5), ["s8"], ["s8"]))
                a_.append(("dve", tt(qt[:, :, 0:64], pq[:, :, 0:64], sn.unsqueeze(2).broadcast_to([128, 4, 64]), ALU.mult), ["kvQ", "s8"], [("qtok", i % 2)]))
                a_.append(("dve", tt(tmpq[:, :, 64:96], pq[:, :, 64:96], sr.unsqueeze(2).broadcast_to([128, 4, 32]), ALU.mult), ["kvQ", "s8"], ["tmpq"]))
                gqr = gbr[:, l * 64:l * 64 + 32]
                a_.append(("dve", tt(tmpq[:, :, 64:96], tmpq[:, :, 64:96], gqr.unsqueeze(1).broadcast_to([128, 4, 32]), ALU.mult), ["tmpq", "gbr"], ["tmpq"]))
                c4 = cs[:, i, 0:16].unsqueeze(1).broadcast_to([128, 4, 16])
                s4b = cs[:, i, 16:32].unsqueeze(1).broadcast_to([128, 4, 16])
                q1 = rt1[:, 0:64].rearrange("p (h c) -> p h c", h=4)
                q2 = rt2[:, 0:64].rearrange("p (h c) -> p h c", h=4)
                x1, x2 = tmpq[:, :, 64:80], tmpq[:, :, 80:96]
                Rk, Wk = ["tmpq", "cs"], [("qtok", i % 2)]
                a_.append(("dve", tt(q1, x1, c4, ALU.mult), Rk, ["rt1"]))
                a_.append(("dve", tt(q2, x2, s4b, ALU.mult), Rk, ["rt2"]))
                a_.append(("dve", tt(qt[:, :, 64:80], q1, q2, ALU.subtract), ["rt1", "rt2"], Wk))
                a_.append(("dve", tt(q1, x2, c4, ALU.mult), Rk, ["rt1"]))
                a_.append(("dve", tt(q2, x1, s4b, ALU.mult), Rk, ["rt2"]))
                a_.append(("dve", tt(qt[:, :, 80:96], q1, q2, ALU.add), ["rt1", "rt2"], Wk))
                ptv = psbf(7).rearrange("p (a b) -> p a b", a=8)
                for hh in range(4):
                    b_.append(("pe", tr(ptv[:, hh, :], qt[:, hh, :], ident_bf[:]), [("qtok", i % 2), "ident_bf"], [("ps", 7)]))
                b_.append(("act", cp_act(qTt[i % 2][0:96, :, :], ptv[0:96, 0:4, :]), [("ps", 7)], [("qTt", i % 2)]))
                return a_, b_

            chains = {}

            def emit_zip(la, lb):
                n = max(len(la), len(lb))
                for k_ in range(n):
                    for lst in (la, lb):
                        if k_ < len(lst):
                            e_, f_, r_, w_ = lst[k_]
                            P.op(e_, f_, R=r_, W=w_)

            def prep_a(i):
                ka, kb = kchain(i)
                qa, qb = qchain(i)
                chains[i] = (kb, qb)
                emit_zip(ka, qa)

            def prep_b(i):
                kb, qb = chains.pop(i)
                emit_zip(kb, qb)

            attention("B", hf, 1,
                      lambda hh, j: kT_b[0:96, hh, j * 128:(j + 1) * 128],
                      lambda hh, i: qTt[i % 2][0:96, hh, :],
                      lambda i: ("qTt", i % 2), vaug, Pt, ytile, float(96.0 ** -0.5), qprep=(prep_a, prep_b),
                      kkey=lambda j: ("kT", j), vkey=lambda j: ("v", j))
            P.barrier()

    def merge_phase(l):
        z = Z0
        mT = bv(z, 8192).rearrange("p (c t) -> p c t", c=8); z += 8192
        wg = [bv(z + k * 3072, 3072).rearrange("p (k n c) -> p k n c", k=8, n=3) for k in range(2)]; z += 6144
        wb = [bv(z + k * 1536, 1536).rearrange("p (k n c) -> p k n c", k=4, n=3) for k in range(2)]; z += 3072
        wo = [bv(z + k * 2048, 2048).rearrange("p (k c) -> p k c", k=8) for k in range(2)]; z += 4096
        gate = [bv(z + k * 512, 512) for k in range(3)]; z += 1536
        acc = fv(z, 512); z += 1024
        tmp = fv(z, 512); z += 1024
        assert z <= AR_N, z
        cnt = 0
        for th in range(2):
            for m in range(8):
                sl = cnt % 2
                cnt += 1
                for n in range(3):
                    c0 = C_G + n * 1024 + m * 128
                    P.dma("pool", dmaf(wg[sl][:, :, n, :], w_in[l, :, c0:c0 + 128].rearrange("(k p) c -> p k c", p=128)), W=[("wg", sl, n)], sem="wg%d_%d" % (sl, n))
                    P.dma("pool", dmaf(wb[sl][:, :, n, :], w_branch[l, n, :, m * 128:(m + 1) * 128].rearrange("(k p) c -> p k c", p=128)), W=[("wb", sl, n)], sem="wb%d_%d" % (sl, n))
                for tq in range(2):
                    tc = th * 2 + tq
                    tsl = slice(tc * 512, (tc + 1) * 512)
                    u = (m * 2 + tq) % 2
                    for n in range(3):
                        gb = u * 4 + n if n < 2 else u * 4 + 2
                        gb = u * 4 + n
                        for kc in range(8):
                            P.op("pe", mm(psb[gb][:], wg[sl][:, kc, n, :], hT[:, kc, tsl], kc == 0, kc == 7), R=[("wg", sl, n), ("hT", tc)], W=[("ps", gb)])
                        P.op("act", act(gate[n], psb[gb][:], AF.Sigmoid, bias=vecT[:, l * 24 + n * 8 + m:l * 24 + n * 8 + m + 1]),
                             R=[("ps", gb), "vecT"], W=[("gate", n)])
                        pb = u * 4 + 3
                        for kc in range(4):
                            P.op("pe", mm(psb[pb][:], wb[sl][:, kc, n, :], yT[:, n, kc, tsl], kc == 0, kc == 3), R=[("wb", sl, n), ("yT", n)], W=[("ps", pb)])
                        if n == 0:
                            P.op("dve", tt(acc, psb[pb][:], gate[n], ALU.mult), R=[("ps", pb), ("gate", n)], W=["acc"])
                        else:
                            P.op("dve", tt(tmp, psb[pb][:], gate[n], ALU.mult), R=[("ps", pb), ("gate", n)], W=["tmp"])
                            if n == 1:
                                P.op("dve", tt(acc, acc, tmp, ALU.add), R=["tmp"], W=["acc"])
                            else:
                                P.op("dve", tt(mT[:, m, tq * 512:(tq + 1) * 512], acc, tmp, ALU.add), R=["tmp", "acc"], W=["mT"])
            for cq in range(4):
                sl = cq % 2
                wload(wo[sl], w_out[l, :, cq * 256:(cq + 1) * 256], ("wo", sl))
                for ii in range(8):
                    i = th * 8 + ii
                    bank = ii % 2 + 6 if False else (ii % 4)
                    for kc in range(8):
                        P.op("pe", mm(psb[bank][:, 0:256], mT[:, kc, ii * 128:(ii + 1) * 128], wo[sl][:, kc, :], kc == 0, kc == 7),
                             R=["mT", ("wo", sl)], W=[("ps", bank)])
                    P.op("dve", tt(xs[:, i, cq * 256:(cq + 1) * 256], xs[:, i, cq * 256:(cq + 1) * 256], psb[bank][:, 0:256], ALU.add),
                         R=[("ps", bank)], W=[("x", i)])
        P.barrier()

    def ffn_phase(l):
        aT = [bv(0, 16384).rearrange("p (c t) -> p c t", c=8), bv(Z0, 16384).rearrange("p (c t) -> p c t", c=8)]
        w2 = [bv(16384 + k * 4096, 4096).rearrange("p (k c) -> p k c", k=8) for k in range(2)]
        z = Z0 + 16384
        w1 = [bv(z + k * 2048, 2048).rearrange("p (k c) -> p k c", k=8) for k in range(2)]; z += 4096
        rt = [fv(z + k * 1024, 512) for k in range(2)]; z += 2048
        assert z <= AR_N, z
        c1 = 0
        c2 = 0
        bk = 0
        for g in range(4):
            a = aT[g % 2]
            for fp in range(4):
                sl = c1 % 2
                c1 += 1
                wload(w1[sl], w_ff1[l, :, g * 1024 + fp * 256: g * 1024 + (fp + 1) * 256], ("w1", sl))
                for f2 in range(2):
                    f = fp * 2 + f2
                    for tc in range(4):
                        bank = bk % 4
                        bk += 1
                        for kc in range(8):
                            P.op("pe", mm(psb[bank][:], w1[sl][:, kc, f2 * 128:(f2 + 1) * 128], hT[:, kc, tc * 512:(tc + 1) * 512], kc == 0, kc == 7),
                                 R=[("w1", sl), ("hT", tc)], W=[("ps", bank)])
                        r = rt[bank % 2]
                        P.op("act", act(r, psb[bank][:], AF.Relu), R=[("ps", bank)], W=[("rt", bank % 2)])
                        P.op(SQ_ENG, tt(a[:, f, tc * 512:(tc + 1) * 512], r, r, ALU.mult), R=[("rt", bank % 2)], W=[("aT", g % 2)])
            for ch in range(2):
                sl = c2 % 2
                c2 += 1
                wload(w2[sl], w_ff2[l, g * 1024:(g + 1) * 1024, ch * 512:(ch + 1) * 512], ("w2", sl))
                for i in range(NT):
                    bank = 4 + (i % 4)
                    for f in range(8):
                        P.op("pe", mm(psb[bank][:], a[:, f, i * 128:(i + 1) * 128], w2[sl][:, f, :], f == 0, f == 7),
                             R=[("aT", g % 2), ("w2", sl)], W=[("ps", bank)])
                    P.op("dve", tt(xs[:, i, ch * 512:(ch + 1) * 512], xs[:, i, ch * 512:(ch + 1) * 512], psb[bank][:], ALU.add),
                         R=[("ps", bank)], W=[("x", i)])
        P.barrier()

    MASK_ENG = "dve"
    SQ_ENG = "pool"
    for l in range(nlayers):
        P.tag = "norm1"
        if "N" in PH:
            norm_phase(norm_mix[l, :])
        for nb_, ch_ in enumerate("ABC"):
            if ch_ not in PH:
                P.op("pool", ms(yT[:, nb_, :, :], 0.0), W=[("yT", nb_)])
        P.tag = "B"
        if "B" in PH:
            phase_B(l)
        P.tag = "A"
        if "A" in PH:
            phase_A(l)
        P.tag = "C"
        if "C" in PH:
            phase_C(l)
        P.tag = "merge"
        if DBG == 1 and l == 0:
            P.barrier()
            P.dma("sp", dmaf(dbg_d, arena[:, 0:24576]), R=[("yT", 0), ("yT", 1), ("yT", 2)], sem="dbg")
            P.barrier()
        if "M" in PH:
            merge_phase(l)
        if "F" in PH:
            P.tag = "norm2"
            norm_phase(norm_ffn[l, :])
            P.tag = "ffn"
            ffn_phase(l)
    for i in range(NT):
        P.dma("sp", dmaf(y_d[i * 128:(i + 1) * 128, :], xs[:, i, :]), R=[("x", i)], sem="y%d" % i)
    P.barrier()
    P.emit(nc, st)
    st.close()
    return nc


def _host_consts(rel_bias):
    half = 16
    inv = (10000.0 ** (-np.arange(half, dtype=np.float32) / half)).astype(np.float32)
    ang = np.arange(S, dtype=np.float32)[:, None] * inv[None, :]
    cs_tab = np.concatenate([np.cos(ang), np.sin(ang)], axis=1).astype(np.float32)
    kl = np.arange(128)[:, None]
    c = np.arange(640)[None, :]
    idx = np.clip(c - kl, -256, 256) + 256
    rel_ext = np.ascontiguousarray(rel_bias[:, :, idx]).reshape(DEPTH * 8, 128, 640).astype(np.float32)
    return cs_tab, rel_ext


PH = "NBACMF"
ANNOT = False
DBG = False
BSTOP = 0
_NC_CACHE = {}


def kernel(**inputs):
    inp = {k: np.ascontiguousarray(np.asarray(v, dtype=np.float32)) for k, v in inputs.items()}
    x = inp.pop("x")
    rel_bias = inp.pop("rel_bias")
    cs_tab, rel_ext = _host_consts(rel_bias)
    key = (DEPTH, PH)
    if key not in _NC_CACHE:
        _NC_CACHE[key] = build(DEPTH)
    nc = _NC_CACHE[key]
    B = x.shape[0]
    in_maps = []
    for b in range(B):
        m = dict(inp)
        m["x"] = np.ascontiguousarray(x[b])
        m["rel_ext"] = rel_ext
        m["cs_tab"] = cs_tab
        in_maps.append(m)
    res = run_bass_kernel_spmd(nc, in_maps, core_ids=list(range(B)))
    return np.stack([np.asarray(r["y"], dtype=np.float32) for r in res.results], axis=0)
```

```python
import numpy as np
from contextlib import ExitStack
import concourse.bass as bass
import concourse.mybir as mybir
from concourse.bass_utils import run_bass_kernel_spmd

F32 = mybir.dt.float32
BF16 = mybir.dt.bfloat16
AF = mybir.ActivationFunctionType
ALU = mybir.AluOpType
AX = mybir.AxisListType

S = 2048
D = 1024
NT = 16
DEPTH = 4
EPS = 1e-6
INW = 6824
C_QA, C_KA, C_VA, C_FA, C_QD, C_KVD, C_KR, C_QC, C_KC, C_VC, C_G = 0, 512, 1024, 1536, 1544, 1928, 2184, 2216, 2728, 3240, 3752


class Prog:
    ENG = ("pe", "act", "dve", "pool", "sp")

    def __init__(self):
        self.ops = {e: [] for e in self.ENG}
        self.state = {}
        self.known = {e: {} for e in self.ENG}
        self.flag = {e: set() for e in self.ENG}
        self.semcnt = {}
        self.tag = ""
        self.tags = {e: [] for e in self.ENG}

    def _add(self, eng, fn, R, W, dma_sem):
        self.tags[eng].append(self.tag)
        need = {}
        for k in R:
            st = self.state.get(k)
            if st and st[0] is not None:
                t = st[0]
                need[t[0]] = max(need.get(t[0], -1), t[1])
        for k in W:
            st = self.state.get(k)
            if st:
                if st[0] is not None:
                    t = st[0]
                    need[t[0]] = max(need.get(t[0], -1), t[1])
                for tk, tv in st[1].items():
                    need[tk] = max(need.get(tk, -1), tv)
        kn = self.known[eng]
        waits = []
        for tk, tv in need.items():
            if dma_sem is None and eng == "pe" and tk == ("E", "pe"):
                continue
            if kn.get(tk, -1) >= tv:
                continue
            kn[tk] = tv
            waits.append((tk, tv))
            if tk[0] == "E":
                self.flag[tk[1]].add(tv)
        idx = len(self.ops[eng])
        if dma_sem is None:
            tok = (("E", eng), idx)
        else:
            c = self.semcnt.get(dma_sem, 0) + 1
            self.semcnt[dma_sem] = c
            tok = (("S", dma_sem), c)
        self.ops[eng].append((fn, waits, dma_sem))
        for k in R:
            if k in W:
                continue
            st = self.state.setdefault(k, [None, {}])
            st[1][tok[0]] = max(st[1].get(tok[0], -1), tok[1])
        for k in W:
            self.state[k] = [tok, {}]
        return tok

    def op(self, eng, fn, R=(), W=()):
        return self._add(eng, fn, list(R), list(W), None)

    def dma(self, q, fn, R=(), W=(), sem=None):
        return self._add(q, fn, list(R), list(W), sem)

    def barrier(self):
        last = {}
        for e in self.ENG:
            n = len(self.ops[e])
            for i in range(n - 1, -1, -1):
                if self.ops[e][i][2] is None and self.ops[e][i][0] is not None:
                    last[("E", e)] = i
                    break
        for s, c in self.semcnt.items():
            last[("S", s)] = c
        for e in self.ENG:
            kn = self.known[e]
            waits = []
            for tk, tv in last.items():
                if kn.get(tk, -1) >= tv:
                    continue
                kn[tk] = tv
                waits.append((tk, tv))
                if tk[0] == "E":
                    self.flag[tk[1]].add(tv)
            if waits:
                self.ops[e].append((None, waits, None))
                self.tags[e].append(self.tag)

    def emit(self, nc, stack):
        rank = {}
        for e in self.ENG:
            rank[e] = {idx: r + 1 for r, idx in enumerate(sorted(self.flag[e]))}
        BS = 2000
        esem = {e: [stack.enter_context(nc.semaphore("es_%s_%d" % (e, b))) for b in range(len(rank[e]) // BS + 1)] for e in self.ENG}
        dsem = {s: stack.enter_context(nc.semaphore("ds_" + str(i))) for i, s in enumerate(self.semcnt)}
        block = stack.enter_context(nc.Block())

        def run(e, h):
            for idx, (fn, waits, ds) in enumerate(self.ops[e]):
                for tk, tv in waits:
                    if tk[0] == "E":
                        r_ = rank[tk[1]][tv] - 1
                        h.wait_ge(esem[tk[1]][r_ // BS], r_ % BS + 1)
                    else:
                        h.wait_ge(dsem[tk[1]], tv * 16)
                if fn is None:
                    continue
                ins = fn(h)
                if ANNOT:
                    ins.annotate(self.tags[e][idx])
                if ds is not None:
                    ins.then_inc(dsem[ds], 16)
                elif idx in rank[e]:
                    ins.then_inc(esem[e][(rank[e][idx] - 1) // BS], 1)

        block.tensor(lambda h: run("pe", h))
        block.scalar(lambda h: run("act", h))
        block.vector(lambda h: run("dve", h))
        block.gpsimd(lambda h: run("pool", h))
        block.sync(lambda h: run("sp", h))


def build(nlayers=DEPTH):
    nc = bass.Bass("TRN2", target_bir_lowering=False)
    P = Prog()
    st = ExitStack()

    def din(name, shape):
        return nc.dram_tensor(name, list(shape), F32, kind="ExternalInput")

    x_d = din("x", [S, D]).ap()
    norm_mix = din("norm_mix", [DEPTH, D]).ap()
    w_in = din("w_in", [DEPTH, D, INW]).ap()
    b_forget = din("b_forget", [DEPTH, 8]).ap()
    b_gate = din("b_gate", [DEPTH, 3072]).ap()
    qk_norm_a = din("qk_norm_a", [DEPTH, 2, 64]).ap()
    mla_q_norm = din("mla_q_norm", [DEPTH, 384]).ap()
    mla_kv_norm = din("mla_kv_norm", [DEPTH, 256]).ap()
    w_q_up = din("w_q_up", [DEPTH, 384, 768]).ap()
    w_kv_up = din("w_kv_up", [DEPTH, 256, 1024]).ap()
    qk_norm_b_nope = din("qk_norm_b_nope", [DEPTH, 2, 64]).ap()
    qk_norm_b_rope = din("qk_norm_b_rope", [DEPTH, 2, 32]).ap()
    qk_norm_c = din("qk_norm_c", [DEPTH, 2, 64]).ap()
    relx = din("rel_ext", [DEPTH * 8, 128, 640]).ap()
    w_branch = din("w_branch", [DEPTH, 3, 512, D]).ap()
    w_out = din("w_out", [DEPTH, D, D]).ap()
    norm_ffn = din("norm_ffn", [DEPTH, D]).ap()
    w_ff1 = din("w_ff1", [DEPTH, D, 4096]).ap()
    w_ff2 = din("w_ff2", [DEPTH, 4096, D]).ap()
    cs_d = din("cs_tab", [S, 32]).ap()
    y_d = nc.dram_tensor("y", [S, D], F32, kind="ExternalOutput").ap()
    dbg_d = nc.dram_tensor("dbg", [128, 24576], BF16, kind="ExternalOutput").ap() if DBG else None

    def sb(name, shape, dt):
        return st.enter_context(nc.sbuf_tensor(name, list(shape), dt))

    xs = sb("xs", [128, NT, D], F32)
    hT = sb("hT", [128, 8, S], BF16)
    ident_bf = sb("ident_bf", [128, 128], BF16)
    ident_f = sb("ident_f", [128, 128], F32)
    blockones = sb("blockones", [128, 128], BF16)
    ones_bf = sb("ones_bf", [128, 128], BF16)
    maskA = sb("maskA", [128, 128], BF16)
    maskB = sb("maskB", [128, 128], BF16)
    U_f = sb("U_f", [128, 128], F32)
    ones_f = sb("ones_f", [128, 128], F32)
    validC = sb("validC", [128, 640], BF16)
    vecT = sb("vecT", [128, 132], F32)
    bf_rep = sb("bf_rep", [128, 32], F32)
    gbn = sb("gbn", [128, 512], F32)
    gbr = sb("gbr", [128, 256], F32)
    cs = sb("cs", [128, NT, 32], F32)
    ssq = sb("ssq", [128, NT], F32)
    rstd = sb("rstd", [128, NT], F32)
    small = sb("small", [128, 64], F32)
    cst = sb("cst", [128, 4], F32)
    AR_N = 52000
    arena = sb("arena", [128, AR_N], BF16)
    psb = [st.enter_context(nc.psum_tensor("ps%d" % b, [128, 512], F32)) for b in range(8)]

    def bv(off, n):
        return arena[:, off:off + n]

    def fv(off, n):
        return arena[:, off:off + 2 * n].bitcast(F32)

    def psbf(b):
        return psb[b][:].bitcast(BF16)

    Z0 = 24576
    yT = bv(0, 24576).rearrange("p (n c t) -> p n c t", n=3, c=4)

    def mm(out, lhsT, rhs, start, stop):
        return lambda e: e.matmul(out, lhsT, rhs, start=start, stop=stop)

    def tr(out, in_, ident):
        return lambda e: e.transpose(out, in_, ident)

    def act(out, in_, func, bias=None, scale=None, accum_out=None):
        kw = {}
        if bias is not None:
            kw["bias"] = bias
        if scale is not None:
            kw["scale"] = scale
        if accum_out is not None:
            kw["accum_out"] = accum_out
        return lambda e: e.activation(out=out, in_=in_, func=func, **kw)

    def tt(out, in0, in1, op):
        return lambda e: e.tensor_tensor(out=out, in0=in0, in1=in1, op=op)

    def ts(out, in0, s1, s2, op0, op1=None):
        if op1 is None:
            return lambda e: e.tensor_single_scalar(out=out, in_=in0, scalar=s1, op=op0)
        return lambda e: e.tensor_scalar(out=out, in0=in0, scalar1=s1, scalar2=s2, op0=op0, op1=op1)

    def stt(out, in0, scalar, in1, op0, op1):
        return lambda e: e.scalar_tensor_tensor(out=out, in0=in0, scalar=scalar, in1=in1, op0=op0, op1=op1)

    def cp(out, in_):
        return lambda e: e.tensor_copy(out=out, in_=in_)

    def rsqrt_small(dst, src, mul, eps, Rk, Wk):
        P.op("dve", ts(dst, src, mul, eps, ALU.mult, ALU.add), R=Rk, W=Wk)
        P.op("act", act(dst, dst, AF.Ln), R=Wk, W=Wk)
        P.op("act", act(dst, dst, AF.Exp, scale=-0.5), R=Wk, W=Wk)

    def red(out, in_):
        return lambda e: e.tensor_reduce(out=out, in_=in_, axis=AX.X, op=ALU.add)

    def ms(ap, v):
        return lambda e: e.memset(ap, v)

    def dmaf(out, in_):
        return lambda e: e.dma_start(out=out, in_=in_)

    def wload(dst, src2d, key, R=(), q="pool"):
        P.dma(q, dmaf(dst, src2d.rearrange("(k p) c -> p k c", p=128)), R=R, W=[key], sem=str(key))

    P.op("pool", ms(ones_f[:], 1.0), W=["ones_f"])
    P.op("pool", lambda e: e.affine_select(out=ident_f[:], in_=ones_f[:], pattern=[[-1, 128]], compare_op=ALU.is_equal,
                                           fill=0.0, base=0, channel_multiplier=1), R=["ones_f"], W=["ident_f"])
    P.op("pool", lambda e: e.affine_select(out=U_f[:], in_=ones_f[:], pattern=[[1, 128]], compare_op=ALU.is_ge,
                                           fill=0.0, base=0, channel_multiplier=-1), R=["ones_f"], W=["U_f"])
    P.op("pool", cp(ident_bf[:], ident_f[:]), R=["ident_f"], W=["ident_bf"])
    P.op("pool", ts(maskA[:], U_f[:], 60000.0, -60000.0, ALU.mult, ALU.add), R=["U_f"], W=["maskA"])
    P.op("pool", ms(ones_bf[:], 1.0), W=["ones_bf"])
    P.op("pool", ms(cst[:, 0:1], 1.0), W=["cst"])
    P.op("pool", ms(small[:, 48:49], EPS), W=["epscol"])
    P.op("pool", ms(cst[:, 1:2], 64 * EPS), W=["cst"])
    P.op("pool", ms(cst[:, 2:3], 384 * EPS), W=["cst"])
    P.op("pool", ms(cst[:, 3:4], 256 * EPS), W=["cst"])
    P.op("pool", ms(blockones[:], 0.0), W=["blockones"])
    P.op("pool", ms(blockones[0:64, 0:64], 1.0), W=["blockones"])
    P.op("pool", ms(blockones[64:128, 64:128], 1.0), W=["blockones"])
    P.op("pool", ms(maskB[:], 1.0), W=["maskB"])
    P.op("pool", ms(maskB[64:128, 0:64], 0.0), W=["maskB"])
    P.op("pool", ms(validC[:], 0.0), W=["validC"])
    P.op("pool", ms(validC[0:64, 0:576], 1.0), W=["validC"])
    P.op("pool", ms(validC[64:128, 64:640], 1.0), W=["validC"])

    stage1 = fv(Z0, 128)
    stage2 = fv(Z0 + 256, 128)
    P.dma("sp", dmaf(stage1[0:96, :], b_gate.rearrange("l (c p) -> (l c) p", p=128)), W=["stage1"], sem="stage1")
    qa2 = qk_norm_a.rearrange("l q d -> (l q) d")
    qc2 = qk_norm_c.rearrange("l q d -> (l q) d")
    P.dma("sp", dmaf(stage2[0:8, 0:64], qa2), W=["stage2a"], sem="stage2")
    P.dma("sp", dmaf(stage2[0:8, 64:128], qa2), W=["stage2b"], sem="stage2")
    P.dma("sp", dmaf(stage2[8:16, 0:64], qc2), W=["stage2c"], sem="stage2")
    P.dma("sp", dmaf(stage2[8:16, 64:128], qc2), W=["stage2d"], sem="stage2")
    P.dma("sp", dmaf(stage2[16:28, :], mla_q_norm.rearrange("l (c p) -> (l c) p", p=128)), W=["stage2e"], sem="stage2")
    P.dma("sp", dmaf(stage2[28:36, :], mla_kv_norm.rearrange("l (c p) -> (l c) p", p=128)), W=["stage2f"], sem="stage2")
    P.op("pe", tr(psb[0][:, 0:96], stage1[0:96, :], ident_f[0:96, 0:96]), R=["stage1", "ident_f"], W=[("ps", 0)])
    P.op("pe", tr(psb[1][:, 0:36], stage2[0:36, :], ident_f[0:36, 0:36]),
         R=["stage2a", "stage2b", "stage2c", "stage2d", "stage2e", "stage2f", "ident_f"], W=[("ps", 1)])
    P.op("dve", cp(vecT[:, 0:96], psb[0][:, 0:96]), R=[("ps", 0)], W=["vecT"])
    P.op("dve", ts(vecT[:, 96:112], psb[1][:, 0:16], 8.0, None, ALU.mult), R=[("ps", 1)], W=["vecT"])
    P.op("dve", ts(vecT[:, 112:124], psb[1][:, 16:28], float(np.sqrt(384.0)), None, ALU.mult), R=[("ps", 1)], W=["vecT"])
    P.op("dve", ts(vecT[:, 124:132], psb[1][:, 28:36], 16.0, None, ALU.mult), R=[("ps", 1)], W=["vecT"])
    P.dma("sp", dmaf(bf_rep[:], b_forget.rearrange("l h -> (l h)").partition_broadcast(128)), W=["bf_rep"], sem="c1")
    P.dma("sp", dmaf(gbn[:], qk_norm_b_nope.rearrange("l q d -> (l q d)").partition_broadcast(128)), W=["gbn"], sem="c2")
    P.dma("sp", dmaf(gbr[:], qk_norm_b_rope.rearrange("l q d -> (l q d)").partition_broadcast(128)), W=["gbr"], sem="c3")
    P.dma("sp", dmaf(cs[:], cs_d.rearrange("(t p) c -> p t c", p=128)), W=["cs"], sem="c4")
    for i in range(NT):
        P.dma("sp", dmaf(xs[:, i, :], x_d[i * 128:(i + 1) * 128, :]), W=[("x", i)], sem="x%d" % i)
    P.barrier()

    def norm_phase(gain_row):
        gnorm = fv(Z0, 1024)
        htok = [bv(Z0 + 2048 + k * 1024, 1024) for k in range(2)]
        junk = bv(Z0 + 4096, 1024)
        P.dma("sp", dmaf(gnorm, gain_row.partition_broadcast(128)), W=["gnorm"], sem="gnorm")
        for i in range(NT):
            P.op("act", act(junk, xs[:, i, :], AF.Square, accum_out=ssq[:, i:i + 1]), R=[("x", i)], W=["junk", "ssq"])
        rsqrt_small(rstd[:], ssq[:], 1.0 / D, EPS, ["ssq"], ["rstd"])
        for i in range(NT):
            P.op("dve", stt(htok[i % 2], xs[:, i, :], rstd[:, i:i + 1], gnorm, ALU.mult, ALU.mult),
                 R=[("x", i), "rstd", "gnorm"], W=[("htok", i % 2)])
            bank = 6 + (i % 2)
            ptv = psbf(bank).rearrange("p (a b) -> p a b", a=8)
            for kc in range(8):
                P.op("pe", tr(ptv[:, kc, :], htok[i % 2][:, kc * 128:(kc + 1) * 128], ident_bf[:]),
                     R=[("htok", i % 2), "ident_bf"], W=[("ps", bank)])
            P.op("act", cp_act(hT[:, :, i * 128:(i + 1) * 128], ptv), R=[("ps", bank)], W=[("hT", i // 4)])
        P.barrier()

    def cp_act(out, in_):
        return lambda e: e.activation(out=out, in_=in_, func=AF.Copy)

    unit_ctr = [0]

    def proj_norm(l, col0, nchunk, onesmat, nfeat, gcol0, outs, okey, wslots, sqb, rsb, gstep=0, split=None):
        for c in range(nchunk):
            wload(wslots[c][0], w_in[l, :, col0 + c * 128: col0 + (c + 1) * 128], wslots[c][1])
        for tc in range(4):
            if nchunk == 1:
                u = unit3_ctr[0] % 3
                unit3_ctr[0] += 1
                braw = [2 * u]
                bssq = 2 * u + 1
                sqs = [sqb[0][u]]
                rs = [rsb[0], rsb[1], rs_extra[0]][u]
                sqk = [("sq3", u)]
                rsk = ("rs3", u)
            else:
                u = unit_ctr[0] % 2
                unit_ctr[0] += 1
                b0 = 4 * u
                braw = [b0 + c for c in range(nchunk)]
                bssq = b0 + 3
                sqs = [sqb[u][c] for c in range(nchunk)]
                rs = rsb[u]
                sqk = [("sq", u, c) for c in range(nchunk)]
                rsk = ("rs", u)
            tsl = slice(tc * 512, (tc + 1) * 512)
            for c in range(nchunk):
                for kc in range(8):
                    P.op("pe", mm(psb[braw[c]][:], wslots[c][0][:, kc, :], hT[:, kc, tsl], kc == 0, kc == 7),
                         R=[wslots[c][1], ("hT", tc)], W=[("ps", braw[c])])
                P.op("act", act(sqs[c], psb[braw[c]][:], AF.Square), R=[("ps", braw[c])], W=[sqk[c]])
            for c in range(nchunk):
                P.op("pe", mm(psb[bssq][:], onesmat[:], sqs[c], c == 0, c == nchunk - 1),
                     R=[sqk[c], "ones_bf", "blockones"], W=[("ps", bssq)])
            ecol = {64: 1, 384: 2, 256: 3}[nfeat]
            P.op("act", act(rs, psb[bssq][:], AF.Ln, bias=cst[:, ecol:ecol + 1]), R=[("ps", bssq), "cst"], W=[rsk])
            P.op("act", act(rs, rs, AF.Exp, scale=-0.5), R=[rsk], W=[rsk])
            for c in range(nchunk):
                if split is not None:
                    for (p0, oap) in ((0, split[0]), (64, split[1])):
                        P.op("dve", stt(oap[p0:p0 + 64, tsl], psb[braw[c]][p0:p0 + 64, :], vecT[p0:p0 + 64, gcol0:gcol0 + 1], rs[p0:p0 + 64, :], ALU.mult, ALU.mult),
                             R=[("ps", braw[c]), rsk, "vecT"], W=[okey])
                    continue
                P.op("dve", stt(outs[c][:, tsl], psb[braw[c]][:], vecT[:, gcol0 + gstep * c:gcol0 + gstep * c + 1], rs, ALU.mult, ALU.mult),
                     R=[("ps", braw[c]), rsk, "vecT"], W=[okey])

    unit3_ctr = [0]
    rs_extra = [None]

    attn_ctr = [0]

    def attention(kind, hf, nbr, kslice, qslice, qkey, vaug, Pt, ytile, scale, biasA=None, bias_prep=None, EBrev=None, qprep=None, qterm=None, kkey=None, vkey=None):
        def finish(i):
            ob = 3 + (i % 2)
            pov = psb[ob][:, 0:260].rearrange("p (h c) -> p h c", h=4)
            rec = small[:, (i % 2) * 4:(i % 2) * 4 + 4]
            P.op("dve", (lambda rec, pov: lambda e: e.reciprocal(out=rec.unsqueeze(2), in_=pov[:, :, 64:65]))(rec, pov),
                 R=[("ps", ob)], W=[("rec", i % 2)])
            yt = ytile[i % 2]
            P.op("dve", tt(yt.rearrange("p (h c) -> p h c", h=4), pov[:, :, 0:64], rec.unsqueeze(2).broadcast_to([128, 4, 64]), ALU.mult),
                 R=[("ps", ob), ("rec", i % 2)], W=[("ytile", i % 2)])
            ptv = psbf(5).rearrange("p (a b) -> p a b", a=8)
            for c in range(2):
                P.op("pe", tr(ptv[:, c, :], yt[:, c * 128:(c + 1) * 128], ident_bf[:]), R=[("ytile", i % 2), "ident_bf"], W=[("ps", 5)])
            P.op("act", cp_act(yT[:, nbr, 2 * hf:2 * hf + 2, i * 128:(i + 1) * 128], ptv[:, 0:2, :]), R=[("ps", 5)], W=[("yT", nbr)])

        if qprep is not None:
            qprep[0](0)
            qprep[1](0)
        if bias_prep is not None:
            bias_prep(0)
        fin_pending = None
        for i in range(NT):
            js = list(range(max(0, i - 4), i + 1)) if kind == "C" else list(range(0, i + 1))
            groups = [js[a:a + 4] for a in range(0, len(js), 4)]
            ob = 3 + (i % 2)
            pov = psb[ob][:, 0:260].rearrange("p (h c) -> p h c", h=4)
            items = [(hh, grp) for hh in range(4) for grp in groups]
            pend = []

            def second(hh, grp, sbk, i=i, js=js, ob=ob, pov=pov):
                n = len(grp)
                if kind == "A":
                    h = 4 * hf + hh
                    for jj, j in enumerate(grp):
                        P.op("act", act(Pt[sbk][:, jj * 128:(jj + 1) * 128], psb[sbk][:, jj * 128:(jj + 1) * 128], AF.Exp,
                                        bias=biasA[i % 2][:, j, h:h + 1], scale=scale),
                             R=[("ps", sbk), ("biasA", i % 2)], W=[("Pt", sbk)])
                else:
                    P.op("act", act(Pt[sbk][:, 0:n * 128], psb[sbk][:, 0:n * 128], AF.Exp, scale=scale),
                         R=[("ps", sbk)], W=[("Pt", sbk)])
                if kind == "C":
                    d0 = 4 - (i - grp[0])
                    P.op(MASK_ENG, tt(Pt[sbk][:, 0:n * 128], Pt[sbk][:, 0:n * 128], EBrev[:, hh, d0 * 128:(d0 + n) * 128], ALU.mult),
                         R=["EBrev"], W=[("Pt", sbk)])
                elif i in grp and kind == "B":
                    jj = grp.index(i)
                    P.op(MASK_ENG, tt(Pt[sbk][:, jj * 128:(jj + 1) * 128], Pt[sbk][:, jj * 128:(jj + 1) * 128], maskB[:], ALU.mult),
                         R=["maskB"], W=[("Pt", sbk)])
                for jj, j in enumerate(grp):
                    P.op("pe", mm(pov[:, hh, :], Pt[sbk][:, jj * 128:(jj + 1) * 128], vaug[:, j, hh, :], j == js[0], j == js[-1]),
                         R=[("Pt", sbk), "v"] + ([vkey(j)] if vkey else []), W=[("ps", ob)])

            cnt = 0
            for (hh, grp) in items:
                sbk = attn_ctr[0] % 3
                attn_ctr[0] += 1
                for jj, j in enumerate(grp):
                    osl = psb[sbk][:, jj * 128:(jj + 1) * 128]
                    P.op("pe", mm(osl, kslice(hh, j), qslice(hh, i), True, qterm is None),
                         R=[(kkey(j) if kkey else "kT"), qkey(i)], W=[("ps", sbk)])
                    if qterm is not None:
                        rq_ap, rq_key = qterm(i, hh)
                        P.op("pe", mm(osl, ones_bf[:], rq_ap, False, j != i), R=[rq_key, "ones_bf"], W=[("ps", sbk)])
                        if j == i:
                            P.op("pe", mm(osl, ident_bf[:], maskA[:], False, True), R=["maskA", "ident_bf"], W=[("ps", sbk)])
                pend.append((hh, grp, sbk))
                cnt += 1
                if cnt == 2:
                    if fin_pending is not None:
                        finish(fin_pending)
                        fin_pending = None
                    if i + 1 < NT:
                        if qprep is not None:
                            qprep[0](i + 1)
                        if bias_prep is not None:
                            bias_prep(i + 1)
                if len(pend) > 2:
                    second(*pend.pop(0))
            while pend:
                second(*pend.pop(0))
            if qprep is not None and i + 1 < NT:
                qprep[1](i + 1)
            fin_pending = i
        finish(fin_pending)

    def attention2(kind, hf, nbr, kslice, qchunk, vaug, Pt, ytile4, scale, biasA=None, prep=None, rqm=None, EB=None):
        OB = [3, 4, 6, 7]

        def finish(c):
            ptv = psbf(5).rearrange("p (a b) -> p a b", a=8)
            for t in range(4):
                ob = OB[t]
                pov = psb[ob][:, 0:260].rearrange("p (h c) -> p h c", h=4)
                rec = small[:, t * 4:t * 4 + 4]
                P.op("dve", (lambda rec, pov: lambda e: e.reciprocal(out=rec.unsqueeze(2), in_=pov[:, :, 64:65]))(rec, pov),
                     R=[("ps", ob)], W=[("rec", t)])
                yt = ytile4[t]
                P.op("dve", tt(yt.rearrange("p (h c) -> p h c", h=4), pov[:, :, 0:64], rec.unsqueeze(2).broadcast_to([128, 4, 64]), ALU.mult),
                     R=[("ps", ob), ("rec", t)], W=[("ytile", t)])
                for c2 in range(2):
                    P.op("pe", tr(ptv[:, c2 * 4 + t, :], yt[:, c2 * 128:(c2 + 1) * 128], ident_bf[:]), R=[("ytile", t), "ident_bf"], W=[("ps", 5)])
            P.op("act", cp_act(yT[:, nbr, 2 * hf:2 * hf + 2, c * 512:(c + 1) * 512], psbf(5).rearrange("p (a b) -> p a b", a=2)),
                 R=[("ps", 5)], W=[("yT", nbr)])

        if prep is not None:
            prep(0)
        for c in range(4):
            j_lo = max(0, 4 * c - 4) if kind == "C" else 0
            items = [(hh, j) for hh in range(4) for j in range(j_lo, 4 * c + 4)]
            pend = []

            def second(hh, j, sb, pb, t0, t1, c=c):
                cols = slice(t0 * 128, (t1 + 1) * 128)
                if kind == "A":
                    h = 4 * hf + hh
                    P.op("act", act(Pt[pb][:, cols], psb[sb][:, cols], AF.Exp, bias=biasA[c % 2][:, j, h:h + 1], scale=scale),
                         R=[("ps", sb), ("biasA", c % 2)], W=[("Pt", pb)])
                else:
                    P.op("act", act(Pt[pb][:, cols], psb[sb][:, cols], AF.Exp, scale=scale), R=[("ps", sb)], W=[("Pt", pb)])
                if kind == "C":
                    d0 = 4 * c + t0 - j
                    P.op(MASK_ENG, tt(Pt[pb][:, cols], Pt[pb][:, cols], EB[:, hh, d0 * 128:(d0 + t1 - t0 + 1) * 128], ALU.mult),
                         R=["EBrev"], W=[("Pt", pb)])
                for t in range(t0, t1 + 1):
                    i = 4 * c + t
                    first_j = max(0, i - 4) if kind == "C" else 0
                    pov = psb[OB[t]][:, 0:260].rearrange("p (h c) -> p h c", h=4)
                    P.op("pe", mm(pov[:, hh, :], Pt[pb][:, t * 128:(t + 1) * 128], vaug[:, j, hh, :], j == first_j, j == i),
                         R=[("Pt", pb), "v"], W=[("ps", OB[t])])

            for (hh, j) in items:
                t0 = max(0, j - 4 * c)
                t1 = 3 if kind != "C" else min(3, j + 4 - 4 * c)
                cols = slice(t0 * 128, (t1 + 1) * 128)
                nsb = 3 if kind == "C" else 2
                sb = attn_ctr[0] % nsb
                pb = attn_ctr[0] % 3
                attn_ctr[0] += 1
                diag = (kind == "A") and j >= 4 * c
                P.op("pe", mm(psb[sb][:, cols], kslice(hh, j), qchunk(c, hh)[:, cols], True, kind != "A"),
                     R=["kT", "qT"], W=[("ps", sb)])
                if kind == "A":
                    P.op("pe", mm(psb[sb][:, cols], ones_bf[:], rqm[:, hh, cols], False, True), R=["rqm", "ones_bf"], W=[("ps", sb)])
                    if diag:
                        P.op("pe", lambda e, o=psb[sb][:, t0 * 128:(t0 + 1) * 128]: e.matmul(o, ident_bf[:], maskA[:], start=False, stop=True, skip_group_check=True),
                             R=["maskA", "ident_bf"], W=[("ps", sb)])
                pend.append((hh, j, sb, pb, t0, t1))
                if len(pend) > nsb - 1:
                    second(*pend.pop(0))
            while pend:
                second(*pend.pop(0))
            if prep is not None and c + 1 < 4:
                prep(c + 1)
            finish(c)

    def v_proj(l, col0, hf, wv_unused, vaug):
        wv2 = bv(ZW_ref[0], 2048).rearrange("p (k c) -> p k c", k=8)
        keys = [("wqk", 0), ("wqk", 1)]
        P.dma("pool", dmaf(wv2, w_in[l, :, col0 + hf * 256: col0 + (hf + 1) * 256].rearrange("(k p) c -> p k c", p=128)), W=keys, sem="wv2")
        for i in range(NT):
            bank = 6 + (i % 2)
            for kc in range(8):
                P.op("pe", mm(psb[bank][:, 0:256], hT[:, kc, i * 128:(i + 1) * 128], wv2[:, kc, :], kc == 0, kc == 7),
                     R=keys + [("hT", i // 4)], W=[("ps", bank)])
            P.op("act", cp_act(vaug[:, i, :, 0:64], psb[bank][:, 0:256].rearrange("p (h c) -> p h c", h=4)), R=[("ps", bank)], W=["v"])

    ZW_ref = [None]
    wqk_ref = [None]

    z = Z0
    ZK = z
    kT_ac = bv(z, 4096).rearrange("p (m t) -> p m t", m=2)
    kTm = bv(z, 8192).rearrange("p (m t) -> p m t", m=4)
    kT_b = bv(z, 8192).rearrange("p (m t) -> p m t", m=4)
    z += 8192
    ZQ = z
    qT_ac = bv(z, 4096).rearrange("p (m t) -> p m t", m=2)
    qTt = [bv(z + k * 512, 512).rearrange("p (h t) -> p h t", h=4) for k in range(2)]
    z += 4096
    vaug = bv(z, 4160).rearrange("p (i h c) -> p i h c", i=NT, h=4)
    z += 4160
    rs_extra[0] = fv(z + 2048, 512)
    sqb = [[bv(z + (u * 3 + c) * 512, 512) for c in range(3)] for u in range(2)]
    Pt = [bv(z + k * 512, 512) for k in range(3)]
    ytile = [bv(z + 1536 + k * 256, 256) for k in range(2)]
    ytile4 = [bv(z + 1536 + k * 256, 256) for k in range(4)]
    z += 3072
    rsb = [fv(z + u * 1024, 512) for u in range(2)]
    z += 2048
    ZW = z
    ZW_ref[0] = z
    wqk = [(bv(z + k * 1024, 1024).rearrange("p (k c) -> p k c", k=8), ("wqk", k)) for k in range(4)]
    z += 4096
    wv = None
    wqk_ref[0] = wqk
    assert z <= AR_N, z

    def set_vones():
        P.op("pool", ms(vaug[:, :, :, 64:65], 1.0), W=["v"])

    def phase_A(l):
        z = 16384
        wf = bv(z, 64).rearrange("p (k c) -> p k c", k=8); z += 64
        zt = fv(z, 128); z += 256
        lp = fv(z, 128); z += 256
        Lp = fv(z, 128).rearrange("p (i h) -> p i h", i=NT); z += 256
        PTt = fv(z, 128).rearrange("p (i h) -> p i h", i=NT); z += 256
        biasA = [fv(z + k * 256, 128).rearrange("p (i h) -> p i h", i=NT) for k in range(2)]; z += 512
        rq_bf = bv(z, 128); z += 128
        rqm = bv(z, 2048).rearrange("p (h t) -> p h t", h=4); z += 2048
        assert z <= 24576, z
        P.op("pool", ms(bv(ZK, 8192), 0.0), W=["kT"])
        loc = zt
        set_vones()
        P.op("pool", ms(rqm, 0.0), W=["rqm"])
        wload(wf, w_in[l, :, C_FA:C_FA + 8], "wf")
        for i in range(NT):
            for kc in range(8):
                P.op("pe", mm(psb[5][:, i * 8:(i + 1) * 8], hT[:, kc, i * 128:(i + 1) * 128], wf[:, kc, :], kc == 0, kc == 7),
                     R=["wf", ("hT", i // 4)], W=[("ps", 5)])
        ztv = zt.rearrange("p (i h) -> p i h", i=NT)
        P.op("dve", tt(ztv, psb[5][:, 0:128].rearrange("p (i h) -> p i h", i=NT),
                       bf_rep[:, l * 8:(l + 1) * 8].unsqueeze(1).broadcast_to([128, NT, 8]), ALU.add), R=[("ps", 5), "bf_rep"], W=["zt"])
        P.op("act", act(zt, zt, AF.Exp, scale=-1.0), R=["zt"], W=["zt"])
        P.op("act", act(lp, zt, AF.Ln, bias=cst[:, 0:1]), R=["zt", "cst"], W=["lp"])
        lpv = lp.rearrange("p (i h) -> p i h", i=NT)
        for i in range(NT):
            P.op("pe", mm(psb[7][:, i * 8:(i + 1) * 8], U_f[:], lpv[:, i, :], True, True), R=["lp", "U_f"], W=[("ps", 7)])
            P.op("pe", mm(psb[6][:, i * 8:(i + 1) * 8], ones_f[:], lpv[:, i, :], True, True), R=["lp", "ones_f"], W=[("ps", 6)])
        P.op("dve", ms(PTt[:, 0, :], 0.0), W=["PT"])
        for i in range(1, NT):
            P.op("dve", tt(PTt[:, i, :], PTt[:, i - 1, :], psb[6][:, (i - 1) * 8:i * 8], ALU.add), R=[("ps", 6)], W=["PT"])
        P.op("dve", tt(Lp, psb[7][:, 0:128].rearrange("p (i h) -> p i h", h=8), PTt, ALU.add), R=[("ps", 7), "PT"], W=["Lp"])
        locv = loc.rearrange("p (i h) -> p i h", i=NT)

        for hf in range(2):
            for c_ in range(2):
                proj_norm(l, C_KA + hf * 256 + c_ * 128, 1, blockones, 64, 96 + 2 * l + 1, [None], "kT", [wqk[c_]], sqb, rsb,
                          split=(kTm[:, 2 * c_, :], kTm[:, 2 * c_ + 1, :]))
            for c_ in range(2):
                proj_norm(l, C_QA + hf * 256 + c_ * 128, 1, blockones, 64, 96 + 2 * l, [qT_ac[:, c_, :]], "qT", [wqk[2 + c_]], sqb, rsb)
            v_proj(l, C_VA, hf, wv, vaug)
            P.barrier()
            def prepA(c, hf=hf):
                P.op("dve", tt(biasA[c % 2][:, 0:4 * c + 4, :], Lp[:, 0:4 * c + 4, :], PTt[:, 4 * c:4 * c + 1, :].broadcast_to([128, 4 * c + 4, 8]), ALU.subtract),
                     R=["Lp", "PT"], W=[("biasA", c % 2)])
                P.op("dve", tt(locv[:, 0:4, :], Lp[:, 4 * c:4 * c + 4, :], PTt[:, 4 * c:4 * c + 1, :].broadcast_to([128, 4, 8]), ALU.subtract),
                     R=["Lp", "PT", "zt"], W=["loc"])
                P.op("dve", ts(rq_bf[:, 0:32], loc[:, 0:32], -8.0, None, ALU.mult), R=["loc"], W=["rq_bf"])
                ptv = psbf(2).rearrange("p (a b) -> p a b", a=8)
                for t in range(4):
                    P.op("pe", tr(ptv[0:8, t, :], rq_bf[:, t * 8:(t + 1) * 8], ident_bf[:]), R=["rq_bf", "ident_bf"], W=[("ps", 2)])
                P.op("dve", tt(rqm[0:8, :, :], psbf(2)[0:8, 0:512].unsqueeze(1).broadcast_to([8, 4, 512]),
                               ident_f[0:8, 4 * hf:4 * hf + 4].unsqueeze(2).broadcast_to([8, 4, 512]), ALU.mult),
                     R=[("ps", 2), "ident_f"], W=["rqm"])

            attention2("A", hf, 0,
                       lambda hh, j: kTm[:, hh, j * 128:(j + 1) * 128],
                       lambda c, hh: qT_ac[:, hh // 2, c * 512:(c + 1) * 512],
                       vaug, Pt, ytile4, 0.125, biasA=biasA, prep=prepA, rqm=rqm)
            P.barrier()
            if DBG == 2 and l == 0 and hf == 1:
                P.dma("sp", dmaf(dbg_d, arena[:, Z0:Z0 + 24576]), sem="dbg")
                P.barrier()

    def phase_C(l):
        EBrev = bv(ZW, 2560).rearrange("p (h c) -> p h c", h=4)
        Tst = [fv(ZW + 2560, 640)]
        set_vones()
        P.op("pool", ms(bv(ZK, 8192), 0.0), W=["kT"])
        for hf in range(2):
            for c_ in range(2):
                proj_norm(l, C_KC + hf * 256 + c_ * 128, 1, blockones, 64, 104 + 2 * l + 1, [None], "kT", [wqk[c_]], sqb, rsb,
                          split=(kTm[:, 2 * c_, :], kTm[:, 2 * c_ + 1, :]))
            for c_ in range(2):
                proj_norm(l, C_QC + hf * 256 + c_ * 128, 1, blockones, 64, 104 + 2 * l, [qT_ac[:, c_, :]], "qT", [wqk[2 + c_]], sqb, rsb)
            v_proj(l, C_VC, hf, wv, vaug)
            P.barrier()
            for hh in range(4):
                h = 4 * hf + hh
                src = relx[l * 8 + h, :, :]
                P.dma("sp", dmaf(Tst[0], src), W=[("Tst", 0)], sem="Tst0")
                P.op("act", act(Tst[0], Tst[0], AF.Exp), R=[("Tst", 0)], W=[("Tst", 0)])
                for d in range(5):
                    P.op("dve", tt(EBrev[:, hh, d * 128:(d + 1) * 128], Tst[0][:, d * 128:(d + 1) * 128], validC[:, d * 128:(d + 1) * 128], ALU.mult),
                         R=[("Tst", 0), "validC"], W=["EBrev"])
            attention2("C", hf, 2,
                       lambda hh, j: kTm[:, hh, j * 128:(j + 1) * 128],
                       lambda c, hh: qT_ac[:, hh // 2, c * 512:(c + 1) * 512],
                       vaug, Pt, ytile4, 0.125, EB=EBrev)
            P.barrier()

    def rope_ops(dst1, dst2, x1, x2, cosv, sinv, t1, t2, Rk, Wk):
        P.op("dve", tt(t1, x1, cosv, ALU.mult), R=Rk, W=["rt1"])
        P.op("dve", tt(t2, x2, sinv, ALU.mult), R=Rk, W=["rt2"])
        P.op("dve", tt(dst1, t1, t2, ALU.subtract), R=["rt1", "rt2"], W=Wk)
        P.op("dve", tt(t1, x2, cosv, ALU.mult), R=Rk, W=["rt1"])
        P.op("dve", tt(t2, x1, sinv, ALU.mult), R=Rk, W=["rt2"])
        P.op("dve", tt(dst2, t1, t2, ALU.add), R=["rt1", "rt2"], W=Wk)

    def phase_B(l):
        z = ZQ + 1024
        wkr = bv(z, 256).rearrange("p (k c) -> p k c", k=8); z += 256
        wqup = bv(z, 1152).rearrange("p (k c) -> p k c", k=3); z += 1152
        wkvup = bv(z, 1024).rearrange("p (k c) -> p k c", k=2); z += 1024
        krope = bv(z, 512).rearrange("p (i c) -> p i c", i=NT); z += 512
        assert z <= ZQ + 4096
        z = 6144
        tmpf = fv(z, 512); z += 1024
        tmpq = fv(z, 384).rearrange("p (h c) -> p h c", h=4); z += 768
        assert z <= 8192
        z = 16384 + 4096
        ktok = [bv(z + k * 512, 512).rearrange("p (h c) -> p h c", h=4) for k in range(2)]; z += 1024
        qtok = [bv(z + k * 512, 512).rearrange("p (h c) -> p h c", h=4) for k in range(2)]; z += 1024
        for k_ in range(2):
            P.op("pool", ms(ktok[k_], 0.0), W=[("ktok", k_)])
            P.op("pool", ms(qtok[k_], 0.0), W=[("qtok", k_)])
        rt1 = fv(z, 256); z += 512
        rt2 = fv(z, 256); z += 512
        kvsb = fv(z, 512); z += 1024
        assert z <= 24576, z
        qdnT = bv(0, 6144).rearrange("p (c t) -> p c t", c=3)
        kvdnT = bv(16384, 4096).rearrange("p (c t) -> p c t", c=2)
        set_vones()
        proj_norm(l, C_QD, 3, ones_bf, 384, 112 + 3 * l, [qdnT[:, c, :] for c in range(3)], "qdnT", wqk[0:3], sqb, rsb, gstep=1)
        proj_norm(l, C_KVD, 2, ones_bf, 256, 124 + 2 * l, [kvdnT[:, c, :] for c in range(2)], "kvdnT", [wqk[3], wqk[0]], sqb, rsb, gstep=1)
        if BSTOP == 1:
            P.op("pool", ms(yT[:, 1, :, :], 0.0), W=[("yT", 1)]); P.barrier(); return
        wload(wkr, w_in[l, :, C_KR:C_KR + 32], "wkr")
        for i in range(NT):
            for kc in range(8):
                P.op("pe", mm(psb[5][:, i * 32:(i + 1) * 32], hT[:, kc, i * 128:(i + 1) * 128], wkr[:, kc, :], kc == 0, kc == 7),
                     R=["wkr", ("hT", i // 4)], W=[("ps", 5)])
        pk = psb[5][:].rearrange("p (i c) -> p i c", i=NT)
        tfv = tmpf.rearrange("p (i c) -> p i c", i=NT)
        sm = small[:, 16:32]
        P.op("act", act(tmpf, psb[5][:], AF.Square), R=[("ps", 5)], W=["tmpf"])
        P.op("dve", red(sm, tfv), R=["tmpf"], W=["sm"])
        rsqrt_small(sm, sm, 1.0 / 32, EPS, ["sm"], ["sm"])
        P.op("dve", tt(tfv, pk, sm.unsqueeze(2).broadcast_to([128, NT, 32]), ALU.mult), R=[("ps", 5), "sm"], W=["tmpf"])
        gk = gbr[:, l * 64 + 32:l * 64 + 64]
        P.op("dve", tt(tfv, tfv, gk.unsqueeze(1).broadcast_to([128, NT, 32]), ALU.mult), R=["tmpf", "gbr"], W=["tmpf"])
        r1 = rt1.rearrange("p (i c) -> p i c", i=NT)
        r2 = rt2.rearrange("p (i c) -> p i c", i=NT)
        rope_ops(krope[:, :, 0:16], krope[:, :, 16:32], tfv[:, :, 0:16], tfv[:, :, 16:32], cs[:, :, 0:16], cs[:, :, 16:32],
                 r1, r2, ["tmpf", "cs"], ["krope"])
        P.barrier()
        if BSTOP == 2:
            P.op("pool", ms(yT[:, 1, :, :], 0.0), W=[("yT", 1)]); P.barrier(); return
        kvK = fv(ZW, 512)
        tK = fv(ZW + 1024, 256).rearrange("p (h c) -> p h c", h=4)
        kvQ = fv(ZW + 1536, 384)
        tQ = fv(ZW + 2304, 384).rearrange("p (h c) -> p h c", h=4)
        gg = fv(ZW + 3072, 64)
        epsc = small[:, 48:49]
        P.op("dve", tt(gg, gbn[:, l * 128:l * 128 + 64], gbn[:, l * 128 + 64:l * 128 + 128], ALU.mult), R=["gbn"], W=["gg"])
        for hf in range(2):
            wload(wkvup, w_kv_up[l, :, hf * 512:(hf + 1) * 512], "wkvup")
            wload(wqup, w_q_up[l, :, hf * 384:(hf + 1) * 384], "wqup")

            def kchain(i):
                a_, b_ = [], []
                for kc in range(2):
                    a_.append(("pe", mm(psb[6][:], kvdnT[:, kc, i * 128:(i + 1) * 128], wkvup[:, kc, :], kc == 0, kc == 1), ["kvdnT", "wkvup"], [("ps", 6)]))
                pkv = kvK.rearrange("p (h c) -> p h c", h=4)
                s4 = small[:, 32:36]
                kt = ktok[i % 2]
                a_.append(("act", cp_act(kvK, psb[6][:]), [("ps", 6)], ["kvK"]))
                a_.append(("act", act(tK, pkv[:, :, 0:64], AF.Square), ["kvK"], ["tK"]))
                a_.append(("dve", red(s4, tK), ["tK"], ["s4"]))
                a_.append(("act", act(s4, s4, AF.Ln, bias=epsc, scale=1.0 / 64), ["s4", "epscol"], ["s4"]))
                a_.append(("act", act(s4, s4, AF.Exp, scale=-0.5), ["s4"], ["s4"]))
                a_.append(("dve", tt(tK, pkv[:, :, 0:64], s4.unsqueeze(2).broadcast_to([128, 4, 64]), ALU.mult), ["kvK", "s4"], ["tK"]))
                a_.append(("dve", tt(kt[:, :, 0:64], tK, gg.unsqueeze(1).broadcast_to([128, 4, 64]), ALU.mult), ["tK", "gg"], [("ktok", i % 2)]))
                a_.append(("dve", cp(kt[:, :, 64:96], krope[:, i, :].unsqueeze(1).broadcast_to([128, 4, 32])), ["krope"], [("ktok", i % 2)]))
                a_.append(("act", cp_act(vaug[:, i, :, 0:64], pkv[:, :, 64:128]), ["kvK"], [("v", i)]))
                ptv = psbf(6).rearrange("p (a b) -> p a b", a=8)
                for hh in range(4):
                    b_.append(("pe", tr(ptv[:, hh, :], kt[:, hh, :], ident_bf[:]), [("ktok", i % 2), "ident_bf"], [("ps", 6)]))
                b_.append(("act", cp_act(kT_b[0:96, :, i * 128:(i + 1) * 128], ptv[0:96, 0:4, :]), [("ps", 6)], [("kT", i)]))
                return a_, b_

            def qchain(i):
                a_, b_ = [], []
                for kc in range(3):
                    a_.append(("pe", mm(psb[7][:, 0:384], qdnT[:, kc, i * 128:(i + 1) * 128], wqup[:, kc, :], kc == 0, kc == 2), ["qdnT", "wqup"], [("ps", 7)]))
                pq = kvQ.rearrange("p (h c) -> p h c", h=4)
                sn = small[:, 36:40]
                sr = small[:, 40:44]
                s8 = small[:, 36:44]
                qt = qtok[i % 2]
                a_.append(("act", cp_act(kvQ, psb[7][:, 0:384]), [("ps", 7)], ["kvQ"]))
                a_.append(("act", act(tQ, pq, AF.Square), ["kvQ"], ["tQ"]))
                a_.append(("dve", red(sn, tQ[:, :, 0:64]), ["tQ"], ["s8"]))
                a_.append(("dve", red(sr, tQ[:, :, 64:96]), ["tQ"], ["s8"]))
                a_.append(("act", act(sn, sn, AF.Ln, bias=epsc, scale=1.0 / 64), ["s8", "epscol"], ["s8"]))
                a_.append(("act", act(sr, sr, AF.Ln, bias=epsc, scale=1.0 / 32), ["s8", "epscol"], ["s8"]))
                a_.append(("act", act(s8, s8, AF.Exp, scale=-0.5), ["s8"], ["s8"]))
                a_.append(("dve", tt(qt[:, :, 0:64], pq[:, :, 0:64], sn.unsqueeze(2).broadcast_to([128, 4, 64]), ALU.mult), ["kvQ", "s8"], [("qtok", i % 2)]))
                a_.append(("dve", tt(tmpq[:, :, 64:96], pq[:, :, 64:96], sr.unsqueeze(2).broadcast_to([128, 4, 32]), ALU.mult), ["kvQ", "s8"], ["tmpq"]))
                gqr = gbr[:, l * 64:l * 64 + 32]
                a_.append(("dve", tt(tmpq[:, :, 64:96], tmpq[:, :, 64:96], gqr.unsqueeze(1).broadcast_to([128, 4, 32]), ALU.mult), ["tmpq", "gbr"], ["tmpq"]))
                c4 = cs[:, i, 0:16].unsqueeze(1).broadcast_to([128, 4, 16])
                s4b = cs[:, i, 16:32].unsqueeze(1).broadcast_to([128, 4, 16])
                q1 = rt1[:, 0:64].rearrange("p (h c) -> p h c", h=4)
                q2 = rt2[:, 0:64].rearrange("p (h c) -> p h c", h=4)
                x1, x2 = tmpq[:, :, 64:80], tmpq[:, :, 80:96]
                Rk, Wk = ["tmpq", "cs"], [("qtok", i % 2)]
                a_.append(("dve", tt(q1, x1, c4, ALU.mult), Rk, ["rt1"]))
                a_.append(("dve", tt(q2, x2, s4b, ALU.mult), Rk, ["rt2"]))
                a_.append(("dve", tt(qt[:, :, 64:80], q1, q2, ALU.subtract), ["rt1", "rt2"], Wk))
                a_.append(("dve", tt(q1, x2, c4, ALU.mult), Rk, ["rt1"]))
                a_.append(("dve", tt(q2, x1, s4b, ALU.mult), Rk, ["rt2"]))
                a_.append(("dve", tt(qt[:, :, 80:96], q1, q2, ALU.add), ["rt1", "rt2"], Wk))
                ptv = psbf(7).rearrange("p (a b) -> p a b", a=8)
                for hh in range(4):
                    b_.append(("pe", tr(ptv[:, hh, :], qt[:, hh, :], ident_bf[:]), [("qtok", i % 2), "ident_bf"], [("ps", 7)]))
                b_.append(("act", cp_act(qTt[i % 2][0:96, :, :], ptv[0:96, 0:4, :]), [("ps", 7)], [("qTt", i % 2)]))
                return a_, b_

            chains = {}

            def emit_zip(la, lb):
                n = max(len(la), len(lb))
                for k_ in range(n):
                    for lst in (la, lb):
                        if k_ < len(lst):
                            e_, f_, r_, w_ = lst[k_]
                            P.op(e_, f_, R=r_, W=w_)

            def prep_a(i):
                ka, kb = kchain(i)
                qa, qb = qchain(i)
                chains[i] = (kb, qb)
                emit_zip(ka, qa)

            def prep_b(i):
                kb, qb = chains.pop(i)
                emit_zip(kb, qb)

            attention("B", hf, 1,
                      lambda hh, j: kT_b[0:96, hh, j * 128:(j + 1) * 128],
                      lambda hh, i: qTt[i % 2][0:96, hh, :],
                      lambda i: ("qTt", i % 2), vaug, Pt, ytile, float(96.0 ** -0.5), qprep=(prep_a, prep_b),
                      kkey=lambda j: ("kT", j), vkey=lambda j: ("v", j))
            P.barrier()

    def merge_phase(l):
        z = Z0
        mT = bv(z, 8192).rearrange("p (c t) -> p c t", c=8); z += 8192
        wg = [bv(z + k * 3072, 3072).rearrange("p (k n c) -> p k n c", k=8, n=3) for k in range(2)]; z += 6144
        wb = [bv(z + k * 1536, 1536).rearrange("p (k n c) -> p k n c", k=4, n=3) for k in range(2)]; z += 3072
        wo = [bv(z + k * 2048, 2048).rearrange("p (k c) -> p k c", k=8) for k in range(2)]; z += 4096
        gate = [bv(z + k * 512, 512) for k in range(3)]; z += 1536
        acc = fv(z, 512); z += 1024
        tmp = fv(z, 512); z += 1024
        assert z <= AR_N, z
        cnt = 0
        for th in range(2):
            for m in range(8):
                sl = cnt % 2
                cnt += 1
                for n in range(3):
                    c0 = C_G + n * 1024 + m * 128
                    P.dma("pool", dmaf(wg[sl][:, :, n, :], w_in[l, :, c0:c0 + 128].rearrange("(k p) c -> p k c", p=128)), W=[("wg", sl, n)], sem="wg%d_%d" % (sl, n))
                    P.dma("pool", dmaf(wb[sl][:, :, n, :], w_branch[l, n, :, m * 128:(m + 1) * 128].rearrange("(k p) c -> p k c", p=128)), W=[("wb", sl, n)], sem="wb%d_%d" % (sl, n))
                for tq in range(2):
                    tc = th * 2 + tq
                    tsl = slice(tc * 512, (tc + 1) * 512)
                    u = (m * 2 + tq) % 2
                    for n in range(3):
                        gb = u * 4 + n if n < 2 else u * 4 + 2
                        gb = u * 4 + n
                        for kc in range(8):
                            P.op("pe", mm(psb[gb][:], wg[sl][:, kc, n, :], hT[:, kc, tsl], kc == 0, kc == 7), R=[("wg", sl, n), ("hT", tc)], W=[("ps", gb)])
                        P.op("act", act(gate[n], psb[gb][:], AF.Sigmoid, bias=vecT[:, l * 24 + n * 8 + m:l * 24 + n * 8 + m + 1]),
                             R=[("ps", gb), "vecT"], W=[("gate", n)])
                        pb = u * 4 + 3
                        for kc in range(4):
                            P.op("pe", mm(psb[pb][:], wb[sl][:, kc, n, :], yT[:, n, kc, tsl], kc == 0, kc == 3), R=[("wb", sl, n), ("yT", n)], W=[("ps", pb)])
                        if n == 0:
                            P.op("dve", tt(acc, psb[pb][:], gate[n], ALU.mult), R=[("ps", pb), ("gate", n)], W=["acc"])
                        else:
                            P.op("dve", tt(tmp, psb[pb][:], gate[n], ALU.mult), R=[("ps", pb), ("gate", n)], W=["tmp"])
                            if n == 1:
                                P.op("dve", tt(acc, acc, tmp, ALU.add), R=["tmp"], W=["acc"])
                            else:
                                P.op("dve", tt(mT[:, m, tq * 512:(tq + 1) * 512], acc, tmp, ALU.add), R=["tmp", "acc"], W=["mT"])
            for cq in range(4):
                sl = cq % 2
                wload(wo[sl], w_out[l, :, cq * 256:(cq + 1) * 256], ("wo", sl))
                for ii in range(8):
                    i = th * 8 + ii
                    bank = ii % 2 + 6 if False else (ii % 4)
                    for kc in range(8):
                        P.op("pe", mm(psb[bank][:, 0:256], mT[:, kc, ii * 128:(ii + 1) * 128], wo[sl][:, kc, :], kc == 0, kc == 7),
                             R=["mT", ("wo", sl)], W=[("ps", bank)])
                    P.op("dve", tt(xs[:, i, cq * 256:(cq + 1) * 256], xs[:, i, cq * 256:(cq + 1) * 256], psb[bank][:, 0:256], ALU.add),
                         R=[("ps", bank)], W=[("x", i)])
        P.barrier()

    def ffn_phase(l):
        aT = [bv(0, 16384).rearrange("p (c t) -> p c t", c=8), bv(Z0, 16384).rearrange("p (c t) -> p c t", c=8)]
        w2 = [bv(16384 + k * 4096, 4096).rearrange("p (k c) -> p k c", k=8) for k in range(2)]
        z = Z0 + 16384
        w1 = [bv(z + k * 2048, 2048).rearrange("p (k c) -> p k c", k=8) for k in range(2)]; z += 4096
        rt = [fv(z + k * 1024, 512) for k in range(2)]; z += 2048
        assert z <= AR_N, z
        c1 = 0
        c2 = 0
        bk = 0
        for g in range(4):
            a = aT[g % 2]
            for fp in range(4):
                sl = c1 % 2
                c1 += 1
                wload(w1[sl], w_ff1[l, :, g * 1024 + fp * 256: g * 1024 + (fp + 1) * 256], ("w1", sl))
                for f2 in range(2):
                    f = fp * 2 + f2
                    for tc in range(4):
                        bank = bk % 4
                        bk += 1
                        for kc in range(8):
                            P.op("pe", mm(psb[bank][:], w1[sl][:, kc, f2 * 128:(f2 + 1) * 128], hT[:, kc, tc * 512:(tc + 1) * 512], kc == 0, kc == 7),
                                 R=[("w1", sl), ("hT", tc)], W=[("ps", bank)])
                        r = rt[bank % 2]
                        P.op("act", act(r, psb[bank][:], AF.Relu), R=[("ps", bank)], W=[("rt", bank % 2)])
                        P.op(SQ_ENG, tt(a[:, f, tc * 512:(tc + 1) * 512], r, r, ALU.mult), R=[("rt", bank % 2)], W=[("aT", g % 2)])
            for ch in range(2):
                sl = c2 % 2
                c2 += 1
                wload(w2[sl], w_ff2[l, g * 1024:(g + 1) * 1024, ch * 512:(ch + 1) * 512], ("w2", sl))
                for i in range(NT):
                    bank = 4 + (i % 4)
                    for f in range(8):
                        P.op("pe", mm(psb[bank][:], a[:, f, i * 128:(i + 1) * 128], w2[sl][:, f, :], f == 0, f == 7),
                             R=[("aT", g % 2), ("w2", sl)], W=[("ps", bank)])
                    P.op("dve", tt(xs[:, i, ch * 512:(ch + 1) * 512], xs[:, i, ch * 512:(ch + 1) * 512], psb[bank][:], ALU.add),
                         R=[("ps", bank)], W=[("x", i)])
        P.barrier()

    MASK_ENG = "dve"
    SQ_ENG = "pool"
    for l in range(nlayers):
        P.tag = "norm1"
        if "N" in PH:
            norm_phase(norm_mix[l, :])
        for nb_, ch_ in enumerate("ABC"):
            if ch_ not in PH:
                P.op("pool", ms(yT[:, nb_, :, :], 0.0), W=[("yT", nb_)])
        P.tag = "B"
        if "B" in PH:
            phase_B(l)
        P.tag = "A"
        if "A" in PH:
            phase_A(l)
        P.tag = "C"
        if "C" in PH:
            phase_C(l)
        P.tag = "merge"
        if DBG == 1 and l == 0:
            P.barrier()
            P.dma("sp", dmaf(dbg_d, arena[:, 0:24576]), R=[("yT", 0), ("yT", 1), ("yT", 2)], sem="dbg")
            P.barrier()
        if "M" in PH:
            merge_phase(l)
        if "F" in PH:
            P.tag = "norm2"
            norm_phase(norm_ffn[l, :])
            P.tag = "ffn"
            ffn_phase(l)
    for i in range(NT):
        P.dma("sp", dmaf(y_d[i * 128:(i + 1) * 128, :], xs[:, i, :]), R=[("x", i)], sem="y%d" % i)
    P.barrier()
    P.emit(nc, st)
    st.close()
    return nc


def _host_consts(rel_bias):
    half = 16
    inv = (10000.0 ** (-np.arange(half, dtype=np.float32) / half)).astype(np.float32)
    ang = np.arange(S, dtype=np.float32)[:, None] * inv[None, :]
    cs_tab = np.concatenate([np.cos(ang), np.sin(ang)], axis=1).astype(np.float32)
    kl = np.arange(128)[:, None]
    c = np.arange(640)[None, :]
    idx = np.clip(c - kl, -256, 256) + 256
    rel_ext = np.ascontiguousarray(rel_bias[:, :, idx]).reshape(DEPTH * 8, 128, 640).astype(np.float32)
    return cs_tab, rel_ext


PH = "NBACMF"
ANNOT = False
DBG = False
BSTOP = 0
_NC_CACHE = {}


def kernel(**inputs):
    inp = {k: np.ascontiguousarray(np.asarray(v, dtype=np.float32)) for k, v in inputs.items()}
    x = inp.pop("x")
    rel_bias = inp.pop("rel_bias")
    cs_tab, rel_ext = _host_consts(rel_bias)
    key = (DEPTH, PH)
    if key not in _NC_CACHE:
        _NC_CACHE[key] = build(DEPTH)
    nc = _NC_CACHE[key]
    B = x.shape[0]
    in_maps = []
    for b in range(B):
        m = dict(inp)
        m["x"] = np.ascontiguousarray(x[b])
        m["rel_ext"] = rel_ext
        m["cs_tab"] = cs_tab
        in_maps.append(m)
    res = run_bass_kernel_spmd(nc, in_maps, core_ids=list(range(B)))
    return np.stack([np.asarray(r["y"], dtype=np.float32) for r in res.results], axis=0)
```

```python
import numpy as np
from contextlib import ExitStack
import concourse.bass as bass
import concourse.mybir as mybir
from concourse.bass_utils import run_bass_kernel_spmd

F32 = mybir.dt.float32
BF16 = mybir.dt.bfloat16
AF = mybir.ActivationFunctionType
ALU = mybir.AluOpType
AX = mybir.AxisListType

S = 2048
D = 1024
NT = 16
DEPTH = 4
EPS = 1e-6
INW = 6824
C_QA, C_KA, C_VA, C_FA, C_QD, C_KVD, C_KR, C_QC, C_KC, C_VC, C_G = 0, 512, 1024, 1536, 1544, 1928, 2184, 2216, 2728, 3240, 3752


class Prog:
    ENG = ("pe", "act", "dve", "pool", "sp")

    def __init__(self):
        self.ops = {e: [] for e in self.ENG}
        self.state = {}
        self.known = {e: {} for e in self.ENG}
        self.flag = {e: set() for e in self.ENG}
        self.semcnt = {}
        self.tag = ""
        self.tags = {e: [] for e in self.ENG}

    def _add(self, eng, fn, R, W, dma_sem):
        self.tags[eng].append(self.tag)
        need = {}
        for k in R:
            st = self.state.get(k)
            if st and st[0] is not None:
                t = st[0]
                need[t[0]] = max(need.get(t[0], -1), t[1])
        for k in W:
            st = self.state.get(k)
            if st:
                if st[0] is not None:
                    t = st[0]
                    need[t[0]] = max(need.get(t[0], -1), t[1])
                for tk, tv in st[1].items():
                    need[tk] = max(need.get(tk, -1), tv)
        kn = self.known[eng]
        waits = []
        for tk, tv in need.items():
            if dma_sem is None and eng == "pe" and tk == ("E", "pe"):
                continue
            if kn.get(tk, -1) >= tv:
                continue
            kn[tk] = tv
            waits.append((tk, tv))
            if tk[0] == "E":
                self.flag[tk[1]].add(tv)
        idx = len(self.ops[eng])
        if dma_sem is None:
            tok = (("E", eng), idx)
        else:
            c = self.semcnt.get(dma_sem, 0) + 1
            self.semcnt[dma_sem] = c
            tok = (("S", dma_sem), c)
        self.ops[eng].append((fn, waits, dma_sem))
        for k in R:
            if k in W:
                continue
            st = self.state.setdefault(k, [None, {}])
            st[1][tok[0]] = max(st[1].get(tok[0], -1), tok[1])
        for k in W:
            self.state[k] = [tok, {}]
        return tok

    def op(self, eng, fn, R=(), W=()):
        return self._add(eng, fn, list(R), list(W), None)

    def dma(self, q, fn, R=(), W=(), sem=None):
        return self._add(q, fn, list(R), list(W), sem)

    def barrier(self):
        last = {}
        for e in self.ENG:
            n = len(self.ops[e])
            for i in range(n - 1, -1, -1):
                if self.ops[e][i][2] is None and self.ops[e][i][0] is not None:
                    last[("E", e)] = i
                    break
        for s, c in self.semcnt.items():
            last[("S", s)] = c
        for e in self.ENG:
            kn = self.known[e]
            waits = []
            for tk, tv in last.items():
                if kn.get(tk, -1) >= tv:
                    continue
                kn[tk] = tv
                waits.append((tk, tv))
                if tk[0] == "E":
                    self.flag[tk[1]].add(tv)
            if waits:
                self.ops[e].append((None, waits, None))
                self.tags[e].append(self.tag)

    def emit(self, nc, stack):
        rank = {}
        for e in self.ENG:
            rank[e] = {idx: r + 1 for r, idx in enumerate(sorted(self.flag[e]))}
        BS = 2000
        esem = {e: [stack.enter_context(nc.semaphore("es_%s_%d" % (e, b))) for b in range(len(rank[e]) // BS + 1)] for e in self.ENG}
        dsem = {s: stack.enter_context(nc.semaphore("ds_" + str(i))) for i, s in enumerate(self.semcnt)}
        block = stack.enter_context(nc.Block())

        def run(e, h):
            for idx, (fn, waits, ds) in enumerate(self.ops[e]):
                for tk, tv in waits:
                    if tk[0] == "E":
                        r_ = rank[tk[1]][tv] - 1
                        h.wait_ge(esem[tk[1]][r_ // BS], r_ % BS + 1)
                    else:
                        h.wait_ge(dsem[tk[1]], tv * 16)
                if fn is None:
                    continue
                ins = fn(h)
                if ANNOT:
                    ins.annotate(self.tags[e][idx])
                if ds is not None:
                    ins.then_inc(dsem[ds], 16)
                elif idx in rank[e]:
                    ins.then_inc(esem[e][(rank[e][idx] - 1) // BS], 1)

        block.tensor(lambda h: run("pe", h))
        block.scalar(lambda h: run("act", h))
        block.vector(lambda h: run("dve", h))
        block.gpsimd(lambda h: run("pool", h))
        block.sync(lambda h: run("sp", h))


def build(nlayers=DEPTH):
    nc = bass.Bass("TRN2", target_bir_lowering=False)
    P = Prog()
    st = ExitStack()

    def din(name, shape):
        return nc.dram_tensor(name, list(shape), F32, kind="ExternalInput")

    x_d = din("x", [S, D]).ap()
    norm_mix = din("norm_mix", [DEPTH, D]).ap()
    w_in = din("w_in", [DEPTH, D, INW]).ap()
    b_forget = din("b_forget", [DEPTH, 8]).ap()
    b_gate = din("b_gate", [DEPTH, 3072]).ap()
    qk_norm_a = din("qk_norm_a", [DEPTH, 2, 64]).ap()
    mla_q_norm = din("mla_q_norm", [DEPTH, 384]).ap()
    mla_kv_norm = din("mla_kv_norm", [DEPTH, 256]).ap()
    w_q_up = din("w_q_up", [DEPTH, 384, 768]).ap()
    w_kv_up = din("w_kv_up", [DEPTH, 256, 1024]).ap()
    qk_norm_b_nope = din("qk_norm_b_nope", [DEPTH, 2, 64]).ap()
    qk_norm_b_rope = din("qk_norm_b_rope", [DEPTH, 2, 32]).ap()
    qk_norm_c = din("qk_norm_c", [DEPTH, 2, 64]).ap()
    relx = din("rel_ext", [DEPTH * 8, 128, 640]).ap()
    w_branch = din("w_branch", [DEPTH, 3, 512, D]).ap()
    w_out = din("w_out", [DEPTH, D, D]).ap()
    norm_ffn = din("norm_ffn", [DEPTH, D]).ap()
    w_ff1 = din("w_ff1", [DEPTH, D, 4096]).ap()
    w_ff2 = din("w_ff2", [DEPTH, 4096, D]).ap()
    cs_d = din("cs_tab", [S, 32]).ap()
    y_d = nc.dram_tensor("y", [S, D], F32, kind="ExternalOutput").ap()
    dbg_d = nc.dram_tensor("dbg", [128, 24576], BF16, kind="ExternalOutput").ap() if DBG else None

    def sb(name, shape, dt):
        return st.enter_context(nc.sbuf_tensor(name, list(shape), dt))

    xs = sb("xs", [128, NT, D], F32)
    hT = sb("hT", [128, 8, S], BF16)
    ident_bf = sb("ident_bf", [128, 128], BF16)
    ident_f = sb("ident_f", [128, 128], F32)
    blockones = sb("blockones", [128, 128], BF16)
    ones_bf = sb("ones_bf", [128, 128], BF16)
    maskA = sb("maskA", [128, 128], BF16)
    maskB = sb("maskB", [128, 128], BF16)
    U_f = sb("U_f", [128, 128], F32)
    ones_f = sb("ones_f", [128, 128], F32)
    validC = sb("validC", [128, 640], BF16)
    vecT = sb("vecT", [128, 132], F32)
    bf_rep = sb("bf_rep", [128, 32], F32)
    gbn = sb("gbn", [128, 512], F32)
    gbr = sb("gbr", [128, 256], F32)
    cs = sb("cs", [128, NT, 32], F32)
    ssq = sb("ssq", [128, NT], F32)
    rstd = sb("rstd", [128, NT], F32)
    small = sb("small", [128, 64], F32)
    cst = sb("cst", [128, 4], F32)
    AR_N = 52000
    arena = sb("arena", [128, AR_N], BF16)
    psb = [st.enter_context(nc.psum_tensor("ps%d" % b, [128, 512], F32)) for b in range(8)]

    def bv(off, n):
        return arena[:, off:off + n]

    def fv(off, n):
        return arena[:, off:off + 2 * n].bitcast(F32)

    def psbf(b):
        return psb[b][:].bitcast(BF16)

    Z0 = 24576
    yT = bv(0, 24576).rearrange("p (n c t) -> p n c t", n=3, c=4)

    def mm(out, lhsT, rhs, start, stop):
        return lambda e: e.matmul(out, lhsT, rhs, start=start, stop=stop)

    def tr(out, in_, ident):
        return lambda e: e.transpose(out, in_, ident)

    def act(out, in_, func, bias=None, scale=None, accum_out=None):
        kw = {}
        if bias is not None:
            kw["bias"] = bias
        if scale is not None:
            kw["scale"] = scale
        if accum_out is not None:
            kw["accum_out"] = accum_out
        return lambda e: e.activation(out=out, in_=in_, func=func, **kw)

    def tt(out, in0, in1, op):
        return lambda e: e.tensor_tensor(out=out, in0=in0, in1=in1, op=op)

    def ts(out, in0, s1, s2, op0, op1=None):
        if op1 is None:
            return lambda e: e.tensor_single_scalar(out=out, in_=in0, scalar=s1, op=op0)
        return lambda e: e.tensor_scalar(out=out, in0=in0, scalar1=s1, scalar2=s2, op0=op0, op1=op1)

    def stt(out, in0, scalar, in1, op0, op1):
        return lambda e: e.scalar_tensor_tensor(out=out, in0=in0, scalar=scalar, in1=in1, op0=op0, op1=op1)

    def cp(out, in_):
        return lambda e: e.tensor_copy(out=out, in_=in_)

    def rsqrt_small(dst, src, mul, eps, Rk, Wk):
        P.op("dve", ts(dst, src, mul, eps, ALU.mult, ALU.add), R=Rk, W=Wk)
        P.op("act", act(dst, dst, AF.Ln), R=Wk, W=Wk)
        P.op("act", act(dst, dst, AF.Exp, scale=-0.5), R=Wk, W=Wk)

    def red(out, in_):
        return lambda e: e.tensor_reduce(out=out, in_=in_, axis=AX.X, op=ALU.add)

    def ms(ap, v):
        return lambda e: e.memset(ap, v)

    def dmaf(out, in_):
        return lambda e: e.dma_start(out=out, in_=in_)

    def wload(dst, src2d, key, R=(), q="pool"):
        P.dma(q, dmaf(dst, src2d.rearrange("(k p) c -> p k c", p=128)), R=R, W=[key], sem=str(key))

    P.op("pool", ms(ones_f[:], 1.0), W=["ones_f"])
    P.op("pool", lambda e: e.affine_select(out=ident_f[:], in_=ones_f[:], pattern=[[-1, 128]], compare_op=ALU.is_equal,
                                           fill=0.0, base=0, channel_multiplier=1), R=["ones_f"], W=["ident_f"])
    P.op("pool", lambda e: e.affine_select(out=U_f[:], in_=ones_f[:], pattern=[[1, 128]], compare_op=ALU.is_ge,
                                           fill=0.0, base=0, channel_multiplier=-1), R=["ones_f"], W=["U_f"])
    P.op("pool", cp(ident_bf[:], ident_f[:]), R=["ident_f"], W=["ident_bf"])
    P.op("pool", ts(maskA[:], U_f[:], 60000.0, -60000.0, ALU.mult, ALU.add), R=["U_f"], W=["maskA"])
    P.op("pool", ms(ones_bf[:], 1.0), W=["ones_bf"])
    P.op("pool", ms(cst[:, 0:1], 1.0), W=["cst"])
    P.op("pool", ms(small[:, 48:49], EPS), W=["epscol"])
    P.op("pool", ms(cst[:, 1:2], 64 * EPS), W=["cst"])
    P.op("pool", ms(cst[:, 2:3], 384 * EPS), W=["cst"])
    P.op("pool", ms(cst[:, 3:4], 256 * EPS), W=["cst"])
    P.op("pool", ms(blockones[:], 0.0), W=["blockones"])
    P.op("pool", ms(blockones[0:64, 0:64], 1.0), W=["blockones"])
    P.op("pool", ms(blockones[64:128, 64:128], 1.0), W=["blockones"])
    P.op("pool", ms(maskB[:], 1.0), W=["maskB"])
    P.op("pool", ms(maskB[64:128, 0:64], 0.0), W=["maskB"])
    P.op("pool", ms(validC[:], 0.0), W=["validC"])
    P.op("pool", ms(validC[0:64, 0:576], 1.0), W=["validC"])
    P.op("pool", ms(validC[64:128, 64:640], 1.0), W=["validC"])

    stage1 = fv(Z0, 128)
    stage2 = fv(Z0 + 256, 128)
    P.dma("sp", dmaf(stage1[0:96, :], b_gate.rearrange("l (c p) -> (l c) p", p=128)), W=["stage1"], sem="stage1")
    qa2 = qk_norm_a.rearrange("l q d -> (l q) d")
    qc2 = qk_norm_c.rearrange("l q d -> (l q) d")
    P.dma("sp", dmaf(stage2[0:8, 0:64], qa2), W=["stage2a"], sem="stage2")
    P.dma("sp", dmaf(stage2[0:8, 64:128], qa2), W=["stage2b"], sem="stage2")
    P.dma("sp", dmaf(stage2[8:16, 0:64], qc2), W=["stage2c"], sem="stage2")
    P.dma("sp", dmaf(stage2[8:16, 64:128], qc2), W=["stage2d"], sem="stage2")
    P.dma("sp", dmaf(stage2[16:28, :], mla_q_norm.rearrange("l (c p) -> (l c) p", p=128)), W=["stage2e"], sem="stage2")
    P.dma("sp", dmaf(stage2[28:36, :], mla_kv_norm.rearrange("l (c p) -> (l c) p", p=128)), W=["stage2f"], sem="stage2")
    P.op("pe", tr(psb[0][:, 0:96], stage1[0:96, :], ident_f[0:96, 0:96]), R=["stage1", "ident_f"], W=[("ps", 0)])
    P.op("pe", tr(psb[1][:, 0:36], stage2[0:36, :], ident_f[0:36, 0:36]),
         R=["stage2a", "stage2b", "stage2c", "stage2d", "stage2e", "stage2f", "ident_f"], W=[("ps", 1)])
    P.op("dve", cp(vecT[:, 0:96], psb[0][:, 0:96]), R=[("ps", 0)], W=["vecT"])
    P.op("dve", ts(vecT[:, 96:112], psb[1][:, 0:16], 8.0, None, ALU.mult), R=[("ps", 1)], W=["vecT"])
    P.op("dve", ts(vecT[:, 112:124], psb[1][:, 16:28], float(np.sqrt(384.0)), None, ALU.mult), R=[("ps", 1)], W=["vecT"])
    P.op("dve", ts(vecT[:, 124:132], psb[1][:, 28:36], 16.0, None, ALU.mult), R=[("ps", 1)], W=["vecT"])
    P.dma("sp", dmaf(bf_rep[:], b_forget.rearrange("l h -> (l h)").partition_broadcast(128)), W=["bf_rep"], sem="c1")
    P.dma("sp", dmaf(gbn[:], qk_norm_b_nope.rearrange("l q d -> (l q d)").partition_broadcast(128)), W=["gbn"], sem="c2")
    P.dma("sp", dmaf(gbr[:], qk_norm_b_rope.rearrange("l q d -> (l q d)").partition_broadcast(128)), W=["gbr"], sem="c3")
    P.dma("sp", dmaf(cs[:], cs_d.rearrange("(t p) c -> p t c", p=128)), W=["cs"], sem="c4")
    for i in range(NT):
        P.dma("sp", dmaf(xs[:, i, :], x_d[i * 128:(i + 1) * 128, :]), W=[("x", i)], sem="x%d" % i)
    P.barrier()

    def norm_phase(gain_row):
        gnorm = fv(Z0, 1024)
        htok = [bv(Z0 + 2048 + k * 1024, 1024) for k in range(2)]
        junk = bv(Z0 + 4096, 1024)
        P.dma("sp", dmaf(gnorm, gain_row.partition_broadcast(128)), W=["gnorm"], sem="gnorm")
        for i in range(NT):
            P.op("act", act(junk, xs[:, i, :], AF.Square, accum_out=ssq[:, i:i + 1]), R=[("x", i)], W=["junk", "ssq"])
        rsqrt_small(rstd[:], ssq[:], 1.0 / D, EPS, ["ssq"], ["rstd"])
        for i in range(NT):
            P.op("dve", stt(htok[i % 2], xs[:, i, :], rstd[:, i:i + 1], gnorm, ALU.mult, ALU.mult),
                 R=[("x", i), "rstd", "gnorm"], W=[("htok", i % 2)])
            bank = 6 + (i % 2)
            ptv = psbf(bank).rearrange("p (a b) -> p a b", a=8)
            for kc in range(8):
                P.op("pe", tr(ptv[:, kc, :], htok[i % 2][:, kc * 128:(kc + 1) * 128], ident_bf[:]),
                     R=[("htok", i % 2), "ident_bf"], W=[("ps", bank)])
            P.op("act", cp_act(hT[:, :, i * 128:(i + 1) * 128], ptv), R=[("ps", bank)], W=[("hT", i // 4)])
        P.barrier()

    def cp_act(out, in_):
        return lambda e: e.activation(out=out, in_=in_, func=AF.Copy)

    unit_ctr = [0]

    def proj_norm(l, col0, nchunk, onesmat, nfeat, gcol0, outs, okey, wslots, sqb, rsb, gstep=0, split=None):
        for c in range(nchunk):
            wload(wslots[c][0], w_in[l, :, col0 + c * 128: col0 + (c + 1) * 128], wslots[c][1])
        for tc in range(4):
            if nchunk == 1:
                u = unit3_ctr[0] % 3
                unit3_ctr[0] += 1
                braw = [2 * u]
                bssq = 2 * u + 1
                sqs = [sqb[0][u]]
                rs = [rsb[0], rsb[1], rs_extra[0]][u]
                sqk = [("sq3", u)]
                rsk = ("rs3", u)
            else:
                u = unit_ctr[0] % 2
                unit_ctr[0] += 1
                b0 = 4 * u
                braw = [b0 + c for c in range(nchunk)]
                bssq = b0 + 3
                sqs = [sqb[u][c] for c in range(nchunk)]
                rs = rsb[u]
                sqk = [("sq", u, c) for c in range(nchunk)]
                rsk = ("rs", u)
            tsl = slice(tc * 512, (tc + 1) * 512)
            for c in range(nchunk):
                for kc in range(8):
                    P.op("pe", mm(psb[braw[c]][:], wslots[c][0][:, kc, :], hT[:, kc, tsl], kc == 0, kc == 7),
                         R=[wslots[c][1], ("hT", tc)], W=[("ps", braw[c])])
                P.op("act", act(sqs[c], psb[braw[c]][:], AF.Square), R=[("ps", braw[c])], W=[sqk[c]])
            for c in range(nchunk):
                P.op("pe", mm(psb[bssq][:], onesmat[:], sqs[c], c == 0, c == nchunk - 1),
                     R=[sqk[c], "ones_bf", "blockones"], W=[("ps", bssq)])
            ecol = {64: 1, 384: 2, 256: 3}[nfeat]
            P.op("act", act(rs, psb[bssq][:], AF.Ln, bias=cst[:, ecol:ecol + 1]), R=[("ps", bssq), "cst"], W=[rsk])
            P.op("act", act(rs, rs, AF.Exp, scale=-0.5), R=[rsk], W=[rsk])
            for c in range(nchunk):
                if split is not None:
                    for (p0, oap) in ((0, split[0]), (64, split[1])):
                        P.op("dve", stt(oap[p0:p0 + 64, tsl], psb[braw[c]][p0:p0 + 64, :], vecT[p0:p0 + 64, gcol0:gcol0 + 1], rs[p0:p0 + 64, :], ALU.mult, ALU.mult),
                             R=[("ps", braw[c]), rsk, "vecT"], W=[okey])
                    continue
                P.op("dve", stt(outs[c][:, tsl], psb[braw[c]][:], vecT[:, gcol0 + gstep * c:gcol0 + gstep * c + 1], rs, ALU.mult, ALU.mult),
                     R=[("ps", braw[c]), rsk, "vecT"], W=[okey])

    unit3_ctr = [0]
    rs_extra = [None]

    attn_ctr = [0]

    def attention(kind, hf, nbr, kslice, qslice, qkey, vaug, Pt, ytile, scale, biasA=None, bias_prep=None, EBrev=None, qprep=None, qterm=None, kkey=None, vkey=None):
        def finish(i):
            ob = 3 + (i % 2)
            pov = psb[ob][:, 0:260].rearrange("p (h c) -> p h c", h=4)
            rec = small[:, (i % 2) * 4:(i % 2) * 4 + 4]
            P.op("dve", (lambda rec, pov: lambda e: e.reciprocal(out=rec.unsqueeze(2), in_=pov[:, :, 64:65]))(rec, pov),
                 R=[("ps", ob)], W=[("rec", i % 2)])
            yt = ytile[i % 2]
            P.op("dve", tt(yt.rearrange("p (h c) -> p h c", h=4), pov[:, :, 0:64], rec.unsqueeze(2).broadcast_to([128, 4, 64]), ALU.mult),
                 R=[("ps", ob), ("rec", i % 2)], W=[("ytile", i % 2)])
            ptv = psbf(5).rearrange("p (a b) -> p a b", a=8)
            for c in range(2):
                P.op("pe", tr(ptv[:, c, :], yt[:, c * 128:(c + 1) * 128], ident_bf[:]), R=[("ytile", i % 2), "ident_bf"], W=[("ps", 5)])
            P.op("act", cp_act(yT[:, nbr, 2 * hf:2 * hf + 2, i * 128:(i + 1) * 128], ptv[:, 0:2, :]), R=[("ps", 5)], W=[("yT", nbr)])

        if qprep is not None:
            qprep[0](0)
            qprep[1](0)
        if bias_prep is not None:
            bias_prep(0)
        fin_pending = None
        for i in range(NT):
            js = list(range(max(0, i - 4), i + 1)) if kind == "C" else list(range(0, i + 1))
            groups = [js[a:a + 4] for a in range(0, len(js), 4)]
            ob = 3 + (i % 2)
            pov = psb[ob][:, 0:260].rearrange("p (h c) -> p h c", h=4)
            items = [(hh, grp) for hh in range(4) for grp in groups]
            pend = []

            def second(hh, grp, sbk, i=i, js=js, ob=ob, pov=pov):
                n = len(grp)
                if kind == "A":
                    h = 4 * hf + hh
                    for jj, j in enumerate(grp):
                        P.op("act", act(Pt[sbk][:, jj * 128:(jj + 1) * 128], psb[sbk][:, jj * 128:(jj + 1) * 128], AF.Exp,
                                        bias=biasA[i % 2][:, j, h:h + 1], scale=scale),
                             R=[("ps", sbk), ("biasA", i % 2)], W=[("Pt", sbk)])
                else:
                    P.op("act", act(Pt[sbk][:, 0:n * 128], psb[sbk][:, 0:n * 128], AF.Exp, scale=scale),
                         R=[("ps", sbk)], W=[("Pt", sbk)])
                if kind == "C":
                    d0 = 4 - (i - grp[0])
                    P.op(MASK_ENG, tt(Pt[sbk][:, 0:n * 128], Pt[sbk][:, 0:n * 128], EBrev[:, hh, d0 * 128:(d0 + n) * 128], ALU.mult),
                         R=["EBrev"], W=[("Pt", sbk)])
                elif i in grp and kind == "B":
                    jj = grp.index(i)
                    P.op(MASK_ENG, tt(Pt[sbk][:, jj * 128:(jj + 1) * 128], Pt[sbk][:, jj * 128:(jj + 1) * 128], maskB[:], ALU.mult),
                         R=["maskB"], W=[("Pt", sbk)])
                for jj, j in enumerate(grp):
                    P.op("pe", mm(pov[:, hh, :], Pt[sbk][:, jj * 128:(jj + 1) * 128], vaug[:, j, hh, :], j == js[0], j == js[-1]),
                         R=[("Pt", sbk), "v"] + ([vkey(j)] if vkey else []), W=[("ps", ob)])

            cnt = 0
            for (hh, grp) in items:
                sbk = attn_ctr[0] % 3
                attn_ctr[0] += 1
                for jj, j in enumerate(grp):
                    osl = psb[sbk][:, jj * 128:(jj + 1) * 128]
                    P.op("pe", mm(osl, kslice(hh, j), qslice(hh, i), True, qterm is None),
                         R=[(kkey(j) if kkey else "kT"), qkey(i)], W=[("ps", sbk)])
                    if qterm is not None:
                        rq_ap, rq_key = qterm(i, hh)
                        P.op("pe", mm(osl, ones_bf[:], rq_ap, False, j != i), R=[rq_key, "ones_bf"], W=[("ps", sbk)])
                        if j == i:
                            P.op("pe", mm(osl, ident_bf[:], maskA[:], False, True), R=["maskA", "ident_bf"], W=[("ps", sbk)])
                pend.append((hh, grp, sbk))
                cnt += 1
                if cnt == 2:
                    if fin_pending is not None:
                        finish(fin_pending)
                        fin_pending = None
                    if i + 1 < NT:
                        if qprep is not None:
                            qprep[0](i + 1)
                        if bias_prep is not None:
                            bias_prep(i + 1)
                if len(pend) > 2:
                    second(*pend.pop(0))
            while pend:
                second(*pend.pop(0))
            if qprep is not None and i + 1 < NT:
                qprep[1](i + 1)
            fin_pending = i
        finish(fin_pending)

    def attention2(kind, hf, nbr, kslice, qchunk, vaug, Pt, ytile4, scale, biasA=None, prep=None, rqm=None, EB=None):
        OB = [3, 4, 6, 7]

        def finish(c):
            ptv = psbf(5).rearrange("p (a b) -> p a b", a=8)
            for t in range(4):
                ob = OB[t]
                pov = psb[ob][:, 0:260].rearrange("p (h c) -> p h c", h=4)
                rec = small[:, t * 4:t * 4 + 4]
                P.op("dve", (lambda rec, pov: lambda e: e.reciprocal(out=rec.unsqueeze(2), in_=pov[:, :, 64:65]))(rec, pov),
                     R=[("ps", ob)], W=[("rec", t)])
                yt = ytile4[t]
                P.op("dve", tt(yt.rearrange("p (h c) -> p h c", h=4), pov[:, :, 0:64], rec.unsqueeze(2).broadcast_to([128, 4, 64]), ALU.mult),
                     R=[("ps", ob), ("rec", t)], W=[("ytile", t)])
                for c2 in range(2):
                    P.op("pe", tr(ptv[:, c2 * 4 + t, :], yt[:, c2 * 128:(c2 + 1) * 128], ident_bf[:]), R=[("ytile", t), "ident_bf"], W=[("ps", 5)])
            P.op("act", cp_act(yT[:, nbr, 2 * hf:2 * hf + 2, c * 512:(c + 1) * 512], psbf(5).rearrange("p (a b) -> p a b", a=2)),
                 R=[("ps", 5)], W=[("yT", nbr)])

        if prep is not None:
            prep(0)
        for c in range(4):
            j_lo = max(0, 4 * c - 4) if kind == "C" else 0
            items = [(hh, j) for hh in range(4) for j in range(j_lo, 4 * c + 4)]
            pend = []

            def second(hh, j, sb, pb, t0, t1, c=c):
                cols = slice(t0 * 128, (t1 + 1) * 128)
                if kind == "A":
                    h = 4 * hf + hh
                    P.op("act", act(Pt[pb][:, cols], psb[sb][:, cols], AF.Exp, bias=biasA[c % 2][:, j, h:h + 1], scale=scale),
                         R=[("ps", sb), ("biasA", c % 2)], W=[("Pt", pb)])
                else:
                    P.op("act", act(Pt[pb][:, cols], psb[sb][:, cols], AF.Exp, scale=scale), R=[("ps", sb)], W=[("Pt", pb)])
                if kind == "C":
                    d0 = 4 * c + t0 - j
                    P.op(MASK_ENG, tt(Pt[pb][:, cols], Pt[pb][:, cols], EB[:, hh, d0 * 128:(d0 + t1 - t0 + 1) * 128], ALU.mult),
                         R=["EBrev"], W=[("Pt", pb)])
                for t in range(t0, t1 + 1):
                    i = 4 * c + t
                    first_j = max(0, i - 4) if kind == "C" else 0
                    pov = psb[OB[t]][:, 0:260].rearrange("p (h c) -> p h c", h=4)
                    P.op("pe", mm(pov[:, hh, :], Pt[pb][:, t * 128:(t + 1) * 128], vaug[:, j, hh, :], j == first_j, j == i),
                         R=[("Pt", pb), "v"], W=[("ps", OB[t])])

            for (hh, j) in items:
                t0 = max(0, j - 4 * c)
                t1 = 3 if kind != "C" else min(3, j + 4 - 4 * c)
                cols = slice(t0 * 128, (t1 + 1) * 128)
                nsb = 3 if kind == "C" else 2
                sb = attn_ctr[0] % nsb
                pb = attn_ctr[0] % 3
                attn_ctr[0] += 1
                diag = (kind == "A") and j >= 4 * c
                P.op("pe", mm(psb[sb][:, cols], kslice(hh, j), qchunk(c, hh)[:, cols], True, kind != "A"),
                     R=["kT", "qT"], W=[("ps", sb)])
                if kind == "A":
                    P.op("pe", mm(psb[sb][:, cols], ones_bf[:], rqm[:, hh, cols], False, True), R=["rqm", "ones_bf"], W=[("ps", sb)])
                    if diag:
                        P.op("pe", lambda e, o=psb[sb][:, t0 * 128:(t0 + 1) * 128]: e.matmul(o, ident_bf[:], maskA[:], start=False, stop=True, skip_group_check=True),
                             R=["maskA", "ident_bf"], W=[("ps", sb)])
                pend.append((hh, j, sb, pb, t0, t1))
                if len(pend) > nsb - 1:
                    second(*pend.pop(0))
            while pend:
                second(*pend.pop(0))
            if prep is not None and c + 1 < 4:
                prep(c + 1)
            finish(c)

    def v_proj(l, col0, hf, wv_unused, vaug):
        wv2 = bv(ZW_ref[0], 2048).rearrange("p (k c) -> p k c", k=8)
        keys = [("wqk", 0), ("wqk", 1)]
        P.dma("pool", dmaf(wv2, w_in[l, :, col0 + hf * 256: col0 + (hf + 1) * 256].rearrange("(k p) c -> p k c", p=128)), W=keys, sem="wv2")
        for i in range(NT):
            bank = 6 + (i % 2)
            for kc in range(8):
                P.op("pe", mm(psb[bank][:, 0:256], hT[:, kc, i * 128:(i + 1) * 128], wv2[:, kc, :], kc == 0, kc == 7),
                     R=keys + [("hT", i // 4)], W=[("ps", bank)])
            P.op("act", cp_act(vaug[:, i, :, 0:64], psb[bank][:, 0:256].rearrange("p (h c) -> p h c", h=4)), R=[("ps", bank)], W=["v"])

    ZW_ref = [None]
    wqk_ref = [None]

    z = Z0
    ZK = z
    kT_ac = bv(z, 4096).rearrange("p (m t) -> p m t", m=2)
    kTm = bv(z, 8192).rearrange("p (m t) -> p m t", m=4)
    kT_b = bv(z, 8192).rearrange("p (m t) -> p m t", m=4)
    z += 8192
    ZQ = z
    qT_ac = bv(z, 4096).rearrange("p (m t) -> p m t", m=2)
    qTt = [bv(z + k * 512, 512).rearrange("p (h t) -> p h t", h=4) for k in range(2)]
    z += 4096
    vaug = bv(z, 4160).rearrange("p (i h c) -> p i h c", i=NT, h=4)
    z += 4160
    rs_extra[0] = fv(z + 2048, 512)
    sqb = [[bv(z + (u * 3 + c) * 512, 512) for c in range(3)] for u in range(2)]
    Pt = [bv(z + k * 512, 512) for k in range(3)]
    ytile = [bv(z + 1536 + k * 256, 256) for k in range(2)]
    ytile4 = [bv(z + 1536 + k * 256, 256) for k in range(4)]
    z += 3072
    rsb = [fv(z + u * 1024, 512) for u in range(2)]
    z += 2048
    ZW = z
    ZW_ref[0] = z
    wqk = [(bv(z + k * 1024, 1024).rearrange("p (k c) -> p k c", k=8), ("wqk", k)) for k in range(4)]
    z += 4096
    wv = None
    wqk_ref[0] = wqk
    assert z <= AR_N, z

    def set_vones():
        P.op("pool", ms(vaug[:, :, :, 64:65], 1.0), W=["v"])

    def phase_A(l):
        z = 16384
        wf = bv(z, 64).rearrange("p (k c) -> p k c", k=8); z += 64
        zt = fv(z, 128); z += 256
        lp = fv(z, 128); z += 256
        Lp = fv(z, 128).rearrange("p (i h) -> p i h", i=NT); z += 256
        PTt = fv(z, 128).rearrange("p (i h) -> p i h", i=NT); z += 256
        biasA = [fv(z + k * 256, 128).rearrange("p (i h) -> p i h", i=NT) for k in range(2)]; z += 512
        rq_bf = bv(z, 128); z += 128
        rqm = bv(z, 2048).rearrange("p (h t) -> p h t", h=4); z += 2048
        assert z <= 24576, z
        P.op("pool", ms(bv(ZK, 8192), 0.0), W=["kT"])
        loc = zt
        set_vones()
        P.op("pool", ms(rqm, 0.0), W=["rqm"])
        wload(wf, w_in[l, :, C_FA:C_FA + 8], "wf")
        for i in range(NT):
            for kc in range(8):
                P.op("pe", mm(psb[5][:, i * 8:(i + 1) * 8], hT[:, kc, i * 128:(i + 1) * 128], wf[:, kc, :], kc == 0, kc == 7),
                     R=["wf", ("hT", i // 4)], W=[("ps", 5)])
        ztv = zt.rearrange("p (i h) -> p i h", i=NT)
        P.op("dve", tt(ztv, psb[5][:, 0:128].rearrange("p (i h) -> p i h", i=NT),
                       bf_rep[:, l * 8:(l + 1) * 8].unsqueeze(1).broadcast_to([128, NT, 8]), ALU.add), R=[("ps", 5), "bf_rep"], W=["zt"])
        P.op("act", act(zt, zt, AF.Exp, scale=-1.0), R=["zt"], W=["zt"])
        P.op("act", act(lp, zt, AF.Ln, bias=cst[:, 0:1]), R=["zt", "cst"], W=["lp"])
        lpv = lp.rearrange("p (i h) -> p i h", i=NT)
        for i in range(NT):
            P.op("pe", mm(psb[7][:, i * 8:(i + 1) * 8], U_f[:], lpv[:, i, :], True, True), R=["lp", "U_f"], W=[("ps", 7)])
            P.op("pe", mm(psb[6][:, i * 8:(i + 1) * 8], ones_f[:], lpv[:, i, :], True, True), R=["lp", "ones_f"], W=[("ps", 6)])
        P.op("dve", ms(PTt[:, 0, :], 0.0), W=["PT"])
        for i in range(1, NT):
            P.op("dve", tt(PTt[:, i, :], PTt[:, i - 1, :], psb[6][:, (i - 1) * 8:i * 8], ALU.add), R=[("ps", 6)], W=["PT"])
        P.op("dve", tt(Lp, psb[7][:, 0:128].rearrange("p (i h) -> p i h", h=8), PTt, ALU.add), R=[("ps", 7), "PT"], W=["Lp"])
        locv = loc.rearrange("p (i h) -> p i h", i=NT)

        for hf in range(2):
            for c_ in range(2):
                proj_norm(l, C_KA + hf * 256 + c_ * 128, 1, blockones, 64, 96 + 2 * l + 1, [None], "kT", [wqk[c_]], sqb, rsb,
                          split=(kTm[:, 2 * c_, :], kTm[:, 2 * c_ + 1, :]))
            for c_ in range(2):
                proj_norm(l, C_QA + hf * 256 + c_ * 128, 1, blockones, 64, 96 + 2 * l, [qT_ac[:, c_, :]], "qT", [wqk[2 + c_]], sqb, rsb)
            v_proj(l, C_VA, hf, wv, vaug)
            P.barrier()
            def prepA(c, hf=hf):
                P.op("dve", tt(biasA[c % 2][:, 0:4 * c + 4, :], Lp[:, 0:4 * c + 4, :], PTt[:, 4 * c:4 * c + 1, :].broadcast_to([128, 4 * c + 4, 8]), ALU.subtract),
                     R=["Lp", "PT"], W=[("biasA", c % 2)])
                P.op("dve", tt(locv[:, 0:4, :], Lp[:, 4 * c:4 * c + 4, :], PTt[:, 4 * c:4 * c + 1, :].broadcast_to([128, 4, 8]), ALU.subtract),
                     R=["Lp", "PT", "zt"], W=["loc"])
                P.op("dve", ts(rq_bf[:, 0:32], loc[:, 0:32], -8.0, None, ALU.mult), R=["loc"], W=["rq_bf"])
                ptv = psbf(2).rearrange("p (a b) -> p a b", a=8)
                for t in range(4):
                    P.op("pe", tr(ptv[0:8, t, :], rq_bf[:, t * 8:(t + 1) * 8], ident_bf[:]), R=["rq_bf", "ident_bf"], W=[("ps", 2)])
                P.op("dve", tt(rqm[0:8, :, :], psbf(2)[0:8, 0:512].unsqueeze(1).broadcast_to([8, 4, 512]),
                               ident_f[0:8, 4 * hf:4 * hf + 4].unsqueeze(2).broadcast_to([8, 4, 512]), ALU.mult),
                     R=[("ps", 2), "ident_f"], W=["rqm"])

            attention2("A", hf, 0,
                       lambda hh, j: kTm[:, hh, j * 128:(j + 1) * 128],
                       lambda c, hh: qT_ac[:, hh // 2, c * 512:(c + 1) * 512],
                       vaug, Pt, ytile4, 0.125, biasA=biasA, prep=prepA, rqm=rqm)
            P.barrier()
            if DBG == 2 and l == 0 and hf == 1:
                P.dma("sp", dmaf(dbg_d, arena[:, Z0:Z0 + 24576]), sem="dbg")
                P.barrier()

    def phase_C(l):
        EBrev = bv(ZW, 2560).rearrange("p (h c) -> p h c", h=4)
        Tst = [fv(ZW + 2560, 640)]
        set_vones()
        P.op("pool", ms(bv(ZK, 8192), 0.0), W=["kT"])
        for hf in range(2):
            for c_ in range(2):
                proj_norm(l, C_KC + hf * 256 + c_ * 128, 1, blockones, 64, 104 + 2 * l + 1, [None], "kT", [wqk[c_]], sqb, rsb,
                          split=(kTm[:, 2 * c_, :], kTm[:, 2 * c_ + 1, :]))
            for c_ in range(2):
                proj_norm(l, C_QC + hf * 256 + c_ * 128, 1, blockones, 64, 104 + 2 * l, [qT_ac[:, c_, :]], "qT", [wqk[2 + c_]], sqb, rsb)
            v_proj(l, C_VC, hf, wv, vaug)
            P.barrier()
            for hh in range(4):
                h = 4 * hf + hh
                src = relx[l * 8 + h, :, :]
                P.dma("sp", dmaf(Tst[0], src), W=[("Tst", 0)], sem="Tst0")
                P.op("act", act(Tst[0], Tst[0], AF.Exp), R=[("Tst", 0)], W=[("Tst", 0)])
                for d in range(5):
                    P.op("dve", tt(EBrev[:, hh, d * 128:(d + 1) * 128], Tst[0][:, d * 128:(d + 1) * 128], validC[:, d * 128:(d + 1) * 128], ALU.mult),
                         R=[("Tst", 0), "validC"], W=["EBrev"])
            attention2("C", hf, 2,
                       lambda hh, j: kTm[:, hh, j * 128:(j + 1) * 128],
                       lambda c, hh: qT_ac[:, hh // 2, c * 512:(c + 1) * 512],
                       vaug, Pt, ytile4, 0.125, EB=EBrev)
            P.barrier()

    def rope_ops(dst1, dst2, x1, x2, cosv, sinv, t1, t2, Rk, Wk):
        P.op("dve", tt(t1, x1, cosv, ALU.mult), R=Rk, W=["rt1"])
        P.op("dve", tt(t2, x2, sinv, ALU.mult), R=Rk, W=["rt2"])
        P.op("dve", tt(dst1, t1, t2, ALU.subtract), R=["rt1", "rt2"], W=Wk)
        P.op("dve", tt(t1, x2, cosv, ALU.mult), R=Rk, W=["rt1"])
        P.op("dve", tt(t2, x1, sinv, ALU.mult), R=Rk, W=["rt2"])
        P.op("dve", tt(dst2, t1, t2, ALU.add), R=["rt1", "rt2"], W=Wk)

    def phase_B(l):
        z = ZQ + 1024
        wkr = bv(z, 256).rearrange("p (k c) -> p k c", k=8); z += 256
        wqup = bv(z, 1152).rearrange("p (k c) -> p k c", k=3); z += 1152
        wkvup = bv(z, 1024).rearrange("p (k c) -> p k c", k=2); z += 1024
        krope = bv(z, 512).rearrange("p (i c) -> p i c", i=NT); z += 512
        assert z <= ZQ + 4096
        z = 6144
        tmpf = fv(z, 512); z += 1024
        tmpq = fv(z, 384).rearrange("p (h c) -> p h c", h=4); z += 768
        assert z <= 8192
        z = 16384 + 4096
        ktok = [bv(z + k * 512, 512).rearrange("p (h c) -> p h c", h=4) for k in range(2)]; z += 1024
        qtok = [bv(z + k * 512, 512).rearrange("p (h c) -> p h c", h=4) for k in range(2)]; z += 1024
        for k_ in range(2):
            P.op("pool", ms(ktok[k_], 0.0), W=[("ktok", k_)])
            P.op("pool", ms(qtok[k_], 0.0), W=[("qtok", k_)])
        rt1 = fv(z, 256); z += 512
        rt2 = fv(z, 256); z += 512
        kvsb = fv(z, 512); z += 1024
        assert z <= 24576, z
        qdnT = bv(0, 6144).rearrange("p (c t) -> p c t", c=3)
        kvdnT = bv(16384, 4096).rearrange("p (c t) -> p c t", c=2)
        set_vones()
        proj_norm(l, C_QD, 3, ones_bf, 384, 112 + 3 * l, [qdnT[:, c, :] for c in range(3)], "qdnT", wqk[0:3], sqb, rsb, gstep=1)
        proj_norm(l, C_KVD, 2, ones_bf, 256, 124 + 2 * l, [kvdnT[:, c, :] for c in range(2)], "kvdnT", [wqk[3], wqk[0]], sqb, rsb, gstep=1)
        if BSTOP == 1:
            P.op("pool", ms(yT[:, 1, :, :], 0.0), W=[("yT", 1)]); P.barrier(); return
        wload(wkr, w_in[l, :, C_KR:C_KR + 32], "wkr")
        for i in range(NT):
            for kc in range(8):
                P.op("pe", mm(psb[5][:, i * 32:(i + 1) * 32], hT[:, kc, i * 128:(i + 1) * 128], wkr[:, kc, :], kc == 0, kc == 7),
                     R=["wkr", ("hT", i // 4)], W=[("ps", 5)])
        pk = psb[5][:].rearrange("p (i c) -> p i c", i=NT)
        tfv = tmpf.rearrange("p (i c) -> p i c", i=NT)
        sm = small[:, 16:32]
        P.op("act", act(tmpf, psb[5][:], AF.Square), R=[("ps", 5)], W=["tmpf"])
        P.op("dve", red(sm, tfv), R=["tmpf"], W=["sm"])
        rsqrt_small(sm, sm, 1.0 / 32, EPS, ["sm"], ["sm"])
        P.op("dve", tt(tfv, pk, sm.unsqueeze(2).broadcast_to([128, NT, 32]), ALU.mult), R=[("ps", 5), "sm"], W=["tmpf"])
        gk = gbr[:, l * 64 + 32:l * 64 + 64]
        P.op("dve", tt(tfv, tfv, gk.unsqueeze(1).broadcast_to([128, NT, 32]), ALU.mult), R=["tmpf", "gbr"], W=["tmpf"])
        r1 = rt1.rearrange("p (i c) -> p i c", i=NT)
        r2 = rt2.rearrange("p (i c) -> p i c", i=NT)
        rope_ops(krope[:, :, 0:16], krope[:, :, 16:32], tfv[:, :, 0:16], tfv[:, :, 16:32], cs[:, :, 0:16], cs[:, :, 16:32],
                 r1, r2, ["tmpf", "cs"], ["krope"])
        P.barrier()
        if BSTOP == 2:
            P.op("pool", ms(yT[:, 1, :, :], 0.0), W=[("yT", 1)]); P.barrier(); return
        kvK = fv(ZW, 512)
        tK = fv(ZW + 1024, 256).rearrange("p (h c) -> p h c", h=4)
        kvQ = fv(ZW + 1536, 384)
        tQ = fv(ZW + 2304, 384).rearrange("p (h c) -> p h c", h=4)
        gg = fv(ZW + 3072, 64)
        epsc = small[:, 48:49]
        P.op("dve", tt(gg, gbn[:, l * 128:l * 128 + 64], gbn[:, l * 128 + 64:l * 128 + 128], ALU.mult), R=["gbn"], W=["gg"])
        for hf in range(2):
            wload(wkvup, w_kv_up[l, :, hf * 512:(hf + 1) * 512], "wkvup")
            wload(wqup, w_q_up[l, :, hf * 384:(hf + 1) * 384], "wqup")

            def kchain(i):
                a_, b_ = [], []
                for kc in range(2):
                    a_.append(("pe", mm(psb[6][:], kvdnT[:, kc, i * 128:(i + 1) * 128], wkvup[:, kc, :], kc == 0, kc == 1), ["kvdnT", "wkvup"], [("ps", 6)]))
                pkv = kvK.rearrange("p (h c) -> p h c", h=4)
                s4 = small[:, 32:36]
                kt = ktok[i % 2]
                a_.append(("act", cp_act(kvK, psb[6][:]), [("ps", 6)], ["kvK"]))
                a_.append(("act", act(tK, pkv[:, :, 0:64], AF.Square), ["kvK"], ["tK"]))
                a_.append(("dve", red(s4, tK), ["tK"], ["s4"]))
                a_.append(("act", act(s4, s4, AF.Ln, bias=epsc, scale=1.0 / 64), ["s4", "epscol"], ["s4"]))
                a_.append(("act", act(s4, s4, AF.Exp, scale=-0.5), ["s4"], ["s4"]))
                a_.append(("dve", tt(tK, pkv[:, :, 0:64], s4.unsqueeze(2).broadcast_to([128, 4, 64]), ALU.mult), ["kvK", "s4"], ["tK"]))
                a_.append(("dve", tt(kt[:, :, 0:64], tK, gg.unsqueeze(1).broadcast_to([128, 4, 64]), ALU.mult), ["tK", "gg"], [("ktok", i % 2)]))
                a_.append(("dve", cp(kt[:, :, 64:96], krope[:, i, :].unsqueeze(1).broadcast_to([128, 4, 32])), ["krope"], [("ktok", i % 2)]))
                a_.append(("act", cp_act(vaug[:, i, :, 0:64], pkv[:, :, 64:128]), ["kvK"], [("v", i)]))
                ptv = psbf(6).rearrange("p (a b) -> p a b", a=8)
                for hh in range(4):
                    b_.append(("pe", tr(ptv[:, hh, :], kt[:, hh, :], ident_bf[:]), [("ktok", i % 2), "ident_bf"], [("ps", 6)]))
                b_.append(("act", cp_act(kT_b[0:96, :, i * 128:(i + 1) * 128], ptv[0:96, 0:4, :]), [("ps", 6)], [("kT", i)]))
                return a_, b_

            def qchain(i):
                a_, b_ = [], []
                for kc in range(3):
                    a_.append(("pe", mm(psb[7][:, 0:384], qdnT[:, kc, i * 128:(i + 1) * 128], wqup[:, kc, :], kc == 0, kc == 2), ["qdnT", "wqup"], [("ps", 7)]))
                pq = kvQ.rearrange("p (h c) -> p h c", h=4)
                sn = small[:, 36:40]
                sr = small[:, 40:44]
                s8 = small[:, 36:44]
                qt = qtok[i % 2]
                a_.append(("act", cp_act(kvQ, psb[7][:, 0:384]), [("ps", 7)], ["kvQ"]))
                a_.append(("act", act(tQ, pq, AF.Square), ["kvQ"], ["tQ"]))
                a_.append(("dve", red(sn, tQ[:, :, 0:64]), ["tQ"], ["s8"]))
                a_.append(("dve", red(sr, tQ[:, :, 64:96]), ["tQ"], ["s8"]))
                a_.append(("act", act(sn, sn, AF.Ln, bias=epsc, scale=1.0 / 64), ["s8", "epscol"], ["s8"]))
                a_.append(("act", act(sr, sr, AF.Ln, bias=epsc, scale=1.0 / 32), ["s8", "epscol"], ["s8"]))
                a_.append(("act", act(s8, s8, AF.Exp, scale=-0.5), ["s8"], ["s8"]))
                a_.append(("dve", tt(qt[:, :, 0:64], pq[:, :, 0:64], sn.unsqueeze(2).broadcast_to([128, 4, 64]), ALU.mult), ["kvQ", "s8"], [("qtok", i % 2)]))
                a_.append(("dve", tt(tmpq[:, :, 64:96], pq[:, :, 64:96], sr.unsqueeze(2).broadcast_to([128, 4, 32]), ALU.mult), ["kvQ", "s8"], ["tmpq"]))
                gqr = gbr[:, l * 64:l * 64 + 32]
                a_.append(("dve", tt(tmpq[:, :, 64:96], tmpq[:, :, 64:96], gqr.unsqueeze(1).broadcast_to([128, 4, 32]), ALU.mult), ["tmpq", "gbr"], ["tmpq"]))
                c4 = cs[:, i, 0:16].unsqueeze(1).broadcast_to([128, 4, 16])
                s4b = cs[:, i, 16:32].unsqueeze(1).broadcast_to([128, 4, 16])
                q1 = rt1[:, 0:64].rearrange("p (h c) -> p h c", h=4)
                q2 = rt2[:, 0:64].rearrange("p (h c) -> p h c", h=4)
                x1, x2 = tmpq[:, :, 64:80], tmpq[:, :, 80:96]
                Rk, Wk = ["tmpq", "cs"], [("qtok", i % 2)]
                a_.append(("dve", tt(q1, x1, c4, ALU.mult), Rk, ["rt1"]))
                a_.append(("dve", tt(q2, x2, s4b, ALU.mult), Rk, ["rt2"]))
                a_.append(("dve", tt(qt[:, :, 64:80], q1, q2, ALU.subtract), ["rt1", "rt2"], Wk))
                a_.append(("dve", tt(q1, x2, c4, ALU.mult), Rk, ["rt1"]))
                a_.append(("dve", tt(q2, x1, s4b, ALU.mult), Rk, ["rt2"]))
                a_.append(("dve", tt(qt[:, :, 80:96], q1, q2, ALU.add), ["rt1", "rt2"], Wk))
                ptv = psbf(7).rearrange("p (a b) -> p a b", a=8)
                for hh in range(4):
                    b_.append(("pe", tr(ptv[:, hh, :], qt[:, hh, :], ident_bf[:]), [("qtok", i % 2), "ident_bf"], [("ps", 7)]))
                b_.append(("act", cp_act(qTt[i % 2][0:96, :, :], ptv[0:96, 0:4, :]), [("ps", 7)], [("qTt", i % 2)]))
                return a_, b_

            chains = {}

            def emit_zip(la, lb):
                n = max(len(la), len(lb))
                for k_ in range(n):
                    for lst in (la, lb):
                        if k_ < len(lst):
                            e_, f_, r_, w_ = lst[k_]
                            P.op(e_, f_, R=r_, W=w_)

            def prep_a(i):
                ka, kb = kchain(i)
                qa, qb = qchain(i)
                chains[i] = (kb, qb)
                emit_zip(ka, qa)

            def prep_b(i):
                kb, qb = chains.pop(i)
                emit_zip(kb, qb)

            attention("B", hf, 1,
                      lambda hh, j: kT_b[0:96, hh, j * 128:(j + 1) * 128],
                      lambda hh, i: qTt[i % 2][0:96, hh, :],
                      lambda i: ("qTt", i % 2), vaug, Pt, ytile, float(96.0 ** -0.5), qprep=(prep_a, prep_b),
                      kkey=lambda j: ("kT", j), vkey=lambda j: ("v", j))
            P.barrier()

    def merge_phase(l):
        z = Z0
        mT = bv(z, 8192).rearrange("p (c t) -> p c t", c=8); z += 8192
        wg = [bv(z + k * 3072, 3072).rearrange("p (k n c) -> p k n c", k=8, n=3) for k in range(2)]; z += 6144
        wb = [bv(z + k * 1536, 1536).rearrange("p (k n c) -> p k n c", k=4, n=3) for k in range(2)]; z += 3072
        wo = [bv(z + k * 2048, 2048).rearrange("p (k c) -> p k c", k=8) for k in range(2)]; z += 4096
        gate = [bv(z + k * 512, 512) for k in range(3)]; z += 1536
        acc = fv(z, 512); z += 1024
        tmp = fv(z, 512); z += 1024
        assert z <= AR_N, z
        cnt = 0
        for th in range(2):
            for m in range(8):
                sl = cnt % 2
                cnt += 1
                for n in range(3):
                    c0 = C_G + n * 1024 + m * 128
                    P.dma("pool", dmaf(wg[sl][:, :, n, :], w_in[l, :, c0:c0 + 128].rearrange("(k p) c -> p k c", p=128)), W=[("wg", sl, n)], sem="wg%d_%d" % (sl, n))
                    P.dma("pool", dmaf(wb[sl][:, :, n, :], w_branch[l, n, :, m * 128:(m + 1) * 128].rearrange("(k p) c -> p k c", p=128)), W=[("wb", sl, n)], sem="wb%d_%d" % (sl, n))
                for tq in range(2):
                    tc = th * 2 + tq
                    tsl = slice(tc * 512, (tc + 1) * 512)
                    u = (m * 2 + tq) % 2
                    for n in range(3):
                        gb = u * 4 + n if n < 2 else u * 4 + 2
                        gb = u * 4 + n
                        for kc in range(8):
                            P.op("pe", mm(psb[gb][:], wg[sl][:, kc, n, :], hT[:, kc, tsl], kc == 0, kc == 7), R=[("wg", sl, n), ("hT", tc)], W=[("ps", gb)])
                        P.op("act", act(gate[n], psb[gb][:], AF.Sigmoid, bias=vecT[:, l * 24 + n * 8 + m:l * 24 + n * 8 + m + 1]),
                             R=[("ps", gb), "vecT"], W=[("gate", n)])
                        pb = u * 4 + 3
                        for kc in range(4):
                            P.op("pe", mm(psb[pb][:], wb[sl][:, kc, n, :], yT[:, n, kc, tsl], kc == 0, kc == 3), R=[("wb", sl, n), ("yT", n)], W=[("ps", pb)])
                        if n == 0:
                            P.op("dve", tt(acc, psb[pb][:], gate[n], ALU.mult), R=[("ps", pb), ("gate", n)], W=["acc"])
                        else:
                            P.op("dve", tt(tmp, psb[pb][:], gate[n], ALU.mult), R=[("ps", pb), ("gate", n)], W=["tmp"])
                            if n == 1:
                                P.op("dve", tt(acc, acc, tmp, ALU.add), R=["tmp"], W=["acc"])
                            else:
                                P.op("dve", tt(mT[:, m, tq * 512:(tq + 1) * 512], acc, tmp, ALU.add), R=["tmp", "acc"], W=["mT"])
            for cq in range(4):
                sl = cq % 2
                wload(wo[sl], w_out[l, :, cq * 256:(cq + 1) * 256], ("wo", sl))
                for ii in range(8):
                    i = th * 8 + ii
                    bank = ii % 2 + 6 if False else (ii % 4)
                    for kc in range(8):
                        P.op("pe", mm(psb[bank][:, 0:256], mT[:, kc, ii * 128:(ii + 1) * 128], wo[sl][:, kc, :], kc == 0, kc == 7),
                             R=["mT", ("wo", sl)], W=[("ps", bank)])
                    P.op("dve", tt(xs[:, i, cq * 256:(cq + 1) * 256], xs[:, i, cq * 256:(cq + 1) * 256], psb[bank][:, 0:256], ALU.add),
                         R=[("ps", bank)], W=[("x", i)])
        P.barrier()

    def ffn_phase(l):
        aT = [bv(0, 16384).rearrange("p (c t) -> p c t", c=8), bv(Z0, 16384).rearrange("p (c t) -> p c t", c=8)]
        w2 = [bv(16384 + k * 4096, 4096).rearrange("p (k c) -> p k c", k=8) for k in range(2)]
        z = Z0 + 16384
        w1 = [bv(z + k * 2048, 2048).rearrange("p (k c) -> p k c", k=8) for k in range(2)]; z += 4096
        rt = [fv(z + k * 1024, 512) for k in range(2)]; z += 2048
        assert z <= AR_N, z
        c1 = 0
        c2 = 0
        bk = 0
        for g in range(4):
            a = aT[g % 2]
            for fp in range(4):
                sl = c1 % 2
                c1 += 1
                wload(w1[sl], w_ff1[l, :, g * 1024 + fp * 256: g * 1024 + (fp + 1) * 256], ("w1", sl))
                for f2 in range(2):
                    f = fp * 2 + f2
                    for tc in range(4):
                        bank = bk % 4
                        bk += 1
                        for kc in range(8):
                            P.op("pe", mm(psb[bank][:], w1[sl][:, kc, f2 * 128:(f2 + 1) * 128], hT[:, kc, tc * 512:(tc + 1) * 512], kc == 0, kc == 7),
                                 R=[("w1", sl), ("hT", tc)], W=[("ps", bank)])
                        r = rt[bank % 2]
                        P.op("act", act(r, psb[bank][:], AF.Relu), R=[("ps", bank)], W=[("rt", bank % 2)])
                        P.op(SQ_ENG, tt(a[:, f, tc * 512:(tc + 1) * 512], r, r, ALU.mult), R=[("rt", bank % 2)], W=[("aT", g % 2)])
            for ch in range(2):
                sl = c2 % 2
                c2 += 1
                wload(w2[sl], w_ff2[l, g * 1024:(g + 1) * 1024, ch * 512:(ch + 1) * 512], ("w2", sl))
                for i in range(NT):
                    bank = 4 + (i % 4)
                    for f in range(8):
                        P.op("pe", mm(psb[bank][:], a[:, f, i * 128:(i + 1) * 128], w2[sl][:, f, :], f == 0, f == 7),
                             R=[("aT", g % 2), ("w2", sl)], W=[("ps", bank)])
                    P.op("dve", tt(xs[:, i, ch * 512:(ch + 1) * 512], xs[:, i, ch * 512:(ch + 1) * 512], psb[bank][:], ALU.add),
                         R=[("ps", bank)], W=[("x", i)])
        P.barrier()

    MASK_ENG = "dve"
    SQ_ENG = "dve"
    for l in range(nlayers):
        P.tag = "norm1"
        if "N" in PH:
            norm_phase(norm_mix[l, :])
        for nb_, ch_ in enumerate("ABC"):
            if ch_ not in PH:
                P.op("pool", ms(yT[:, nb_, :, :], 0.0), W=[("yT", nb_)])
        P.tag = "B"
        if "B" in PH:
            phase_B(l)
        P.tag = "A"
        if "A" in PH:
            phase_A(l)
        P.tag = "C"
        if "C" in PH:
            phase_C(l)
        P.tag = "merge"
        if DBG == 1 and l == 0:
            P.barrier()
            P.dma("sp", dmaf(dbg_d, arena[:, 0:24576]), R=[("yT", 0), ("yT", 1), ("yT", 2)], sem="dbg")
            P.barrier()
        if "M" in PH:
            merge_phase(l)
        if "F" in PH:
            P.tag = "norm2"
            norm_phase(norm_ffn[l, :])
            P.tag = "ffn"
            ffn_phase(l)
    for i in range(NT):
        P.dma("sp", dmaf(y_d[i * 128:(i + 1) * 128, :], xs[:, i, :]), R=[("x", i)], sem="y%d" % i)
    P.barrier()
    P.emit(nc, st)
    st.close()
    return nc


def _host_consts(rel_bias):
    half = 16
    inv = (10000.0 ** (-np.arange(half, dtype=np.float32) / half)).astype(np.float32)
    ang = np.arange(S, dtype=np.float32)[:, None] * inv[None, :]
    cs_tab = np.concatenate([np.cos(ang), np.sin(ang)], axis=1).astype(np.float32)
    kl = np.arange(128)[:, None]
    c = np.arange(640)[None, :]
    idx = np.clip(c - kl, -256, 256) + 256
    rel_ext = np.ascontiguousarray(rel_bias[:, :, idx]).reshape(DEPTH * 8, 128, 640).astype(np.float32)
    return cs_tab, rel_ext


PH = "NBACMF"
ANNOT = False
DBG = False
BSTOP = 0
_NC_CACHE = {}


def kernel(**inputs):
    inp = {k: np.ascontiguousarray(np.asarray(v, dtype=np.float32)) for k, v in inputs.items()}
    x = inp.pop("x")
    rel_bias = inp.pop("rel_bias")
    cs_tab, rel_ext = _host_consts(rel_bias)
    key = (DEPTH, PH)
    if key not in _NC_CACHE:
        _NC_CACHE[key] = build(DEPTH)
    nc = _NC_CACHE[key]
    B = x.shape[0]
    in_maps = []
    for b in range(B):
        m = dict(inp)
        m["x"] = np.ascontiguousarray(x[b])
        m["rel_ext"] = rel_ext
        m["cs_tab"] = cs_tab
        in_maps.append(m)
    res = run_bass_kernel_spmd(nc, in_maps, core_ids=list(range(B)))
    return np.stack([np.asarray(r["y"], dtype=np.float32) for r in res.results], axis=0)
```

```python
import numpy as np
from contextlib import ExitStack
import concourse.bass as bass
import concourse.mybir as mybir
from concourse.bass_utils import run_bass_kernel_spmd

F32 = mybir.dt.float32
BF16 = mybir.dt.bfloat16
AF = mybir.ActivationFunctionType
ALU = mybir.AluOpType
AX = mybir.AxisListType

S = 2048
D = 1024
NT = 16
DEPTH = 4
EPS = 1e-6
INW = 6824
C_QA, C_KA, C_VA, C_FA, C_QD, C_KVD, C_KR, C_QC, C_KC, C_VC, C_G = 0, 512, 1024, 1536, 1544, 1928, 2184, 2216, 2728, 3240, 3752


class Prog:
    ENG = ("pe", "act", "dve", "pool", "sp")

    def __init__(self):
        self.ops = {e: [] for e in self.ENG}
        self.state = {}
        self.known = {e: {} for e in self.ENG}
        self.flag = {e: set() for e in self.ENG}
        self.semcnt = {}
        self.tag = ""
        self.tags = {e: [] for e in self.ENG}

    def _add(self, eng, fn, R, W, dma_sem):
        self.tags[eng].append(self.tag)
        need = {}
        for k in R:
            st = self.state.get(k)
            if st and st[0] is not None:
                t = st[0]
                need[t[0]] = max(need.get(t[0], -1), t[1])
        for k in W:
            st = self.state.get(k)
            if st:
                if st[0] is not None:
                    t = st[0]
                    need[t[0]] = max(need.get(t[0], -1), t[1])
                for tk, tv in st[1].items():
                    need[tk] = max(need.get(tk, -1), tv)
        kn = self.known[eng]
        waits = []
        for tk, tv in need.items():
            if dma_sem is None and eng == "pe" and tk == ("E", "pe"):
                continue
            if kn.get(tk, -1) >= tv:
                continue
            kn[tk] = tv
            waits.append((tk, tv))
            if tk[0] == "E":
                self.flag[tk[1]].add(tv)
        idx = len(self.ops[eng])
        if dma_sem is None:
            tok = (("E", eng), idx)
        else:
            c = self.semcnt.get(dma_sem, 0) + 1
            self.semcnt[dma_sem] = c
            tok = (("S", dma_sem), c)
        self.ops[eng].append((fn, waits, dma_sem))
        for k in R:
            if k in W:
                continue
            st = self.state.setdefault(k, [None, {}])
            st[1][tok[0]] = max(st[1].get(tok[0], -1), tok[1])
        for k in W:
            self.state[k] = [tok, {}]
        return tok

    def op(self, eng, fn, R=(), W=()):
        return self._add(eng, fn, list(R), list(W), None)

    def dma(self, q, fn, R=(), W=(), sem=None):
        return self._add(q, fn, list(R), list(W), sem)

    def barrier(self):
        last = {}
        for e in self.ENG:
            n = len(self.ops[e])
            for i in range(n - 1, -1, -1):
                if self.ops[e][i][2] is None and self.ops[e][i][0] is not None:
                    last[("E", e)] = i
                    break
        for s, c in self.semcnt.items():
            last[("S", s)] = c
        for e in self.ENG:
            kn = self.known[e]
            waits = []
            for tk, tv in last.items():
                if kn.get(tk, -1) >= tv:
                    continue
                kn[tk] = tv
                waits.append((tk, tv))
                if tk[0] == "E":
                    self.flag[tk[1]].add(tv)
            if waits:
                self.ops[e].append((None, waits, None))
                self.tags[e].append(self.tag)

    def emit(self, nc, stack):
        rank = {}
        for e in self.ENG:
            rank[e] = {idx: r + 1 for r, idx in enumerate(sorted(self.flag[e]))}
        BS = 2000
        esem = {e: [stack.enter_context(nc.semaphore("es_%s_%d" % (e, b))) for b in range(len(rank[e]) // BS + 1)] for e in self.ENG}
        dsem = {s: stack.enter_context(nc.semaphore("ds_" + str(i))) for i, s in enumerate(self.semcnt)}
        block = stack.enter_context(nc.Block())

        def run(e, h):
            for idx, (fn, waits, ds) in enumerate(self.ops[e]):
                for tk, tv in waits:
                    if tk[0] == "E":
                        r_ = rank[tk[1]][tv] - 1
                        h.wait_ge(esem[tk[1]][r_ // BS], r_ % BS + 1)
                    else:
                        h.wait_ge(dsem[tk[1]], tv * 16)
                if fn is None:
                    continue
                ins = fn(h)
                if ANNOT:
                    ins.annotate(self.tags[e][idx])
                if ds is not None:
                    ins.then_inc(dsem[ds], 16)
                elif idx in rank[e]:
                    ins.then_inc(esem[e][(rank[e][idx] - 1) // BS], 1)

        block.tensor(lambda h: run("pe", h))
        block.scalar(lambda h: run("act", h))
        block.vector(lambda h: run("dve", h))
        block.gpsimd(lambda h: run("pool", h))
        block.sync(lambda h: run("sp", h))


def build(nlayers=DEPTH):
    nc = bass.Bass("TRN2", target_bir_lowering=False)
    P = Prog()
    st = ExitStack()

    def din(name, shape):
        return nc.dram_tensor(name, list(shape), F32, kind="ExternalInput")

    x_d = din("x", [S, D]).ap()
    norm_mix = din("norm_mix", [DEPTH, D]).ap()
    w_in = din("w_in", [DEPTH, D, INW]).ap()
    b_forget = din("b_forget", [DEPTH, 8]).ap()
    b_gate = din("b_gate", [DEPTH, 3072]).ap()
    qk_norm_a = din("qk_norm_a", [DEPTH, 2, 64]).ap()
    mla_q_norm = din("mla_q_norm", [DEPTH, 384]).ap()
    mla_kv_norm = din("mla_kv_norm", [DEPTH, 256]).ap()
    w_q_up = din("w_q_up", [DEPTH, 384, 768]).ap()
    w_kv_up = din("w_kv_up", [DEPTH, 256, 1024]).ap()
    qk_norm_b_nope = din("qk_norm_b_nope", [DEPTH, 2, 64]).ap()
    qk_norm_b_rope = din("qk_norm_b_rope", [DEPTH, 2, 32]).ap()
    qk_norm_c = din("qk_norm_c", [DEPTH, 2, 64]).ap()
    relx = din("rel_ext", [DEPTH * 8, 128, 640]).ap()
    w_branch = din("w_branch", [DEPTH, 3, 512, D]).ap()
    w_out = din("w_out", [DEPTH, D, D]).ap()
    norm_ffn = din("norm_ffn", [DEPTH, D]).ap()
    w_ff1 = din("w_ff1", [DEPTH, D, 4096]).ap()
    w_ff2 = din("w_ff2", [DEPTH, 4096, D]).ap()
    cs_d = din("cs_tab", [S, 32]).ap()
    y_d = nc.dram_tensor("y", [S, D], F32, kind="ExternalOutput").ap()
    dbg_d = nc.dram_tensor("dbg", [128, 24576], BF16, kind="ExternalOutput").ap() if DBG else None

    def sb(name, shape, dt):
        return st.enter_context(nc.sbuf_tensor(name, list(shape), dt))

    xs = sb("xs", [128, NT, D], F32)
    hT = sb("hT", [128, 8, S], BF16)
    ident_bf = sb("ident_bf", [128, 128], BF16)
    ident_f = sb("ident_f", [128, 128], F32)
    blockones = sb("blockones", [128, 128], BF16)
    ones_bf = sb("ones_bf", [128, 128], BF16)
    maskA = sb("maskA", [128, 128], BF16)
    maskB = sb("maskB", [128, 128], BF16)
    U_f = sb("U_f", [128, 128], F32)
    ones_f = sb("ones_f", [128, 128], F32)
    validC = sb("validC", [128, 640], BF16)
    vecT = sb("vecT", [128, 132], F32)
    bf_rep = sb("bf_rep", [128, 32], F32)
    gbn = sb("gbn", [128, 512], F32)
    gbr = sb("gbr", [128, 256], F32)
    cs = sb("cs", [128, NT, 32], F32)
    ssq = sb("ssq", [128, NT], F32)
    rstd = sb("rstd", [128, NT], F32)
    small = sb("small", [128, 64], F32)
    cst = sb("cst", [128, 4], F32)
    AR_N = 52000
    arena = sb("arena", [128, AR_N], BF16)
    psb = [st.enter_context(nc.psum_tensor("ps%d" % b, [128, 512], F32)) for b in range(8)]

    def bv(off, n):
        return arena[:, off:off + n]

    def fv(off, n):
        return arena[:, off:off + 2 * n].bitcast(F32)

    def psbf(b):
        return psb[b][:].bitcast(BF16)

    Z0 = 24576
    yT = bv(0, 24576).rearrange("p (n c t) -> p n c t", n=3, c=4)

    def mm(out, lhsT, rhs, start, stop):
        return lambda e: e.matmul(out, lhsT, rhs, start=start, stop=stop)

    def tr(out, in_, ident):
        return lambda e: e.transpose(out, in_, ident)

    def act(out, in_, func, bias=None, scale=None, accum_out=None):
        kw = {}
        if bias is not None:
            kw["bias"] = bias
        if scale is not None:
            kw["scale"] = scale
        if accum_out is not None:
            kw["accum_out"] = accum_out
        return lambda e: e.activation(out=out, in_=in_, func=func, **kw)

    def tt(out, in0, in1, op):
        return lambda e: e.tensor_tensor(out=out, in0=in0, in1=in1, op=op)

    def ts(out, in0, s1, s2, op0, op1=None):
        if op1 is None:
            return lambda e: e.tensor_single_scalar(out=out, in_=in0, scalar=s1, op=op0)
        return lambda e: e.tensor_scalar(out=out, in0=in0, scalar1=s1, scalar2=s2, op0=op0, op1=op1)

    def stt(out, in0, scalar, in1, op0, op1):
        return lambda e: e.scalar_tensor_tensor(out=out, in0=in0, scalar=scalar, in1=in1, op0=op0, op1=op1)

    def cp(out, in_):
        return lambda e: e.tensor_copy(out=out, in_=in_)

    def rsqrt_small(dst, src, mul, eps, Rk, Wk):
        P.op("dve", ts(dst, src, mul, eps, ALU.mult, ALU.add), R=Rk, W=Wk)
        P.op("act", act(dst, dst, AF.Ln), R=Wk, W=Wk)
        P.op("act", act(dst, dst, AF.Exp, scale=-0.5), R=Wk, W=Wk)

    def red(out, in_):
        return lambda e: e.tensor_reduce(out=out, in_=in_, axis=AX.X, op=ALU.add)

    def ms(ap, v):
        return lambda e: e.memset(ap, v)

    def dmaf(out, in_):
        return lambda e: e.dma_start(out=out, in_=in_)

    def wload(dst, src2d, key, R=(), q="pool"):
        P.dma(q, dmaf(dst, src2d.rearrange("(k p) c -> p k c", p=128)), R=R, W=[key], sem=str(key))

    P.op("pool", ms(ones_f[:], 1.0), W=["ones_f"])
    P.op("pool", lambda e: e.affine_select(out=ident_f[:], in_=ones_f[:], pattern=[[-1, 128]], compare_op=ALU.is_equal,
                                           fill=0.0, base=0, channel_multiplier=1), R=["ones_f"], W=["ident_f"])
    P.op("pool", lambda e: e.affine_select(out=U_f[:], in_=ones_f[:], pattern=[[1, 128]], compare_op=ALU.is_ge,
                                           fill=0.0, base=0, channel_multiplier=-1), R=["ones_f"], W=["U_f"])
    P.op("pool", cp(ident_bf[:], ident_f[:]), R=["ident_f"], W=["ident_bf"])
    P.op("pool", ts(maskA[:], U_f[:], 60000.0, -60000.0, ALU.mult, ALU.add), R=["U_f"], W=["maskA"])
    P.op("pool", ms(ones_bf[:], 1.0), W=["ones_bf"])
    P.op("pool", ms(cst[:, 0:1], 1.0), W=["cst"])
    P.op("pool", ms(small[:, 48:49], EPS), W=["epscol"])
    P.op("pool", ms(cst[:, 1:2], 64 * EPS), W=["cst"])
    P.op("pool", ms(cst[:, 2:3], 384 * EPS), W=["cst"])
    P.op("pool", ms(cst[:, 3:4], 256 * EPS), W=["cst"])
    P.op("pool", ms(blockones[:], 0.0), W=["blockones"])
    P.op("pool", ms(blockones[0:64, 0:64], 1.0), W=["blockones"])
    P.op("pool", ms(blockones[64:128, 64:128], 1.0), W=["blockones"])
    P.op("pool", ms(maskB[:], 1.0), W=["maskB"])
    P.op("pool", ms(maskB[64:128, 0:64], 0.0), W=["maskB"])
    P.op("pool", ms(validC[:], 0.0), W=["validC"])
    P.op("pool", ms(validC[0:64, 0:576], 1.0), W=["validC"])
    P.op("pool", ms(validC[64:128, 64:640], 1.0), W=["validC"])

    stage1 = fv(Z0, 128)
    stage2 = fv(Z0 + 256, 128)
    P.dma("sp", dmaf(stage1[0:96, :], b_gate.rearrange("l (c p) -> (l c) p", p=128)), W=["stage1"], sem="stage1")
    qa2 = qk_norm_a.rearrange("l q d -> (l q) d")
    qc2 = qk_norm_c.rearrange("l q d -> (l q) d")
    P.dma("sp", dmaf(stage2[0:8, 0:64], qa2), W=["stage2a"], sem="stage2")
    P.dma("sp", dmaf(stage2[0:8, 64:128], qa2), W=["stage2b"], sem="stage2")
    P.dma("sp", dmaf(stage2[8:16, 0:64], qc2), W=["stage2c"], sem="stage2")
    P.dma("sp", dmaf(stage2[8:16, 64:128], qc2), W=["stage2d"], sem="stage2")
    P.dma("sp", dmaf(stage2[16:28, :], mla_q_norm.rearrange("l (c p) -> (l c) p", p=128)), W=["stage2e"], sem="stage2")
    P.dma("sp", dmaf(stage2[28:36, :], mla_kv_norm.rearrange("l (c p) -> (l c) p", p=128)), W=["stage2f"], sem="stage2")
    P.op("pe", tr(psb[0][:, 0:96], stage1[0:96, :], ident_f[0:96, 0:96]), R=["stage1", "ident_f"], W=[("ps", 0)])
    P.op("pe", tr(psb[1][:, 0:36], stage2[0:36, :], ident_f[0:36, 0:36]),
         R=["stage2a", "stage2b", "stage2c", "stage2d", "stage2e", "stage2f", "ident_f"], W=[("ps", 1)])
    P.op("dve", cp(vecT[:, 0:96], psb[0][:, 0:96]), R=[("ps", 0)], W=["vecT"])
    P.op("dve", ts(vecT[:, 96:112], psb[1][:, 0:16], 8.0, None, ALU.mult), R=[("ps", 1)], W=["vecT"])
    P.op("dve", ts(vecT[:, 112:124], psb[1][:, 16:28], float(np.sqrt(384.0)), None, ALU.mult), R=[("ps", 1)], W=["vecT"])
    P.op("dve", ts(vecT[:, 124:132], psb[1][:, 28:36], 16.0, None, ALU.mult), R=[("ps", 1)], W=["vecT"])
    P.dma("sp", dmaf(bf_rep[:], b_forget.rearrange("l h -> (l h)").partition_broadcast(128)), W=["bf_rep"], sem="c1")
    P.dma("sp", dmaf(gbn[:], qk_norm_b_nope.rearrange("l q d -> (l q d)").partition_broadcast(128)), W=["gbn"], sem="c2")
    P.dma("sp", dmaf(gbr[:], qk_norm_b_rope.rearrange("l q d -> (l q d)").partition_broadcast(128)), W=["gbr"], sem="c3")
    P.dma("sp", dmaf(cs[:], cs_d.rearrange("(t p) c -> p t c", p=128)), W=["cs"], sem="c4")
    for i in range(NT):
        P.dma("sp", dmaf(xs[:, i, :], x_d[i * 128:(i + 1) * 128, :]), W=[("x", i)], sem="x%d" % i)
    P.barrier()

    def norm_phase(gain_row):
        gnorm = fv(Z0, 1024)
        htok = [bv(Z0 + 2048 + k * 1024, 1024) for k in range(2)]
        junk = bv(Z0 + 4096, 1024)
        P.dma("sp", dmaf(gnorm, gain_row.partition_broadcast(128)), W=["gnorm"], sem="gnorm")
        for i in range(NT):
            P.op("act", act(junk, xs[:, i, :], AF.Square, accum_out=ssq[:, i:i + 1]), R=[("x", i)], W=["junk", "ssq"])
        rsqrt_small(rstd[:], ssq[:], 1.0 / D, EPS, ["ssq"], ["rstd"])
        for i in range(NT):
            P.op("dve", stt(htok[i % 2], xs[:, i, :], rstd[:, i:i + 1], gnorm, ALU.mult, ALU.mult),
                 R=[("x", i), "rstd", "gnorm"], W=[("htok", i % 2)])
            bank = 6 + (i % 2)
            ptv = psbf(bank).rearrange("p (a b) -> p a b", a=8)
            for kc in range(8):
                P.op("pe", tr(ptv[:, kc, :], htok[i % 2][:, kc * 128:(kc + 1) * 128], ident_bf[:]),
                     R=[("htok", i % 2), "ident_bf"], W=[("ps", bank)])
            P.op("act", cp_act(hT[:, :, i * 128:(i + 1) * 128], ptv), R=[("ps", bank)], W=[("hT", i // 4)])
        P.barrier()

    def cp_act(out, in_):
        return lambda e: e.activation(out=out, in_=in_, func=AF.Copy)

    unit_ctr = [0]

    def proj_norm(l, col0, nchunk, onesmat, nfeat, gcol0, outs, okey, wslots, sqb, rsb, gstep=0, split=None):
        for c in range(nchunk):
            wload(wslots[c][0], w_in[l, :, col0 + c * 128: col0 + (c + 1) * 128], wslots[c][1])
        for tc in range(4):
            if nchunk == 1:
                u = unit3_ctr[0] % 3
                unit3_ctr[0] += 1
                braw = [2 * u]
                bssq = 2 * u + 1
                sqs = [sqb[0][u]]
                rs = [rsb[0], rsb[1], rs_extra[0]][u]
                sqk = [("sq3", u)]
                rsk = ("rs3", u)
            else:
                u = unit_ctr[0] % 2
                unit_ctr[0] += 1
                b0 = 4 * u
                braw = [b0 + c for c in range(nchunk)]
                bssq = b0 + 3
                sqs = [sqb[u][c] for c in range(nchunk)]
                rs = rsb[u]
                sqk = [("sq", u, c) for c in range(nchunk)]
                rsk = ("rs", u)
            tsl = slice(tc * 512, (tc + 1) * 512)
            for c in range(nchunk):
                for kc in range(8):
                    P.op("pe", mm(psb[braw[c]][:], wslots[c][0][:, kc, :], hT[:, kc, tsl], kc == 0, kc == 7),
                         R=[wslots[c][1], ("hT", tc)], W=[("ps", braw[c])])
                P.op("act", act(sqs[c], psb[braw[c]][:], AF.Square), R=[("ps", braw[c])], W=[sqk[c]])
            for c in range(nchunk):
                P.op("pe", mm(psb[bssq][:], onesmat[:], sqs[c], c == 0, c == nchunk - 1),
                     R=[sqk[c], "ones_bf", "blockones"], W=[("ps", bssq)])
            ecol = {64: 1, 384: 2, 256: 3}[nfeat]
            P.op("act", act(rs, psb[bssq][:], AF.Ln, bias=cst[:, ecol:ecol + 1]), R=[("ps", bssq), "cst"], W=[rsk])
            P.op("act", act(rs, rs, AF.Exp, scale=-0.5), R=[rsk], W=[rsk])
            for c in range(nchunk):
                if split is not None:
                    for (p0, oap) in ((0, split[0]), (64, split[1])):
                        P.op("dve", stt(oap[p0:p0 + 64, tsl], psb[braw[c]][p0:p0 + 64, :], vecT[p0:p0 + 64, gcol0:gcol0 + 1], rs[p0:p0 + 64, :], ALU.mult, ALU.mult),
                             R=[("ps", braw[c]), rsk, "vecT"], W=[okey])
                    continue
                P.op("dve", stt(outs[c][:, tsl], psb[braw[c]][:], vecT[:, gcol0 + gstep * c:gcol0 + gstep * c + 1], rs, ALU.mult, ALU.mult),
                     R=[("ps", braw[c]), rsk, "vecT"], W=[okey])

    unit3_ctr = [0]
    rs_extra = [None]

    attn_ctr = [0]

    def attention(kind, hf, nbr, kslice, qslice, qkey, vaug, Pt, ytile, scale, biasA=None, bias_prep=None, EBrev=None, qprep=None, qterm=None, kkey=None, vkey=None):
        def finish(i):
            ob = 3 + (i % 2)
            pov = psb[ob][:, 0:260].rearrange("p (h c) -> p h c", h=4)
            rec = small[:, (i % 2) * 4:(i % 2) * 4 + 4]
            P.op("dve", (lambda rec, pov: lambda e: e.reciprocal(out=rec.unsqueeze(2), in_=pov[:, :, 64:65]))(rec, pov),
                 R=[("ps", ob)], W=[("rec", i % 2)])
            yt = ytile[i % 2]
            P.op("dve", tt(yt.rearrange("p (h c) -> p h c", h=4), pov[:, :, 0:64], rec.unsqueeze(2).broadcast_to([128, 4, 64]), ALU.mult),
                 R=[("ps", ob), ("rec", i % 2)], W=[("ytile", i % 2)])
            ptv = psbf(5).rearrange("p (a b) -> p a b", a=8)
            for c in range(2):
                P.op("pe", tr(ptv[:, c, :], yt[:, c * 128:(c + 1) * 128], ident_bf[:]), R=[("ytile", i % 2), "ident_bf"], W=[("ps", 5)])
            P.op("act", cp_act(yT[:, nbr, 2 * hf:2 * hf + 2, i * 128:(i + 1) * 128], ptv[:, 0:2, :]), R=[("ps", 5)], W=[("yT", nbr)])

        if qprep is not None:
            qprep[0](0)
            qprep[1](0)
        if bias_prep is not None:
            bias_prep(0)
        fin_pending = None
        for i in range(NT):
            js = list(range(max(0, i - 4), i + 1)) if kind == "C" else list(range(0, i + 1))
            groups = [js[a:a + 4] for a in range(0, len(js), 4)]
            ob = 3 + (i % 2)
            pov = psb[ob][:, 0:260].rearrange("p (h c) -> p h c", h=4)
            items = [(hh, grp) for hh in range(4) for grp in groups]
            pend = []

            def second(hh, grp, sbk, i=i, js=js, ob=ob, pov=pov):
                n = len(grp)
                if kind == "A":
                    h = 4 * hf + hh
                    for jj, j in enumerate(grp):
                        P.op("act", act(Pt[sbk][:, jj * 128:(jj + 1) * 128], psb[sbk][:, jj * 128:(jj + 1) * 128], AF.Exp,
                                        bias=biasA[i % 2][:, j, h:h + 1], scale=scale),
                             R=[("ps", sbk), ("biasA", i % 2)], W=[("Pt", sbk)])
                else:
                    P.op("act", act(Pt[sbk][:, 0:n * 128], psb[sbk][:, 0:n * 128], AF.Exp, scale=scale),
                         R=[("ps", sbk)], W=[("Pt", sbk)])
                if kind == "C":
                    d0 = 4 - (i - grp[0])
                    P.op(MASK_ENG, tt(Pt[sbk][:, 0:n * 128], Pt[sbk][:, 0:n * 128], EBrev[:, hh, d0 * 128:(d0 + n) * 128], ALU.mult),
                         R=["EBrev"], W=[("Pt", sbk)])
                elif i in grp and kind == "B":
                    jj = grp.index(i)
                    P.op(MASK_ENG, tt(Pt[sbk][:, jj * 128:(jj + 1) * 128], Pt[sbk][:, jj * 128:(jj + 1) * 128], maskB[:], ALU.mult),
                         R=["maskB"], W=[("Pt", sbk)])
                for jj, j in enumerate(grp):
                    P.op("pe", mm(pov[:, hh, :], Pt[sbk][:, jj * 128:(jj + 1) * 128], vaug[:, j, hh, :], j == js[0], j == js[-1]),
                         R=[("Pt", sbk), "v"] + ([vkey(j)] if vkey else []), W=[("ps", ob)])

            cnt = 0
            for (hh, grp) in items:
                sbk = attn_ctr[0] % 3
                attn_ctr[0] += 1
                for jj, j in enumerate(grp):
                    osl = psb[sbk][:, jj * 128:(jj + 1) * 128]
                    P.op("pe", mm(osl, kslice(hh, j), qslice(hh, i), True, qterm is None),
                         R=[(kkey(j) if kkey else "kT"), qkey(i)], W=[("ps", sbk)])
                    if qterm is not None:
                        rq_ap, rq_key = qterm(i, hh)
                        P.op("pe", mm(osl, ones_bf[:], rq_ap, False, j != i), R=[rq_key, "ones_bf"], W=[("ps", sbk)])
                        if j == i:
                            P.op("pe", mm(osl, ident_bf[:], maskA[:], False, True), R=["maskA", "ident_bf"], W=[("ps", sbk)])
                pend.append((hh, grp, sbk))
                cnt += 1
                if cnt == 2:
                    if fin_pending is not None:
                        finish(fin_pending)
                        fin_pending = None
                    if i + 1 < NT:
                        if qprep is not None:
                            qprep[0](i + 1)
                        if bias_prep is not None:
                            bias_prep(i + 1)
                if len(pend) > 2:
                    second(*pend.pop(0))
            while pend:
                second(*pend.pop(0))
            if qprep is not None and i + 1 < NT:
                qprep[1](i + 1)
            fin_pending = i
        finish(fin_pending)

    def attention2(kind, hf, nbr, kslice, qchunk, vaug, Pt, ytile4, scale, biasA=None, prep=None, rqm=None, EB=None):
        OB = [3, 4, 6, 7]

        def finish(c):
            ptv = psbf(5).rearrange("p (a b) -> p a b", a=8)
            for t in range(4):
                ob = OB[t]
                pov = psb[ob][:, 0:260].rearrange("p (h c) -> p h c", h=4)
                rec = small[:, t * 4:t * 4 + 4]
                P.op("dve", (lambda rec, pov: lambda e: e.reciprocal(out=rec.unsqueeze(2), in_=pov[:, :, 64:65]))(rec, pov),
                     R=[("ps", ob)], W=[("rec", t)])
                yt = ytile4[t]
                P.op("dve", tt(yt.rearrange("p (h c) -> p h c", h=4), pov[:, :, 0:64], rec.unsqueeze(2).broadcast_to([128, 4, 64]), ALU.mult),
                     R=[("ps", ob), ("rec", t)], W=[("ytile", t)])
                for c2 in range(2):
                    P.op("pe", tr(ptv[:, c2 * 4 + t, :], yt[:, c2 * 128:(c2 + 1) * 128], ident_bf[:]), R=[("ytile", t), "ident_bf"], W=[("ps", 5)])
            P.op("act", cp_act(yT[:, nbr, 2 * hf:2 * hf + 2, c * 512:(c + 1) * 512], psbf(5).rearrange("p (a b) -> p a b", a=2)),
                 R=[("ps", 5)], W=[("yT", nbr)])

        if prep is not None:
            prep(0)
        for c in range(4):
            j_lo = max(0, 4 * c - 4) if kind == "C" else 0
            items = [(hh, j) for hh in range(4) for j in range(j_lo, 4 * c + 4)]
            pend = []

            def second(hh, j, sb, pb, t0, t1, c=c):
                cols = slice(t0 * 128, (t1 + 1) * 128)
                if kind == "A":
                    h = 4 * hf + hh
                    P.op("act", act(Pt[pb][:, cols], psb[sb][:, cols], AF.Exp, bias=biasA[c % 2][:, j, h:h + 1], scale=scale),
                         R=[("ps", sb), ("biasA", c % 2)], W=[("Pt", pb)])
                else:
                    P.op("act", act(Pt[pb][:, cols], psb[sb][:, cols], AF.Exp, scale=scale), R=[("ps", sb)], W=[("Pt", pb)])
                if kind == "C":
                    d0 = 4 * c + t0 - j
                    P.op(MASK_ENG, tt(Pt[pb][:, cols], Pt[pb][:, cols], EB[:, hh, d0 * 128:(d0 + t1 - t0 + 1) * 128], ALU.mult),
                         R=["EBrev"], W=[("Pt", pb)])
                for t in range(t0, t1 + 1):
                    i = 4 * c + t
                    first_j = max(0, i - 4) if kind == "C" else 0
                    pov = psb[OB[t]][:, 0:260].rearrange("p (h c) -> p h c", h=4)
                    P.op("pe", mm(pov[:, hh, :], Pt[pb][:, t * 128:(t + 1) * 128], vaug[:, j, hh, :], j == first_j, j == i),
                         R=[("Pt", pb), "v"], W=[("ps", OB[t])])

            for (hh, j) in items:
                t0 = max(0, j - 4 * c)
                t1 = 3 if kind != "C" else min(3, j + 4 - 4 * c)
                cols = slice(t0 * 128, (t1 + 1) * 128)
                nsb = 3 if kind == "C" else 2
                sb = attn_ctr[0] % nsb
                pb = attn_ctr[0] % 3
                attn_ctr[0] += 1
                diag = (kind == "A") and j >= 4 * c
                P.op("pe", mm(psb[sb][:, cols], kslice(hh, j), qchunk(c, hh)[:, cols], True, kind != "A"),
                     R=["kT", "qT"], W=[("ps", sb)])
                if kind == "A":
                    P.op("pe", mm(psb[sb][:, cols], ones_bf[:], rqm[:, hh, cols], False, True), R=["rqm", "ones_bf"], W=[("ps", sb)])
                    if diag:
                        P.op("pe", lambda e, o=psb[sb][:, t0 * 128:(t0 + 1) * 128]: e.matmul(o, ident_bf[:], maskA[:], start=False, stop=True, skip_group_check=True),
                             R=["maskA", "ident_bf"], W=[("ps", sb)])
                pend.append((hh, j, sb, pb, t0, t1))
                if len(pend) > nsb - 1:
                    second(*pend.pop(0))
            while pend:
                second(*pend.pop(0))
            if prep is not None and c + 1 < 4:
                prep(c + 1)
            finish(c)

    def v_proj(l, col0, hf, wv_unused, vaug):
        wv2 = bv(ZW_ref[0], 2048).rearrange("p (k c) -> p k c", k=8)
        keys = [("wqk", 0), ("wqk", 1)]
        P.dma("pool", dmaf(wv2, w_in[l, :, col0 + hf * 256: col0 + (hf + 1) * 256].rearrange("(k p) c -> p k c", p=128)), W=keys, sem="wv2")
        for i in range(NT):
            bank = 6 + (i % 2)
            for kc in range(8):
                P.op("pe", mm(psb[bank][:, 0:256], hT[:, kc, i * 128:(i + 1) * 128], wv2[:, kc, :], kc == 0, kc == 7),
                     R=keys + [("hT", i // 4)], W=[("ps", bank)])
            P.op("act", cp_act(vaug[:, i, :, 0:64], psb[bank][:, 0:256].rearrange("p (h c) -> p h c", h=4)), R=[("ps", bank)], W=["v"])

    ZW_ref = [None]
    wqk_ref = [None]

    z = Z0
    ZK = z
    kT_ac = bv(z, 4096).rearrange("p (m t) -> p m t", m=2)
    kTm = bv(z, 8192).rearrange("p (m t) -> p m t", m=4)
    kT_b = bv(z, 8192).rearrange("p (m t) -> p m t", m=4)
    z += 8192
    ZQ = z
    qT_ac = bv(z, 4096).rearrange("p (m t) -> p m t", m=2)
    qTt = [bv(z + k * 512, 512).rearrange("p (h t) -> p h t", h=4) for k in range(2)]
    z += 4096
    vaug = bv(z, 4160).rearrange("p (i h c) -> p i h c", i=NT, h=4)
    z += 4160
    rs_extra[0] = fv(z + 2048, 512)
    sqb = [[bv(z + (u * 3 + c) * 512, 512) for c in range(3)] for u in range(2)]
    Pt = [bv(z + k * 512, 512) for k in range(3)]
    ytile = [bv(z + 1536 + k * 256, 256) for k in range(2)]
    ytile4 = [bv(z + 1536 + k * 256, 256) for k in range(4)]
    z += 3072
    rsb = [fv(z + u * 1024, 512) for u in range(2)]
    z += 2048
    ZW = z
    ZW_ref[0] = z
    wqk = [(bv(z + k * 1024, 1024).rearrange("p (k c) -> p k c", k=8), ("wqk", k)) for k in range(4)]
    z += 4096
    wv = None
    wqk_ref[0] = wqk
    assert z <= AR_N, z

    def set_vones():
        P.op("pool", ms(vaug[:, :, :, 64:65], 1.0), W=["v"])

    def phase_A(l):
        z = 16384
        wf = bv(z, 64).rearrange("p (k c) -> p k c", k=8); z += 64
        zt = fv(z, 128); z += 256
        lp = fv(z, 128); z += 256
        Lp = fv(z, 128).rearrange("p (i h) -> p i h", i=NT); z += 256
        PTt = fv(z, 128).rearrange("p (i h) -> p i h", i=NT); z += 256
        biasA = [fv(z + k * 256, 128).rearrange("p (i h) -> p i h", i=NT) for k in range(2)]; z += 512
        rq_bf = bv(z, 128); z += 128
        rqm = bv(z, 2048).rearrange("p (h t) -> p h t", h=4); z += 2048
        assert z <= 24576, z
        P.op("dve", ms(bv(ZK, 8192), 0.0), W=["kT"])
        loc = zt
        set_vones()
        P.op("dve", ms(rqm, 0.0), W=["rqm"])
        wload(wf, w_in[l, :, C_FA:C_FA + 8], "wf")
        for i in range(NT):
            for kc in range(8):
                P.op("pe", mm(psb[5][:, i * 8:(i + 1) * 8], hT[:, kc, i * 128:(i + 1) * 128], wf[:, kc, :], kc == 0, kc == 7),
                     R=["wf", ("hT", i // 4)], W=[("ps", 5)])
        ztv = zt.rearrange("p (i h) -> p i h", i=NT)
        P.op("dve", tt(ztv, psb[5][:, 0:128].rearrange("p (i h) -> p i h", i=NT),
                       bf_rep[:, l * 8:(l + 1) * 8].unsqueeze(1).broadcast_to([128, NT, 8]), ALU.add), R=[("ps", 5), "bf_rep"], W=["zt"])
        P.op("act", act(zt, zt, AF.Exp, scale=-1.0), R=["zt"], W=["zt"])
        P.op("act", act(lp, zt, AF.Ln, bias=cst[:, 0:1]), R=["zt", "cst"], W=["lp"])
        lpv = lp.rearrange("p (i h) -> p i h", i=NT)
        for i in range(NT):
            P.op("pe", mm(psb[7][:, i * 8:(i + 1) * 8], U_f[:], lpv[:, i, :], True, True), R=["lp", "U_f"], W=[("ps", 7)])
            P.op("pe", mm(psb[6][:, i * 8:(i + 1) * 8], ones_f[:], lpv[:, i, :], True, True), R=["lp", "ones_f"], W=[("ps", 6)])
        P.op("dve", ms(PTt[:, 0, :], 0.0), W=["PT"])
        for i in range(1, NT):
            P.op("dve", tt(PTt[:, i, :], PTt[:, i - 1, :], psb[6][:, (i - 1) * 8:i * 8], ALU.add), R=[("ps", 6)], W=["PT"])
        P.op("dve", tt(Lp, psb[7][:, 0:128].rearrange("p (i h) -> p i h", h=8), PTt, ALU.add), R=[("ps", 7), "PT"], W=["Lp"])
        locv = loc.rearrange("p (i h) -> p i h", i=NT)

        for hf in range(2):
            for c_ in range(2):
                proj_norm(l, C_KA + hf * 256 + c_ * 128, 1, blockones, 64, 96 + 2 * l + 1, [None], "kT", [wqk[c_]], sqb, rsb,
                          split=(kTm[:, 2 * c_, :], kTm[:, 2 * c_ + 1, :]))
            for c_ in range(2):
                proj_norm(l, C_QA + hf * 256 + c_ * 128, 1, blockones, 64, 96 + 2 * l, [qT_ac[:, c_, :]], "qT", [wqk[2 + c_]], sqb, rsb)
            v_proj(l, C_VA, hf, wv, vaug)
            P.barrier()
            def prepA(c, hf=hf):
                P.op("dve", tt(biasA[c % 2][:, 0:4 * c + 4, :], Lp[:, 0:4 * c + 4, :], PTt[:, 4 * c:4 * c + 1, :].broadcast_to([128, 4 * c + 4, 8]), ALU.subtract),
                     R=["Lp", "PT"], W=[("biasA", c % 2)])
                P.op("dve", tt(locv[:, 0:4, :], Lp[:, 4 * c:4 * c + 4, :], PTt[:, 4 * c:4 * c + 1, :].broadcast_to([128, 4, 8]), ALU.subtract),
                     R=["Lp", "PT", "zt"], W=["loc"])
                P.op("dve", ts(rq_bf[:, 0:32], loc[:, 0:32], -8.0, None, ALU.mult), R=["loc"], W=["rq_bf"])
                ptv = psbf(2).rearrange("p (a b) -> p a b", a=8)
                for t in range(4):
                    P.op("pe", tr(ptv[0:8, t, :], rq_bf[:, t * 8:(t + 1) * 8], ident_bf[:]), R=["rq_bf", "ident_bf"], W=[("ps", 2)])
                P.op("dve", tt(rqm[0:8, :, :], psbf(2)[0:8, 0:512].unsqueeze(1).broadcast_to([8, 4, 512]),
                               ident_f[0:8, 4 * hf:4 * hf + 4].unsqueeze(2).broadcast_to([8, 4, 512]), ALU.mult),
                     R=[("ps", 2), "ident_f"], W=["rqm"])

            attention2("A", hf, 0,
                       lambda hh, j: kTm[:, hh, j * 128:(j + 1) * 128],
                       lambda c, hh: qT_ac[:, hh // 2, c * 512:(c + 1) * 512],
                       vaug, Pt, ytile4, 0.125, biasA=biasA, prep=prepA, rqm=rqm)
            P.barrier()
            if DBG == 2 and l == 0 and hf == 1:
                P.dma("sp", dmaf(dbg_d, arena[:, Z0:Z0 + 24576]), sem="dbg")
                P.barrier()

    def phase_C(l):
        EBrev = bv(ZW, 2560).rearrange("p (h c) -> p h c", h=4)
        Tst = [fv(ZW + 2560, 640)]
        set_vones()
        P.op("dve", ms(bv(ZK, 8192), 0.0), W=["kT"])
        for hf in range(2):
            for c_ in range(2):
                proj_norm(l, C_KC + hf * 256 + c_ * 128, 1, blockones, 64, 104 + 2 * l + 1, [None], "kT", [wqk[c_]], sqb, rsb,
                          split=(kTm[:, 2 * c_, :], kTm[:, 2 * c_ + 1, :]))
            for c_ in range(2):
                proj_norm(l, C_QC + hf * 256 + c_ * 128, 1, blockones, 64, 104 + 2 * l, [qT_ac[:, c_, :]], "qT", [wqk[2 + c_]], sqb, rsb)
            v_proj(l, C_VC, hf, wv, vaug)
            P.barrier()
            for hh in range(4):
                h = 4 * hf + hh
                src = relx[l * 8 + h, :, :]
                P.dma("sp", dmaf(Tst[0], src), W=[("Tst", 0)], sem="Tst0")
                P.op("act", act(Tst[0], Tst[0], AF.Exp), R=[("Tst", 0)], W=[("Tst", 0)])
                for d in range(5):
                    P.op("dve", tt(EBrev[:, hh, d * 128:(d + 1) * 128], Tst[0][:, d * 128:(d + 1) * 128], validC[:, d * 128:(d + 1) * 128], ALU.mult),
                         R=[("Tst", 0), "validC"], W=["EBrev"])
            attention2("C", hf, 2,
                       lambda hh, j: kTm[:, hh, j * 128:(j + 1) * 128],
                       lambda c, hh: qT_ac[:, hh // 2, c * 512:(c + 1) * 512],
                       vaug, Pt, ytile4, 0.125, EB=EBrev)
            P.barrier()

    def rope_ops(dst1, dst2, x1, x2, cosv, sinv, t1, t2, Rk, Wk):
        P.op("dve", tt(t1, x1, cosv, ALU.mult), R=Rk, W=["rt1"])
        P.op("dve", tt(t2, x2, sinv, ALU.mult), R=Rk, W=["rt2"])
        P.op("dve", tt(dst1, t1, t2, ALU.subtract), R=["rt1", "rt2"], W=Wk)
        P.op("dve", tt(t1, x2, cosv, ALU.mult), R=Rk, W=["rt1"])
        P.op("dve", tt(t2, x1, sinv, ALU.mult), R=Rk, W=["rt2"])
        P.op("dve", tt(dst2, t1, t2, ALU.add), R=["rt1", "rt2"], W=Wk)

    def phase_B(l):
        z = ZQ + 1024
        wkr = bv(z, 256).rearrange("p (k c) -> p k c", k=8); z += 256
        wqup = bv(z, 1152).rearrange("p (k c) -> p k c", k=3); z += 1152
        wkvup = bv(z, 1024).rearrange("p (k c) -> p k c", k=2); z += 1024
        krope = bv(z, 512).rearrange("p (i c) -> p i c", i=NT); z += 512
        assert z <= ZQ + 4096
        z = 6144
        tmpf = fv(z, 512); z += 1024
        tmpq = fv(z, 384).rearrange("p (h c) -> p h c", h=4); z += 768
        assert z <= 8192
        z = 16384 + 4096
        ktok = [bv(z + k * 512, 512).rearrange("p (h c) -> p h c", h=4) for k in range(2)]; z += 1024
        qtok = [bv(z + k * 512, 512).rearrange("p (h c) -> p h c", h=4) for k in range(2)]; z += 1024
        for k_ in range(2):
            P.op("pool", ms(ktok[k_], 0.0), W=[("ktok", k_)])
            P.op("pool", ms(qtok[k_], 0.0), W=[("qtok", k_)])
        rt1 = fv(z, 256); z += 512
        rt2 = fv(z, 256); z += 512
        kvsb = fv(z, 512); z += 1024
        assert z <= 24576, z
        qdnT = bv(0, 6144).rearrange("p (c t) -> p c t", c=3)
        kvdnT = bv(16384, 4096).rearrange("p (c t) -> p c t", c=2)
        set_vones()
        proj_norm(l, C_QD, 3, ones_bf, 384, 112 + 3 * l, [qdnT[:, c, :] for c in range(3)], "qdnT", wqk[0:3], sqb, rsb, gstep=1)
        proj_norm(l, C_KVD, 2, ones_bf, 256, 124 + 2 * l, [kvdnT[:, c, :] for c in range(2)], "kvdnT", [wqk[3], wqk[0]], sqb, rsb, gstep=1)
        if BSTOP == 1:
            P.op("pool", ms(yT[:, 1, :, :], 0.0), W=[("yT", 1)]); P.barrier(); return
        wload(wkr, w_in[l, :, C_KR:C_KR + 32], "wkr")
        for i in range(NT):
            for kc in range(8):
                P.op("pe", mm(psb[5][:, i * 32:(i + 1) * 32], hT[:, kc, i * 128:(i + 1) * 128], wkr[:, kc, :], kc == 0, kc == 7),
                     R=["wkr", ("hT", i // 4)], W=[("ps", 5)])
        pk = psb[5][:].rearrange("p (i c) -> p i c", i=NT)
        tfv = tmpf.rearrange("p (i c) -> p i c", i=NT)
        sm = small[:, 16:32]
        P.op("act", act(tmpf, psb[5][:], AF.Square), R=[("ps", 5)], W=["tmpf"])
        P.op("dve", red(sm, tfv), R=["tmpf"], W=["sm"])
        rsqrt_small(sm, sm, 1.0 / 32, EPS, ["sm"], ["sm"])
        P.op("dve", tt(tfv, pk, sm.unsqueeze(2).broadcast_to([128, NT, 32]), ALU.mult), R=[("ps", 5), "sm"], W=["tmpf"])
        gk = gbr[:, l * 64 + 32:l * 64 + 64]
        P.op("dve", tt(tfv, tfv, gk.unsqueeze(1).broadcast_to([128, NT, 32]), ALU.mult), R=["tmpf", "gbr"], W=["tmpf"])
        r1 = rt1.rearrange("p (i c) -> p i c", i=NT)
        r2 = rt2.rearrange("p (i c) -> p i c", i=NT)
        rope_ops(krope[:, :, 0:16], krope[:, :, 16:32], tfv[:, :, 0:16], tfv[:, :, 16:32], cs[:, :, 0:16], cs[:, :, 16:32],
                 r1, r2, ["tmpf", "cs"], ["krope"])
        P.barrier()
        if BSTOP == 2:
            P.op("pool", ms(yT[:, 1, :, :], 0.0), W=[("yT", 1)]); P.barrier(); return
        kvK = fv(ZW, 512)
        tK = fv(ZW + 1024, 256).rearrange("p (h c) -> p h c", h=4)
        kvQ = fv(ZW + 1536, 384)
        tQ = fv(ZW + 2304, 384).rearrange("p (h c) -> p h c", h=4)
        gg = fv(ZW + 3072, 64)
        epsc = small[:, 48:49]
        P.op("dve", tt(gg, gbn[:, l * 128:l * 128 + 64], gbn[:, l * 128 + 64:l * 128 + 128], ALU.mult), R=["gbn"], W=["gg"])
        for hf in range(2):
            wload(wkvup, w_kv_up[l, :, hf * 512:(hf + 1) * 512], "wkvup")
            wload(wqup, w_q_up[l, :, hf * 384:(hf + 1) * 384], "wqup")

            def kchain(i):
                a_, b_ = [], []
                for kc in range(2):
                    a_.append(("pe", mm(psb[6][:], kvdnT[:, kc, i * 128:(i + 1) * 128], wkvup[:, kc, :], kc == 0, kc == 1), ["kvdnT", "wkvup"], [("ps", 6)]))
                pkv = kvK.rearrange("p (h c) -> p h c", h=4)
                s4 = small[:, 32:36]
                kt = ktok[i % 2]
                a_.append(("act", cp_act(kvK, psb[6][:]), [("ps", 6)], ["kvK"]))
                a_.append(("act", act(tK, pkv[:, :, 0:64], AF.Square), ["kvK"], ["tK"]))
                a_.append(("dve", red(s4, tK), ["tK"], ["s4"]))
                a_.append(("act", act(s4, s4, AF.Ln, bias=epsc, scale=1.0 / 64), ["s4", "epscol"], ["s4"]))
                a_.append(("act", act(s4, s4, AF.Exp, scale=-0.5), ["s4"], ["s4"]))
                a_.append(("dve", tt(tK, pkv[:, :, 0:64], s4.unsqueeze(2).broadcast_to([128, 4, 64]), ALU.mult), ["kvK", "s4"], ["tK"]))
                a_.append(("dve", tt(kt[:, :, 0:64], tK, gg.unsqueeze(1).broadcast_to([128, 4, 64]), ALU.mult), ["tK", "gg"], [("ktok", i % 2)]))
                a_.append(("dve", cp(kt[:, :, 64:96], krope[:, i, :].unsqueeze(1).broadcast_to([128, 4, 32])), ["krope"], [("ktok", i % 2)]))
                a_.append(("act", cp_act(vaug[:, i, :, 0:64], pkv[:, :, 64:128]), ["kvK"], [("v", i)]))
                ptv = psbf(6).rearrange("p (a b) -> p a b", a=8)
                for hh in range(4):
                    b_.append(("pe", tr(ptv[:, hh, :], kt[:, hh, :], ident_bf[:]), [("ktok", i % 2), "ident_bf"], [("ps", 6)]))
                b_.append(("act", cp_act(kT_b[0:96, :, i * 128:(i + 1) * 128], ptv[0:96, 0:4, :]), [("ps", 6)], [("kT", i)]))
                return a_, b_

            def qchain(i):
                a_, b_ = [], []
                for kc in range(3):
                    a_.append(("pe", mm(psb[7][:, 0:384], qdnT[:, kc, i * 128:(i + 1) * 128], wqup[:, kc, :], kc == 0, kc == 2), ["qdnT", "wqup"], [("ps", 7)]))
                pq = kvQ.rearrange("p (h c) -> p h c", h=4)
                sn = small[:, 36:40]
                sr = small[:, 40:44]
                s8 = small[:, 36:44]
                qt = qtok[i % 2]
                a_.append(("act", cp_act(kvQ, psb[7][:, 0:384]), [("ps", 7)], ["kvQ"]))
                a_.append(("act", act(tQ, pq, AF.Square), ["kvQ"], ["tQ"]))
                a_.append(("dve", red(sn, tQ[:, :, 0:64]), ["tQ"], ["s8"]))
                a_.append(("dve", red(sr, tQ[:, :, 64:96]), ["tQ"], ["s8"]))
                a_.append(("act", act(sn, sn, AF.Ln, bias=epsc, scale=1.0 / 64), ["s8", "epscol"], ["s8"]))
                a_.append(("act", act(sr, sr, AF.Ln, bias=epsc, scale=1.0 / 32), ["s8", "epscol"], ["s8"]))
                a_.append(("act", act(s8, s8, AF.Exp, scale=-0.5), ["s8"], ["s8"]))
                a_.append(("dve", tt(qt[:, :, 0:64], pq[:, :, 0:64], sn.unsqueeze(2).broadcast_to([128, 4, 64]), ALU.mult), ["kvQ", "s8"], [("qtok", i % 2)]))
                a_.append(("dve", tt(tmpq[:, :, 64:96], pq[:, :, 64:96], sr.unsqueeze(2).broadcast_to([128, 4, 32]), ALU.mult), ["kvQ", "s8"], ["tmpq"]))
                gqr = gbr[:, l * 64:l * 64 + 32]
                a_.append(("dve", tt(tmpq[:, :, 64:96], tmpq[:, :, 64:96], gqr.unsqueeze(1).broadcast_to([128, 4, 32]), ALU.mult), ["tmpq", "gbr"], ["tmpq"]))
                c4 = cs[:, i, 0:16].unsqueeze(1).broadcast_to([128, 4, 16])
                s4b = cs[:, i, 16:32].unsqueeze(1).broadcast_to([128, 4, 16])
                q1 = rt1[:, 0:64].rearrange("p (h c) -> p h c", h=4)
                q2 = rt2[:, 0:64].rearrange("p (h c) -> p h c", h=4)
                x1, x2 = tmpq[:, :, 64:80], tmpq[:, :, 80:96]
                Rk, Wk = ["tmpq", "cs"], [("qtok", i % 2)]
                a_.append(("dve", tt(q1, x1, c4, ALU.mult), Rk, ["rt1"]))
                a_.append(("dve", tt(q2, x2, s4b, ALU.mult), Rk, ["rt2"]))
                a_.append(("dve", tt(qt[:, :, 64:80], q1, q2, ALU.subtract), ["rt1", "rt2"], Wk))
                a_.append(("dve", tt(q1, x2, c4, ALU.mult), Rk, ["rt1"]))
                a_.append(("dve", tt(q2, x1, s4b, ALU.mult), Rk, ["rt2"]))
                a_.append(("dve", tt(qt[:, :, 80:96], q1, q2, ALU.add), ["rt1", "rt2"], Wk))
                ptv = psbf(7).rearrange("p (a b) -> p a b", a=8)
                for hh in range(4):
                    b_.append(("pe", tr(ptv[:, hh, :], qt[:, hh, :], ident_bf[:]), [("qtok", i % 2), "ident_bf"], [("ps", 7)]))
                b_.append(("act", cp_act(qTt[i % 2][0:96, :, :], ptv[0:96, 0:4, :]), [("ps", 7)], [("qTt", i % 2)]))
                return a_, b_

            chains = {}

            def emit_zip(la, lb):
                n = max(len(la), len(lb))
                for k_ in range(n):
                    for lst in (la, lb):
                        if k_ < len(lst):
                            e_, f_, r_, w_ = lst[k_]
                            P.op(e_, f_, R=r_, W=w_)

            def prep_a(i):
                ka, kb = kchain(i)
                qa, qb = qchain(i)
                chains[i] = (kb, qb)
                emit_zip(ka, qa)

            def prep_b(i):
                kb, qb = chains.pop(i)
                emit_zip(kb, qb)

            attention("B", hf, 1,
                      lambda hh, j: kT_b[0:96, hh, j * 128:(j + 1) * 128],
                      lambda hh, i: qTt[i % 2][0:96, hh, :],
                      lambda i: ("qTt", i % 2), vaug, Pt, ytile, float(96.0 ** -0.5), qprep=(prep_a, prep_b),
                      kkey=lambda j: ("kT", j), vkey=lambda j: ("v", j))
            P.barrier()

    def merge_phase(l):
        z = Z0
        mT = bv(z, 8192).rearrange("p (c t) -> p c t", c=8); z += 8192
        wg = [bv(z + k * 3072, 3072).rearrange("p (k n c) -> p k n c", k=8, n=3) for k in range(2)]; z += 6144
        wb = [bv(z + k * 1536, 1536).rearrange("p (k n c) -> p k n c", k=4, n=3) for k in range(2)]; z += 3072
        wo = [bv(z + k * 2048, 2048).rearrange("p (k c) -> p k c", k=8) for k in range(2)]; z += 4096
        gate = [bv(z + k * 512, 512) for k in range(3)]; z += 1536
        acc = fv(z, 512); z += 1024
        tmp = fv(z, 512); z += 1024
        assert z <= AR_N, z
        cnt = 0
        for th in range(2):
            for m in range(8):
                sl = cnt % 2
                cnt += 1
                for n in range(3):
                    c0 = C_G + n * 1024 + m * 128
                    P.dma("pool", dmaf(wg[sl][:, :, n, :], w_in[l, :, c0:c0 + 128].rearrange("(k p) c -> p k c", p=128)), W=[("wg", sl, n)], sem="wg%d_%d" % (sl, n))
                    P.dma("pool", dmaf(wb[sl][:, :, n, :], w_branch[l, n, :, m * 128:(m + 1) * 128].rearrange("(k p) c -> p k c", p=128)), W=[("wb", sl, n)], sem="wb%d_%d" % (sl, n))
                for tq in range(2):
                    tc = th * 2 + tq
                    tsl = slice(tc * 512, (tc + 1) * 512)
                    u = (m * 2 + tq) % 2
                    for n in range(3):
                        gb = u * 4 + n if n < 2 else u * 4 + 2
                        gb = u * 4 + n
                        for kc in range(8):
                            P.op("pe", mm(psb[gb][:], wg[sl][:, kc, n, :], hT[:, kc, tsl], kc == 0, kc == 7), R=[("wg", sl, n), ("hT", tc)], W=[("ps", gb)])
                        P.op("act", act(gate[n], psb[gb][:], AF.Sigmoid, bias=vecT[:, l * 24 + n * 8 + m:l * 24 + n * 8 + m + 1]),
                             R=[("ps", gb), "vecT"], W=[("gate", n)])
                        pb = u * 4 + 3
                        for kc in range(4):
                            P.op("pe", mm(psb[pb][:], wb[sl][:, kc, n, :], yT[:, n, kc, tsl], kc == 0, kc == 3), R=[("wb", sl, n), ("yT", n)], W=[("ps", pb)])
                        if n == 0:
                            P.op("dve", tt(acc, psb[pb][:], gate[n], ALU.mult), R=[("ps", pb), ("gate", n)], W=["acc"])
                        else:
                            P.op("dve", tt(tmp, psb[pb][:], gate[n], ALU.mult), R=[("ps", pb), ("gate", n)], W=["tmp"])
                            if n == 1:
                                P.op("dve", tt(acc, acc, tmp, ALU.add), R=["tmp"], W=["acc"])
                            else:
                                P.op("dve", tt(mT[:, m, tq * 512:(tq + 1) * 512], acc, tmp, ALU.add), R=["tmp", "acc"], W=["mT"])
            for cq in range(4):
                sl = cq % 2
                wload(wo[sl], w_out[l, :, cq * 256:(cq + 1) * 256], ("wo", sl))
                for ii in range(8):
                    i = th * 8 + ii
                    bank = ii % 2 + 6 if False else (ii % 4)
                    for kc in range(8):
                        P.op("pe", mm(psb[bank][:, 0:256], mT[:, kc, ii * 128:(ii + 1) * 128], wo[sl][:, kc, :], kc == 0, kc == 7),
                             R=["mT", ("wo", sl)], W=[("ps", bank)])
                    P.op("dve", tt(xs[:, i, cq * 256:(cq + 1) * 256], xs[:, i, cq * 256:(cq + 1) * 256], psb[bank][:, 0:256], ALU.add),
                         R=[("ps", bank)], W=[("x", i)])
        P.barrier()

    def ffn_phase(l):
        aT = [bv(0, 16384).rearrange("p (c t) -> p c t", c=8), bv(Z0, 16384).rearrange("p (c t) -> p c t", c=8)]
        w2 = [bv(16384 + k * 4096, 4096).rearrange("p (k c) -> p k c", k=8) for k in range(2)]
        z = Z0 + 16384
        w1 = [bv(z + k * 2048, 2048).rearrange("p (k c) -> p k c", k=8) for k in range(2)]; z += 4096
        rt = [fv(z + k * 1024, 512) for k in range(2)]; z += 2048
        assert z <= AR_N, z
        c1 = 0
        c2 = 0
        bk = 0
        for g in range(4):
            a = aT[g % 2]
            for fp in range(4):
                sl = c1 % 2
                c1 += 1
                wload(w1[sl], w_ff1[l, :, g * 1024 + fp * 256: g * 1024 + (fp + 1) * 256], ("w1", sl))
                for f2 in range(2):
                    f = fp * 2 + f2
                    for tc in range(4):
                        bank = bk % 4
                        bk += 1
                        for kc in range(8):
                            P.op("pe", mm(psb[bank][:], w1[sl][:, kc, f2 * 128:(f2 + 1) * 128], hT[:, kc, tc * 512:(tc + 1) * 512], kc == 0, kc == 7),
                                 R=[("w1", sl), ("hT", tc)], W=[("ps", bank)])
                        r = rt[bank % 2]
                        P.op("act", act(r, psb[bank][:], AF.Relu), R=[("ps", bank)], W=[("rt", bank % 2)])
                        P.op(SQ_ENG, tt(a[:, f, tc * 512:(tc + 1) * 512], r, r, ALU.mult), R=[("rt", bank % 2)], W=[("aT", g % 2)])
            for ch in range(2):
                sl = c2 % 2
                c2 += 1
                wload(w2[sl], w_ff2[l, g * 1024:(g + 1) * 1024, ch * 512:(ch + 1) * 512], ("w2", sl))
                for i in range(NT):
                    bank = 4 + (i % 4)
                    for f in range(8):
                        P.op("pe", mm(psb[bank][:], a[:, f, i * 128:(i + 1) * 128], w2[sl][:, f, :], f == 0, f == 7),
                             R=[("aT", g % 2), ("w2", sl)], W=[("ps", bank)])
                    P.op("dve", tt(xs[:, i, ch * 512:(ch + 1) * 512], xs[:, i, ch * 512:(ch + 1) * 512], psb[bank][:], ALU.add),
                         R=[("ps", bank)], W=[("x", i)])
        P.barrier()

    MASK_ENG = "dve"
    SQ_ENG = "dve"
    for l in range(nlayers):
        P.tag = "norm1"
        if "N" in PH:
            norm_phase(norm_mix[l, :])
        for nb_, ch_ in enumerate("ABC"):
            if ch_ not in PH:
                P.op("pool", ms(yT[:, nb_, :, :], 0.0), W=[("yT", nb_)])
        P.tag = "B"
        if "B" in PH:
            phase_B(l)
        P.tag = "A"
        if "A" in PH:
            phase_A(l)
        P.tag = "C"
        if "C" in PH:
            phase_C(l)
        P.tag = "merge"
        if DBG == 1 and l == 0:
            P.barrier()
            P.dma("sp", dmaf(dbg_d, arena[:, 0:24576]), R=[("yT", 0), ("yT", 1), ("yT", 2)], sem="dbg")
            P.barrier()
        if "M" in PH:
            merge_phase(l)
        if "F" in PH:
            P.tag = "norm2"
            norm_phase(norm_ffn[l, :])
            P.tag = "ffn"
            ffn_phase(l)
    for i in range(NT):
        P.dma("sp", dmaf(y_d[i * 128:(i + 1) * 128, :], xs[:, i, :]), R=[("x", i)], sem="y%d" % i)
    P.barrier()
    P.emit(nc, st)
    st.close()
    return nc


def _host_consts(rel_bias):
    half = 16
    inv = (10000.0 ** (-np.arange(half, dtype=np.float32) / half)).astype(np.float32)
    ang = np.arange(S, dtype=np.float32)[:, None] * inv[None, :]
    cs_tab = np.concatenate([np.cos(ang), np.sin(ang)], axis=1).astype(np.float32)
    kl = np.arange(128)[:, None]
    c = np.arange(640)[None, :]
    idx = np.clip(c - kl, -256, 256) + 256
    rel_ext = np.ascontiguousarray(rel_bias[:, :, idx]).reshape(DEPTH * 8, 128, 640).astype(np.float32)
    return cs_tab, rel_ext


PH = "NBACMF"
ANNOT = False
DBG = False
BSTOP = 0
_NC_CACHE = {}


def kernel(**inputs):
    inp = {k: np.ascontiguousarray(np.asarray(v, dtype=np.float32)) for k, v in inputs.items()}
    x = inp.pop("x")
    rel_bias = inp.pop("rel_bias")
    cs_tab, rel_ext = _host_consts(rel_bias)
    key = (DEPTH, PH)
    if key not in _NC_CACHE:
        _NC_CACHE[key] = build(DEPTH)
    nc = _NC_CACHE[key]
    B = x.shape[0]
    in_maps = []
    for b in range(B):
        m = dict(inp)
        m["x"] = np.ascontiguousarray(x[b])
        m["rel_ext"] = rel_ext
        m["cs_tab"] = cs_tab
        in_maps.append(m)
    res = run_bass_kernel_spmd(nc, in_maps, core_ids=list(range(B)))
    return np.stack([np.asarray(r["y"], dtype=np.float32) for r in res.results], axis=0)
```
